# Optimizing a Trainium2 kernel written in Bass

```python
import functools
import jax, jax.numpy as jnp
from jax import lax
import numpy as np

D_MODEL = 1024
BATCH = 2
SEQ = 8192
DEPTH = 1
DEC_BATCH = 128
DEC_SEQ = 1
PAST_LEN = 16384
PAGE_SIZE = 128

N_HEADS = 8
N_KV_HEADS = 2
HEAD_DIM = 64
GROUP = N_HEADS // N_KV_HEADS
WINDOW = 128
BLOCK = WINDOW
D_ATTN = N_HEADS * HEAD_DIM
D_KV = N_KV_HEADS * HEAD_DIM
D_CONV = D_MODEL // 2
CONV_W = 3
D_FF = 2816
FFN_CONV_W = 3
SPLITS = [D_ATTN, D_KV, D_KV, D_CONV, D_CONV, D_CONV, D_MODEL, D_MODEL]
N_PROJ = sum(SPLITS)
DN_ALPHA = (2 * DEPTH) ** 0.25
DN_BETA = (8 * DEPTH) ** -0.25
LN_EPS = 1e-5
NEG_INF = -1e30

kernel_name = "hybrid_swa_sink_alibi_shortconv_convffn_deepnorm_step"


def alibi_slopes():
    s = 2.0 ** (-8.0 * np.arange(1, N_HEADS + 1) / N_HEADS)
    return jnp.asarray(s, dtype=jnp.float32).reshape(N_KV_HEADS, GROUP)


def layer_norm(x, g, b):
    xf = x.astype(jnp.float32)
    mu = jnp.mean(xf, axis=-1, keepdims=True)
    var = jnp.mean(jnp.square(xf - mu), axis=-1, keepdims=True)
    y = (xf - mu) * lax.rsqrt(var + LN_EPS) * g.astype(jnp.float32) + b.astype(jnp.float32)
    return y.astype(x.dtype)


def causal_dwconv(hist, u, w):
    k_w = w.shape[0]
    full = jnp.concatenate([hist, u], axis=1)
    n_l = u.shape[1]
    y = sum(w[i] * full[:, i:i + n_l] for i in range(k_w))
    return y, full[:, -(k_w - 1):]


def band_softmax(q, k, v, dist, valid, sinks):
    scale = HEAD_DIM ** -0.5
    slopes = alibi_slopes()[:, :, None, None]
    s = jnp.einsum('...qkgd,...skd->...kgqs', q, k).astype(jnp.float32) * scale
    s = s - slopes * dist.astype(jnp.float32)
    s = jnp.where(valid, s, NEG_INF)
    sink = sinks.astype(jnp.float32).reshape(N_KV_HEADS, GROUP)[:, :, None, None]
    m = jnp.maximum(jnp.max(s, axis=-1, keepdims=True), sink)
    p = jnp.exp(s - m)
    denom = jnp.sum(p, axis=-1, keepdims=True) + jnp.exp(sink - m)
    probs = (p / denom).astype(v.dtype)
    return jnp.einsum('...kgqs,...skd->...qkgd', probs, v)


def attend_prompt(q, k, v, sinks):
    n_b, n_l = q.shape[0], q.shape[1]
    nb = n_l // BLOCK
    qb = q.reshape(n_b, nb, BLOCK, N_KV_HEADS, GROUP, HEAD_DIM)
    def band(t):
        tb = t.reshape(n_b, nb, BLOCK, N_KV_HEADS, HEAD_DIM)
        prev = jnp.concatenate([jnp.zeros_like(tb[:, :1]), tb[:, :-1]], axis=1)
        return jnp.concatenate([prev, tb], axis=2)
    kb, vb = band(k), band(v)
    a = jnp.arange(BLOCK)[:, None]
    j = jnp.arange(2 * BLOCK)[None, :]
    dist = a + BLOCK - j
    blk = jnp.arange(nb)[:, None, None]
    valid = (dist >= 0) & (dist <= WINDOW) & (blk * BLOCK - BLOCK + j >= 0)
    o = band_softmax(qb, kb, vb, dist, valid[:, None, None], sinks)
    return o.reshape(n_b, n_l, D_ATTN), k[:, -WINDOW:], v[:, -WINDOW:]


def attend_sample(q, k, v, sinks, cache_k, cache_v):
    w_c = cache_k.shape[1]
    keys = jnp.concatenate([cache_k, k], axis=1)
    vals = jnp.concatenate([cache_v, v], axis=1)
    n_t, n_s = q.shape[1], keys.shape[1]
    dist = jnp.arange(n_t)[:, None] + w_c - jnp.arange(n_s)[None, :]
    valid = (dist >= 0) & (dist <= WINDOW)
    o = band_softmax(q, keys, vals, dist, valid, sinks)
    return o.reshape(q.shape[0], n_t, D_ATTN), keys[:, -w_c:], vals[:, -w_c:]


def decoder_layer(x, attend, conv_hist, ffn_hist, w_in, conv_w, w_attn_out, w_conv_out,
                  w_mix_out, ln1_g, ln1_b, w_gate, w_up, ffn_conv_w, w_down, ln2_g, ln2_b):
    n, n_l = x.shape[0], x.shape[1]
    p = x @ w_in
    offs = list(np.cumsum(SPLITS)[:-1])
    q, k, v, cb, cc, ch, ga, gb = jnp.split(p, offs, axis=-1)
    q = q.reshape(n, n_l, N_KV_HEADS, GROUP, HEAD_DIM)
    k = k.reshape(n, n_l, N_KV_HEADS, HEAD_DIM)
    v = v.reshape(n, n_l, N_KV_HEADS, HEAD_DIM)
    o_att, new_k, new_v = attend(q, k, v)
    o_a = o_att @ w_attn_out
    u = cc * ch
    uc, new_conv = causal_dwconv(conv_hist, u, conv_w)
    o_c = (cb * uc) @ w_conv_out
    merged = jax.nn.sigmoid(ga) * o_a + jax.nn.sigmoid(gb) * o_c
    x1 = layer_norm(DN_ALPHA * x + merged @ w_mix_out, ln1_g, ln1_b)
    g_pre = x1 @ w_gate
    g_conv, new_ffn = causal_dwconv(ffn_hist, g_pre, ffn_conv_w)
    f = (jax.nn.gelu(g_conv, approximate=False) * (x1 @ w_up)) @ w_down
    x2 = layer_norm(DN_ALPHA * x1 + f, ln2_g, ln2_b)
    return x2, new_k, new_v, new_conv, new_ffn


def setup_inputs(seed: int = 0) -> dict:
    key = jax.random.key(seed)
    ks = jax.random.split(key, 20)
    f32 = jnp.float32
    nrm = lambda k, shape, s: jax.random.normal(k, shape, f32) * s
    col_scale = jnp.ones((N_PROJ,), f32).at[D_ATTN + D_KV:D_ATTN + 2 * D_KV].set(DN_BETA)
    return {
        "x_prompt": nrm(ks[0], (BATCH, SEQ, D_MODEL), 1.0),
        "x_sample": nrm(ks[1], (DEC_BATCH, DEC_SEQ, D_MODEL), 1.0),
        "cache_k": nrm(ks[2], (DEPTH, DEC_BATCH, WINDOW, N_KV_HEADS, HEAD_DIM), 1.0),
        "cache_v": nrm(ks[3], (DEPTH, DEC_BATCH, WINDOW, N_KV_HEADS, HEAD_DIM), 1.0),
        "state_conv": nrm(ks[4], (DEPTH, DEC_BATCH, CONV_W - 1, D_CONV), 1.0),
        "state_ffn_conv": nrm(ks[5], (DEPTH, DEC_BATCH, FFN_CONV_W - 1, D_FF), 1.0),
        "w_in": nrm(ks[6], (DEPTH, D_MODEL, N_PROJ), D_MODEL ** -0.5) * col_scale,
        "conv_w": nrm(ks[7], (DEPTH, CONV_W, D_CONV), CONV_W ** -0.5),
        "attn_sinks": nrm(ks[8], (DEPTH, N_HEADS), 0.5),
        "w_attn_out": nrm(ks[9], (DEPTH, D_ATTN, D_MODEL), D_ATTN ** -0.5 * DN_BETA),
        "w_conv_out": nrm(ks[10], (DEPTH, D_CONV, D_MODEL), D_CONV ** -0.5 * DN_BETA),
        "w_mix_out": nrm(ks[11], (DEPTH, D_MODEL, D_MODEL), D_MODEL ** -0.5 * DN_BETA),
        "ln1_g": 1.0 + nrm(ks[12], (DEPTH, D_MODEL), 0.02),
        "ln1_b": nrm(ks[13], (DEPTH, D_MODEL), 0.02),
        "w_gate": nrm(ks[14], (DEPTH, D_MODEL, D_FF), D_MODEL ** -0.5),
        "w_up": nrm(ks[15], (DEPTH, D_MODEL, D_FF), D_MODEL ** -0.5),
        "ffn_conv_w": nrm(ks[16], (DEPTH, FFN_CONV_W, D_FF), FFN_CONV_W ** -0.5),
        "w_down": nrm(ks[17], (DEPTH, D_FF, D_MODEL), D_FF ** -0.5 * DN_BETA),
        "ln2_g": 1.0 + nrm(ks[18], (DEPTH, D_MODEL), 0.02),
        "ln2_b": nrm(ks[19], (DEPTH, D_MODEL), 0.02),
    }


def reference(x_prompt, x_sample, cache_k, cache_v, state_conv, state_ffn_conv, w_in, conv_w,
              attn_sinks, w_attn_out, w_conv_out, w_mix_out, ln1_g, ln1_b, w_gate, w_up,
              ffn_conv_w, w_down, ln2_g, ln2_b):
    yp, ys = x_prompt, x_sample
    kp_l, vp_l, cp_l, fp_l, ks_l, vs_l, cs_l, fs_l = [], [], [], [], [], [], [], []
    for l in range(DEPTH):
        params = (w_in[l], conv_w[l], w_attn_out[l], w_conv_out[l], w_mix_out[l], ln1_g[l], ln1_b[l],
                  w_gate[l], w_up[l], ffn_conv_w[l], w_down[l], ln2_g[l], ln2_b[l])
        zc = jnp.zeros((yp.shape[0], CONV_W - 1, D_CONV), yp.dtype)
        zf = jnp.zeros((yp.shape[0], FFN_CONV_W - 1, D_FF), yp.dtype)
        yp, kp, vp, cp, fp = decoder_layer(
            yp, functools.partial(attend_prompt, sinks=attn_sinks[l]), zc, zf, *params)
        ys, k_s, v_s, c_s, f_s = decoder_layer(
            ys, functools.partial(attend_sample, sinks=attn_sinks[l], cache_k=cache_k[l], cache_v=cache_v[l]),
            state_conv[l], state_ffn_conv[l], *params)
        kp_l.append(kp); vp_l.append(vp); cp_l.append(cp); fp_l.append(fp)
        ks_l.append(k_s); vs_l.append(v_s); cs_l.append(c_s); fs_l.append(f_s)
    return (yp, ys, jnp.stack(kp_l), jnp.stack(vp_l), jnp.stack(cp_l), jnp.stack(fp_l),
            jnp.stack(ks_l), jnp.stack(vs_l), jnp.stack(cs_l), jnp.stack(fs_l))
```

```python
import contextlib
import numpy as np
import concourse.bass as bass
import concourse.mybir as mybir
from concourse.bass_utils import run_bass_kernel_spmd

F32 = mybir.dt.float32
BF16 = mybir.dt.bfloat16
AF = mybir.ActivationFunctionType
ALU = mybir.AluOpType

D = 1024
NPROJ = 4352
DFF = 2816
NF = 22
ALPHA = 2.0 ** 0.25
EPS = 1e-5
EPS_S = EPS / (ALPHA * ALPHA)
NCORES = 8
TOK_CORE = 2048
HALO = 256
NTOK = TOK_CORE + HALO
NBLK = NTOK // 128
NS = 16
NSP = 32
RING = 5


class Sched:
    def __init__(self, nc):
        self.nc = nc
        self.ops = []
        self.last_w = {}
        self.readers = {}

    def op(self, eng, fn, reads=(), writes=(), dma=None):
        deps = set()
        for k in reads:
            if k in self.last_w:
                deps.add(self.last_w[k])
            if k.startswith('ps'):
                deps.update(r for r in self.readers.get(k, ()) if self.ops[r]['eng'] != eng)
        for k in writes:
            if k in self.last_w:
                deps.add(self.last_w[k])
            deps.update(self.readers.get(k, ()))
        oid = len(self.ops)
        self.ops.append(dict(eng=eng, fn=fn, deps=deps, dma=dma, signal=False, waits=[]))
        for k in reads:
            self.readers.setdefault(k, []).append(oid)
        for k in writes:
            self.last_w[k] = oid
            self.readers[k] = []
        return oid

    def finalize(self, stack):
        nc = self.nc
        engs = {'pe': nc.tensor, 'act': nc.scalar, 'dve': nc.vector, 'pool': nc.gpsimd, 'sp': nc.sync}
        ops = self.ops
        cnt = {}
        for o in ops:
            st = ('dma:' + o['dma']) if o['dma'] else o['eng']
            o['stream'] = st
            cnt[st] = cnt.get(st, 0) + 1
            o['seq'] = cnt[st]
        for o in ops:
            if o['dma'] and o['dma'].startswith('G:'):
                o['seq'] = cnt[o['stream']]
        clk = {e: {} for e in engs}
        eng_seq = {e: 0 for e in engs}
        for o in ops:
            e = o['eng']
            c = clk[e]
            myseq = o['seq'] if not o['dma'] else None
            need = {}
            for d in o['deps']:
                od = ops[d]
                st = od['stream']
                if st == e and not o['dma']:
                    if e == 'pe':
                        continue
                    if o['seq'] - od['seq'] > 2:
                        continue
                    if c.get('self_' + e, 0) >= od['seq']:
                        continue
                elif c.get(st, 0) >= od['seq']:
                    continue
                if st not in need or ops[need[st]]['seq'] < od['seq']:
                    need[st] = d
            for st, d in need.items():
                od = ops[d]
                if st == e and not o['dma']:
                    c['self_' + e] = od['seq']
                elif c.get(st, 0) >= od['seq']:
                    continue
                od['signal'] = True
                o['waits'].append(d)
                for k, v in od['vc'].items():
                    if c.get(k, 0) < v:
                        c[k] = v
            if not o['dma']:
                c[e] = o['seq']
                o['vc'] = dict(c)
            else:
                vc = dict(c)
                vc[o['stream']] = o['seq']
                o['vc'] = vc
        sems = {}
        val = {}
        for o in ops:
            st = o['stream']
            if o['dma']:
                val[st] = val.get(st, 0) + 16
                o['val'] = val[st]
            elif o['signal']:
                val[st] = val.get(st, 0) + 1
                o['val'] = val[st]
        for o in ops:
            if o['dma'] and o['dma'].startswith('G:'):
                o['val'] = val[o['stream']]
        for st in val:
            sems[st] = stack.enter_context(nc.semaphore('s_' + st.replace(':', '_')))
        self.nsem = len(sems)
        for o in ops:
            e = engs[o['eng']]
            for d in o['waits']:
                od = ops[d]
                e.wait_ge(sems[od['stream']], od['val'])
            if o['fn'] is None:
                continue
            ins = o['fn']()
            if o['dma']:
                ins.then_inc(sems[o['stream']], 16)
            elif o['signal']:
                ins.then_inc(sems[o['stream']], 1)


def slot_table():
    t = []
    for i in range(17):
        t.append(('w_in', 0, 8, i * 256, 256))
    for h in range(2):
        t.append(('w_ao', 0, 4, h * 512, 512))
    for h in range(2):
        t.append(('w_co', 0, 4, h * 512, 512))
    for n in range(2):
        for kh in range(2):
            t.append(('w_mix', kh * 4, 4, n * 512, 512))
    for j in range(11):
        t.append(('w_gate', 0, 8, j * 256, 256))
        t.append(('w_up', 0, 8, j * 256, 256))
    for n in range(2):
        for fg in range(6):
            nk = 4 if fg < 5 else 2
            t.append(('w_down', fg * 4, nk, n * 512, 512))
    return t


SLOTS = slot_table()
S_IN, S_AO, S_CO, S_MIX, S_FFN, S_DOWN = 0, 17, 19, 21, 25, 47
WSHAPES = {'w_in': (D, NPROJ), 'w_ao': (512, D), 'w_co': (512, D), 'w_mix': (D, D),
           'w_gate': (D, DFF), 'w_up': (D, DFF), 'w_down': (DFF, D)}


def build_program():
    nc = bass.Bass("TRN2", target_bir_lowering=False)
    S = Sched(nc)
    es = contextlib.ExitStack()

    def din(name, shape, dt=F32):
        return nc.dram_tensor(name, list(shape), dt, kind="ExternalInput").ap()

    def dout(name, shape, dt=F32):
        return nc.dram_tensor(name, list(shape), dt, kind="ExternalOutput").ap()

    xp = din("xp", [NTOK, D])
    xs = din("xs", [NS, D])
    ck = din("ck", [NS, 128, 128])
    cv = din("cv", [NS, 128, 128])
    sc = din("sc", [2 * NS, 512])
    sf = din("sf", [2 * NS, DFF])
    W = {k: din(k, v) for k, v in WSHAPES.items()}
    cw_d = din("cw", [128, 12])
    fcw_d = din("fcw", [128, 66])
    sk_d = din("sk", [128, 8])
    g1b_d = din("g1b", [128, D]); b1b_d = din("b1b", [128, D])
    g2b_d = din("g2b", [128, D]); b2b_d = din("b2b", [128, D])
    g1f_d = din("g1f", [128, 8]); b1f_d = din("b1f", [128, 8])
    ident_d = din("ident", [128, 128])
    eprev_d = din("eprev", [128, 1024]); ecur_d = din("ecur", [128, 1024])
    esm_d = din("esm", [128, 128])
    bones_d = din("bones", [128, 128])
    hv_d = din("hv", [128, 1])

    yp = dout("yp", [TOK_CORE, D])
    ys = dout("ys", [NS, D])
    nk_o = dout("nk", [128, 128]); nv_o = dout("nv", [128, 128])
    ncv_o = dout("ncv", [2, 512]); nfc_o = dout("nfc", [2, DFF])
    nks_o = dout("nks", [NS, 128, 128]); nvs_o = dout("nvs", [NS, 128, 128])
    ncs_o = dout("ncs", [NS, 2, 512]); nfs_o = dout("nfs", [NS, 2, DFF])
    wscr = nc.dram_tensor("wscr", [len(SLOTS), 128, 2048], BF16, kind="Internal").ap()

    def sb(name, shape, dt):
        return es.enter_context(nc.sbuf_tensor("s_" + name, list(shape), dt))

    xres = [sb("xres%d" % i, [128, 4, D], F32) for i in range(2)]
    RB = sb("RB", [128, 8, 512], BF16)
    RA = sb("RA", [128, NF, 512], BF16)
    RC = sb("RC", [128, 12288], BF16)
    qT = RC[:, 0:2048].rearrange("p (c n) -> p c n", c=4)
    oT = RC[:, 2048:4096].rearrange("p (c n) -> p c n", c=4)
    mg = RC[:, 4096:8192].rearrange("p (c n) -> p c n", c=8)
    gcv = RC[:, 8192:10240].rearrange("p (c n) -> p c n", c=4)
    cbs = RC[:, 10240:12288].rearrange("p (c n) -> p c n", c=4)
    gpa = RC[:, 0:NF * 514].rearrange("p (f n) -> p f n", f=NF)
    kT = sb("kT", [128, NBLK * 128], BF16)
    Vx = sb("Vx", [128, NBLK, 2, 65], BF16)
    ccs = [sb("ccs%d" % i, [128, 512], F32) for i in range(2)]
    uext = sb("uext", [128, 4, 514], F32)
    cvt = [sb("cvt%d" % i, [128, 512], F32) for i in range(2)]
    praw = [sb("praw%d" % i, [128, 512], BF16) for i in range(3)]
    pT = [sb("pT%d" % i, [128, 512], BF16) for i in range(8)]
    onb = [sb("on%d" % i, [128, 512], F32) for i in range(2)]
    rdt = [sb("rdt%d" % i, [128, 8], F32) for i in range(2)]
    m12 = [sb("m12_%d" % i, [128, 512], BF16) for i in range(4)]
    ct = [sb("ct%d" % i, [128, 512], F32) for i in range(2)]
    gl = [sb("gl%d" % i, [128, 512], BF16) for i in range(2)]
    ring = sb("ring", [128, RING, 2048], BF16)
    g1b = sb("g1b", [128, D], F32); b1b = sb("b1b", [128, D], F32)
    g2b = sb("g2b", [128, D], F32); b2b = sb("b2b", [128, D], F32)
    g1f = sb("g1f", [128, 8], F32); b1f = sb("b1f", [128, 8], F32)
    ident = sb("ident", [128, 128], F32)
    E_prev = sb("E_prev", [128, 1024], BF16)
    E_cur = sb("E_cur", [128, 1024], BF16)
    E_first = sb("E_first", [128, 1024], BF16)
    esm = sb("esm", [128, 128], BF16)
    bones = sb("bones", [128, 128], BF16)
    hv = sb("hv", [128, 1], F32)
    cw = sb("cw", [128, 12], F32)
    fcw = sb("fcw", [128, 66], F32)
    skt = sb("skt", [128, 8], F32)
    esk = sb("esk", [128, 8], F32)
    epsb = sb("epsb", [128, 1], F32)
    stt = [sb("stt%d" % i, [128, 2, 6], F32) for i in range(2)]
    mv = sb("mv", [128, 8, 2], F32)
    lnv = sb("lnv", [128, 8], F32)
    rstd = sb("rstd", [128, 8], F32)
    nkv = sb("nkv", [128, 256], F32)
    gl2 = sb("gl2", [128, NF, 32], F32)
    fh = sb("fh", [128, NF, 2], BF16)
    xsr = sb("xsr", [NSP, D], F32)
    scT = sb("scT", [128, 4, 2 * NS], F32)
    sfT = sb("sfT", [128, NF, 2 * NS], F32)
    vTs = sb("vTs", [128, NS], F32)
    prod = sb("prod", [128, 4, NS], BF16)
    pnb = sb("pnb", [128, 4, NS], F32)
    ons = sb("ons", [128, 4, NS], F32)
    dns = sb("dns", [128, 4, NS], F32)
    pTs = sb("pTs", [128, NS * 8], BF16)
    praws = sb("praws", [128, NS * 8], BF16)
    kvs = sb("kvs", [NSP, 256], F32)
    x0f = xres[0][:, :, :].rearrange("p b f -> p (b f)")
    x1f = xres[1][:, :, :].rearrange("p b f -> p (b f)")
    ncf = x0f[0:32, 0:512 + DFF]
    ones_t = sb("ones_t", [128, 64], BF16)
    ckc = xres[0][:, :, :].rearrange("p b f -> p (b f)")[:, 0:2048].rearrange("p (n f) -> p n f", n=NS)
    cvb = xres[0][:, :, :].rearrange("p b f -> p (b f)")[:, 2048:3072].bitcast(BF16).rearrange("p (n f) -> p n f", n=NS)
    kcT = xres[0][:, :, :].rearrange("p b f -> p (b f)")[:, 3072:4096].bitcast(BF16).rearrange("p (n f) -> p n f", n=NS)

    ps = [es.enter_context(nc.psum_tensor("ps%d" % i, [128, 512], F32)) for i in range(8)]
    PSK = ["ps%d" % i for i in range(8)]

    T, V, A, G, SP = nc.tensor, nc.vector, nc.scalar, nc.gpsimd, nc.sync

    def mmg(out_ap, pairs, reads, writes):
        def fn():
            last = None
            n = len(pairs)
            for i, (l, r) in enumerate(pairs):
                last = T.matmul(out_ap, l, r, start=(i == 0), stop=(i == n - 1))
            return last
        return S.op('pe', fn, reads, writes)

    def tpg(items, reads, writes):
        def fn():
            last = None
            for (o, i, npart) in items:
                last = T.transpose(o, i, ident[0:npart, 0:npart])
            return last
        return S.op('pe', fn, reads + ['ident'], writes)

    def acopy(out, in_):
        return A.activation(out=out, in_=in_, func=AF.Copy)

    def act(out, in_, func, reads, writes, bias=None, scale=None):
        kw = {}
        if bias is not None:
            kw['bias'] = bias
        if scale is not None:
            kw['scale'] = scale
        return S.op('act', lambda: A.activation(out=out, in_=in_, func=func, **kw), reads, writes)

    def dma(q, out, in_, key, reads, writes):
        e = {'sp': SP, 'pool': G, 'act': A}[q]
        return S.op(q, lambda: e.dma_start(out=out, in_=in_), reads, writes, dma=key)

    for i, (t, d, k) in enumerate([(g1b, g1b_d, 'g1b'), (b1b, b1b_d, 'b1b'), (g2b, g2b_d, 'g2b'), (b2b, b2b_d, 'b2b'),
                                   (g1f, g1f_d, 'g1f'), (b1f, b1f_d, 'b1f'), (ident, ident_d, 'ident'),
                                   (hv, hv_d, 'hv'), (cw, cw_d, 'cw'), (fcw, fcw_d, 'fcw'), (skt, sk_d, 'skt')]):
        dma('sp', t[:], d, 'G:c', [], [k])
    dma('pool', E_prev[:], eprev_d, 'G:cE', [], ['E_prev'])
    dma('pool', E_cur[:], ecur_d, 'G:cE', [], ['E_cur'])
    dma('pool', esm[:], esm_d, 'G:cE', [], ['esm'])
    dma('pool', bones[:], bones_d, 'G:cE', [], ['bones'])
    S.op('dve', lambda: V.memset(Vx[:, :, :, 64:65], 1.0), [], ['Vx1'])
    S.op('dve', lambda: V.memset(uext[:], 0.0), [], ['uext%d' % j for j in range(4)])
    S.op('dve', lambda: V.memset(RC[:], 0.0), [], ['RC1', 'RC2'])
    S.op('dve', lambda: V.memset(epsb[:], EPS_S), [], ['epsb'])
    S.op('dve', lambda: V.memset(ones_t[:], 1.0), [], ['ones_t'])
    S.op('dve', lambda: V.memset(xsr[:], 0.0), [], ['xsr'])
    S.op('dve', lambda: V.memset(fh[:], 0.0), [], ['fh%d' % f for f in range(NF)])
    S.op('dve', lambda: V.tensor_scalar(out=E_first[:], in0=E_prev[:], scalar1=hv[:, 0:1], scalar2=None,
                                        op0=ALU.mult), ['E_prev', 'hv'], ['E_first'])
    act(esk[:], skt[:], AF.Exp, ['skt'], ['esk'])

    for s, (wn, k0, nk, c0, ncol) in enumerate(SLOTS):
        src = W[wn].rearrange("(k p) n -> p k n", p=128)[:, k0:k0 + nk, c0:c0 + ncol]
        dst = wscr[s].rearrange("p (k c) -> p k c", c=ncol)[:, 0:nk, :]
        dma('pool', dst, src, 'G:cv%d' % (s // 3), [], ['scr%d' % s])

    rstate = {'n': 0}

    def wload(s):
        r = rstate['n'] % RING
        rstate['n'] += 1
        wn, k0, nk, c0, ncol = SLOTS[s]
        dma('sp', ring[:, r, :], wscr[s], 'ring%d' % r, ['scr%d' % s], ['ring%d' % r])
        view = ring[:, r, :].rearrange("p (k c) -> p k c", c=ncol)
        return view, 'ring%d' % r

    deferred = []
    import os
    DBG = os.environ.get('KDEBUG', '') == '1'
    taps = {}

    def tap(name, ap, reads):
        if not DBG or name in taps:
            return
        shp = list(ap.shape)
        d = nc.dram_tensor("dbg_" + name, shp, ap.dtype, kind="ExternalOutput").ap()
        taps[name] = d
        dma('sp', d, ap, 'dbg%d' % len(taps), reads, ['o_dbg_' + name])

    def xload(kind, ci):
        if kind == 'samp':
            dma('sp', xsr[0:NS, :], xs, 'xs', [], ['xsr'])
        elif kind == 'halo':
            dma('sp', xres[1][:, 0:2, :], xp[0:256, :].rearrange("(b p) f -> p b f", p=128), 'x1', [], ['xres1'])
        else:
            par = (ci + 1) % 2
            t0 = ci * 512
            nbl = 4 if ci < 4 else 2
            dma('sp', xres[par][:, 0:nbl, :], xp[t0:t0 + nbl * 128, :].rearrange("(b p) f -> p b f", p=128),
                'x%d' % par, [], ['xres%d' % par])

    def flush_deferred():
        for f in deferred:
            f()
        deferred.clear()

    def chunk(kind, ci, nextload=None, preloaded=False):
        samp = kind == 'samp'
        halo = kind == 'halo'
        if kind == 'main':
            par = (ci + 1) % 2
            tok0 = ci * 512
            nb = 4 if ci < 4 else 2
            Nx, a0, N, NP = nb * 128, 0, nb * 128, 128
            blk0 = tok0 // 128
            xr = xres[par]
            xkey = 'xres%d' % par
        elif halo:
            par = 1
            tok0 = 0
            Nx, a0, N, nb, NP = 256, 128, 128, 1, 128
            blk0 = 0
            xr = xres[1]
            xkey = 'xres1'
        else:
            Nx, a0, N, nb, NP = NSP, 0, NSP, 1, NSP
            blk0 = None
            xkey = 'xsr'
        nbx = Nx // 128 if not samp else 1
        last_main = (kind == 'main' and ci == 4)

        def xrow(b):
            if samp:
                return xsr[:, :]
            return xr[:, (a0 // 128) + b, :]

        def xrow_all(b):
            if samp:
                return xsr[:, :]
            return xr[:, b, :]

        kp = 's' if samp else ''

        def K(name):
            return kp + name
        if samp:
            RBt, RAt, qTt, oTt, mgt, gcvt, cbst, uextt = RB_s, RA_s, qT_s, oT_s, mg_s, gcv_s, cbs_s, uext_s
        else:
            RBt, RAt, qTt, oTt, mgt, gcvt, cbst, uextt = RB, RA, qT, oT, mg, gcv, cbs, uext
        xT = RBt
        x1T = RBt

        if not preloaded:
            xload(kind, ci)

        first = True
        for kc in range(8):
            bank = 6 + (kc % 2)
            items = [(ps[bank][:, b * 128:b * 128 + NP], xrow_all(b)[:, kc * 128:(kc + 1) * 128], NP)
                     for b in range(nbx)]
            tpg(items, [xkey], [PSK[bank]])
            w = [K('xT%d' % kc), K('RB2')] if first else [K('xT%d' % kc)]
            first = False
            act(xT[:, kc, 0:Nx], ps[bank][:, 0:Nx], AF.Copy, [PSK[bank], K('RB1')], w)

        mmb = {'i': 0}

        def nextbank():
            b = mmb['i'] % 4
            mmb['i'] += 1
            return b

        xTk = [K('xT%d' % kc) for kc in range(8)] + [K('RB1')]
        firstRC = {'v': True}
        firstRA = {'v': True}

        def rcw(keys):
            if firstRC['v']:
                firstRC['v'] = False
                return keys + [K('RC2')]
            return keys

        for i in range(17):
            wv, wk = yield (S_IN + i)
            if i == 3:
                flush_deferred()
            for half in range(2):
                m = 2 * i + half
                lhs = lambda kc, half=half, wv=wv: wv[:, kc, half * 128:(half + 1) * 128]
                if m == 5:
                    for b in range(nbx):
                        mmg(ps[4][0:NP, b * 128:(b + 1) * 128],
                            [(xT[:, kc, b * 128:b * 128 + NP], wv[:, kc, 128:256]) for kc in range(8)],
                            xTk + [wk], [PSK[4]])
                    if samp:
                        S.op('act', lambda: acopy(out=kvs[:, 128:256], in_=ps[4][0:NSP, 0:128]),
                             [PSK[4]], ['kvs_v'])
                        bk = nextbank()
                        mmg(ps[bk][:, 0:N], [(lhs(kc), xT[:, kc, 0:N]) for kc in range(8)], xTk + [wk], [PSK[bk]])
                        act(vTs[:, :], ps[bk][:, 0:NS], AF.Copy, [PSK[bk]], ['vTs'])
                    else:
                        S.op('dve', lambda: V.tensor_copy(
                            out=Vx[:, blk0:blk0 + nbx, :, 0:64],
                            in_=ps[4][:, 0:Nx].rearrange("p (b g d) -> p b g d", b=nbx, g=2)),
                            [PSK[4]], ['Vx%d' % (blk0 + b) for b in range(nbx)])
                        if last_main and 'lmv' not in os.environ.get('KSKIP', ''):
                            S.op('act', lambda: acopy(out=nkv[:, 128:256], in_=ps[4][:, N - 128:N]),
                                 [PSK[4]], ['nkv_v'])
                    continue
                bk = nextbank()
                if m == 4:
                    mmg(ps[bk][:, 0:Nx], [(lhs(kc), xT[:, kc, 0:Nx]) for kc in range(8)], xTk + [wk], [PSK[bk]])
                    if samp:
                        act(kcT_new[:, :], ps[bk][:, 0:NS], AF.Copy, [PSK[bk]], ['kTs'])
                        mmg(ps[4][0:NSP, 0:128], [(xT[:, kc, 0:NSP], wv[:, kc, 0:128]) for kc in range(8)],
                            xTk + [wk], [PSK[4]])
                        S.op('act', lambda: acopy(out=kvs[:, 0:128], in_=ps[4][0:NSP, 0:128]),
                             [PSK[4]], ['kvs_k'])
                    else:
                        act(kT[:, blk0 * 128:blk0 * 128 + Nx], ps[bk][:, 0:Nx], AF.Copy, [PSK[bk]],
                            ['kT%d' % (blk0 + b) for b in range(nbx)])
                        if last_main and 'lmk' not in os.environ.get('KSKIP', ''):
                            mmg(ps[5][:, 0:128], [(xT[:, kc, N - 128:N], wv[:, kc, 0:128]) for kc in range(8)],
                                xTk + [wk], [PSK[5]])
                            S.op('act', lambda: acopy(out=nkv[:, 0:128], in_=ps[5][:, 0:128]),
                                 [PSK[5]], ['nkv_k'])
                    continue
                mmg(ps[bk][:, 0:N], [(lhs(kc), xT[:, kc, a0:a0 + N]) for kc in range(8)], xTk + [wk], [PSK[bk]])
                pin = ps[bk][:, 0:N]
                if m < 4:
                    act(qTt[:, m, 0:N], pin, AF.Copy, [PSK[bk], K('RC1')], rcw([K('qT%d' % m)]), scale=0.125)
                elif m < 10:
                    j = m - 6
                    act(cbst[:, j, 0:N], pin, AF.Copy, [PSK[bk], K('RC1')], [K('cbs%d' % j)])
                elif m < 18:
                    j = (m - 10) // 2
                    if (m - 10) % 2 == 0:
                        act(ccs[j % 2][:, 0:N], pin, AF.Copy, [PSK[bk]], ['ccs%d' % (j % 2)])
                    else:
                        S.op('dve', lambda pin=pin, j=j: V.tensor_tensor(out=uextt[:, j, 2:2 + N], in0=pin,
                                                                        in1=ccs[j % 2][:, 0:N], op=ALU.mult),
                             [PSK[bk], 'ccs%d' % (j % 2)], [K('uext%d' % j)])
                        (conv_branch_samp if samp else conv_branch)(j, N)
                else:
                    t = m - 18
                    w = [K('tg%d' % t)]
                    if firstRA['v']:
                        firstRA['v'] = False
                        w = w + [K('RA2')]
                    act(RAt[:, t, 0:N], pin, AF.Tanh, [PSK[bk], K('RA1')], w, scale=0.5)

        if kind == 'main' and ci == 0:
            tap('xT', xT[:, 0, 0:512], ['xT0'])
            tap('qT', qTt[:, 0, 0:512], ['qT0'])
            tap('kT', kT[:, 256:768], ['kT2', 'kT3', 'kT4', 'kT5'])
            tap('Vx', Vx[:, 2, :, :], ['Vx2'])
            tap('tg', RAt[:, 0, 0:512], ['tg0'])
            tap('gcv', gcvt[:, 0, 0:512], ['gcv0'])
            tap('uext', uextt[:, 0, :], ['uext0'])
        if not samp:
            S.op('pool', lambda: G.tensor_copy(out=uextt[:, :, 0:2], in_=uextt[:, :, N:N + 2]),
                 [K('uext%d' % j) for j in range(4)], [K('uext%d' % j) for j in range(4)])

        if samp:
            attention_samples()
        else:
            for b in range(nb):
                attention_block(blk0 + (a0 // 128) + b, b, first_blk=(blk0 + b == 2))

        if kind == 'main' and ci == 0:
            tap(K('oT'), oTt[:, 0, 0:512], [K('oT')])
        ao = []
        co = []
        for h in range(2):
            ao.append((yield (S_AO + h)))
        for h in range(2):
            co.append((yield (S_CO + h)))
        for m in range(8):
            h, ml = m // 4, m % 4
            ba, bc = (0, 1) if m % 2 == 0 else (2, 3)
            mmg(ps[ba][:, 0:N], [(ao[h][0][:, c, ml * 128:(ml + 1) * 128], oTt[:, c, 0:N]) for c in range(4)],
                [K('oT'), ao[h][1], K('RC1')], [PSK[ba]])
            mmg(ps[bc][:, 0:N], [(co[h][0][:, c, ml * 128:(ml + 1) * 128], gcvt[:, c, 0:N]) for c in range(4)],
                [K('gcv%d' % c) for c in range(4)] + [co[h][1], K('RC1')], [PSK[bc]])
            i1, i2 = (0, 1) if m % 2 == 0 else (2, 3)
            S.op('dve', lambda m=m, ba=ba, i1=i1: V.scalar_tensor_tensor(
                out=m12[i1][:, 0:N], in0=RAt[:, m, 0:N], scalar=1.0, in1=ps[ba][:, 0:N],
                op0=ALU.add, op1=ALU.mult), [PSK[ba], K('tg%d' % m), K('RA1')], ['m12_%d' % i1])
            S.op('dve', lambda m=m, bc=bc, i2=i2: V.scalar_tensor_tensor(
                out=m12[i2][:, 0:N], in0=RAt[:, 8 + m, 0:N], scalar=1.0, in1=ps[bc][:, 0:N],
                op0=ALU.add, op1=ALU.mult), [PSK[bc], K('tg%d' % (8 + m)), K('RA1')], ['m12_%d' % i2])
            S.op('pool', lambda m=m, i1=i1, i2=i2: G.tensor_tensor(out=mgt[:, m, 0:N], in0=m12[i1][:, 0:N],
                                                                  in1=m12[i2][:, 0:N], op=ALU.add),
                 ['m12_%d' % i1, 'm12_%d' % i2, K('RC1')], [K('mg%d' % m)])

        if kind == 'main' and ci == 0:
            tap('mg', mgt[:, 0, 0:512], ['mg0'])
        mgk = [K('mg%d' % m) for m in range(8)]
        mb = 0
        for n in range(2):
            wm = []
            for kh in range(2):
                wm.append((yield (S_MIX + n * 2 + kh)))
            for b in range(nb):
                bank = 4 + (mb % 4)
                mb += 1
                mmg(ps[bank][0:NP, :],
                    [(mgt[:, kc, b * 128:b * 128 + NP], wm[kc // 4][0][:, kc % 4, :]) for kc in range(8)],
                    mgk + [wm[0][1], wm[1][1], K('RC1')], [PSK[bank]])
                xs_ = xrow(b)[0:NP, n * 512:(n + 1) * 512]
                S.op('dve', lambda bank=bank, xs_=xs_: V.scalar_tensor_tensor(
                    out=xs_, in0=ps[bank][0:NP, :], scalar=0.5 / ALPHA, in1=xs_, op0=ALU.mult, op1=ALU.add),
                    [PSK[bank], xkey], [xkey + 'h%d_%d' % (b, n)])

        for b in range(nb):
            layer_norm(xrow(b), NP, b, [xkey + 'h%d_%d' % (b, n) for n in range(2)], xkey + 'n%d' % b)

        first = True
        for kc in range(8):
            bank = kc % 2
            items = [(ps[bank][:, b * 128:b * 128 + NP], xrow(b)[0:NP, kc * 128:(kc + 1) * 128], NP)
                     for b in range(nb)]
            tpg(items, [xkey + 'n%d' % b for b in range(nb)], [PSK[bank]])
            w = [K('x1T%d' % kc), K('RB1')] if first else [K('x1T%d' % kc)]
            first = False
            act(x1T[:, kc, 0:N], ps[bank][:, 0:N], AF.Identity, [PSK[bank], K('RB2'), 'g1f', 'b1f'], w,
                bias=b1f[:, kc:kc + 1], scale=g1f[:, kc:kc + 1])
        if not halo:
            for b in range(nb):
                r = xrow(b)[0:NP, :]
                S.op('pool', lambda r=r: G.tensor_tensor(out=r, in0=r, in1=g1b[0:NP, :], op=ALU.mult),
                     [xkey + 'n%d' % b, 'g1b'], [xkey + 'n%d' % b])
                S.op('pool', lambda r=r: G.tensor_tensor(out=r, in0=r, in1=b1b[0:NP, :], op=ALU.add),
                     [xkey + 'n%d' % b, 'b1b'], [xkey + 'n%d' % b])

        if kind == 'main' and ci == 0:
            tap('x1', xr[:, 0, :], [xkey + 'n0'])
            tap('x1T', x1T[:, 0, 0:512], ['x1T0'])
        x1k = [K('x1T%d' % kc) for kc in range(8)]
        firstG = True
        firstH = True
        if samp:
            S.op('pool', lambda: G.tensor_copy(out=gpas[:, :, :, 0:2],
                                               in_=sfT[:, :, :].rearrange("p f (n t) -> p f n t", t=2)),
                 ['sfT0', 'sfT1', 'sfT2', 'sfT3', K('RC2')], [K('gpa%d' % f) for f in range(NF)] + [K('RC1')])
            firstG = False
        elif not halo:
            S.op('pool', lambda: G.tensor_copy(out=gpa[:, :, 0:2], in_=fh[:, :, :]),
                 ['fh%d' % f for f in range(NF)] + [K('RC2')], [K('gpa%d' % f) for f in range(NF)] + [K('RC1')])
            firstG = False
        if nextload is not None:
            nextload()
        for j in range(11):
            wg = yield (S_FFN + 2 * j)
            wu = None
            if not halo:
                wu = yield (S_FFN + 2 * j + 1)
            for fl in range(2):
                f = 2 * j + fl
                bg, bu = (0, 1) if f % 2 == 0 else (2, 3)
                mmg(ps[bg][:, 0:N], [(wg[0][:, kc, fl * 128:(fl + 1) * 128], x1T[:, kc, 0:N]) for kc in range(8)],
                    x1k + [wg[1], K('RB2')], [PSK[bg]])
                w = [K('gpa%d' % f)]
                if firstG:
                    firstG = False
                    w = w + [K('RC1')]
                if halo:
                    S.op('dve', lambda f=f, bg=bg: V.tensor_scalar(
                        out=fh[:, f, :], in0=ps[bg][:, N - 2:N], scalar1=hv[:, 0:1], scalar2=None,
                        op0=ALU.mult), [PSK[bg], 'hv'], ['fh%d' % f])
                    continue
                if samp:
                    S.op('act', lambda f=f, bg=bg: A.activation(out=gpas[:, f, :, 2], in_=ps[bg][:, 0:NS], func=AF.Copy),
                         [PSK[bg], K('RC2')], w)
                    S.op('act', lambda f=f, bg=bg: acopy(out=gsT[:, f, :], in_=ps[bg][:, 0:NSP]),
                         [PSK[bg]], ['gsT%d' % f])
                else:
                    act(gpa[:, f, 2:2 + N], ps[bg][:, 0:N], AF.Copy, [PSK[bg], K('RC2')], w)
                    if kind == 'main' and ci == 0:
                        S.op('dve', lambda f=f: V.tensor_scalar(
                            out=gpa[:, f, 256:258], in0=gpa[:, f, 256:258], scalar1=hv[:, 0:1], scalar2=None,
                            op0=ALU.mult), [K('gpa%d' % f), 'hv', K('RC2')], [K('gpa%d' % f)])
                    if last_main and 'lmg' not in os.environ.get('KSKIP', ''):
                        act(gl2[:, f, :], ps[bg][:, N - 32:N], AF.Copy, [PSK[bg]], ['gl2_%d' % f])
                if halo:
                    continue
                mmg(ps[bu][:, 0:N], [(wu[0][:, kc, fl * 128:(fl + 1) * 128], x1T[:, kc, 0:N]) for kc in range(8)],
                    x1k + [wu[1], K('RB2')], [PSK[bu]])
                ci2 = f % 2
                if samp:
                    g0, g1_, g2_ = gpas[:, f, :, 0], gpas[:, f, :, 1], gpas[:, f, :, 2]
                else:
                    g0, g1_, g2_ = gpa[:, f, 0:N], gpa[:, f, 1:N + 1], gpa[:, f, 2:N + 2]
                NN = NS if samp else N
                c_ = ct[ci2][:, 0:NN]
                rk = [K('gpa%d' % f), 'fcw', K('RC2')]
                S.op('dve', lambda f=f, c_=c_, g0=g0: V.tensor_scalar(
                    out=c_, in0=g0, scalar1=fcw[:, 3 * f:3 * f + 1], scalar2=None, op0=ALU.mult), rk, ['ct%d' % ci2])
                S.op('dve', lambda f=f, c_=c_, g1_=g1_: V.scalar_tensor_tensor(
                    out=c_, in0=g1_, scalar=fcw[:, 3 * f + 1:3 * f + 2], in1=c_, op0=ALU.mult, op1=ALU.add),
                    rk + ['ct%d' % ci2], ['ct%d' % ci2])
                S.op('dve', lambda f=f, c_=c_, g2_=g2_: V.scalar_tensor_tensor(
                    out=c_, in0=g2_, scalar=fcw[:, 3 * f + 2:3 * f + 3], in1=c_, op0=ALU.mult, op1=ALU.add),
                    rk + ['ct%d' % ci2], ['ct%d' % ci2])
                act(gl[ci2][:, 0:NN], c_, AF.Gelu, ['ct%d' % ci2], ['gl%d' % ci2])
                w = [K('hT%d' % f)]
                if firstH:
                    firstH = False
                    w = w + [K('RA1')]
                S.op('dve', lambda f=f, bu=bu, ci2=ci2, NN=NN: V.tensor_tensor(
                    out=RAt[:, f, 0:NN], in0=ps[bu][:, 0:NN], in1=gl[ci2][:, 0:NN], op=ALU.mult),
                    [PSK[bu], 'gl%d' % ci2, K('RA2')], w)
        if halo:
            return
        if not samp:
            S.op('pool', lambda: G.tensor_copy(out=fh[:, :, :], in_=gpa[:, :, N:N + 2]),
                 [K('gpa%d' % f) for f in range(NF)] + [K('RC2')], ['fh%d' % f for f in range(NF)])

        if kind == 'main' and ci == 0:
            tap('hT', RAt[:, 0, 0:512], ['hT0'])
        hk = [K('hT%d' % f) for f in range(NF)]
        for n in range(2):
            for fg in range(6):
                wd = yield (S_DOWN + n * 6 + fg)
                nk = 4 if fg < 5 else 2
                for b in range(nb):
                    bank = 0 if samp else 4 + b

                    def fn(b=b, bank=bank, fg=fg, nk=nk, wd=wd):
                        last = None
                        for kk in range(nk):
                            f = fg * 4 + kk
                            last = T.matmul(ps[bank][0:NP, :], RAt[:, f, b * 128:b * 128 + NP], wd[0][:, kk, :],
                                            start=(f == 0), stop=(f == NF - 1))
                        return last
                    S.op('pe', fn, hk + [wd[1], K('RA2')], [PSK[bank]])
            for b in range(nb):
                bank = 0 if samp else 4 + b
                xs_ = xrow(b)[0:NP, n * 512:(n + 1) * 512]
                S.op('dve', lambda bank=bank, xs_=xs_: V.scalar_tensor_tensor(
                    out=xs_, in0=ps[bank][0:NP, :], scalar=1.0 / ALPHA, in1=xs_, op0=ALU.mult, op1=ALU.add),
                    [PSK[bank], xkey + 'n%d' % b], [xkey + 'z%d_%d' % (b, n)])
        for b in range(nb):
            layer_norm(xrow(b), NP, 4 + b, [xkey + 'z%d_%d' % (b, n) for n in range(2)], xkey + 'y%d' % b)
            r = xrow(b)[0:NP, :]
            S.op('pool', lambda r=r: G.tensor_tensor(out=r, in0=r, in1=g2b[0:NP, :], op=ALU.mult),
                 [xkey + 'y%d' % b, 'g2b'], [xkey + 'y%d' % b])
            S.op('pool', lambda r=r: G.tensor_tensor(out=r, in0=r, in1=b2b[0:NP, :], op=ALU.add),
                 [xkey + 'y%d' % b, 'b2b'], [xkey + 'y%d' % b])

        if samp:
            dma('sp', ys, xsr[0:NS, :], 'G:out', ['xsry0'], ['o_ys'])
        else:
            def store(ci=ci, par=par, xr=xr, xkey=xkey, nb=nb):
                b0 = 2 if ci == 0 else 0
                r0 = 0 if ci == 0 else 256 + (ci - 1) * 512
                nr = (nb - b0) * 128
                dma('sp', yp[r0:r0 + nr, :].rearrange("(b p) f -> p b f", p=128), xr[:, b0:nb, :],
                    'oy%d' % par, [xkey + 'y%d' % b for b in range(nb)], [xkey, 'o_yp%d' % ci])
            deferred.append(store)
        if last_main and 'final' not in os.environ.get('KSKIP', ''):
            final_prompt_outputs(N)
        if samp:
            final_sample_outputs()

    def layer_norm(r, NP, slot, rkeys, wkey):
        st = stt[slot % 2]
        S.op('dve', lambda: V.bn_stats(out=st[0:NP, 0, :], in_=r[0:NP, 0:512]), rkeys, ['stt%d' % (slot % 2)])
        S.op('dve', lambda: V.bn_stats(out=st[0:NP, 1, :], in_=r[0:NP, 512:1024]), rkeys, ['stt%d' % (slot % 2)])
        S.op('dve', lambda: V.bn_aggr(out=mv[0:NP, slot, :], in_=st[0:NP, :, :]), ['stt%d' % (slot % 2)], ['mv%d' % slot])
        act(lnv[0:NP, slot:slot + 1], mv[0:NP, slot, 1:2], AF.Ln, ['mv%d' % slot, 'epsb'], ['lnv%d' % slot],
            bias=epsb[0:NP, :], scale=1.0)
        act(rstd[0:NP, slot:slot + 1], lnv[0:NP, slot:slot + 1], AF.Exp, ['lnv%d' % slot], ['rstd%d' % slot], scale=-0.5)
        S.op('dve', lambda: V.tensor_scalar(out=r[0:NP, :], in0=r[0:NP, :], scalar1=mv[0:NP, slot, 0:1],
                                            scalar2=rstd[0:NP, slot:slot + 1], op0=ALU.subtract, op1=ALU.mult),
             rkeys + ['mv%d' % slot, 'rstd%d' % slot], [wkey])

    def conv_branch(j, N, samp_views=None):
        c_ = cvt[j % 2][:, 0:N]
        if samp_views is None:
            u0, u1, u2 = uext[:, j, 0:N], uext[:, j, 1:N + 1], uext[:, j, 2:N + 2]
            rk = ['uext%d' % j, 'cw']
        else:
            u0, u1, u2 = samp_views
            rk = ['uext%d' % j, 'cw', 'scT']
        ck_ = 'cvt%d' % (j % 2)
        S.op('dve', lambda: V.tensor_scalar(out=c_, in0=u0, scalar1=cw[:, 3 * j:3 * j + 1], scalar2=None,
                                             op0=ALU.mult), rk, [ck_])
        S.op('dve', lambda: V.scalar_tensor_tensor(out=c_, in0=u1, scalar=cw[:, 3 * j + 1:3 * j + 2], in1=c_,
                                                    op0=ALU.mult, op1=ALU.add), rk + [ck_], [ck_])
        S.op('dve', lambda: V.scalar_tensor_tensor(out=c_, in0=u2, scalar=cw[:, 3 * j + 2:3 * j + 3], in1=c_,
                                                    op0=ALU.mult, op1=ALU.add), rk + [ck_], [ck_])
        S.op('pool', lambda: G.tensor_tensor(out=gcv[:, j, 0:N], in0=c_, in1=cbs[:, j, 0:N], op=ALU.mult),
             [ck_, 'cbs%d' % j, 'RC1'], ['gcv%d' % j])

    pstate = {'praw': 0, 'pT': 0, 'on': 0}

    def attention_block(gb, b, first_blk):
        pts = {}
        sbank = 0
        for g in range(2):
            for kb, kblk in enumerate((max(gb - 1, 0), gb)):
                bank = (2 * g + kb) % 4
                mmg(ps[bank][:, :],
                    [(kT[64 * g:64 * g + 64, kblk * 128:(kblk + 1) * 128], qT[64 * g:64 * g + 64, :, b * 128:(b + 1) * 128])],
                    ['kT%d' % kblk] + ['qT%d' % c for c in range(4)] + ['RC1'], [PSK[bank]])
                pr = pstate['praw'] % 3
                pstate['praw'] += 1
                act(praw[pr][:], ps[bank][:, :], AF.Exp, [PSK[bank]], ['praw%d' % pr])
                pi = pstate['pT'] % 8
                pstate['pT'] += 1
                if kb == 0:
                    E = E_first if first_blk else E_prev
                    ek = 'E_first' if first_blk else 'E_prev'
                else:
                    E, ek = E_cur, 'E_cur'
                S.op('dve', lambda pi=pi, pr=pr, E=E, g=g: V.tensor_tensor(
                    out=pT[pi][:], in0=praw[pr][:], in1=E[:, g * 512:(g + 1) * 512], op=ALU.mult),
                    ['praw%d' % pr, ek], ['pT%d' % pi])
                pts[(g, kb)] = pi
                if gb == 2:
                    tap('praw_%d%d' % (g, kb), praw[pr][:], ['praw%d' % pr])
                    tap('pT_%d%d' % (g, kb), pT[pi][:], ['pT%d' % pi])
        for g in range(2):
            bank = 4 + g
            pairs_r = ['pT%d' % pts[(g, 0)], 'pT%d' % pts[(g, 1)], 'Vx%d' % max(gb - 1, 0), 'Vx%d' % gb, 'Vx1']

            def fn(g=g, bank=bank):
                last = None
                for c in range(4):
                    for kb, kblk in enumerate((max(gb - 1, 0), gb)):
                        last = T.matmul(ps[bank][:, c * 65:(c + 1) * 65],
                                        pT[pts[(g, kb)]][:, c * 128:(c + 1) * 128],
                                        Vx[:, kblk, g, :], start=(kb == 0), stop=(kb == 1))
                return last
            S.op('pe', fn, pairs_r, [PSK[bank]])
        ri = pstate['on'] % 2
        pstate['on'] += 1
        for g in range(2):
            bank = 4 + g
            pv = ps[bank][:, 0:260].rearrange("p (c e) -> p c e", c=4)
            S.op('dve', lambda g=g, pv=pv, ri=ri: V.tensor_tensor(
                out=rdt[ri][:, 4 * g:4 * g + 4], in0=pv[:, :, 64], in1=esk[:, 4 * g:4 * g + 4], op=ALU.add),
                [PSK[bank], 'esk'], ['rdt%d_%d' % (ri, g)])
            S.op('dve', lambda g=g, ri=ri: V.reciprocal(out=rdt[ri][:, 4 * g:4 * g + 4], in_=rdt[ri][:, 4 * g:4 * g + 4]),
                 ['rdt%d_%d' % (ri, g)], ['rdt%d_%d' % (ri, g)])
            S.op('dve', lambda g=g, pv=pv, ri=ri: V.tensor_tensor(
                out=onb[ri][:, g * 256:(g + 1) * 256].rearrange("p (c d) -> p c d", c=4),
                in0=pv[:, :, 0:64],
                in1=rdt[ri][:, 4 * g:4 * g + 4].unsqueeze(2).to_broadcast([128, 4, 64]), op=ALU.mult),
                [PSK[bank], 'rdt%d_%d' % (ri, g)], ['on%d_%d' % (ri, g)])
        if gb == 2:
            tap('rdt', rdt[ri][:], ['rdt%d_0' % ri, 'rdt%d_1' % ri])
            tap('onb', onb[ri][:], ['on%d_0' % ri, 'on%d_1' % ri])
        tb = 6 + (gb % 2)
        items = [(ps[tb][:, c * 128:(c + 1) * 128], onb[ri][:, c * 128:(c + 1) * 128], 128) for c in range(4)]
        tpg(items, ['on%d_0' % ri, 'on%d_1' % ri], [PSK[tb]])
        act(oT[:, :, b * 128:(b + 1) * 128], ps[tb][:, :].rearrange("p (c q) -> p c q", c=4), AF.Copy,
            [PSK[tb], 'RC1'], ['oT'])

    gpas_t = sb("gpas", [128, NF * NS * 3], BF16)
    gpas = gpas_t[:, :].rearrange("p (f n t) -> p f n t", f=NF, t=3)
    RB_s = sb("RB_s", [128, 8, NSP], BF16)
    RA_s = sb("RA_s", [128, NF, NSP], BF16)
    qT_s = sb("qT_s", [128, 4, NSP], BF16)
    oT_s = sb("oT_s", [128, 4, NSP], BF16)
    mg_s = sb("mg_s", [128, 8, NSP], BF16)
    gcv_s = sb("gcv_s", [128, 4, NSP], BF16)
    cbs_s = sb("cbs_s", [128, 4, NSP], BF16)
    uext_s = sb("uext_s", [128, 4, NSP + 2], F32)
    stg = sb("stg", [NSP, 768], F32)
    gsT = sb("gsT", [128, NF, NSP], F32)
    kcT_new = sb("kcT_new", [128, NS], BF16)
    usT = sb("usT", [128, 4, NS, 3], F32)

    def attention_samples():
        N = NS
        dma('sp', ckc, ck.rearrange("n s f -> s n f"), 'sck', [], ['ckc', 'xres0'])
        dma('pool', cvb, cv.rearrange("n s f -> s n f"), 'scv', ['xres0'], ['cvb'])
        dma('sp', nks_o[:, 0:127, :], ck[:, 1:128, :], 'G:out', [], ['o_nks'])
        dma('sp', nvs_o[:, 0:127, :], cv[:, 1:128, :], 'G:out', [], ['o_nvs'])
        for n4 in range(4):
            bank = n4 % 2
            items = [(ps[bank][:, i * 128:(i + 1) * 128], ckc[:, n4 * 4 + i, :], 128) for i in range(4)]
            tpg(items, ['ckc', 'xres0'], [PSK[bank]])
            act(kcT[:, n4 * 4:n4 * 4 + 4, :], ps[bank][:, :].rearrange("p (n s) -> p n s", n=4), AF.Copy,
                [PSK[bank], 'ckc', 'xres0'], ['kcT%d' % n4])
        def fn():
            last = None
            for n in range(NS):
                for g in range(2):
                    last = T.matmul(ps[2][:, n * 8 + g * 4:n * 8 + g * 4 + 4],
                                    kcT[64 * g:64 * g + 64, n, :], qT_s[64 * g:64 * g + 64, :, n],
                                    start=True, stop=True)
            return last
        S.op('pe', fn, ['xres0'] + ['kcT%d' % i for i in range(4)] + ['sqT%d' % c for c in range(4)] + ['sRC1'], [PSK[2]])
        act(praws[:], ps[2][:, 0:128], AF.Exp, [PSK[2]], ['praws'])
        S.op('dve', lambda: V.tensor_tensor(out=pTs[:], in0=praws[:], in1=esm[:], op=ALU.mult),
             ['praws', 'esm'], ['pTs'])
        S.op('dve', lambda: V.tensor_tensor(out=prod[:], in0=qT_s[:, :, 0:NS],
                                            in1=kcT_new[:, :].unsqueeze(1).to_broadcast([128, 4, NS]), op=ALU.mult),
             ['sqT%d' % c for c in range(4)] + ['kTs', 'sRC1'], ['prod'])
        mmg(ps[3][:, 0:64], [(bones[:, :], prod[:].rearrange("p c n -> p (c n)"))], ['bones', 'prod'], [PSK[3]])
        act(pnb[:].rearrange("p c n -> p (c n)"), ps[3][:, 0:64], AF.Exp, [PSK[3]], ['pnb'])
        def fn2():
            last = None
            for n in range(NS):
                for g in range(2):
                    last = T.matmul(ps[0][64 * g:64 * g + 64, n * 4:n * 4 + 4], cvb[:, n, 64 * g:64 * g + 64],
                                    pTs[:, n * 8 + g * 4:n * 8 + g * 4 + 4], start=True, stop=True)
            return last
        S.op('pe', fn2, ['cvb', 'pTs', 'xres0'], [PSK[0]])
        ones64 = ones_t[:, :]

        def fn3():
            last = None
            for g in range(2):
                last = T.matmul(ps[1][64 * g:64 * g + 64, 0:64], ones64,
                                pTs[:].rearrange("p (n g c) -> p n g c", g=2, c=4)[:, :, g, :], start=True, stop=True)
            return last
        S.op('pe', fn3, ['ones_t', 'pTs'], [PSK[1]])
        S.op('dve', lambda: V.tensor_tensor(out=ons[:], in0=pnb[:], in1=vTs[:, :].unsqueeze(1).to_broadcast([128, 4, NS]),
                                            op=ALU.mult), ['pnb', 'vTs'], ['ons'])
        S.op('dve', lambda: V.tensor_tensor(out=ons[:], in0=ons[:],
                                            in1=ps[0][:, 0:64].rearrange("p (n c) -> p c n", c=4), op=ALU.add),
             ['ons', PSK[0]], ['ons'])
        S.op('dve', lambda: V.tensor_tensor(out=dns[:], in0=pnb[:],
                                            in1=ps[1][:, 0:64].rearrange("p (n c) -> p c n", c=4), op=ALU.add),
             ['pnb', PSK[1]], ['dns'])
        for g in range(2):
            sl = slice(64 * g, 64 * g + 64)
            S.op('dve', lambda g=g, sl=sl: V.tensor_tensor(
                out=dns[sl], in0=dns[sl], in1=esk[sl, 4 * g:4 * g + 4].unsqueeze(2).to_broadcast([64, 4, NS]), op=ALU.add),
                ['dns', 'esk'], ['dns'])
        S.op('dve', lambda: V.reciprocal(out=dns[:], in_=dns[:]), ['dns'], ['dns'])
        S.op('dve', lambda: V.tensor_tensor(out=ons[:], in0=ons[:], in1=dns[:], op=ALU.mult), ['ons', 'dns'], ['ons'])
        for h in range(8):
            g, c = h // 4, h % 4
            S.op('dve', lambda h=h, g=g, c=c: V.tensor_copy(out=oT_s[64 * (h % 2):64 * (h % 2) + 64, h // 2, 0:NS],
                                                           in_=ons[64 * g:64 * g + 64, c, :]),
                 ['ons', 'sRC1'], ['soT'])

    def final_prompt_outputs(N):
        dma('sp', nk_o, nkv[:, 0:128], 'G:out', ['nkv_k'], ['o_nk'])
        dma('sp', nv_o, nkv[:, 128:256], 'G:out', ['nkv_v'], ['o_nv'])
        items = [(ps[0][0:32, j * 128:(j + 1) * 128], uext[:, j, N - 30:N + 2], 128) for j in range(4)]
        tpg(items, ['uext%d' % j for j in range(4)], [PSK[0]])
        S.op('act', lambda: acopy(out=ncf[:, 0:512], in_=ps[0][0:32, :]), [PSK[0]], ['ncf_c', 'xres0'])
        dma('sp', ncv_o, ncf[30:32, 0:512], 'G:ncf', ['ncf_c', 'xres0'], ['o_ncv'])
        for grp in range(6):
            f0 = grp * 4
            nf = min(4, NF - f0)
            bank = 1 + (grp % 2)
            items = [(ps[bank][0:32, i * 128:(i + 1) * 128], gl2[:, f0 + i, :], 128) for i in range(nf)]
            tpg(items, ['gl2_%d' % (f0 + i) for i in range(nf)], [PSK[bank]])
            S.op('act', lambda f0=f0, nf=nf, bank=bank: acopy(
                out=ncf[:, 512 + f0 * 128:512 + (f0 + nf) * 128], in_=ps[bank][0:32, 0:nf * 128]),
                [PSK[bank], 'xres0'], ['ncf_f%d' % grp])
        dma('sp', nfc_o, ncf[30:32, 512:512 + DFF], 'G:ncf', ['ncf_f%d' % g for g in range(6)] + ['xres0'], ['o_nfc'])

    def final_sample_outputs():
        dma('sp', nks_o[:, 127, :], kvs[0:NS, 0:128], 'G:out', ['kvs_k'], ['o_nks2'])
        dma('sp', nvs_o[:, 127, :], kvs[0:NS, 128:256], 'G:out', ['kvs_v'], ['o_nvs2'])
        dma('sp', ncs_o[:, 0, :], sc.rearrange("(n t) c -> n t c", t=2)[:, 1, :], 'G:out', [], ['o_ncs0'])
        dma('sp', nfs_o[:, 0, :], sf.rearrange("(n t) c -> n t c", t=2)[:, 1, :], 'G:out', [], ['o_nfs0'])
        items = [(ps[0][0:NSP, j * 128:(j + 1) * 128], uext_s[:, j, 2:2 + NSP], 128) for j in range(4)]
        tpg(items, ['suext%d' % j for j in range(4)], [PSK[0]])
        S.op('act', lambda: acopy(out=stg[:, 0:512], in_=ps[0][0:NSP, :]), [PSK[0]], ['stg'])
        dma('sp', ncs_o[:, 1, :], stg[0:NS, 0:512], 'stg_o', ['stg'], ['o_ncs1'])
        for pi, (f0, nf) in enumerate(OPIECES):
            bank = 1 + (pi % 2)
            items = [(ps[bank][0:NSP, i * 128:(i + 1) * 128], gsT[:, f0 + i, :], 128) for i in range(nf)]
            tpg(items, ['gsT%d' % (f0 + i) for i in range(nf)], [PSK[bank]])
            S.op('act', lambda nf=nf, bank=bank: acopy(out=stg[:, 0:nf * 128], in_=ps[bank][0:NSP, 0:nf * 128]),
                 [PSK[bank]], ['stg'])
            dma('sp', nfs_o[:, 1, f0 * 128:(f0 + nf) * 128], stg[0:NS, 0:nf * 128], 'stg_o', ['stg'], ['o_nfs1_%d' % pi])


    PIECES = [(0, 6), (6, 6), (12, 6), (18, 4)]
    OPIECES = [(0, 4), (4, 4), (8, 4), (12, 4), (16, 4), (20, 2)]

    def samp_prologue():
        dma('sp', stg[:, 0:512], sc, 'stg_i', [], ['stg'])
        items = [(ps[0][:, j * 32:(j + 1) * 32], stg[:, j * 128:(j + 1) * 128], 2 * NS) for j in range(4)]
        tpg(items, ['stg'], [PSK[0]])
        S.op('act', lambda: acopy(out=scT[:], in_=ps[0][:, 0:128].rearrange("p (j m) -> p j m", j=4)),
             [PSK[0]], ['scT'])
        for pi, (f0, nf) in enumerate(PIECES):
            dma('sp', stg[:, 0:nf * 128], sf[:, f0 * 128:(f0 + nf) * 128], 'stg_i', [], ['stg'])
            bank = 1 + (pi % 2)
            items = [(ps[bank][:, i * 32:(i + 1) * 32], stg[:, i * 128:(i + 1) * 128], 2 * NS) for i in range(nf)]
            tpg(items, ['stg'], [PSK[bank]])
            S.op('act', lambda f0=f0, nf=nf, bank=bank: acopy(
                out=sfT[:, f0:f0 + nf, :], in_=ps[bank][:, 0:nf * 32].rearrange("p (j m) -> p j m", j=nf)),
                [PSK[bank]], ['sfT%d' % pi])

    def run_pass(gens):
        pend = []
        for g in gens:
            try:
                pend.append([g, next(g)])
            except StopIteration:
                pass
        while pend:
            sl = min(p[1] for p in pend)
            view = wload(sl)
            nxt = []
            for p in pend:
                if p[1] == sl:
                    try:
                        p[1] = p[0].send(view)
                        nxt.append(p)
                    except StopIteration:
                        pass
                else:
                    nxt.append(p)
            pend = nxt

    import os
    LIM = int(os.environ.get('KSTAGE', '99'))
    def conv_branch_samp(j, N):
        S.op('pool', lambda: G.tensor_copy(out=usT[:, j, :, 2], in_=uext_s[:, j, 2:2 + NS]), ['suext%d' % j], ['usT%d' % j])
        S.op('pool', lambda: G.tensor_copy(out=usT[:, j, :, 0:2],
                                           in_=scT[:, j, :].rearrange("p (n t) -> p n t", t=2)),
             ['scT'], ['usT%d' % j])
        c_ = cvt[j % 2][:, 0:NS]
        ck_ = 'cvt%d' % (j % 2)
        rk = ['usT%d' % j, 'cw']
        S.op('dve', lambda: V.tensor_scalar(out=c_, in0=usT[:, j, :, 0], scalar1=cw[:, 3 * j:3 * j + 1], scalar2=None,
                                             op0=ALU.mult), rk, [ck_])
        S.op('dve', lambda: V.scalar_tensor_tensor(out=c_, in0=usT[:, j, :, 1], scalar=cw[:, 3 * j + 1:3 * j + 2],
                                                    in1=c_, op0=ALU.mult, op1=ALU.add), rk + [ck_], [ck_])
        S.op('dve', lambda: V.scalar_tensor_tensor(out=c_, in0=usT[:, j, :, 2], scalar=cw[:, 3 * j + 2:3 * j + 3],
                                                    in1=c_, op0=ALU.mult, op1=ALU.add), rk + [ck_], [ck_])
        S.op('pool', lambda: G.tensor_tensor(out=gcv_s[:, j, 0:NS], in0=c_, in1=cbs_s[:, j, 0:NS], op=ALU.mult),
             [ck_, 'scbs%d' % j, 'sRC1'], ['sgcv%d' % j])

    NCH = min(5, max(0, LIM - 1)) if LIM < 99 else 5
    for ci in range(NCH):
        nl = (lambda ci=ci: xload('main', ci + 1)) if ci < 4 else None
        gens = [chunk('main', ci, nextload=nl, preloaded=(ci > 0))]
        if ci == 0 and LIM >= 7:
            xload('samp', 0)
            samp_prologue()
            gens.append(chunk('samp', 0, preloaded=True))
        run_pass(gens)
    flush_deferred()

    S.op('sp', None, ['o_ys', 'o_nk', 'o_nv', 'o_ncv', 'o_nfc', 'o_nks', 'o_nvs', 'o_nks2', 'o_nvs2',
                      'o_ncs0', 'o_nfs0', 'o_ncs1', 'o_nfs1_0', 'o_nfs1_1', 'o_nfs1_2', 'o_nfs1_3', 'o_nfs1_4', 'o_nfs1_5', 'o_yp0', 'o_yp1', 'o_yp2', 'o_yp3', 'o_yp4'], [])
    S.finalize(es)
    es.close()
    return nc, None


_PROG = {}


def _consts():
    slopes = 2.0 ** (-8.0 * np.arange(1, 9) / 8.0)
    s = np.arange(128)[:, None]
    q = np.arange(128)[None, :]
    eprev = np.zeros((128, 2, 4, 128), np.float32)
    ecur = np.zeros((128, 2, 4, 128), np.float32)
    esm = np.zeros((128, NS, 2, 4), np.float32)
    for g in range(2):
        for c in range(4):
            sl = slopes[4 * g + c]
            dprev = q + 128 - s
            dcur = q - s
            eprev[:, g, c, :] = np.where(s >= q, np.exp(-sl * dprev), 0.0)
            ecur[:, g, c, :] = np.where(q >= s, np.exp(-sl * dcur), 0.0)
            esm[:, :, g, c] = np.exp(-sl * (128 - np.arange(128)))[:, None]
    bones = (np.arange(128)[:, None] // 64 == np.arange(128)[None, :] // 64).astype(np.float32)
    return dict(eprev=eprev.reshape(128, 1024), ecur=ecur.reshape(128, 1024), esm=esm.reshape(128, 128),
                bones=bones, ident=np.eye(128, dtype=np.float32))


def kernel(x_prompt, x_sample, cache_k, cache_v, state_conv, state_ffn_conv, w_in, conv_w, attn_sinks,
           w_attn_out, w_conv_out, w_mix_out, ln1_g, ln1_b, w_gate, w_up, ffn_conv_w, w_down, ln2_g, ln2_b):
    f32 = np.float32
    A_ = lambda a: np.ascontiguousarray(np.asarray(a, dtype=f32))
    x_prompt = A_(x_prompt); x_sample = A_(x_sample)
    if 'nc' not in _PROG:
        _PROG['nc'], _PROG['es'] = build_program()
    nc = _PROG['nc']
    cols = []
    for c in range(4):
        cols += list(range(c * 64, c * 64 + 64)) + list(range((4 + c) * 64, (4 + c) * 64 + 64))
    cols += list(range(512, 768))
    cols += list(range(768, 1280))
    for j in range(4):
        cols += list(range(1280 + j * 128, 1280 + (j + 1) * 128)) + list(range(1792 + j * 128, 1792 + (j + 1) * 128))
    cols += list(range(2304, 4352))
    w_in_p = A_(A_(w_in)[0][:, cols])
    bc = lambda v: A_(np.broadcast_to(A_(v).reshape(1, -1), (128, A_(v).size)))
    fm = lambda v, n: A_(A_(v).reshape(n, 128).T)
    common = dict(
        w_in=w_in_p, w_ao=A_(w_attn_out)[0], w_co=A_(w_conv_out)[0], w_mix=A_(w_mix_out)[0],
        w_gate=A_(w_gate)[0], w_up=A_(w_up)[0], w_down=A_(w_down)[0],
        cw=A_(A_(conv_w)[0].reshape(3, 4, 128).transpose(2, 1, 0).reshape(128, 12)),
        fcw=A_(A_(ffn_conv_w)[0].reshape(3, NF, 128).transpose(2, 1, 0).reshape(128, 66)),
        sk=bc(A_(attn_sinks)[0]),
        g1b=bc(ln1_g[0]), b1b=bc(ln1_b[0]), g2b=bc(ln2_g[0]), b2b=bc(ln2_b[0]),
        g1f=fm(ln1_g[0], 8), b1f=fm(ln1_b[0], 8),
    )
    common.update(_consts())
    ck = A_(cache_k)[0].reshape(128, 128, 128)
    cv = A_(cache_v)[0].reshape(128, 128, 128)
    sc = A_(state_conv)[0].reshape(256, 512)
    sf = A_(state_ffn_conv)[0].reshape(256, DFF)
    in_maps = []
    for c in range(NCORES):
        b, r = c // 4, c % 4
        s0 = r * TOK_CORE
        xp = np.zeros((NTOK, D), f32)
        if r > 0:
            xp[0:HALO] = x_prompt[b, s0 - HALO:s0]
        xp[HALO:] = x_prompt[b, s0:s0 + TOK_CORE]
        m = dict(common)
        m.update(xp=xp, xs=A_(x_sample[c * NS:(c + 1) * NS, 0, :]),
                 ck=A_(ck[c * NS:(c + 1) * NS]), cv=A_(cv[c * NS:(c + 1) * NS]),
                 sc=A_(sc[c * 2 * NS:(c + 1) * 2 * NS]), sf=A_(sf[c * 2 * NS:(c + 1) * 2 * NS]),
                 hv=np.full((128, 1), 0.0 if r == 0 else 1.0, f32))
        in_maps.append(m)
    res = run_bass_kernel_spmd(nc, in_maps, core_ids=list(range(NCORES)))
    R = res.results
    y_prompt = np.zeros((2, 8192, D), f32)
    y_sample = np.zeros((128, 1, D), f32)
    nk_p = np.zeros((1, 2, 128, 2, 64), f32); nv_p = np.zeros((1, 2, 128, 2, 64), f32)
    nc_p = np.zeros((1, 2, 2, 512), f32); nf_p = np.zeros((1, 2, 2, DFF), f32)
    nk_s = np.zeros((1, 128, 128, 2, 64), f32); nv_s = np.zeros((1, 128, 128, 2, 64), f32)
    nc_s = np.zeros((1, 128, 2, 512), f32); nf_s = np.zeros((1, 128, 2, DFF), f32)
    for c in range(NCORES):
        b, r = c // 4, c % 4
        y_prompt[b, r * TOK_CORE:(r + 1) * TOK_CORE] = R[c]["yp"]
        y_sample[c * NS:(c + 1) * NS, 0] = R[c]["ys"]
        if r == 3:
            nk_p[0, b] = R[c]["nk"].reshape(128, 2, 64)
            nv_p[0, b] = R[c]["nv"].reshape(128, 2, 64)
            nc_p[0, b] = R[c]["ncv"]
            nf_p[0, b] = R[c]["nfc"]
        nk_s[0, c * NS:(c + 1) * NS] = R[c]["nks"].reshape(NS, 128, 2, 64)
        nv_s[0, c * NS:(c + 1) * NS] = R[c]["nvs"].reshape(NS, 128, 2, 64)
        nc_s[0, c * NS:(c + 1) * NS] = R[c]["ncs"]
        nf_s[0, c * NS:(c + 1) * NS] = R[c]["nfs"]
    return (y_prompt, y_sample, nk_p, nv_p, nc_p, nf_p, nk_s, nv_s, nc_s, nf_s)
```

```python
import contextlib
import numpy as np
import concourse.bass as bass
import concourse.mybir as mybir
from concourse.bass_utils import run_bass_kernel_spmd

F32 = mybir.dt.float32
BF16 = mybir.dt.bfloat16
AF = mybir.ActivationFunctionType
ALU = mybir.AluOpType

D = 1024
NPROJ = 4352
DFF = 2816
NF = 22
ALPHA = 2.0 ** 0.25
EPS = 1e-5
EPS_S = EPS / (ALPHA * ALPHA)
NCORES = 8
TOK_CORE = 2048
HALO = 256
NTOK = TOK_CORE + HALO
NBLK = NTOK // 128
NS = 16
NSP = 32
RING = 5


class Sched:
    def __init__(self, nc):
        self.nc = nc
        self.ops = []
        self.last_w = {}
        self.readers = {}

    def op(self, eng, fn, reads=(), writes=(), dma=None):
        deps = set()
        for k in reads:
            if k in self.last_w:
                deps.add(self.last_w[k])
            if k.startswith('ps'):
                deps.update(r for r in self.readers.get(k, ()) if self.ops[r]['eng'] != eng)
        for k in writes:
            if k in self.last_w:
                deps.add(self.last_w[k])
            deps.update(self.readers.get(k, ()))
        oid = len(self.ops)
        self.ops.append(dict(eng=eng, fn=fn, deps=deps, dma=dma, signal=False, waits=[]))
        for k in reads:
            self.readers.setdefault(k, []).append(oid)
        for k in writes:
            self.last_w[k] = oid
            self.readers[k] = []
        return oid

    def finalize(self, stack):
        nc = self.nc
        engs = {'pe': nc.tensor, 'act': nc.scalar, 'dve': nc.vector, 'pool': nc.gpsimd, 'sp': nc.sync}
        ops = self.ops
        cnt = {}
        for o in ops:
            st = ('dma:' + o['dma']) if o['dma'] else o['eng']
            o['stream'] = st
            cnt[st] = cnt.get(st, 0) + 1
            o['seq'] = cnt[st]
        for o in ops:
            if o['dma'] and o['dma'].startswith('G:'):
                o['seq'] = cnt[o['stream']]
        clk = {e: {} for e in engs}
        eng_seq = {e: 0 for e in engs}
        for o in ops:
            e = o['eng']
            c = clk[e]
            myseq = o['seq'] if not o['dma'] else None
            need = {}
            for d in o['deps']:
                od = ops[d]
                st = od['stream']
                if st == e and not o['dma']:
                    if e == 'pe':
                        continue
                    if o['seq'] - od['seq'] > 2:
                        continue
                    if c.get('self_' + e, 0) >= od['seq']:
                        continue
                elif c.get(st, 0) >= od['seq']:
                    continue
                if st not in need or ops[need[st]]['seq'] < od['seq']:
                    need[st] = d
            for st, d in need.items():
                od = ops[d]
                if st == e and not o['dma']:
                    c['self_' + e] = od['seq']
                elif c.get(st, 0) >= od['seq']:
                    continue
                od['signal'] = True
                o['waits'].append(d)
                for k, v in od['vc'].items():
                    if c.get(k, 0) < v:
                        c[k] = v
            if not o['dma']:
                c[e] = o['seq']
                o['vc'] = dict(c)
            else:
                vc = dict(c)
                vc[o['stream']] = o['seq']
                o['vc'] = vc
        sems = {}
        val = {}
        for o in ops:
            st = o['stream']
            if o['dma']:
                val[st] = val.get(st, 0) + 16
                o['val'] = val[st]
            elif o['signal']:
                val[st] = val.get(st, 0) + 1
                o['val'] = val[st]
        for o in ops:
            if o['dma'] and o['dma'].startswith('G:'):
                o['val'] = val[o['stream']]
        for st in val:
            sems[st] = stack.enter_context(nc.semaphore('s_' + st.replace(':', '_')))
        self.nsem = len(sems)
        for o in ops:
            e = engs[o['eng']]
            for d in o['waits']:
                od = ops[d]
                e.wait_ge(sems[od['stream']], od['val'])
            if o['fn'] is None:
                continue
            ins = o['fn']()
            if o['dma']:
                ins.then_inc(sems[o['stream']], 16)
            elif o['signal']:
                ins.then_inc(sems[o['stream']], 1)


def slot_table():
    t = []
    for i in range(17):
        t.append(('w_in', 0, 8, i * 256, 256))
    for h in range(2):
        t.append(('w_ao', 0, 4, h * 512, 512))
        t.append(('w_co', 0, 4, h * 512, 512))
    for n in range(2):
        for kh in range(2):
            t.append(('w_mix', kh * 4, 4, n * 512, 512))
    for j in range(11):
        t.append(('w_gate', 0, 8, j * 256, 256))
        t.append(('w_up', 0, 8, j * 256, 256))
    for n in range(2):
        for fg in range(6):
            nk = 4 if fg < 5 else 2
            t.append(('w_down', fg * 4, nk, n * 512, 512))
    return t


SLOTS = slot_table()
S_IN, S_AO, S_CO, S_MIX, S_FFN, S_DOWN = 0, 17, 19, 21, 25, 47
WSHAPES = {'w_in': (D, NPROJ), 'w_ao': (512, D), 'w_co': (512, D), 'w_mix': (D, D),
           'w_gate': (D, DFF), 'w_up': (D, DFF), 'w_down': (DFF, D)}


def build_program():
    nc = bass.Bass("TRN2", target_bir_lowering=False)
    S = Sched(nc)
    es = contextlib.ExitStack()

    def din(name, shape, dt=F32):
        return nc.dram_tensor(name, list(shape), dt, kind="ExternalInput").ap()

    def dout(name, shape, dt=F32):
        return nc.dram_tensor(name, list(shape), dt, kind="ExternalOutput").ap()

    xp = din("xp", [NTOK, D])
    xs = din("xs", [NS, D])
    ck = din("ck", [NS, 128, 128])
    cv = din("cv", [NS, 128, 128])
    sc = din("sc", [2 * NS, 512])
    sf = din("sf", [2 * NS, DFF])
    W = {k: din(k, v) for k, v in WSHAPES.items()}
    cw_d = din("cw", [128, 12])
    fcw_d = din("fcw", [128, 66])
    sk_d = din("sk", [128, 8])
    g1b_d = din("g1b", [128, D]); b1b_d = din("b1b", [128, D])
    g2b_d = din("g2b", [128, D]); b2b_d = din("b2b", [128, D])
    g1f_d = din("g1f", [128, 8]); b1f_d = din("b1f", [128, 8])
    ident_d = din("ident", [128, 128])
    eprev_d = din("eprev", [128, 1024]); ecur_d = din("ecur", [128, 1024])
    esm_d = din("esm", [128, 128])
    bones_d = din("bones", [128, 128])
    hv_d = din("hv", [128, 1])

    yp = dout("yp", [TOK_CORE, D])
    ys = dout("ys", [NS, D])
    nk_o = dout("nk", [128, 128]); nv_o = dout("nv", [128, 128])
    ncv_o = dout("ncv", [2, 512]); nfc_o = dout("nfc", [2, DFF])
    nks_o = dout("nks", [NS, 128, 128]); nvs_o = dout("nvs", [NS, 128, 128])
    ncs_o = dout("ncs", [NS, 2, 512]); nfs_o = dout("nfs", [NS, 2, DFF])
    wscr = nc.dram_tensor("wscr", [len(SLOTS), 128, 2048], BF16, kind="Internal").ap()

    def sb(name, shape, dt):
        return es.enter_context(nc.sbuf_tensor("s_" + name, list(shape), dt))

    xres = [sb("xres%d" % i, [128, 4, D], F32) for i in range(2)]
    RB = sb("RB", [128, 8, 512], BF16)
    RA = sb("RA", [128, NF, 512], BF16)
    RC = sb("RC", [128, 12288], BF16)
    qT = RC[:, 0:2048].rearrange("p (c n) -> p c n", c=4)
    oT = RC[:, 2048:4096].rearrange("p (c n) -> p c n", c=4)
    mg = RC[:, 4096:8192].rearrange("p (c n) -> p c n", c=8)
    gcv = RC[:, 8192:10240].rearrange("p (c n) -> p c n", c=4)
    cbs = RC[:, 10240:12288].rearrange("p (c n) -> p c n", c=4)
    gpa = RC[:, 0:NF * 514].rearrange("p (f n) -> p f n", f=NF)
    kT = sb("kT", [128, NBLK * 128], BF16)
    Vx = sb("Vx", [128, NBLK, 2, 65], BF16)
    ccs = [sb("ccs%d" % i, [128, 512], F32) for i in range(2)]
    uext = sb("uext", [128, 4, 514], F32)
    cvt = [sb("cvt%d" % i, [128, 512], F32) for i in range(2)]
    praw = [sb("praw%d" % i, [128, 512], BF16) for i in range(3)]
    pT = [sb("pT%d" % i, [128, 512], BF16) for i in range(8)]
    onb = [sb("on%d" % i, [128, 512], F32) for i in range(2)]
    rdt = [sb("rdt%d" % i, [128, 8], F32) for i in range(2)]
    m12 = [sb("m12_%d" % i, [128, 512], BF16) for i in range(4)]
    ct = [sb("ct%d" % i, [128, 512], F32) for i in range(2)]
    gl = [sb("gl%d" % i, [128, 512], BF16) for i in range(2)]
    ring = sb("ring", [128, RING, 2048], BF16)
    g1b = sb("g1b", [128, D], F32); b1b = sb("b1b", [128, D], F32)
    g2b = sb("g2b", [128, D], F32); b2b = sb("b2b", [128, D], F32)
    g1f = sb("g1f", [128, 8], F32); b1f = sb("b1f", [128, 8], F32)
    ident = sb("ident", [128, 128], F32)
    E_prev = sb("E_prev", [128, 1024], BF16)
    E_cur = sb("E_cur", [128, 1024], BF16)
    E_first = sb("E_first", [128, 1024], BF16)
    esm = sb("esm", [128, 128], BF16)
    bones = sb("bones", [128, 128], BF16)
    hv = sb("hv", [128, 1], F32)
    cw = sb("cw", [128, 12], F32)
    fcw = sb("fcw", [128, 66], F32)
    skt = sb("skt", [128, 8], F32)
    esk = sb("esk", [128, 8], F32)
    epsb = sb("epsb", [128, 1], F32)
    stt = [sb("stt%d" % i, [128, 2, 6], F32) for i in range(2)]
    mv = sb("mv", [128, 8, 2], F32)
    lnv = sb("lnv", [128, 8], F32)
    rstd = sb("rstd", [128, 8], F32)
    nkv = sb("nkv", [128, 256], F32)
    gl2 = sb("gl2", [128, NF, 32], F32)
    fh = sb("fh", [128, NF, 2], BF16)
    xsr = sb("xsr", [NSP, D], F32)
    scT = sb("scT", [128, 4, 2 * NS], F32)
    sfT = sb("sfT", [128, NF, 2 * NS], F32)
    vTs = sb("vTs", [128, NS], F32)
    prod = sb("prod", [128, 4, NS], BF16)
    pnb = sb("pnb", [128, 4, NS], F32)
    ons = sb("ons", [128, 4, NS], F32)
    dns = sb("dns", [128, 4, NS], F32)
    pTs = sb("pTs", [128, NS * 8], BF16)
    praws = sb("praws", [128, NS * 8], BF16)
    kvs = sb("kvs", [NSP, 256], F32)
    x0f = xres[0][:, :, :].rearrange("p b f -> p (b f)")
    x1f = xres[1][:, :, :].rearrange("p b f -> p (b f)")
    ncf = x0f[0:32, 0:512 + DFF]
    ones_t = sb("ones_t", [128, 64], BF16)
    ckc = xres[0][:, :, :].rearrange("p b f -> p (b f)")[:, 0:2048].rearrange("p (n f) -> p n f", n=NS)
    cvb = xres[0][:, :, :].rearrange("p b f -> p (b f)")[:, 2048:3072].bitcast(BF16).rearrange("p (n f) -> p n f", n=NS)
    kcT = xres[0][:, :, :].rearrange("p b f -> p (b f)")[:, 3072:4096].bitcast(BF16).rearrange("p (n f) -> p n f", n=NS)

    ps = [es.enter_context(nc.psum_tensor("ps%d" % i, [128, 512], F32)) for i in range(8)]
    PSK = ["ps%d" % i for i in range(8)]

    T, V, A, G, SP = nc.tensor, nc.vector, nc.scalar, nc.gpsimd, nc.sync

    def mmg(out_ap, pairs, reads, writes):
        def fn():
            last = None
            n = len(pairs)
            for i, (l, r) in enumerate(pairs):
                last = T.matmul(out_ap, l, r, start=(i == 0), stop=(i == n - 1))
            return last
        return S.op('pe', fn, reads, writes)

    def tpg(items, reads, writes):
        def fn():
            last = None
            for (o, i, npart) in items:
                last = T.transpose(o, i, ident[0:npart, 0:npart])
            return last
        return S.op('pe', fn, reads + ['ident'], writes)

    def acopy(out, in_):
        return A.activation(out=out, in_=in_, func=AF.Copy)

    def act(out, in_, func, reads, writes, bias=None, scale=None):
        kw = {}
        if bias is not None:
            kw['bias'] = bias
        if scale is not None:
            kw['scale'] = scale
        return S.op('act', lambda: A.activation(out=out, in_=in_, func=func, **kw), reads, writes)

    def dma(q, out, in_, key, reads, writes):
        e = {'sp': SP, 'pool': G, 'act': A}[q]
        return S.op(q, lambda: e.dma_start(out=out, in_=in_), reads, writes, dma=key)

    for i, (t, d, k) in enumerate([(g1b, g1b_d, 'g1b'), (b1b, b1b_d, 'b1b'), (g2b, g2b_d, 'g2b'), (b2b, b2b_d, 'b2b'),
                                   (g1f, g1f_d, 'g1f'), (b1f, b1f_d, 'b1f'), (ident, ident_d, 'ident'),
                                   (hv, hv_d, 'hv'), (cw, cw_d, 'cw'), (fcw, fcw_d, 'fcw'), (skt, sk_d, 'skt')]):
        dma('sp', t[:], d, 'G:c', [], [k])
    dma('pool', E_prev[:], eprev_d, 'G:cE', [], ['E_prev'])
    dma('pool', E_cur[:], ecur_d, 'G:cE', [], ['E_cur'])
    dma('pool', esm[:], esm_d, 'G:cE', [], ['esm'])
    dma('pool', bones[:], bones_d, 'G:cE', [], ['bones'])
    S.op('dve', lambda: V.memset(Vx[:, :, :, 64:65], 1.0), [], ['Vx1'])
    S.op('dve', lambda: V.memset(uext[:], 0.0), [], ['uext%d' % j for j in range(4)])
    S.op('dve', lambda: V.memset(RC[:], 0.0), [], ['RC1', 'RC2'])
    S.op('dve', lambda: V.memset(epsb[:], EPS_S), [], ['epsb'])
    S.op('dve', lambda: V.memset(ones_t[:], 1.0), [], ['ones_t'])
    S.op('dve', lambda: V.memset(xsr[:], 0.0), [], ['xsr'])
    S.op('dve', lambda: V.memset(fh[:], 0.0), [], ['fh%d' % f for f in range(NF)])
    S.op('dve', lambda: V.tensor_scalar(out=E_first[:], in0=E_prev[:], scalar1=hv[:, 0:1], scalar2=None,
                                        op0=ALU.mult), ['E_prev', 'hv'], ['E_first'])
    act(esk[:], skt[:], AF.Exp, ['skt'], ['esk'])

    for s, (wn, k0, nk, c0, ncol) in enumerate(SLOTS):
        src = W[wn].rearrange("(k p) n -> p k n", p=128)[:, k0:k0 + nk, c0:c0 + ncol]
        dst = wscr[s].rearrange("p (k c) -> p k c", c=ncol)[:, 0:nk, :]
        dma('pool', dst, src, 'G:cv%d' % (s // 3), [], ['scr%d' % s])

    rstate = {'n': 0}

    def wload(s):
        r = rstate['n'] % RING
        rstate['n'] += 1
        wn, k0, nk, c0, ncol = SLOTS[s]
        dma('sp', ring[:, r, :], wscr[s], 'ring%d' % r, ['scr%d' % s], ['ring%d' % r])
        view = ring[:, r, :].rearrange("p (k c) -> p k c", c=ncol)
        return view, 'ring%d' % r

    deferred = []
    import os
    DBG = os.environ.get('KDEBUG', '') == '1'
    taps = {}

    def tap(name, ap, reads):
        if not DBG or name in taps:
            return
        shp = list(ap.shape)
        d = nc.dram_tensor("dbg_" + name, shp, ap.dtype, kind="ExternalOutput").ap()
        taps[name] = d
        dma('sp', d, ap, 'dbg%d' % len(taps), reads, ['o_dbg_' + name])

    def xload(kind, ci):
        if kind == 'samp':
            dma('sp', xsr[0:NS, :], xs, 'xs', [], ['xsr'])
        elif kind == 'halo':
            dma('sp', xres[1][:, 0:2, :], xp[0:256, :].rearrange("(b p) f -> p b f", p=128), 'x1', [], ['xres1'])
        else:
            par = (ci + 1) % 2
            t0 = ci * 512
            nbl = 4 if ci < 4 else 2
            dma('sp', xres[par][:, 0:nbl, :], xp[t0:t0 + nbl * 128, :].rearrange("(b p) f -> p b f", p=128),
                'x%d' % par, [], ['xres%d' % par])

    def flush_deferred():
        for f in deferred:
            f()
        deferred.clear()

    def chunk(kind, ci, nextload=None, preloaded=False):
        samp = kind == 'samp'
        halo = kind == 'halo'
        if kind == 'main':
            par = (ci + 1) % 2
            tok0 = ci * 512
            nb = 4 if ci < 4 else 2
            Nx, a0, N, NP = nb * 128, 0, nb * 128, 128
            blk0 = tok0 // 128
            xr = xres[par]
            xkey = 'xres%d' % par
        elif halo:
            par = 1
            tok0 = 0
            Nx, a0, N, nb, NP = 256, 128, 128, 1, 128
            blk0 = 0
            xr = xres[1]
            xkey = 'xres1'
        else:
            Nx, a0, N, nb, NP = NSP, 0, NSP, 1, NSP
            blk0 = None
            xkey = 'xsr'
        nbx = Nx // 128 if not samp else 1
        last_main = (kind == 'main' and ci == 4)

        def xrow(b):
            if samp:
                return xsr[:, :]
            return xr[:, (a0 // 128) + b, :]

        def xrow_all(b):
            if samp:
                return xsr[:, :]
            return xr[:, b, :]

        kp = 's' if samp else ''

        def K(name):
            return kp + name
        if samp:
            RBt, RAt, qTt, oTt, mgt, gcvt, cbst, uextt = RB_s, RA_s, qT_s, oT_s, mg_s, gcv_s, cbs_s, uext_s
        else:
            RBt, RAt, qTt, oTt, mgt, gcvt, cbst, uextt = RB, RA, qT, oT, mg, gcv, cbs, uext
        xT = RBt
        x1T = RBt

        if not preloaded:
            xload(kind, ci)

        first = True
        for kc in range(8):
            bank = 6 + (kc % 2)
            items = [(ps[bank][:, b * 128:b * 128 + NP], xrow_all(b)[:, kc * 128:(kc + 1) * 128], NP)
                     for b in range(nbx)]
            tpg(items, [xkey], [PSK[bank]])
            w = [K('xT%d' % kc), K('RB2')] if first else [K('xT%d' % kc)]
            first = False
            act(xT[:, kc, 0:Nx], ps[bank][:, 0:Nx], AF.Copy, [PSK[bank], K('RB1')], w)

        mmb = {'i': 0}

        def nextbank():
            b = mmb['i'] % 4
            mmb['i'] += 1
            return b

        xTk = [K('xT%d' % kc) for kc in range(8)] + [K('RB1')]
        firstRC = {'v': True}
        firstRA = {'v': True}

        def rcw(keys):
            if firstRC['v']:
                firstRC['v'] = False
                return keys + [K('RC2')]
            return keys

        for i in range(17):
            wv, wk = yield (S_IN + i)
            if i == 3:
                flush_deferred()
            for half in range(2):
                m = 2 * i + half
                lhs = lambda kc, half=half, wv=wv: wv[:, kc, half * 128:(half + 1) * 128]
                if m == 5:
                    for b in range(nbx):
                        mmg(ps[4][0:NP, b * 128:(b + 1) * 128],
                            [(xT[:, kc, b * 128:b * 128 + NP], wv[:, kc, 128:256]) for kc in range(8)],
                            xTk + [wk], [PSK[4]])
                    if samp:
                        S.op('act', lambda: acopy(out=kvs[:, 128:256], in_=ps[4][0:NSP, 0:128]),
                             [PSK[4]], ['kvs_v'])
                        bk = nextbank()
                        mmg(ps[bk][:, 0:N], [(lhs(kc), xT[:, kc, 0:N]) for kc in range(8)], xTk + [wk], [PSK[bk]])
                        act(vTs[:, :], ps[bk][:, 0:NS], AF.Copy, [PSK[bk]], ['vTs'])
                    else:
                        S.op('dve', lambda: V.tensor_copy(
                            out=Vx[:, blk0:blk0 + nbx, :, 0:64],
                            in_=ps[4][:, 0:Nx].rearrange("p (b g d) -> p b g d", b=nbx, g=2)),
                            [PSK[4]], ['Vx%d' % (blk0 + b) for b in range(nbx)])
                        if last_main and 'lmv' not in os.environ.get('KSKIP', ''):
                            S.op('act', lambda: acopy(out=nkv[:, 128:256], in_=ps[4][:, N - 128:N]),
                                 [PSK[4]], ['nkv_v'])
                    continue
                bk = nextbank()
                if m == 4:
                    mmg(ps[bk][:, 0:Nx], [(lhs(kc), xT[:, kc, 0:Nx]) for kc in range(8)], xTk + [wk], [PSK[bk]])
                    if samp:
                        act(kcT_new[:, :], ps[bk][:, 0:NS], AF.Copy, [PSK[bk]], ['kTs'])
                        mmg(ps[4][0:NSP, 0:128], [(xT[:, kc, 0:NSP], wv[:, kc, 0:128]) for kc in range(8)],
                            xTk + [wk], [PSK[4]])
                        S.op('act', lambda: acopy(out=kvs[:, 0:128], in_=ps[4][0:NSP, 0:128]),
                             [PSK[4]], ['kvs_k'])
                    else:
                        act(kT[:, blk0 * 128:blk0 * 128 + Nx], ps[bk][:, 0:Nx], AF.Copy, [PSK[bk]],
                            ['kT%d' % (blk0 + b) for b in range(nbx)])
                        if last_main and 'lmk' not in os.environ.get('KSKIP', ''):
                            mmg(ps[5][:, 0:128], [(xT[:, kc, N - 128:N], wv[:, kc, 0:128]) for kc in range(8)],
                                xTk + [wk], [PSK[5]])
                            S.op('act', lambda: acopy(out=nkv[:, 0:128], in_=ps[5][:, 0:128]),
                                 [PSK[5]], ['nkv_k'])
                    continue
                mmg(ps[bk][:, 0:N], [(lhs(kc), xT[:, kc, a0:a0 + N]) for kc in range(8)], xTk + [wk], [PSK[bk]])
                pin = ps[bk][:, 0:N]
                if m < 4:
                    act(qTt[:, m, 0:N], pin, AF.Copy, [PSK[bk], K('RC1')], rcw([K('qT%d' % m)]), scale=0.125)
                elif m < 10:
                    j = m - 6
                    act(cbst[:, j, 0:N], pin, AF.Copy, [PSK[bk], K('RC1')], [K('cbs%d' % j)])
                elif m < 18:
                    j = (m - 10) // 2
                    if (m - 10) % 2 == 0:
                        act(ccs[j % 2][:, 0:N], pin, AF.Copy, [PSK[bk]], ['ccs%d' % (j % 2)])
                    else:
                        S.op('dve', lambda pin=pin, j=j: V.tensor_tensor(out=uextt[:, j, 2:2 + N], in0=pin,
                                                                        in1=ccs[j % 2][:, 0:N], op=ALU.mult),
                             [PSK[bk], 'ccs%d' % (j % 2)], [K('uext%d' % j)])
                        (conv_branch_samp if samp else conv_branch)(j, N)
                else:
                    t = m - 18
                    w = [K('tg%d' % t)]
                    if firstRA['v']:
                        firstRA['v'] = False
                        w = w + [K('RA2')]
                    act(RAt[:, t, 0:N], pin, AF.Tanh, [PSK[bk], K('RA1')], w, scale=0.5)

        if kind == 'main' and ci == 0:
            tap('xT', xT[:, 0, 0:512], ['xT0'])
            tap('qT', qTt[:, 0, 0:512], ['qT0'])
            tap('kT', kT[:, 256:768], ['kT2', 'kT3', 'kT4', 'kT5'])
            tap('Vx', Vx[:, 2, :, :], ['Vx2'])
            tap('tg', RAt[:, 0, 0:512], ['tg0'])
            tap('gcv', gcvt[:, 0, 0:512], ['gcv0'])
            tap('uext', uextt[:, 0, :], ['uext0'])
        if not samp:
            S.op('pool', lambda: G.tensor_copy(out=uextt[:, :, 0:2], in_=uextt[:, :, N:N + 2]),
                 [K('uext%d' % j) for j in range(4)], [K('uext%d' % j) for j in range(4)])

        if samp:
            attention_samples()
        else:
            for b in range(nb):
                attention_block(blk0 + (a0 // 128) + b, b, first_blk=(blk0 + b == 2))

        if kind == 'main' and ci == 0:
            tap(K('oT'), oTt[:, 0, 0:512], [K('oT')])
        ao = [None, None]
        co = [None, None]
        for m in range(8):
            h, ml = m // 4, m % 4
            if ml == 0:
                ao[h] = yield (S_AO + 2 * h)
                co[h] = yield (S_AO + 2 * h + 1)
            ba, bc = (0, 1) if m % 2 == 0 else (2, 3)
            mmg(ps[ba][:, 0:N], [(ao[h][0][:, c, ml * 128:(ml + 1) * 128], oTt[:, c, 0:N]) for c in range(4)],
                [K('oT'), ao[h][1], K('RC1')], [PSK[ba]])
            mmg(ps[bc][:, 0:N], [(co[h][0][:, c, ml * 128:(ml + 1) * 128], gcvt[:, c, 0:N]) for c in range(4)],
                [K('gcv%d' % c) for c in range(4)] + [co[h][1], K('RC1')], [PSK[bc]])
            i1, i2 = (0, 1) if m % 2 == 0 else (2, 3)
            S.op('dve', lambda m=m, ba=ba, i1=i1: V.scalar_tensor_tensor(
                out=m12[i1][:, 0:N], in0=RAt[:, m, 0:N], scalar=1.0, in1=ps[ba][:, 0:N],
                op0=ALU.add, op1=ALU.mult), [PSK[ba], K('tg%d' % m), K('RA1')], ['m12_%d' % i1])
            S.op('dve', lambda m=m, bc=bc, i2=i2: V.scalar_tensor_tensor(
                out=m12[i2][:, 0:N], in0=RAt[:, 8 + m, 0:N], scalar=1.0, in1=ps[bc][:, 0:N],
                op0=ALU.add, op1=ALU.mult), [PSK[bc], K('tg%d' % (8 + m)), K('RA1')], ['m12_%d' % i2])
            S.op('pool', lambda m=m, i1=i1, i2=i2: G.tensor_tensor(out=mgt[:, m, 0:N], in0=m12[i1][:, 0:N],
                                                                  in1=m12[i2][:, 0:N], op=ALU.add),
                 ['m12_%d' % i1, 'm12_%d' % i2, K('RC1')], [K('mg%d' % m)])

        if kind == 'main' and ci == 0:
            tap('mg', mgt[:, 0, 0:512], ['mg0'])
        mgk = [K('mg%d' % m) for m in range(8)]
        mb = 0
        for n in range(2):
            wm = []
            for kh in range(2):
                wm.append((yield (S_MIX + n * 2 + kh)))
            for b in range(nb):
                bank = 4 + (mb % 4)
                mb += 1
                mmg(ps[bank][0:NP, :],
                    [(mgt[:, kc, b * 128:b * 128 + NP], wm[kc // 4][0][:, kc % 4, :]) for kc in range(8)],
                    mgk + [wm[0][1], wm[1][1], K('RC1')], [PSK[bank]])
                xs_ = xrow(b)[0:NP, n * 512:(n + 1) * 512]
                S.op('dve', lambda bank=bank, xs_=xs_: V.scalar_tensor_tensor(
                    out=xs_, in0=ps[bank][0:NP, :], scalar=0.5 / ALPHA, in1=xs_, op0=ALU.mult, op1=ALU.add),
                    [PSK[bank], xkey], [xkey + 'h%d_%d' % (b, n)])

        for b in range(nb):
            layer_norm(xrow(b), NP, b, [xkey + 'h%d_%d' % (b, n) for n in range(2)], xkey + 'n%d' % b)

        first = True
        for kc in range(8):
            bank = kc % 2
            items = [(ps[bank][:, b * 128:b * 128 + NP], xrow(b)[0:NP, kc * 128:(kc + 1) * 128], NP)
                     for b in range(nb)]
            tpg(items, [xkey + 'n%d' % b for b in range(nb)], [PSK[bank]])
            w = [K('x1T%d' % kc), K('RB1')] if first else [K('x1T%d' % kc)]
            first = False
            act(x1T[:, kc, 0:N], ps[bank][:, 0:N], AF.Identity, [PSK[bank], K('RB2'), 'g1f', 'b1f'], w,
                bias=b1f[:, kc:kc + 1], scale=g1f[:, kc:kc + 1])
        if not halo:
            for b in range(nb):
                r = xrow(b)[0:NP, :]
                S.op('pool', lambda r=r: G.tensor_tensor(out=r, in0=r, in1=g1b[0:NP, :], op=ALU.mult),
                     [xkey + 'n%d' % b, 'g1b'], [xkey + 'n%d' % b])
                S.op('pool', lambda r=r: G.tensor_tensor(out=r, in0=r, in1=b1b[0:NP, :], op=ALU.add),
                     [xkey + 'n%d' % b, 'b1b'], [xkey + 'n%d' % b])

        if kind == 'main' and ci == 0:
            tap('x1', xr[:, 0, :], [xkey + 'n0'])
            tap('x1T', x1T[:, 0, 0:512], ['x1T0'])
        x1k = [K('x1T%d' % kc) for kc in range(8)]
        firstG = True
        firstH = True
        if samp:
            S.op('pool', lambda: G.tensor_copy(out=gpas[:, :, :, 0:2],
                                               in_=sfT[:, :, :].rearrange("p f (n t) -> p f n t", t=2)),
                 ['sfT0', 'sfT1', 'sfT2', 'sfT3', K('RC2')], [K('gpa%d' % f) for f in range(NF)] + [K('RC1')])
            firstG = False
        elif not halo:
            S.op('pool', lambda: G.tensor_copy(out=gpa[:, :, 0:2], in_=fh[:, :, :]),
                 ['fh%d' % f for f in range(NF)] + [K('RC2')], [K('gpa%d' % f) for f in range(NF)] + [K('RC1')])
            firstG = False
        if nextload is not None:
            nextload()
        for j in range(11):
            wg = yield (S_FFN + 2 * j)
            wu = None
            if not halo:
                wu = yield (S_FFN + 2 * j + 1)
            for fl in range(2):
                f = 2 * j + fl
                bg, bu = (0, 1) if f % 2 == 0 else (2, 3)
                mmg(ps[bg][:, 0:N], [(wg[0][:, kc, fl * 128:(fl + 1) * 128], x1T[:, kc, 0:N]) for kc in range(8)],
                    x1k + [wg[1], K('RB2')], [PSK[bg]])
                w = [K('gpa%d' % f)]
                if firstG:
                    firstG = False
                    w = w + [K('RC1')]
                if halo:
                    S.op('dve', lambda f=f, bg=bg: V.tensor_scalar(
                        out=fh[:, f, :], in0=ps[bg][:, N - 2:N], scalar1=hv[:, 0:1], scalar2=None,
                        op0=ALU.mult), [PSK[bg], 'hv'], ['fh%d' % f])
                    continue
                if samp:
                    S.op('act', lambda f=f, bg=bg: A.activation(out=gpas[:, f, :, 2], in_=ps[bg][:, 0:NS], func=AF.Copy),
                         [PSK[bg], K('RC2')], w)
                    S.op('act', lambda f=f, bg=bg: acopy(out=gsT[:, f, :], in_=ps[bg][:, 0:NSP]),
                         [PSK[bg]], ['gsT%d' % f])
                else:
                    act(gpa[:, f, 2:2 + N], ps[bg][:, 0:N], AF.Copy, [PSK[bg], K('RC2')], w)
                    if kind == 'main' and ci == 0:
                        S.op('dve', lambda f=f: V.tensor_scalar(
                            out=gpa[:, f, 256:258], in0=gpa[:, f, 256:258], scalar1=hv[:, 0:1], scalar2=None,
                            op0=ALU.mult), [K('gpa%d' % f), 'hv', K('RC2')], [K('gpa%d' % f)])
                    if last_main and 'lmg' not in os.environ.get('KSKIP', ''):
                        act(gl2[:, f, :], ps[bg][:, N - 32:N], AF.Copy, [PSK[bg]], ['gl2_%d' % f])
                if halo:
                    continue
                mmg(ps[bu][:, 0:N], [(wu[0][:, kc, fl * 128:(fl + 1) * 128], x1T[:, kc, 0:N]) for kc in range(8)],
                    x1k + [wu[1], K('RB2')], [PSK[bu]])
                ci2 = f % 2
                if samp:
                    g0, g1_, g2_ = gpas[:, f, :, 0], gpas[:, f, :, 1], gpas[:, f, :, 2]
                else:
                    g0, g1_, g2_ = gpa[:, f, 0:N], gpa[:, f, 1:N + 1], gpa[:, f, 2:N + 2]
                NN = NS if samp else N
                c_ = ct[ci2][:, 0:NN]
                rk = [K('gpa%d' % f), 'fcw', K('RC2')]
                S.op('dve', lambda f=f, c_=c_, g0=g0: V.tensor_scalar(
                    out=c_, in0=g0, scalar1=fcw[:, 3 * f:3 * f + 1], scalar2=None, op0=ALU.mult), rk, ['ct%d' % ci2])
                S.op('dve', lambda f=f, c_=c_, g1_=g1_: V.scalar_tensor_tensor(
                    out=c_, in0=g1_, scalar=fcw[:, 3 * f + 1:3 * f + 2], in1=c_, op0=ALU.mult, op1=ALU.add),
                    rk + ['ct%d' % ci2], ['ct%d' % ci2])
                S.op('dve', lambda f=f, c_=c_, g2_=g2_: V.scalar_tensor_tensor(
                    out=c_, in0=g2_, scalar=fcw[:, 3 * f + 2:3 * f + 3], in1=c_, op0=ALU.mult, op1=ALU.add),
                    rk + ['ct%d' % ci2], ['ct%d' % ci2])
                act(gl[ci2][:, 0:NN], c_, AF.Gelu, ['ct%d' % ci2], ['gl%d' % ci2])
                w = [K('hT%d' % f)]
                if firstH:
                    firstH = False
                    w = w + [K('RA1')]
                S.op('dve', lambda f=f, bu=bu, ci2=ci2, NN=NN: V.tensor_tensor(
                    out=RAt[:, f, 0:NN], in0=ps[bu][:, 0:NN], in1=gl[ci2][:, 0:NN], op=ALU.mult),
                    [PSK[bu], 'gl%d' % ci2, K('RA2')], w)
        if halo:
            return
        if not samp:
            S.op('pool', lambda: G.tensor_copy(out=fh[:, :, :], in_=gpa[:, :, N:N + 2]),
                 [K('gpa%d' % f) for f in range(NF)] + [K('RC2')], ['fh%d' % f for f in range(NF)])

        if kind == 'main' and ci == 0:
            tap('hT', RAt[:, 0, 0:512], ['hT0'])
        hk = [K('hT%d' % f) for f in range(NF)]
        for n in range(2):
            for fg in range(6):
                wd = yield (S_DOWN + n * 6 + fg)
                nk = 4 if fg < 5 else 2
                for b in range(nb):
                    bank = 0 if samp else 4 + b

                    def fn(b=b, bank=bank, fg=fg, nk=nk, wd=wd):
                        last = None
                        for kk in range(nk):
                            f = fg * 4 + kk
                            last = T.matmul(ps[bank][0:NP, :], RAt[:, f, b * 128:b * 128 + NP], wd[0][:, kk, :],
                                            start=(f == 0), stop=(f == NF - 1))
                        return last
                    S.op('pe', fn, hk + [wd[1], K('RA2')], [PSK[bank]])
            for b in range(nb):
                bank = 0 if samp else 4 + b
                xs_ = xrow(b)[0:NP, n * 512:(n + 1) * 512]
                S.op('dve', lambda bank=bank, xs_=xs_: V.scalar_tensor_tensor(
                    out=xs_, in0=ps[bank][0:NP, :], scalar=1.0 / ALPHA, in1=xs_, op0=ALU.mult, op1=ALU.add),
                    [PSK[bank], xkey + 'n%d' % b], [xkey + 'z%d_%d' % (b, n)])
        for b in range(nb):
            layer_norm(xrow(b), NP, 4 + b, [xkey + 'z%d_%d' % (b, n) for n in range(2)], xkey + 'y%d' % b)
            r = xrow(b)[0:NP, :]
            S.op('pool', lambda r=r: G.tensor_tensor(out=r, in0=r, in1=g2b[0:NP, :], op=ALU.mult),
                 [xkey + 'y%d' % b, 'g2b'], [xkey + 'y%d' % b])
            S.op('pool', lambda r=r: G.tensor_tensor(out=r, in0=r, in1=b2b[0:NP, :], op=ALU.add),
                 [xkey + 'y%d' % b, 'b2b'], [xkey + 'y%d' % b])

        if samp:
            dma('sp', ys, xsr[0:NS, :], 'G:out', ['xsry0'], ['o_ys'])
        else:
            def store(ci=ci, par=par, xr=xr, xkey=xkey, nb=nb):
                b0 = 2 if ci == 0 else 0
                r0 = 0 if ci == 0 else 256 + (ci - 1) * 512
                nr = (nb - b0) * 128
                dma('sp', yp[r0:r0 + nr, :].rearrange("(b p) f -> p b f", p=128), xr[:, b0:nb, :],
                    'oy%d' % par, [xkey + 'y%d' % b for b in range(nb)], [xkey, 'o_yp%d' % ci])
            deferred.append(store)
        if last_main and 'final' not in os.environ.get('KSKIP', ''):
            final_prompt_outputs(N)
        if samp:
            final_sample_outputs()

    def layer_norm(r, NP, slot, rkeys, wkey):
        st = stt[slot % 2]
        S.op('dve', lambda: V.bn_stats(out=st[0:NP, 0, :], in_=r[0:NP, 0:512]), rkeys, ['stt%d' % (slot % 2)])
        S.op('dve', lambda: V.bn_stats(out=st[0:NP, 1, :], in_=r[0:NP, 512:1024]), rkeys, ['stt%d' % (slot % 2)])
        S.op('dve', lambda: V.bn_aggr(out=mv[0:NP, slot, :], in_=st[0:NP, :, :]), ['stt%d' % (slot % 2)], ['mv%d' % slot])
        act(lnv[0:NP, slot:slot + 1], mv[0:NP, slot, 1:2], AF.Ln, ['mv%d' % slot, 'epsb'], ['lnv%d' % slot],
            bias=epsb[0:NP, :], scale=1.0)
        act(rstd[0:NP, slot:slot + 1], lnv[0:NP, slot:slot + 1], AF.Exp, ['lnv%d' % slot], ['rstd%d' % slot], scale=-0.5)
        S.op('dve', lambda: V.tensor_scalar(out=r[0:NP, :], in0=r[0:NP, :], scalar1=mv[0:NP, slot, 0:1],
                                            scalar2=rstd[0:NP, slot:slot + 1], op0=ALU.subtract, op1=ALU.mult),
             rkeys + ['mv%d' % slot, 'rstd%d' % slot], [wkey])

    def conv_branch(j, N, samp_views=None):
        c_ = cvt[j % 2][:, 0:N]
        if samp_views is None:
            u0, u1, u2 = uext[:, j, 0:N], uext[:, j, 1:N + 1], uext[:, j, 2:N + 2]
            rk = ['uext%d' % j, 'cw']
        else:
            u0, u1, u2 = samp_views
            rk = ['uext%d' % j, 'cw', 'scT']
        ck_ = 'cvt%d' % (j % 2)
        S.op('dve', lambda: V.tensor_scalar(out=c_, in0=u0, scalar1=cw[:, 3 * j:3 * j + 1], scalar2=None,
                                             op0=ALU.mult), rk, [ck_])
        S.op('dve', lambda: V.scalar_tensor_tensor(out=c_, in0=u1, scalar=cw[:, 3 * j + 1:3 * j + 2], in1=c_,
                                                    op0=ALU.mult, op1=ALU.add), rk + [ck_], [ck_])
        S.op('dve', lambda: V.scalar_tensor_tensor(out=c_, in0=u2, scalar=cw[:, 3 * j + 2:3 * j + 3], in1=c_,
                                                    op0=ALU.mult, op1=ALU.add), rk + [ck_], [ck_])
        S.op('pool', lambda: G.tensor_tensor(out=gcv[:, j, 0:N], in0=c_, in1=cbs[:, j, 0:N], op=ALU.mult),
             [ck_, 'cbs%d' % j, 'RC1'], ['gcv%d' % j])

    pstate = {'praw': 0, 'pT': 0, 'on': 0}

    def attention_block(gb, b, first_blk):
        pts = {}
        sbank = 0
        for g in range(2):
            for kb, kblk in enumerate((max(gb - 1, 0), gb)):
                bank = (2 * g + kb) % 4
                mmg(ps[bank][:, :],
                    [(kT[64 * g:64 * g + 64, kblk * 128:(kblk + 1) * 128], qT[64 * g:64 * g + 64, :, b * 128:(b + 1) * 128])],
                    ['kT%d' % kblk] + ['qT%d' % c for c in range(4)] + ['RC1'], [PSK[bank]])
                pr = pstate['praw'] % 3
                pstate['praw'] += 1
                act(praw[pr][:], ps[bank][:, :], AF.Exp, [PSK[bank]], ['praw%d' % pr])
                pi = pstate['pT'] % 8
                pstate['pT'] += 1
                if kb == 0:
                    E = E_first if first_blk else E_prev
                    ek = 'E_first' if first_blk else 'E_prev'
                else:
                    E, ek = E_cur, 'E_cur'
                S.op('dve', lambda pi=pi, pr=pr, E=E, g=g: V.tensor_tensor(
                    out=pT[pi][:], in0=praw[pr][:], in1=E[:, g * 512:(g + 1) * 512], op=ALU.mult),
                    ['praw%d' % pr, ek], ['pT%d' % pi])
                pts[(g, kb)] = pi
                if gb == 2:
                    tap('praw_%d%d' % (g, kb), praw[pr][:], ['praw%d' % pr])
                    tap('pT_%d%d' % (g, kb), pT[pi][:], ['pT%d' % pi])
        for g in range(2):
            bank = 4 + g
            pairs_r = ['pT%d' % pts[(g, 0)], 'pT%d' % pts[(g, 1)], 'Vx%d' % max(gb - 1, 0), 'Vx%d' % gb, 'Vx1']

            def fn(g=g, bank=bank):
                last = None
                for c in range(4):
                    for kb, kblk in enumerate((max(gb - 1, 0), gb)):
                        last = T.matmul(ps[bank][:, c * 65:(c + 1) * 65],
                                        pT[pts[(g, kb)]][:, c * 128:(c + 1) * 128],
                                        Vx[:, kblk, g, :], start=(kb == 0), stop=(kb == 1))
                return last
            S.op('pe', fn, pairs_r, [PSK[bank]])
        ri = pstate['on'] % 2
        pstate['on'] += 1
        for g in range(2):
            bank = 4 + g
            pv = ps[bank][:, 0:260].rearrange("p (c e) -> p c e", c=4)
            S.op('dve', lambda g=g, pv=pv, ri=ri: V.tensor_tensor(
                out=rdt[ri][:, 4 * g:4 * g + 4], in0=pv[:, :, 64], in1=esk[:, 4 * g:4 * g + 4], op=ALU.add),
                [PSK[bank], 'esk'], ['rdt%d_%d' % (ri, g)])
            S.op('dve', lambda g=g, ri=ri: V.reciprocal(out=rdt[ri][:, 4 * g:4 * g + 4], in_=rdt[ri][:, 4 * g:4 * g + 4]),
                 ['rdt%d_%d' % (ri, g)], ['rdt%d_%d' % (ri, g)])
            S.op('dve', lambda g=g, pv=pv, ri=ri: V.tensor_tensor(
                out=onb[ri][:, g * 256:(g + 1) * 256].rearrange("p (c d) -> p c d", c=4),
                in0=pv[:, :, 0:64],
                in1=rdt[ri][:, 4 * g:4 * g + 4].unsqueeze(2).to_broadcast([128, 4, 64]), op=ALU.mult),
                [PSK[bank], 'rdt%d_%d' % (ri, g)], ['on%d_%d' % (ri, g)])
        if gb == 2:
            tap('rdt', rdt[ri][:], ['rdt%d_0' % ri, 'rdt%d_1' % ri])
            tap('onb', onb[ri][:], ['on%d_0' % ri, 'on%d_1' % ri])
        tb = 6 + (gb % 2)
        items = [(ps[tb][:, c * 128:(c + 1) * 128], onb[ri][:, c * 128:(c + 1) * 128], 128) for c in range(4)]
        tpg(items, ['on%d_0' % ri, 'on%d_1' % ri], [PSK[tb]])
        act(oT[:, :, b * 128:(b + 1) * 128], ps[tb][:, :].rearrange("p (c q) -> p c q", c=4), AF.Copy,
            [PSK[tb], 'RC1'], ['oT'])

    gpas_t = sb("gpas", [128, NF * NS * 3], BF16)
    gpas = gpas_t[:, :].rearrange("p (f n t) -> p f n t", f=NF, t=3)
    RB_s = sb("RB_s", [128, 8, NSP], BF16)
    RA_s = sb("RA_s", [128, NF, NSP], BF16)
    qT_s = sb("qT_s", [128, 4, NSP], BF16)
    oT_s = sb("oT_s", [128, 4, NSP], BF16)
    mg_s = sb("mg_s", [128, 8, NSP], BF16)
    gcv_s = sb("gcv_s", [128, 4, NSP], BF16)
    cbs_s = sb("cbs_s", [128, 4, NSP], BF16)
    uext_s = sb("uext_s", [128, 4, NSP + 2], F32)
    stg = sb("stg", [NSP, 768], F32)
    gsT = sb("gsT", [128, NF, NSP], F32)
    kcT_new = sb("kcT_new", [128, NS], BF16)
    usT = sb("usT", [128, 4, NS, 3], F32)

    def attention_samples():
        N = NS
        dma('sp', ckc, ck.rearrange("n s f -> s n f"), 'sck', [], ['ckc', 'xres0'])
        dma('pool', cvb, cv.rearrange("n s f -> s n f"), 'scv', ['xres0'], ['cvb'])
        dma('sp', nks_o[:, 0:127, :], ck[:, 1:128, :], 'G:out', [], ['o_nks'])
        dma('sp', nvs_o[:, 0:127, :], cv[:, 1:128, :], 'G:out', [], ['o_nvs'])
        for n4 in range(4):
            bank = n4 % 2
            items = [(ps[bank][:, i * 128:(i + 1) * 128], ckc[:, n4 * 4 + i, :], 128) for i in range(4)]
            tpg(items, ['ckc', 'xres0'], [PSK[bank]])
            act(kcT[:, n4 * 4:n4 * 4 + 4, :], ps[bank][:, :].rearrange("p (n s) -> p n s", n=4), AF.Copy,
                [PSK[bank], 'ckc', 'xres0'], ['kcT%d' % n4])
        def fn():
            last = None
            for n in range(NS):
                for g in range(2):
                    last = T.matmul(ps[2][:, n * 8 + g * 4:n * 8 + g * 4 + 4],
                                    kcT[64 * g:64 * g + 64, n, :], qT_s[64 * g:64 * g + 64, :, n],
                                    start=True, stop=True)
            return last
        S.op('pe', fn, ['xres0'] + ['kcT%d' % i for i in range(4)] + ['sqT%d' % c for c in range(4)] + ['sRC1'], [PSK[2]])
        act(praws[:], ps[2][:, 0:128], AF.Exp, [PSK[2]], ['praws'])
        S.op('dve', lambda: V.tensor_tensor(out=pTs[:], in0=praws[:], in1=esm[:], op=ALU.mult),
             ['praws', 'esm'], ['pTs'])
        S.op('dve', lambda: V.tensor_tensor(out=prod[:], in0=qT_s[:, :, 0:NS],
                                            in1=kcT_new[:, :].unsqueeze(1).to_broadcast([128, 4, NS]), op=ALU.mult),
             ['sqT%d' % c for c in range(4)] + ['kTs', 'sRC1'], ['prod'])
        mmg(ps[3][:, 0:64], [(bones[:, :], prod[:].rearrange("p c n -> p (c n)"))], ['bones', 'prod'], [PSK[3]])
        act(pnb[:].rearrange("p c n -> p (c n)"), ps[3][:, 0:64], AF.Exp, [PSK[3]], ['pnb'])
        def fn2():
            last = None
            for n in range(NS):
                for g in range(2):
                    last = T.matmul(ps[0][64 * g:64 * g + 64, n * 4:n * 4 + 4], cvb[:, n, 64 * g:64 * g + 64],
                                    pTs[:, n * 8 + g * 4:n * 8 + g * 4 + 4], start=True, stop=True)
            return last
        S.op('pe', fn2, ['cvb', 'pTs', 'xres0'], [PSK[0]])
        ones64 = ones_t[:, :]

        def fn3():
            last = None
            for g in range(2):
                last = T.matmul(ps[1][64 * g:64 * g + 64, 0:64], ones64,
                                pTs[:].rearrange("p (n g c) -> p n g c", g=2, c=4)[:, :, g, :], start=True, stop=True)
            return last
        S.op('pe', fn3, ['ones_t', 'pTs'], [PSK[1]])
        S.op('dve', lambda: V.tensor_tensor(out=ons[:], in0=pnb[:], in1=vTs[:, :].unsqueeze(1).to_broadcast([128, 4, NS]),
                                            op=ALU.mult), ['pnb', 'vTs'], ['ons'])
        S.op('dve', lambda: V.tensor_tensor(out=ons[:], in0=ons[:],
                                            in1=ps[0][:, 0:64].rearrange("p (n c) -> p c n", c=4), op=ALU.add),
             ['ons', PSK[0]], ['ons'])
        S.op('dve', lambda: V.tensor_tensor(out=dns[:], in0=pnb[:],
                                            in1=ps[1][:, 0:64].rearrange("p (n c) -> p c n", c=4), op=ALU.add),
             ['pnb', PSK[1]], ['dns'])
        for g in range(2):
            sl = slice(64 * g, 64 * g + 64)
            S.op('dve', lambda g=g, sl=sl: V.tensor_tensor(
                out=dns[sl], in0=dns[sl], in1=esk[sl, 4 * g:4 * g + 4].unsqueeze(2).to_broadcast([64, 4, NS]), op=ALU.add),
                ['dns', 'esk'], ['dns'])
        S.op('dve', lambda: V.reciprocal(out=dns[:], in_=dns[:]), ['dns'], ['dns'])
        S.op('dve', lambda: V.tensor_tensor(out=ons[:], in0=ons[:], in1=dns[:], op=ALU.mult), ['ons', 'dns'], ['ons'])
        for h in range(8):
            g, c = h // 4, h % 4
            S.op('dve', lambda h=h, g=g, c=c: V.tensor_copy(out=oT_s[64 * (h % 2):64 * (h % 2) + 64, h // 2, 0:NS],
                                                           in_=ons[64 * g:64 * g + 64, c, :]),
                 ['ons', 'sRC1'], ['soT'])

    def final_prompt_outputs(N):
        dma('sp', nk_o, nkv[:, 0:128], 'G:out', ['nkv_k'], ['o_nk'])
        dma('sp', nv_o, nkv[:, 128:256], 'G:out', ['nkv_v'], ['o_nv'])
        items = [(ps[0][0:32, j * 128:(j + 1) * 128], uext[:, j, N - 30:N + 2], 128) for j in range(4)]
        tpg(items, ['uext%d' % j for j in range(4)], [PSK[0]])
        S.op('act', lambda: acopy(out=ncf[:, 0:512], in_=ps[0][0:32, :]), [PSK[0]], ['ncf_c', 'xres0'])
        dma('sp', ncv_o, ncf[30:32, 0:512], 'G:ncf', ['ncf_c', 'xres0'], ['o_ncv'])
        for grp in range(6):
            f0 = grp * 4
            nf = min(4, NF - f0)
            bank = 1 + (grp % 2)
            items = [(ps[bank][0:32, i * 128:(i + 1) * 128], gl2[:, f0 + i, :], 128) for i in range(nf)]
            tpg(items, ['gl2_%d' % (f0 + i) for i in range(nf)], [PSK[bank]])
            S.op('act', lambda f0=f0, nf=nf, bank=bank: acopy(
                out=ncf[:, 512 + f0 * 128:512 + (f0 + nf) * 128], in_=ps[bank][0:32, 0:nf * 128]),
                [PSK[bank], 'xres0'], ['ncf_f%d' % grp])
        dma('sp', nfc_o, ncf[30:32, 512:512 + DFF], 'G:ncf', ['ncf_f%d' % g for g in range(6)] + ['xres0'], ['o_nfc'])

    def final_sample_outputs():
        dma('sp', nks_o[:, 127, :], kvs[0:NS, 0:128], 'G:out', ['kvs_k'], ['o_nks2'])
        dma('sp', nvs_o[:, 127, :], kvs[0:NS, 128:256], 'G:out', ['kvs_v'], ['o_nvs2'])
        dma('sp', ncs_o[:, 0, :], sc.rearrange("(n t) c -> n t c", t=2)[:, 1, :], 'G:out', [], ['o_ncs0'])
        dma('sp', nfs_o[:, 0, :], sf.rearrange("(n t) c -> n t c", t=2)[:, 1, :], 'G:out', [], ['o_nfs0'])
        items = [(ps[0][0:NSP, j * 128:(j + 1) * 128], uext_s[:, j, 2:2 + NSP], 128) for j in range(4)]
        tpg(items, ['suext%d' % j for j in range(4)], [PSK[0]])
        S.op('act', lambda: acopy(out=stg[:, 0:512], in_=ps[0][0:NSP, :]), [PSK[0]], ['stg'])
        dma('sp', ncs_o[:, 1, :], stg[0:NS, 0:512], 'stg_o', ['stg'], ['o_ncs1'])
        for pi, (f0, nf) in enumerate(OPIECES):
            bank = 1 + (pi % 2)
            items = [(ps[bank][0:NSP, i * 128:(i + 1) * 128], gsT[:, f0 + i, :], 128) for i in range(nf)]
            tpg(items, ['gsT%d' % (f0 + i) for i in range(nf)], [PSK[bank]])
            S.op('act', lambda nf=nf, bank=bank: acopy(out=stg[:, 0:nf * 128], in_=ps[bank][0:NSP, 0:nf * 128]),
                 [PSK[bank]], ['stg'])
            dma('sp', nfs_o[:, 1, f0 * 128:(f0 + nf) * 128], stg[0:NS, 0:nf * 128], 'stg_o', ['stg'], ['o_nfs1_%d' % pi])


    PIECES = [(0, 6), (6, 6), (12, 6), (18, 4)]
    OPIECES = [(0, 4), (4, 4), (8, 4), (12, 4), (16, 4), (20, 2)]

    def samp_prologue():
        dma('sp', stg[:, 0:512], sc, 'stg_i', [], ['stg'])
        items = [(ps[0][:, j * 32:(j + 1) * 32], stg[:, j * 128:(j + 1) * 128], 2 * NS) for j in range(4)]
        tpg(items, ['stg'], [PSK[0]])
        S.op('act', lambda: acopy(out=scT[:], in_=ps[0][:, 0:128].rearrange("p (j m) -> p j m", j=4)),
             [PSK[0]], ['scT'])
        for pi, (f0, nf) in enumerate(PIECES):
            dma('sp', stg[:, 0:nf * 128], sf[:, f0 * 128:(f0 + nf) * 128], 'stg_i', [], ['stg'])
            bank = 1 + (pi % 2)
            items = [(ps[bank][:, i * 32:(i + 1) * 32], stg[:, i * 128:(i + 1) * 128], 2 * NS) for i in range(nf)]
            tpg(items, ['stg'], [PSK[bank]])
            S.op('act', lambda f0=f0, nf=nf, bank=bank: acopy(
                out=sfT[:, f0:f0 + nf, :], in_=ps[bank][:, 0:nf * 32].rearrange("p (j m) -> p j m", j=nf)),
                [PSK[bank]], ['sfT%d' % pi])

    def run_pass(gens):
        pend = []
        for g in gens:
            try:
                pend.append([g, next(g)])
            except StopIteration:
                pass
        while pend:
            sl = min(p[1] for p in pend)
            view = wload(sl)
            nxt = []
            for p in pend:
                if p[1] == sl:
                    try:
                        p[1] = p[0].send(view)
                        nxt.append(p)
                    except StopIteration:
                        pass
                else:
                    nxt.append(p)
            pend = nxt

    import os
    LIM = int(os.environ.get('KSTAGE', '99'))
    def conv_branch_samp(j, N):
        S.op('pool', lambda: G.tensor_copy(out=usT[:, j, :, 2], in_=uext_s[:, j, 2:2 + NS]), ['suext%d' % j], ['usT%d' % j])
        S.op('pool', lambda: G.tensor_copy(out=usT[:, j, :, 0:2],
                                           in_=scT[:, j, :].rearrange("p (n t) -> p n t", t=2)),
             ['scT'], ['usT%d' % j])
        c_ = cvt[j % 2][:, 0:NS]
        ck_ = 'cvt%d' % (j % 2)
        rk = ['usT%d' % j, 'cw']
        S.op('dve', lambda: V.tensor_scalar(out=c_, in0=usT[:, j, :, 0], scalar1=cw[:, 3 * j:3 * j + 1], scalar2=None,
                                             op0=ALU.mult), rk, [ck_])
        S.op('dve', lambda: V.scalar_tensor_tensor(out=c_, in0=usT[:, j, :, 1], scalar=cw[:, 3 * j + 1:3 * j + 2],
                                                    in1=c_, op0=ALU.mult, op1=ALU.add), rk + [ck_], [ck_])
        S.op('dve', lambda: V.scalar_tensor_tensor(out=c_, in0=usT[:, j, :, 2], scalar=cw[:, 3 * j + 2:3 * j + 3],
                                                    in1=c_, op0=ALU.mult, op1=ALU.add), rk + [ck_], [ck_])
        S.op('pool', lambda: G.tensor_tensor(out=gcv_s[:, j, 0:NS], in0=c_, in1=cbs_s[:, j, 0:NS], op=ALU.mult),
             [ck_, 'scbs%d' % j, 'sRC1'], ['sgcv%d' % j])

    NCH = min(5, max(0, LIM - 1)) if LIM < 99 else 5
    for ci in range(NCH):
        nl = (lambda ci=ci: xload('main', ci + 1)) if ci < 4 else None
        gens = [chunk('main', ci, nextload=nl, preloaded=(ci > 0))]
        if ci == 0 and LIM >= 7:
            xload('samp', 0)
            samp_prologue()
            gens.append(chunk('samp', 0, preloaded=True))
        run_pass(gens)
    flush_deferred()

    S.op('sp', None, ['o_ys', 'o_nk', 'o_nv', 'o_ncv', 'o_nfc', 'o_nks', 'o_nvs', 'o_nks2', 'o_nvs2',
                      'o_ncs0', 'o_nfs0', 'o_ncs1', 'o_nfs1_0', 'o_nfs1_1', 'o_nfs1_2', 'o_nfs1_3', 'o_nfs1_4', 'o_nfs1_5', 'o_yp0', 'o_yp1', 'o_yp2', 'o_yp3', 'o_yp4'], [])
    S.finalize(es)
    es.close()
    return nc, None


_PROG = {}


def _consts():
    slopes = 2.0 ** (-8.0 * np.arange(1, 9) / 8.0)
    s = np.arange(128)[:, None]
    q = np.arange(128)[None, :]
    eprev = np.zeros((128, 2, 4, 128), np.float32)
    ecur = np.zeros((128, 2, 4, 128), np.float32)
    esm = np.zeros((128, NS, 2, 4), np.float32)
    for g in range(2):
        for c in range(4):
            sl = slopes[4 * g + c]
            dprev = q + 128 - s
            dcur = q - s
            eprev[:, g, c, :] = np.where(s >= q, np.exp(-sl * dprev), 0.0)
            ecur[:, g, c, :] = np.where(q >= s, np.exp(-sl * dcur), 0.0)
            esm[:, :, g, c] = np.exp(-sl * (128 - np.arange(128)))[:, None]
    bones = (np.arange(128)[:, None] // 64 == np.arange(128)[None, :] // 64).astype(np.float32)
    return dict(eprev=eprev.reshape(128, 1024), ecur=ecur.reshape(128, 1024), esm=esm.reshape(128, 128),
                bones=bones, ident=np.eye(128, dtype=np.float32))


def kernel(x_prompt, x_sample, cache_k, cache_v, state_conv, state_ffn_conv, w_in, conv_w, attn_sinks,
           w_attn_out, w_conv_out, w_mix_out, ln1_g, ln1_b, w_gate, w_up, ffn_conv_w, w_down, ln2_g, ln2_b):
    f32 = np.float32
    A_ = lambda a: np.ascontiguousarray(np.asarray(a, dtype=f32))
    x_prompt = A_(x_prompt); x_sample = A_(x_sample)
    if 'nc' not in _PROG:
        _PROG['nc'], _PROG['es'] = build_program()
    nc = _PROG['nc']
    cols = []
    for c in range(4):
        cols += list(range(c * 64, c * 64 + 64)) + list(range((4 + c) * 64, (4 + c) * 64 + 64))
    cols += list(range(512, 768))
    cols += list(range(768, 1280))
    for j in range(4):
        cols += list(range(1280 + j * 128, 1280 + (j + 1) * 128)) + list(range(1792 + j * 128, 1792 + (j + 1) * 128))
    cols += list(range(2304, 4352))
    w_in_p = A_(A_(w_in)[0][:, cols])
    bc = lambda v: A_(np.broadcast_to(A_(v).reshape(1, -1), (128, A_(v).size)))
    fm = lambda v, n: A_(A_(v).reshape(n, 128).T)
    common = dict(
        w_in=w_in_p, w_ao=A_(w_attn_out)[0], w_co=A_(w_conv_out)[0], w_mix=A_(w_mix_out)[0],
        w_gate=A_(w_gate)[0], w_up=A_(w_up)[0], w_down=A_(w_down)[0],
        cw=A_(A_(conv_w)[0].reshape(3, 4, 128).transpose(2, 1, 0).reshape(128, 12)),
        fcw=A_(A_(ffn_conv_w)[0].reshape(3, NF, 128).transpose(2, 1, 0).reshape(128, 66)),
        sk=bc(A_(attn_sinks)[0]),
        g1b=bc(ln1_g[0]), b1b=bc(ln1_b[0]), g2b=bc(ln2_g[0]), b2b=bc(ln2_b[0]),
        g1f=fm(ln1_g[0], 8), b1f=fm(ln1_b[0], 8),
    )
    common.update(_consts())
    ck = A_(cache_k)[0].reshape(128, 128, 128)
    cv = A_(cache_v)[0].reshape(128, 128, 128)
    sc = A_(state_conv)[0].reshape(256, 512)
    sf = A_(state_ffn_conv)[0].reshape(256, DFF)
    in_maps = []
    for c in range(NCORES):
        b, r = c // 4, c % 4
        s0 = r * TOK_CORE
        xp = np.zeros((NTOK, D), f32)
        if r > 0:
            xp[0:HALO] = x_prompt[b, s0 - HALO:s0]
        xp[HALO:] = x_prompt[b, s0:s0 + TOK_CORE]
        m = dict(common)
        m.update(xp=xp, xs=A_(x_sample[c * NS:(c + 1) * NS, 0, :]),
                 ck=A_(ck[c * NS:(c + 1) * NS]), cv=A_(cv[c * NS:(c + 1) * NS]),
                 sc=A_(sc[c * 2 * NS:(c + 1) * 2 * NS]), sf=A_(sf[c * 2 * NS:(c + 1) * 2 * NS]),
                 hv=np.full((128, 1), 0.0 if r == 0 else 1.0, f32))
        in_maps.append(m)
    res = run_bass_kernel_spmd(nc, in_maps, core_ids=list(range(NCORES)))
    R = res.results
    y_prompt = np.zeros((2, 8192, D), f32)
    y_sample = np.zeros((128, 1, D), f32)
    nk_p = np.zeros((1, 2, 128, 2, 64), f32); nv_p = np.zeros((1, 2, 128, 2, 64), f32)
    nc_p = np.zeros((1, 2, 2, 512), f32); nf_p = np.zeros((1, 2, 2, DFF), f32)
    nk_s = np.zeros((1, 128, 128, 2, 64), f32); nv_s = np.zeros((1, 128, 128, 2, 64), f32)
    nc_s = np.zeros((1, 128, 2, 512), f32); nf_s = np.zeros((1, 128, 2, DFF), f32)
    for c in range(NCORES):
        b, r = c // 4, c % 4
        y_prompt[b, r * TOK_CORE:(r + 1) * TOK_CORE] = R[c]["yp"]
        y_sample[c * NS:(c + 1) * NS, 0] = R[c]["ys"]
        if r == 3:
            nk_p[0, b] = R[c]["nk"].reshape(128, 2, 64)
            nv_p[0, b] = R[c]["nv"].reshape(128, 2, 64)
            nc_p[0, b] = R[c]["ncv"]
            nf_p[0, b] = R[c]["nfc"]
        nk_s[0, c * NS:(c + 1) * NS] = R[c]["nks"].reshape(NS, 128, 2, 64)
        nv_s[0, c * NS:(c + 1) * NS] = R[c]["nvs"].reshape(NS, 128, 2, 64)
        nc_s[0, c * NS:(c + 1) * NS] = R[c]["ncs"]
        nf_s[0, c * NS:(c + 1) * NS] = R[c]["nfs"]
    return (y_prompt, y_sample, nk_p, nv_p, nc_p, nf_p, nk_s, nv_s, nc_s, nf_s)
```

```python
import contextlib
import numpy as np
import concourse.bass as bass
import concourse.mybir as mybir
from concourse.bass_utils import run_bass_kernel_spmd

F32 = mybir.dt.float32
BF16 = mybir.dt.bfloat16
AF = mybir.ActivationFunctionType
ALU = mybir.AluOpType

D = 1024
NPROJ = 4352
DFF = 2816
NF = 22
ALPHA = 2.0 ** 0.25
EPS = 1e-5
EPS_S = EPS / (ALPHA * ALPHA)
NCORES = 8
TOK_CORE = 2048
HALO = 256
NTOK = TOK_CORE + HALO
NBLK = NTOK // 128
NS = 16
NSP = 32
RING = 5


class Sched:
    def __init__(self, nc):
        self.nc = nc
        self.ops = []
        self.last_w = {}
        self.readers = {}

    def op(self, eng, fn, reads=(), writes=(), dma=None):
        deps = set()
        for k in reads:
            if k in self.last_w:
                deps.add(self.last_w[k])
            if k.startswith('ps'):
                deps.update(r for r in self.readers.get(k, ()) if self.ops[r]['eng'] != eng)
        for k in writes:
            if k in self.last_w:
                deps.add(self.last_w[k])
            deps.update(self.readers.get(k, ()))
        oid = len(self.ops)
        self.ops.append(dict(eng=eng, fn=fn, deps=deps, dma=dma, signal=False, waits=[]))
        for k in reads:
            self.readers.setdefault(k, []).append(oid)
        for k in writes:
            self.last_w[k] = oid
            self.readers[k] = []
        return oid

    def finalize(self, stack):
        nc = self.nc
        engs = {'pe': nc.tensor, 'act': nc.scalar, 'dve': nc.vector, 'pool': nc.gpsimd, 'sp': nc.sync}
        ops = self.ops
        cnt = {}
        for o in ops:
            st = ('dma:' + o['dma']) if o['dma'] else o['eng']
            o['stream'] = st
            cnt[st] = cnt.get(st, 0) + 1
            o['seq'] = cnt[st]
        for o in ops:
            if o['dma'] and o['dma'].startswith('G:'):
                o['seq'] = cnt[o['stream']]
        clk = {e: {} for e in engs}
        eng_seq = {e: 0 for e in engs}
        for o in ops:
            e = o['eng']
            c = clk[e]
            myseq = o['seq'] if not o['dma'] else None
            need = {}
            for d in o['deps']:
                od = ops[d]
                st = od['stream']
                if st == e and not o['dma']:
                    if e == 'pe':
                        continue
                    if o['seq'] - od['seq'] > 2:
                        continue
                    if c.get('self_' + e, 0) >= od['seq']:
                        continue
                elif c.get(st, 0) >= od['seq']:
                    continue
                if st not in need or ops[need[st]]['seq'] < od['seq']:
                    need[st] = d
            for st, d in need.items():
                od = ops[d]
                if st == e and not o['dma']:
                    c['self_' + e] = od['seq']
                elif c.get(st, 0) >= od['seq']:
                    continue
                od['signal'] = True
                o['waits'].append(d)
                for k, v in od['vc'].items():
                    if c.get(k, 0) < v:
                        c[k] = v
            if not o['dma']:
                c[e] = o['seq']
                o['vc'] = dict(c)
            else:
                vc = dict(c)
                vc[o['stream']] = o['seq']
                o['vc'] = vc
        sems = {}
        val = {}
        for o in ops:
            st = o['stream']
            if o['dma']:
                val[st] = val.get(st, 0) + 16
                o['val'] = val[st]
            elif o['signal']:
                val[st] = val.get(st, 0) + 1
                o['val'] = val[st]
        for o in ops:
            if o['dma'] and o['dma'].startswith('G:'):
                o['val'] = val[o['stream']]
        for st in val:
            sems[st] = stack.enter_context(nc.semaphore('s_' + st.replace(':', '_')))
        self.nsem = len(sems)
        for o in ops:
            e = engs[o['eng']]
            for d in o['waits']:
                od = ops[d]
                e.wait_ge(sems[od['stream']], od['val'])
            if o['fn'] is None:
                continue
            ins = o['fn']()
            if o['dma']:
                ins.then_inc(sems[o['stream']], 16)
            elif o['signal']:
                ins.then_inc(sems[o['stream']], 1)


def slot_table():
    t = []
    for i in range(17):
        t.append(('w_in', 0, 8, i * 256, 256))
    for h in range(2):
        t.append(('w_ao', 0, 4, h * 512, 512))
        t.append(('w_co', 0, 4, h * 512, 512))
    for n in range(2):
        for kh in range(2):
            t.append(('w_mix', kh * 4, 4, n * 512, 512))
    for j in range(11):
        t.append(('w_gate', 0, 8, j * 256, 256))
        t.append(('w_up', 0, 8, j * 256, 256))
    for n in range(2):
        for fg in range(6):
            nk = 4 if fg < 5 else 2
            t.append(('w_down', fg * 4, nk, n * 512, 512))
    return t


SLOTS = slot_table()
S_IN, S_AO, S_CO, S_MIX, S_FFN, S_DOWN = 0, 17, 19, 21, 25, 47
WSHAPES = {'w_in': (D, NPROJ), 'w_ao': (512, D), 'w_co': (512, D), 'w_mix': (D, D),
           'w_gate': (D, DFF), 'w_up': (D, DFF), 'w_down': (DFF, D)}


def build_program():
    nc = bass.Bass("TRN2", target_bir_lowering=False)
    S = Sched(nc)
    es = contextlib.ExitStack()

    def din(name, shape, dt=F32):
        return nc.dram_tensor(name, list(shape), dt, kind="ExternalInput").ap()

    def dout(name, shape, dt=F32):
        return nc.dram_tensor(name, list(shape), dt, kind="ExternalOutput").ap()

    xp = din("xp", [NTOK, D])
    xs = din("xs", [NS, D])
    ck = din("ck", [NS, 128, 128])
    cv = din("cv", [NS, 128, 128])
    sc = din("sc", [2 * NS, 512])
    sf = din("sf", [2 * NS, DFF])
    W = {k: din(k, v) for k, v in WSHAPES.items()}
    cw_d = din("cw", [128, 12])
    fcw_d = din("fcw", [128, 66])
    sk_d = din("sk", [128, 8])
    g1b_d = din("g1b", [128, D]); b1b_d = din("b1b", [128, D])
    g2b_d = din("g2b", [128, D]); b2b_d = din("b2b", [128, D])
    g1f_d = din("g1f", [128, 8]); b1f_d = din("b1f", [128, 8])
    ident_d = din("ident", [128, 128])
    eprev_d = din("eprev", [128, 1024]); ecur_d = din("ecur", [128, 1024])
    esm_d = din("esm", [128, 128])
    bones_d = din("bones", [128, 128])
    hv_d = din("hv", [128, 1])

    yp = dout("yp", [TOK_CORE, D])
    ys = dout("ys", [NS, D])
    nk_o = dout("nk", [128, 128]); nv_o = dout("nv", [128, 128])
    ncv_o = dout("ncv", [2, 512]); nfc_o = dout("nfc", [2, DFF])
    nks_o = dout("nks", [NS, 128, 128]); nvs_o = dout("nvs", [NS, 128, 128])
    ncs_o = dout("ncs", [NS, 2, 512]); nfs_o = dout("nfs", [NS, 2, DFF])
    wscr = nc.dram_tensor("wscr", [len(SLOTS), 128, 2048], BF16, kind="Internal").ap()

    def sb(name, shape, dt):
        return es.enter_context(nc.sbuf_tensor("s_" + name, list(shape), dt))

    xres = [sb("xres%d" % i, [128, 4, D], F32) for i in range(2)]
    RB = sb("RB", [128, 8, 512], BF16)
    RA = sb("RA", [128, NF, 512], BF16)
    RC = sb("RC", [128, 12288], BF16)
    qT = RC[:, 0:2048].rearrange("p (c n) -> p c n", c=4)
    oT = RC[:, 2048:4096].rearrange("p (c n) -> p c n", c=4)
    mg = RC[:, 4096:8192].rearrange("p (c n) -> p c n", c=8)
    gcv = RC[:, 8192:10240].rearrange("p (c n) -> p c n", c=4)
    cbs = RC[:, 10240:12288].rearrange("p (c n) -> p c n", c=4)
    gpa = RC[:, 0:NF * 514].rearrange("p (f n) -> p f n", f=NF)
    kT = sb("kT", [128, NBLK * 128], BF16)
    Vx = sb("Vx", [128, NBLK, 2, 65], BF16)
    ccs = [sb("ccs%d" % i, [128, 512], F32) for i in range(2)]
    uext = sb("uext", [128, 4, 514], F32)
    cvt = [sb("cvt%d" % i, [128, 512], F32) for i in range(2)]
    praw = [sb("praw%d" % i, [128, 512], BF16) for i in range(3)]
    pT = [sb("pT%d" % i, [128, 512], BF16) for i in range(8)]
    onb = [sb("on%d" % i, [128, 512], F32) for i in range(2)]
    rdt = [sb("rdt%d" % i, [128, 8], F32) for i in range(2)]
    m12 = [sb("m12_%d" % i, [128, 512], BF16) for i in range(4)]
    ct = [sb("ct%d" % i, [128, 512], F32) for i in range(2)]
    gl = [sb("gl%d" % i, [128, 512], BF16) for i in range(2)]
    ring = sb("ring", [128, RING, 2048], BF16)
    g1b = sb("g1b", [128, D], F32); b1b = sb("b1b", [128, D], F32)
    g2b = sb("g2b", [128, D], F32); b2b = sb("b2b", [128, D], F32)
    g1f = sb("g1f", [128, 8], F32); b1f = sb("b1f", [128, 8], F32)
    ident = sb("ident", [128, 128], F32)
    E_prev = sb("E_prev", [128, 1024], BF16)
    E_cur = sb("E_cur", [128, 1024], BF16)
    E_first = sb("E_first", [128, 1024], BF16)
    esm = sb("esm", [128, 128], BF16)
    bones = sb("bones", [128, 128], BF16)
    hv = sb("hv", [128, 1], F32)
    cw = sb("cw", [128, 12], F32)
    fcw = sb("fcw", [128, 66], F32)
    skt = sb("skt", [128, 8], F32)
    esk = sb("esk", [128, 8], F32)
    epsb = sb("epsb", [128, 1], F32)
    stt = [sb("stt%d" % i, [128, 2, 6], F32) for i in range(2)]
    mv = sb("mv", [128, 8, 2], F32)
    lnv = sb("lnv", [128, 8], F32)
    rstd = sb("rstd", [128, 8], F32)
    nkv = sb("nkv", [128, 256], F32)
    gl2 = sb("gl2", [128, NF, 32], F32)
    fh = sb("fh", [128, NF, 2], BF16)
    xsr = sb("xsr", [NSP, D], F32)
    scT = sb("scT", [128, 4, 2 * NS], F32)
    sfT = sb("sfT", [128, NF, 2 * NS], F32)
    vTs = sb("vTs", [128, NS], F32)
    prod = sb("prod", [128, 4, NS], BF16)
    pnb = sb("pnb", [128, 4, NS], F32)
    ons = sb("ons", [128, 4, NS], F32)
    dns = sb("dns", [128, 4, NS], F32)
    pTs = sb("pTs", [128, NS * 8], BF16)
    praws = sb("praws", [128, NS * 8], BF16)
    kvs = sb("kvs", [NSP, 256], F32)
    x0f = xres[0][:, :, :].rearrange("p b f -> p (b f)")
    x1f = xres[1][:, :, :].rearrange("p b f -> p (b f)")
    ncf = x0f[0:32, 0:512 + DFF]
    ones_t = sb("ones_t", [128, 64], BF16)
    ckc = xres[0][:, :, :].rearrange("p b f -> p (b f)")[:, 0:2048].rearrange("p (n f) -> p n f", n=NS)
    cvb = xres[0][:, :, :].rearrange("p b f -> p (b f)")[:, 2048:3072].bitcast(BF16).rearrange("p (n f) -> p n f", n=NS)
    kcT = xres[0][:, :, :].rearrange("p b f -> p (b f)")[:, 3072:4096].bitcast(BF16).rearrange("p (n f) -> p n f", n=NS)

    ps = [es.enter_context(nc.psum_tensor("ps%d" % i, [128, 512], F32)) for i in range(8)]
    PSK = ["ps%d" % i for i in range(8)]

    T, V, A, G, SP = nc.tensor, nc.vector, nc.scalar, nc.gpsimd, nc.sync

    def mmg(out_ap, pairs, reads, writes):
        def fn():
            last = None
            n = len(pairs)
            for i, (l, r) in enumerate(pairs):
                last = T.matmul(out_ap, l, r, start=(i == 0), stop=(i == n - 1))
            return last
        return S.op('pe', fn, reads, writes)

    def tpg(items, reads, writes):
        def fn():
            last = None
            for (o, i, npart) in items:
                last = T.transpose(o, i, ident[0:npart, 0:npart])
            return last
        return S.op('pe', fn, reads + ['ident'], writes)

    def acopy(out, in_):
        return A.activation(out=out, in_=in_, func=AF.Copy)

    def act(out, in_, func, reads, writes, bias=None, scale=None):
        kw = {}
        if bias is not None:
            kw['bias'] = bias
        if scale is not None:
            kw['scale'] = scale
        return S.op('act', lambda: A.activation(out=out, in_=in_, func=func, **kw), reads, writes)

    def dma(q, out, in_, key, reads, writes):
        e = {'sp': SP, 'pool': G, 'act': A}[q]
        return S.op(q, lambda: e.dma_start(out=out, in_=in_), reads, writes, dma=key)

    for i, (t, d, k) in enumerate([(g1b, g1b_d, 'g1b'), (b1b, b1b_d, 'b1b'), (g2b, g2b_d, 'g2b'), (b2b, b2b_d, 'b2b'),
                                   (g1f, g1f_d, 'g1f'), (b1f, b1f_d, 'b1f'), (ident, ident_d, 'ident'),
                                   (hv, hv_d, 'hv'), (cw, cw_d, 'cw'), (fcw, fcw_d, 'fcw'), (skt, sk_d, 'skt')]):
        dma('sp', t[:], d, 'G:c', [], [k])
    dma('pool', E_prev[:], eprev_d, 'G:cE', [], ['E_prev'])
    dma('pool', E_cur[:], ecur_d, 'G:cE', [], ['E_cur'])
    dma('pool', esm[:], esm_d, 'G:cE', [], ['esm'])
    dma('pool', bones[:], bones_d, 'G:cE', [], ['bones'])
    S.op('dve', lambda: V.memset(Vx[:, :, :, 64:65], 1.0), [], ['Vx1'])
    S.op('dve', lambda: V.memset(uext[:], 0.0), [], ['uext%d' % j for j in range(4)])
    S.op('dve', lambda: V.memset(RC[:], 0.0), [], ['RC1', 'RC2'])
    S.op('dve', lambda: V.memset(epsb[:], EPS_S), [], ['epsb'])
    S.op('dve', lambda: V.memset(ones_t[:], 1.0), [], ['ones_t'])
    S.op('dve', lambda: V.memset(xsr[:], 0.0), [], ['xsr'])
    S.op('dve', lambda: V.memset(fh[:], 0.0), [], ['fh%d' % f for f in range(NF)])
    S.op('dve', lambda: V.tensor_scalar(out=E_first[:], in0=E_prev[:], scalar1=hv[:, 0:1], scalar2=None,
                                        op0=ALU.mult), ['E_prev', 'hv'], ['E_first'])
    act(esk[:], skt[:], AF.Exp, ['skt'], ['esk'])

    for s, (wn, k0, nk, c0, ncol) in enumerate(SLOTS):
        src = W[wn].rearrange("(k p) n -> p k n", p=128)[:, k0:k0 + nk, c0:c0 + ncol]
        dst = wscr[s].rearrange("p (k c) -> p k c", c=ncol)[:, 0:nk, :]
        dma('pool', dst, src, 'G:cv%d' % (s // 3), [], ['scr%d' % s])

    rstate = {'n': 0}

    def wload(s):
        r = rstate['n'] % RING
        rstate['n'] += 1
        wn, k0, nk, c0, ncol = SLOTS[s]
        dma('sp', ring[:, r, :], wscr[s], 'ring%d' % r, ['scr%d' % s], ['ring%d' % r])
        view = ring[:, r, :].rearrange("p (k c) -> p k c", c=ncol)
        return view, 'ring%d' % r

    deferred = []
    import os
    DBG = os.environ.get('KDEBUG', '') == '1'
    taps = {}

    def tap(name, ap, reads):
        if not DBG or name in taps:
            return
        shp = list(ap.shape)
        d = nc.dram_tensor("dbg_" + name, shp, ap.dtype, kind="ExternalOutput").ap()
        taps[name] = d
        dma('sp', d, ap, 'dbg%d' % len(taps), reads, ['o_dbg_' + name])

    def xload(kind, ci):
        if kind == 'samp':
            dma('sp', xsr[0:NS, :], xs, 'xs', [], ['xsr'])
        elif kind == 'halo':
            dma('sp', xres[1][:, 0:2, :], xp[0:256, :].rearrange("(b p) f -> p b f", p=128), 'x1', [], ['xres1'])
        else:
            par = (ci + 1) % 2
            t0 = ci * 512
            nbl = 4 if ci < 4 else 2
            dma('sp', xres[par][:, 0:nbl, :], xp[t0:t0 + nbl * 128, :].rearrange("(b p) f -> p b f", p=128),
                'x%d' % par, [], ['xres%d' % par])

    def flush_deferred():
        for f in deferred:
            f()
        deferred.clear()

    def emit_p0_main(ci):
        par = (ci + 1) % 2
        nbm = 4 if ci < 4 else 2
        xkey = 'xres%d' % par
        first = True
        for kc in range(8):
            bank = 6 + (kc % 2)
            items = [(ps[bank][:, b * 128:(b + 1) * 128], xres[par][:, b, kc * 128:(kc + 1) * 128], 128)
                     for b in range(nbm)]
            tpg(items, [xkey], [PSK[bank]])
            w = ['xT%d' % kc, 'RB2'] if first else ['xT%d' % kc]
            first = False
            act(RB[:, kc, 0:nbm * 128], ps[bank][:, 0:nbm * 128], AF.Copy, [PSK[bank], 'RB1'], w)

    def chunk(kind, ci, nextload=None, preloaded=False, p0done=False, nextp0=None):
        samp = kind == 'samp'
        halo = kind == 'halo'
        if kind == 'main':
            par = (ci + 1) % 2
            tok0 = ci * 512
            nb = 4 if ci < 4 else 2
            Nx, a0, N, NP = nb * 128, 0, nb * 128, 128
            blk0 = tok0 // 128
            xr = xres[par]
            xkey = 'xres%d' % par
        elif halo:
            par = 1
            tok0 = 0
            Nx, a0, N, nb, NP = 256, 128, 128, 1, 128
            blk0 = 0
            xr = xres[1]
            xkey = 'xres1'
        else:
            Nx, a0, N, nb, NP = NSP, 0, NSP, 1, NSP
            blk0 = None
            xkey = 'xsr'
        nbx = Nx // 128 if not samp else 1
        last_main = (kind == 'main' and ci == 4)

        def xrow(b):
            if samp:
                return xsr[:, :]
            return xr[:, (a0 // 128) + b, :]

        def xrow_all(b):
            if samp:
                return xsr[:, :]
            return xr[:, b, :]

        kp = 's' if samp else ''

        def K(name):
            return kp + name
        if samp:
            RBt, RAt, qTt, oTt, mgt, gcvt, cbst, uextt = RB_s, RA_s, qT_s, oT_s, mg_s, gcv_s, cbs_s, uext_s
        else:
            RBt, RAt, qTt, oTt, mgt, gcvt, cbst, uextt = RB, RA, qT, oT, mg, gcv, cbs, uext
        xT = RBt
        x1T = RBt

        if not preloaded:
            xload(kind, ci)

        first = True
        for kc in (range(8) if not p0done else []):
            bank = 6 + (kc % 2)
            items = [(ps[bank][:, b * 128:b * 128 + NP], xrow_all(b)[:, kc * 128:(kc + 1) * 128], NP)
                     for b in range(nbx)]
            tpg(items, [xkey], [PSK[bank]])
            w = [K('xT%d' % kc), K('RB2')] if first else [K('xT%d' % kc)]
            first = False
            act(xT[:, kc, 0:Nx], ps[bank][:, 0:Nx], AF.Copy, [PSK[bank], K('RB1')], w)

        mmb = {'i': 0}

        def nextbank():
            b = mmb['i'] % 4
            mmb['i'] += 1
            return b

        xTk = [K('xT%d' % kc) for kc in range(8)] + [K('RB1')]
        firstRC = {'v': True}
        firstRA = {'v': True}

        def rcw(keys):
            if firstRC['v']:
                firstRC['v'] = False
                return keys + [K('RC2')]
            return keys

        for i in range(17):
            wv, wk = yield (S_IN + i)
            if i == 3:
                flush_deferred()
            for half in range(2):
                m = 2 * i + half
                lhs = lambda kc, half=half, wv=wv: wv[:, kc, half * 128:(half + 1) * 128]
                if m == 5:
                    for b in range(nbx):
                        mmg(ps[4][0:NP, b * 128:(b + 1) * 128],
                            [(xT[:, kc, b * 128:b * 128 + NP], wv[:, kc, 128:256]) for kc in range(8)],
                            xTk + [wk], [PSK[4]])
                    if samp:
                        S.op('act', lambda: acopy(out=kvs[:, 128:256], in_=ps[4][0:NSP, 0:128]),
                             [PSK[4]], ['kvs_v'])
                        bk = nextbank()
                        mmg(ps[bk][:, 0:N], [(lhs(kc), xT[:, kc, 0:N]) for kc in range(8)], xTk + [wk], [PSK[bk]])
                        act(vTs[:, :], ps[bk][:, 0:NS], AF.Copy, [PSK[bk]], ['vTs'])
                    else:
                        S.op('dve', lambda: V.tensor_copy(
                            out=Vx[:, blk0:blk0 + nbx, :, 0:64],
                            in_=ps[4][:, 0:Nx].rearrange("p (b g d) -> p b g d", b=nbx, g=2)),
                            [PSK[4]], ['Vx%d' % (blk0 + b) for b in range(nbx)])
                        if last_main and 'lmv' not in os.environ.get('KSKIP', ''):
                            S.op('act', lambda: acopy(out=nkv[:, 128:256], in_=ps[4][:, N - 128:N]),
                                 [PSK[4]], ['nkv_v'])
                    continue
                bk = nextbank()
                if m == 4:
                    mmg(ps[bk][:, 0:Nx], [(lhs(kc), xT[:, kc, 0:Nx]) for kc in range(8)], xTk + [wk], [PSK[bk]])
                    if samp:
                        act(kcT_new[:, :], ps[bk][:, 0:NS], AF.Copy, [PSK[bk]], ['kTs'])
                        mmg(ps[4][0:NSP, 0:128], [(xT[:, kc, 0:NSP], wv[:, kc, 0:128]) for kc in range(8)],
                            xTk + [wk], [PSK[4]])
                        S.op('act', lambda: acopy(out=kvs[:, 0:128], in_=ps[4][0:NSP, 0:128]),
                             [PSK[4]], ['kvs_k'])
                    else:
                        act(kT[:, blk0 * 128:blk0 * 128 + Nx], ps[bk][:, 0:Nx], AF.Copy, [PSK[bk]],
                            ['kT%d' % (blk0 + b) for b in range(nbx)])
                        if last_main and 'lmk' not in os.environ.get('KSKIP', ''):
                            mmg(ps[5][:, 0:128], [(xT[:, kc, N - 128:N], wv[:, kc, 0:128]) for kc in range(8)],
                                xTk + [wk], [PSK[5]])
                            S.op('act', lambda: acopy(out=nkv[:, 0:128], in_=ps[5][:, 0:128]),
                                 [PSK[5]], ['nkv_k'])
                    continue
                mmg(ps[bk][:, 0:N], [(lhs(kc), xT[:, kc, a0:a0 + N]) for kc in range(8)], xTk + [wk], [PSK[bk]])
                pin = ps[bk][:, 0:N]
                if m < 4:
                    act(qTt[:, m, 0:N], pin, AF.Copy, [PSK[bk], K('RC1')], rcw([K('qT%d' % m)]), scale=0.125)
                elif m < 10:
                    j = m - 6
                    act(cbst[:, j, 0:N], pin, AF.Copy, [PSK[bk], K('RC1')], [K('cbs%d' % j)])
                elif m < 18:
                    j = (m - 10) // 2
                    if (m - 10) % 2 == 0:
                        act(ccs[j % 2][:, 0:N], pin, AF.Copy, [PSK[bk]], ['ccs%d' % (j % 2)])
                    else:
                        S.op('dve', lambda pin=pin, j=j: V.tensor_tensor(out=uextt[:, j, 2:2 + N], in0=pin,
                                                                        in1=ccs[j % 2][:, 0:N], op=ALU.mult),
                             [PSK[bk], 'ccs%d' % (j % 2)], [K('uext%d' % j)])
                        (conv_branch_samp if samp else conv_branch)(j, N)
                else:
                    t = m - 18
                    w = [K('tg%d' % t)]
                    if firstRA['v']:
                        firstRA['v'] = False
                        w = w + [K('RA2')]
                    act(RAt[:, t, 0:N], pin, AF.Tanh, [PSK[bk], K('RA1')], w, scale=0.5)

        if kind == 'main' and ci == 0:
            tap('xT', xT[:, 0, 0:512], ['xT0'])
            tap('qT', qTt[:, 0, 0:512], ['qT0'])
            tap('kT', kT[:, 256:768], ['kT2', 'kT3', 'kT4', 'kT5'])
            tap('Vx', Vx[:, 2, :, :], ['Vx2'])
            tap('tg', RAt[:, 0, 0:512], ['tg0'])
            tap('gcv', gcvt[:, 0, 0:512], ['gcv0'])
            tap('uext', uextt[:, 0, :], ['uext0'])
        if not samp:
            S.op('pool', lambda: G.tensor_copy(out=uextt[:, :, 0:2], in_=uextt[:, :, N:N + 2]),
                 [K('uext%d' % j) for j in range(4)], [K('uext%d' % j) for j in range(4)])

        if samp:
            attention_samples()
        else:
            for b in range(nb):
                attention_block(blk0 + (a0 // 128) + b, b, first_blk=(blk0 + b == 2))

        if kind == 'main' and ci == 0:
            tap(K('oT'), oTt[:, 0, 0:512], [K('oT')])
        ao = [None, None]
        co = [None, None]
        for m in range(8):
            h, ml = m // 4, m % 4
            if ml == 0:
                ao[h] = yield (S_AO + 2 * h)
                co[h] = yield (S_AO + 2 * h + 1)
            ba, bc = (0, 1) if m % 2 == 0 else (2, 3)
            mmg(ps[ba][:, 0:N], [(ao[h][0][:, c, ml * 128:(ml + 1) * 128], oTt[:, c, 0:N]) for c in range(4)],
                [K('oT'), ao[h][1], K('RC1')], [PSK[ba]])
            mmg(ps[bc][:, 0:N], [(co[h][0][:, c, ml * 128:(ml + 1) * 128], gcvt[:, c, 0:N]) for c in range(4)],
                [K('gcv%d' % c) for c in range(4)] + [co[h][1], K('RC1')], [PSK[bc]])
            i1, i2 = (0, 1) if m % 2 == 0 else (2, 3)
            S.op('dve', lambda m=m, ba=ba, i1=i1: V.scalar_tensor_tensor(
                out=m12[i1][:, 0:N], in0=RAt[:, m, 0:N], scalar=1.0, in1=ps[ba][:, 0:N],
                op0=ALU.add, op1=ALU.mult), [PSK[ba], K('tg%d' % m), K('RA1')], ['m12_%d' % i1])
            S.op('dve', lambda m=m, bc=bc, i2=i2: V.scalar_tensor_tensor(
                out=m12[i2][:, 0:N], in0=RAt[:, 8 + m, 0:N], scalar=1.0, in1=ps[bc][:, 0:N],
                op0=ALU.add, op1=ALU.mult), [PSK[bc], K('tg%d' % (8 + m)), K('RA1')], ['m12_%d' % i2])
            S.op('pool', lambda m=m, i1=i1, i2=i2: G.tensor_tensor(out=mgt[:, m, 0:N], in0=m12[i1][:, 0:N],
                                                                  in1=m12[i2][:, 0:N], op=ALU.add),
                 ['m12_%d' % i1, 'm12_%d' % i2, K('RC1')], [K('mg%d' % m)])

        if kind == 'main' and ci == 0:
            tap('mg', mgt[:, 0, 0:512], ['mg0'])
        mgk = [K('mg%d' % m) for m in range(8)]
        mb = 0
        wm = []
        for si in range(4):
            wm.append((yield (S_MIX + si)))
        for b in range(nb):
            for n in range(2):
                bank = 4 + (mb % 4)
                mb += 1
                mmg(ps[bank][0:NP, :],
                    [(mgt[:, kc, b * 128:b * 128 + NP], wm[2 * n + kc // 4][0][:, kc % 4, :]) for kc in range(8)],
                    mgk + [wm[2 * n][1], wm[2 * n + 1][1], K('RC1')], [PSK[bank]])
                xs_ = xrow(b)[0:NP, n * 512:(n + 1) * 512]
                S.op('dve', lambda bank=bank, xs_=xs_: V.scalar_tensor_tensor(
                    out=xs_, in0=ps[bank][0:NP, :], scalar=0.5 / ALPHA, in1=xs_, op0=ALU.mult, op1=ALU.add),
                    [PSK[bank], xkey], [xkey + 'h%d_%d' % (b, n)])
            layer_norm(xrow(b), NP, b, [xkey + 'h%d_%d' % (b, n) for n in range(2)], xkey + 'n%d' % b)

        first = True
        for kc in range(8):
            bank = kc % 2
            items = [(ps[bank][:, b * 128:b * 128 + NP], xrow(b)[0:NP, kc * 128:(kc + 1) * 128], NP)
                     for b in range(nb)]
            tpg(items, [xkey + 'n%d' % b for b in range(nb)], [PSK[bank]])
            w = [K('x1T%d' % kc), K('RB1')] if first else [K('x1T%d' % kc)]
            first = False
            act(x1T[:, kc, 0:N], ps[bank][:, 0:N], AF.Identity, [PSK[bank], K('RB2'), 'g1f', 'b1f'], w,
                bias=b1f[:, kc:kc + 1], scale=g1f[:, kc:kc + 1])
        if not halo:
            for b in range(nb):
                r = xrow(b)[0:NP, :]
                S.op('pool', lambda r=r: G.tensor_tensor(out=r, in0=r, in1=g1b[0:NP, :], op=ALU.mult),
                     [xkey + 'n%d' % b, 'g1b'], [xkey + 'n%d' % b])
                S.op('pool', lambda r=r: G.tensor_tensor(out=r, in0=r, in1=b1b[0:NP, :], op=ALU.add),
                     [xkey + 'n%d' % b, 'b1b'], [xkey + 'n%d' % b])

        if kind == 'main' and ci == 0:
            tap('x1', xr[:, 0, :], [xkey + 'n0'])
            tap('x1T', x1T[:, 0, 0:512], ['x1T0'])
        x1k = [K('x1T%d' % kc) for kc in range(8)]
        firstG = True
        firstH = True
        if samp:
            S.op('pool', lambda: G.tensor_copy(out=gpas[:, :, :, 0:2],
                                               in_=sfT[:, :, :].rearrange("p f (n t) -> p f n t", t=2)),
                 ['sfT0', 'sfT1', 'sfT2', 'sfT3', K('RC2')], [K('gpa%d' % f) for f in range(NF)] + [K('RC1')])
            firstG = False
        elif not halo:
            S.op('pool', lambda: G.tensor_copy(out=gpa[:, :, 0:2], in_=fh[:, :, :]),
                 ['fh%d' % f for f in range(NF)] + [K('RC2')], [K('gpa%d' % f) for f in range(NF)] + [K('RC1')])
            firstG = False
        if nextload is not None:
            nextload()
        for j in range(11):
            wg = yield (S_FFN + 2 * j)
            wu = None
            if not halo:
                wu = yield (S_FFN + 2 * j + 1)
            for fl in range(2):
                f = 2 * j + fl
                bg, bu = (0, 1) if f % 2 == 0 else (2, 3)
                mmg(ps[bg][:, 0:N], [(wg[0][:, kc, fl * 128:(fl + 1) * 128], x1T[:, kc, 0:N]) for kc in range(8)],
                    x1k + [wg[1], K('RB2')], [PSK[bg]])
                w = [K('gpa%d' % f)]
                if firstG:
                    firstG = False
                    w = w + [K('RC1')]
                if halo:
                    S.op('dve', lambda f=f, bg=bg: V.tensor_scalar(
                        out=fh[:, f, :], in0=ps[bg][:, N - 2:N], scalar1=hv[:, 0:1], scalar2=None,
                        op0=ALU.mult), [PSK[bg], 'hv'], ['fh%d' % f])
                    continue
                if samp:
                    S.op('act', lambda f=f, bg=bg: A.activation(out=gpas[:, f, :, 2], in_=ps[bg][:, 0:NS], func=AF.Copy),
                         [PSK[bg], K('RC2')], w)
                    S.op('act', lambda f=f, bg=bg: acopy(out=gsT[:, f, :], in_=ps[bg][:, 0:NSP]),
                         [PSK[bg]], ['gsT%d' % f])
                else:
                    act(gpa[:, f, 2:2 + N], ps[bg][:, 0:N], AF.Copy, [PSK[bg], K('RC2')], w)
                    if kind == 'main' and ci == 0:
                        S.op('dve', lambda f=f: V.tensor_scalar(
                            out=gpa[:, f, 256:258], in0=gpa[:, f, 256:258], scalar1=hv[:, 0:1], scalar2=None,
                            op0=ALU.mult), [K('gpa%d' % f), 'hv', K('RC2')], [K('gpa%d' % f)])
                    if last_main and 'lmg' not in os.environ.get('KSKIP', ''):
                        act(gl2[:, f, :], ps[bg][:, N - 32:N], AF.Copy, [PSK[bg]], ['gl2_%d' % f])
                if halo:
                    continue
                mmg(ps[bu][:, 0:N], [(wu[0][:, kc, fl * 128:(fl + 1) * 128], x1T[:, kc, 0:N]) for kc in range(8)],
                    x1k + [wu[1], K('RB2')], [PSK[bu]])
                ci2 = f % 2
                if samp:
                    g0, g1_, g2_ = gpas[:, f, :, 0], gpas[:, f, :, 1], gpas[:, f, :, 2]
                else:
                    g0, g1_, g2_ = gpa[:, f, 0:N], gpa[:, f, 1:N + 1], gpa[:, f, 2:N + 2]
                NN = NS if samp else N
                c_ = ct[ci2][:, 0:NN]
                rk = [K('gpa%d' % f), 'fcw', K('RC2')]
                S.op('dve', lambda f=f, c_=c_, g0=g0: V.tensor_scalar(
                    out=c_, in0=g0, scalar1=fcw[:, 3 * f:3 * f + 1], scalar2=None, op0=ALU.mult), rk, ['ct%d' % ci2])
                S.op('dve', lambda f=f, c_=c_, g1_=g1_: V.scalar_tensor_tensor(
                    out=c_, in0=g1_, scalar=fcw[:, 3 * f + 1:3 * f + 2], in1=c_, op0=ALU.mult, op1=ALU.add),
                    rk + ['ct%d' % ci2], ['ct%d' % ci2])
                S.op('dve', lambda f=f, c_=c_, g2_=g2_: V.scalar_tensor_tensor(
                    out=c_, in0=g2_, scalar=fcw[:, 3 * f + 2:3 * f + 3], in1=c_, op0=ALU.mult, op1=ALU.add),
                    rk + ['ct%d' % ci2], ['ct%d' % ci2])
                act(gl[ci2][:, 0:NN], c_, AF.Gelu, ['ct%d' % ci2], ['gl%d' % ci2])
                w = [K('hT%d' % f)]
                if firstH:
                    firstH = False
                    w = w + [K('RA1')]
                S.op('dve', lambda f=f, bu=bu, ci2=ci2, NN=NN: V.tensor_tensor(
                    out=RAt[:, f, 0:NN], in0=ps[bu][:, 0:NN], in1=gl[ci2][:, 0:NN], op=ALU.mult),
                    [PSK[bu], 'gl%d' % ci2, K('RA2')], w)
        if halo:
            return
        if not samp:
            S.op('pool', lambda: G.tensor_copy(out=fh[:, :, :], in_=gpa[:, :, N:N + 2]),
                 [K('gpa%d' % f) for f in range(NF)] + [K('RC2')], ['fh%d' % f for f in range(NF)])

        if kind == 'main' and ci == 0:
            tap('hT', RAt[:, 0, 0:512], ['hT0'])
        hk = [K('hT%d' % f) for f in range(NF)]
        for n in range(2):
            for fg in range(6):
                wd = yield (S_DOWN + n * 6 + fg)
                nk = 4 if fg < 5 else 2
                for b in range(nb):
                    bank = 0 if samp else 4 + b

                    def fn(b=b, bank=bank, fg=fg, nk=nk, wd=wd):
                        last = None
                        for kk in range(nk):
                            f = fg * 4 + kk
                            last = T.matmul(ps[bank][0:NP, :], RAt[:, f, b * 128:b * 128 + NP], wd[0][:, kk, :],
                                            start=(f == 0), stop=(f == NF - 1))
                        return last
                    S.op('pe', fn, hk + [wd[1], K('RA2')], [PSK[bank]])
            for b in range(nb):
                bank = 0 if samp else 4 + b
                xs_ = xrow(b)[0:NP, n * 512:(n + 1) * 512]
                S.op('dve', lambda bank=bank, xs_=xs_: V.scalar_tensor_tensor(
                    out=xs_, in0=ps[bank][0:NP, :], scalar=1.0 / ALPHA, in1=xs_, op0=ALU.mult, op1=ALU.add),
                    [PSK[bank], xkey + 'n%d' % b], [xkey + 'z%d_%d' % (b, n)])
        if nextp0 is not None:
            nextp0()
        for b in range(nb):
            layer_norm(xrow(b), NP, 4 + b, [xkey + 'z%d_%d' % (b, n) for n in range(2)], xkey + 'y%d' % b)
            r = xrow(b)[0:NP, :]
            S.op('pool', lambda r=r: G.tensor_tensor(out=r, in0=r, in1=g2b[0:NP, :], op=ALU.mult),
                 [xkey + 'y%d' % b, 'g2b'], [xkey + 'y%d' % b])
            S.op('pool', lambda r=r: G.tensor_tensor(out=r, in0=r, in1=b2b[0:NP, :], op=ALU.add),
                 [xkey + 'y%d' % b, 'b2b'], [xkey + 'y%d' % b])

        if samp:
            dma('sp', ys, xsr[0:NS, :], 'G:out', ['xsry0'], ['o_ys'])
        else:
            def store(ci=ci, par=par, xr=xr, xkey=xkey, nb=nb):
                b0 = 2 if ci == 0 else 0
                r0 = 0 if ci == 0 else 256 + (ci - 1) * 512
                nr = (nb - b0) * 128
                dma('sp', yp[r0:r0 + nr, :].rearrange("(b p) f -> p b f", p=128), xr[:, b0:nb, :],
                    'oy%d' % par, [xkey + 'y%d' % b for b in range(nb)], [xkey, 'o_yp%d' % ci])
            deferred.append(store)
        if last_main and 'final' not in os.environ.get('KSKIP', ''):
            final_prompt_outputs(N)
        if samp:
            final_sample_outputs()

    def layer_norm(r, NP, slot, rkeys, wkey):
        st = stt[slot % 2]
        S.op('dve', lambda: V.bn_stats(out=st[0:NP, 0, :], in_=r[0:NP, 0:512]), rkeys, ['stt%d' % (slot % 2)])
        S.op('dve', lambda: V.bn_stats(out=st[0:NP, 1, :], in_=r[0:NP, 512:1024]), rkeys, ['stt%d' % (slot % 2)])
        S.op('dve', lambda: V.bn_aggr(out=mv[0:NP, slot, :], in_=st[0:NP, :, :]), ['stt%d' % (slot % 2)], ['mv%d' % slot])
        act(lnv[0:NP, slot:slot + 1], mv[0:NP, slot, 1:2], AF.Ln, ['mv%d' % slot, 'epsb'], ['lnv%d' % slot],
            bias=epsb[0:NP, :], scale=1.0)
        act(rstd[0:NP, slot:slot + 1], lnv[0:NP, slot:slot + 1], AF.Exp, ['lnv%d' % slot], ['rstd%d' % slot], scale=-0.5)
        S.op('dve', lambda: V.tensor_scalar(out=r[0:NP, :], in0=r[0:NP, :], scalar1=mv[0:NP, slot, 0:1],
                                            scalar2=rstd[0:NP, slot:slot + 1], op0=ALU.subtract, op1=ALU.mult),
             rkeys + ['mv%d' % slot, 'rstd%d' % slot], [wkey])

    def conv_branch(j, N, samp_views=None):
        c_ = cvt[j % 2][:, 0:N]
        if samp_views is None:
            u0, u1, u2 = uext[:, j, 0:N], uext[:, j, 1:N + 1], uext[:, j, 2:N + 2]
            rk = ['uext%d' % j, 'cw']
        else:
            u0, u1, u2 = samp_views
            rk = ['uext%d' % j, 'cw', 'scT']
        ck_ = 'cvt%d' % (j % 2)
        S.op('dve', lambda: V.tensor_scalar(out=c_, in0=u0, scalar1=cw[:, 3 * j:3 * j + 1], scalar2=None,
                                             op0=ALU.mult), rk, [ck_])
        S.op('dve', lambda: V.scalar_tensor_tensor(out=c_, in0=u1, scalar=cw[:, 3 * j + 1:3 * j + 2], in1=c_,
                                                    op0=ALU.mult, op1=ALU.add), rk + [ck_], [ck_])
        S.op('dve', lambda: V.scalar_tensor_tensor(out=c_, in0=u2, scalar=cw[:, 3 * j + 2:3 * j + 3], in1=c_,
                                                    op0=ALU.mult, op1=ALU.add), rk + [ck_], [ck_])
        S.op('pool', lambda: G.tensor_tensor(out=gcv[:, j, 0:N], in0=c_, in1=cbs[:, j, 0:N], op=ALU.mult),
             [ck_, 'cbs%d' % j, 'RC1'], ['gcv%d' % j])

    pstate = {'praw': 0, 'pT': 0, 'on': 0}

    def attention_block(gb, b, first_blk):
        pts = {}
        sbank = 0
        for g in range(2):
            for kb, kblk in enumerate((max(gb - 1, 0), gb)):
                bank = (2 * g + kb) % 4
                mmg(ps[bank][:, :],
                    [(kT[64 * g:64 * g + 64, kblk * 128:(kblk + 1) * 128], qT[64 * g:64 * g + 64, :, b * 128:(b + 1) * 128])],
                    ['kT%d' % kblk] + ['qT%d' % c for c in range(4)] + ['RC1'], [PSK[bank]])
                pr = pstate['praw'] % 3
                pstate['praw'] += 1
                act(praw[pr][:], ps[bank][:, :], AF.Exp, [PSK[bank]], ['praw%d' % pr])
                pi = pstate['pT'] % 8
                pstate['pT'] += 1
                if kb == 0:
                    E = E_first if first_blk else E_prev
                    ek = 'E_first' if first_blk else 'E_prev'
                else:
                    E, ek = E_cur, 'E_cur'
                S.op('dve', lambda pi=pi, pr=pr, E=E, g=g: V.tensor_tensor(
                    out=pT[pi][:], in0=praw[pr][:], in1=E[:, g * 512:(g + 1) * 512], op=ALU.mult),
                    ['praw%d' % pr, ek], ['pT%d' % pi])
                pts[(g, kb)] = pi
                if gb == 2:
                    tap('praw_%d%d' % (g, kb), praw[pr][:], ['praw%d' % pr])
                    tap('pT_%d%d' % (g, kb), pT[pi][:], ['pT%d' % pi])
        for g in range(2):
            bank = 4 + g
            pairs_r = ['pT%d' % pts[(g, 0)], 'pT%d' % pts[(g, 1)], 'Vx%d' % max(gb - 1, 0), 'Vx%d' % gb, 'Vx1']

            def fn(g=g, bank=bank):
                last = None
                for c in range(4):
                    for kb, kblk in enumerate((max(gb - 1, 0), gb)):
                        last = T.matmul(ps[bank][:, c * 65:(c + 1) * 65],
                                        pT[pts[(g, kb)]][:, c * 128:(c + 1) * 128],
                                        Vx[:, kblk, g, :], start=(kb == 0), stop=(kb == 1))
                return last
            S.op('pe', fn, pairs_r, [PSK[bank]])
        ri = pstate['on'] % 2
        pstate['on'] += 1
        for g in range(2):
            bank = 4 + g
            pv = ps[bank][:, 0:260].rearrange("p (c e) -> p c e", c=4)
            S.op('dve', lambda g=g, pv=pv, ri=ri: V.tensor_tensor(
                out=rdt[ri][:, 4 * g:4 * g + 4], in0=pv[:, :, 64], in1=esk[:, 4 * g:4 * g + 4], op=ALU.add),
                [PSK[bank], 'esk'], ['rdt%d_%d' % (ri, g)])
            S.op('dve', lambda g=g, ri=ri: V.reciprocal(out=rdt[ri][:, 4 * g:4 * g + 4], in_=rdt[ri][:, 4 * g:4 * g + 4]),
                 ['rdt%d_%d' % (ri, g)], ['rdt%d_%d' % (ri, g)])
            S.op('dve', lambda g=g, pv=pv, ri=ri: V.tensor_tensor(
                out=onb[ri][:, g * 256:(g + 1) * 256].rearrange("p (c d) -> p c d", c=4),
                in0=pv[:, :, 0:64],
                in1=rdt[ri][:, 4 * g:4 * g + 4].unsqueeze(2).to_broadcast([128, 4, 64]), op=ALU.mult),
                [PSK[bank], 'rdt%d_%d' % (ri, g)], ['on%d_%d' % (ri, g)])
        if gb == 2:
            tap('rdt', rdt[ri][:], ['rdt%d_0' % ri, 'rdt%d_1' % ri])
            tap('onb', onb[ri][:], ['on%d_0' % ri, 'on%d_1' % ri])
        tb = 6 + (gb % 2)
        items = [(ps[tb][:, c * 128:(c + 1) * 128], onb[ri][:, c * 128:(c + 1) * 128], 128) for c in range(4)]
        tpg(items, ['on%d_0' % ri, 'on%d_1' % ri], [PSK[tb]])
        act(oT[:, :, b * 128:(b + 1) * 128], ps[tb][:, :].rearrange("p (c q) -> p c q", c=4), AF.Copy,
            [PSK[tb], 'RC1'], ['oT'])

    gpas_t = sb("gpas", [128, NF * NS * 3], BF16)
    gpas = gpas_t[:, :].rearrange("p (f n t) -> p f n t", f=NF, t=3)
    RB_s = sb("RB_s", [128, 8, NSP], BF16)
    RA_s = sb("RA_s", [128, NF, NSP], BF16)
    qT_s = sb("qT_s", [128, 4, NSP], BF16)
    oT_s = sb("oT_s", [128, 4, NSP], BF16)
    mg_s = sb("mg_s", [128, 8, NSP], BF16)
    gcv_s = sb("gcv_s", [128, 4, NSP], BF16)
    cbs_s = sb("cbs_s", [128, 4, NSP], BF16)
    uext_s = sb("uext_s", [128, 4, NSP + 2], F32)
    stg = sb("stg", [NSP, 768], F32)
    gsT = sb("gsT", [128, NF, NSP], F32)
    kcT_new = sb("kcT_new", [128, NS], BF16)
    usT = sb("usT", [128, 4, NS, 3], F32)

    def attention_samples():
        N = NS
        dma('sp', ckc, ck.rearrange("n s f -> s n f"), 'sck', [], ['ckc', 'xres0'])
        dma('pool', cvb, cv.rearrange("n s f -> s n f"), 'scv', ['xres0'], ['cvb'])
        dma('sp', nks_o[:, 0:127, :], ck[:, 1:128, :], 'G:out', [], ['o_nks'])
        dma('sp', nvs_o[:, 0:127, :], cv[:, 1:128, :], 'G:out', [], ['o_nvs'])
        for n4 in range(4):
            bank = n4 % 2
            items = [(ps[bank][:, i * 128:(i + 1) * 128], ckc[:, n4 * 4 + i, :], 128) for i in range(4)]
            tpg(items, ['ckc', 'xres0'], [PSK[bank]])
            act(kcT[:, n4 * 4:n4 * 4 + 4, :], ps[bank][:, :].rearrange("p (n s) -> p n s", n=4), AF.Copy,
                [PSK[bank], 'ckc', 'xres0'], ['kcT%d' % n4])
        def fn():
            last = None
            for n in range(NS):
                for g in range(2):
                    last = T.matmul(ps[2][:, n * 8 + g * 4:n * 8 + g * 4 + 4],
                                    kcT[64 * g:64 * g + 64, n, :], qT_s[64 * g:64 * g + 64, :, n],
                                    start=True, stop=True)
            return last
        S.op('pe', fn, ['xres0'] + ['kcT%d' % i for i in range(4)] + ['sqT%d' % c for c in range(4)] + ['sRC1'], [PSK[2]])
        act(praws[:], ps[2][:, 0:128], AF.Exp, [PSK[2]], ['praws'])
        S.op('dve', lambda: V.tensor_tensor(out=pTs[:], in0=praws[:], in1=esm[:], op=ALU.mult),
             ['praws', 'esm'], ['pTs'])
        S.op('dve', lambda: V.tensor_tensor(out=prod[:], in0=qT_s[:, :, 0:NS],
                                            in1=kcT_new[:, :].unsqueeze(1).to_broadcast([128, 4, NS]), op=ALU.mult),
             ['sqT%d' % c for c in range(4)] + ['kTs', 'sRC1'], ['prod'])
        mmg(ps[3][:, 0:64], [(bones[:, :], prod[:].rearrange("p c n -> p (c n)"))], ['bones', 'prod'], [PSK[3]])
        act(pnb[:].rearrange("p c n -> p (c n)"), ps[3][:, 0:64], AF.Exp, [PSK[3]], ['pnb'])
        def fn2():
            last = None
            for n in range(NS):
                for g in range(2):
                    last = T.matmul(ps[0][64 * g:64 * g + 64, n * 4:n * 4 + 4], cvb[:, n, 64 * g:64 * g + 64],
                                    pTs[:, n * 8 + g * 4:n * 8 + g * 4 + 4], start=True, stop=True)
            return last
        S.op('pe', fn2, ['cvb', 'pTs', 'xres0'], [PSK[0]])
        ones64 = ones_t[:, :]

        def fn3():
            last = None
            for g in range(2):
                last = T.matmul(ps[1][64 * g:64 * g + 64, 0:64], ones64,
                                pTs[:].rearrange("p (n g c) -> p n g c", g=2, c=4)[:, :, g, :], start=True, stop=True)
            return last
        S.op('pe', fn3, ['ones_t', 'pTs'], [PSK[1]])
        S.op('dve', lambda: V.tensor_tensor(out=ons[:], in0=pnb[:], in1=vTs[:, :].unsqueeze(1).to_broadcast([128, 4, NS]),
                                            op=ALU.mult), ['pnb', 'vTs'], ['ons'])
        S.op('dve', lambda: V.tensor_tensor(out=ons[:], in0=ons[:],
                                            in1=ps[0][:, 0:64].rearrange("p (n c) -> p c n", c=4), op=ALU.add),
             ['ons', PSK[0]], ['ons'])
        S.op('dve', lambda: V.tensor_tensor(out=dns[:], in0=pnb[:],
                                            in1=ps[1][:, 0:64].rearrange("p (n c) -> p c n", c=4), op=ALU.add),
             ['pnb', PSK[1]], ['dns'])
        for g in range(2):
            sl = slice(64 * g, 64 * g + 64)
            S.op('dve', lambda g=g, sl=sl: V.tensor_tensor(
                out=dns[sl], in0=dns[sl], in1=esk[sl, 4 * g:4 * g + 4].unsqueeze(2).to_broadcast([64, 4, NS]), op=ALU.add),
                ['dns', 'esk'], ['dns'])
        S.op('dve', lambda: V.reciprocal(out=dns[:], in_=dns[:]), ['dns'], ['dns'])
        S.op('dve', lambda: V.tensor_tensor(out=ons[:], in0=ons[:], in1=dns[:], op=ALU.mult), ['ons', 'dns'], ['ons'])
        for h in range(8):
            g, c = h // 4, h % 4
            S.op('dve', lambda h=h, g=g, c=c: V.tensor_copy(out=oT_s[64 * (h % 2):64 * (h % 2) + 64, h // 2, 0:NS],
                                                           in_=ons[64 * g:64 * g + 64, c, :]),
                 ['ons', 'sRC1'], ['soT'])

    def final_prompt_outputs(N):
        dma('sp', nk_o, nkv[:, 0:128], 'G:out', ['nkv_k'], ['o_nk'])
        dma('sp', nv_o, nkv[:, 128:256], 'G:out', ['nkv_v'], ['o_nv'])
        items = [(ps[0][0:32, j * 128:(j + 1) * 128], uext[:, j, N - 30:N + 2], 128) for j in range(4)]
        tpg(items, ['uext%d' % j for j in range(4)], [PSK[0]])
        S.op('act', lambda: acopy(out=ncf[:, 0:512], in_=ps[0][0:32, :]), [PSK[0]], ['ncf_c', 'xres0'])
        dma('sp', ncv_o, ncf[30:32, 0:512], 'G:ncf', ['ncf_c', 'xres0'], ['o_ncv'])
        for grp in range(6):
            f0 = grp * 4
            nf = min(4, NF - f0)
            bank = 1 + (grp % 2)
            items = [(ps[bank][0:32, i * 128:(i + 1) * 128], gl2[:, f0 + i, :], 128) for i in range(nf)]
            tpg(items, ['gl2_%d' % (f0 + i) for i in range(nf)], [PSK[bank]])
            S.op('act', lambda f0=f0, nf=nf, bank=bank: acopy(
                out=ncf[:, 512 + f0 * 128:512 + (f0 + nf) * 128], in_=ps[bank][0:32, 0:nf * 128]),
                [PSK[bank], 'xres0'], ['ncf_f%d' % grp])
        dma('sp', nfc_o, ncf[30:32, 512:512 + DFF], 'G:ncf', ['ncf_f%d' % g for g in range(6)] + ['xres0'], ['o_nfc'])

    def final_sample_outputs():
        dma('sp', nks_o[:, 127, :], kvs[0:NS, 0:128], 'G:out', ['kvs_k'], ['o_nks2'])
        dma('sp', nvs_o[:, 127, :], kvs[0:NS, 128:256], 'G:out', ['kvs_v'], ['o_nvs2'])
        dma('sp', ncs_o[:, 0, :], sc.rearrange("(n t) c -> n t c", t=2)[:, 1, :], 'G:out', [], ['o_ncs0'])
        dma('sp', nfs_o[:, 0, :], sf.rearrange("(n t) c -> n t c", t=2)[:, 1, :], 'G:out', [], ['o_nfs0'])
        items = [(ps[0][0:NSP, j * 128:(j + 1) * 128], uext_s[:, j, 2:2 + NSP], 128) for j in range(4)]
        tpg(items, ['suext%d' % j for j in range(4)], [PSK[0]])
        S.op('act', lambda: acopy(out=stg[:, 0:512], in_=ps[0][0:NSP, :]), [PSK[0]], ['stg'])
        dma('sp', ncs_o[:, 1, :], stg[0:NS, 0:512], 'stg_o', ['stg'], ['o_ncs1'])
        for pi, (f0, nf) in enumerate(OPIECES):
            bank = 1 + (pi % 2)
            items = [(ps[bank][0:NSP, i * 128:(i + 1) * 128], gsT[:, f0 + i, :], 128) for i in range(nf)]
            tpg(items, ['gsT%d' % (f0 + i) for i in range(nf)], [PSK[bank]])
            S.op('act', lambda nf=nf, bank=bank: acopy(out=stg[:, 0:nf * 128], in_=ps[bank][0:NSP, 0:nf * 128]),
                 [PSK[bank]], ['stg'])
            dma('sp', nfs_o[:, 1, f0 * 128:(f0 + nf) * 128], stg[0:NS, 0:nf * 128], 'stg_o', ['stg'], ['o_nfs1_%d' % pi])


    PIECES = [(0, 6), (6, 6), (12, 6), (18, 4)]
    OPIECES = [(0, 4), (4, 4), (8, 4), (12, 4), (16, 4), (20, 2)]

    def samp_prologue():
        dma('sp', stg[:, 0:512], sc, 'stg_i', [], ['stg'])
        items = [(ps[0][:, j * 32:(j + 1) * 32], stg[:, j * 128:(j + 1) * 128], 2 * NS) for j in range(4)]
        tpg(items, ['stg'], [PSK[0]])
        S.op('act', lambda: acopy(out=scT[:], in_=ps[0][:, 0:128].rearrange("p (j m) -> p j m", j=4)),
             [PSK[0]], ['scT'])
        for pi, (f0, nf) in enumerate(PIECES):
            dma('sp', stg[:, 0:nf * 128], sf[:, f0 * 128:(f0 + nf) * 128], 'stg_i', [], ['stg'])
            bank = 1 + (pi % 2)
            items = [(ps[bank][:, i * 32:(i + 1) * 32], stg[:, i * 128:(i + 1) * 128], 2 * NS) for i in range(nf)]
            tpg(items, ['stg'], [PSK[bank]])
            S.op('act', lambda f0=f0, nf=nf, bank=bank: acopy(
                out=sfT[:, f0:f0 + nf, :], in_=ps[bank][:, 0:nf * 32].rearrange("p (j m) -> p j m", j=nf)),
                [PSK[bank]], ['sfT%d' % pi])

    def run_pass(gens):
        pend = []
        for g in gens:
            try:
                pend.append([g, next(g)])
            except StopIteration:
                pass
        while pend:
            sl = min(p[1] for p in pend)
            view = wload(sl)
            nxt = []
            for p in pend:
                if p[1] == sl:
                    try:
                        p[1] = p[0].send(view)
                        nxt.append(p)
                    except StopIteration:
                        pass
                else:
                    nxt.append(p)
            pend = nxt

    import os
    LIM = int(os.environ.get('KSTAGE', '99'))
    def conv_branch_samp(j, N):
        S.op('pool', lambda: G.tensor_copy(out=usT[:, j, :, 2], in_=uext_s[:, j, 2:2 + NS]), ['suext%d' % j], ['usT%d' % j])
        S.op('pool', lambda: G.tensor_copy(out=usT[:, j, :, 0:2],
                                           in_=scT[:, j, :].rearrange("p (n t) -> p n t", t=2)),
             ['scT'], ['usT%d' % j])
        c_ = cvt[j % 2][:, 0:NS]
        ck_ = 'cvt%d' % (j % 2)
        rk = ['usT%d' % j, 'cw']
        S.op('dve', lambda: V.tensor_scalar(out=c_, in0=usT[:, j, :, 0], scalar1=cw[:, 3 * j:3 * j + 1], scalar2=None,
                                             op0=ALU.mult), rk, [ck_])
        S.op('dve', lambda: V.scalar_tensor_tensor(out=c_, in0=usT[:, j, :, 1], scalar=cw[:, 3 * j + 1:3 * j + 2],
                                                    in1=c_, op0=ALU.mult, op1=ALU.add), rk + [ck_], [ck_])
        S.op('dve', lambda: V.scalar_tensor_tensor(out=c_, in0=usT[:, j, :, 2], scalar=cw[:, 3 * j + 2:3 * j + 3],
                                                    in1=c_, op0=ALU.mult, op1=ALU.add), rk + [ck_], [ck_])
        S.op('pool', lambda: G.tensor_tensor(out=gcv_s[:, j, 0:NS], in0=c_, in1=cbs_s[:, j, 0:NS], op=ALU.mult),
             [ck_, 'scbs%d' % j, 'sRC1'], ['sgcv%d' % j])

    NCH = min(5, max(0, LIM - 1)) if LIM < 99 else 5
    for ci in range(NCH):
        nl = (lambda ci=ci: xload('main', ci + 1)) if ci < 4 else None
        np0 = (lambda ci=ci: emit_p0_main(ci + 1)) if ci < 4 else None
        gens = [chunk('main', ci, nextload=nl, preloaded=(ci > 0), p0done=(ci > 0), nextp0=np0)]
        if ci == 0 and LIM >= 7:
            xload('samp', 0)
            samp_prologue()
            gens.append(chunk('samp', 0, preloaded=True))
        run_pass(gens)
    flush_deferred()

    S.op('sp', None, ['o_ys', 'o_nk', 'o_nv', 'o_ncv', 'o_nfc', 'o_nks', 'o_nvs', 'o_nks2', 'o_nvs2',
                      'o_ncs0', 'o_nfs0', 'o_ncs1', 'o_nfs1_0', 'o_nfs1_1', 'o_nfs1_2', 'o_nfs1_3', 'o_nfs1_4', 'o_nfs1_5', 'o_yp0', 'o_yp1', 'o_yp2', 'o_yp3', 'o_yp4'], [])
    S.finalize(es)
    es.close()
    return nc, None


_PROG = {}


def _consts():
    slopes = 2.0 ** (-8.0 * np.arange(1, 9) / 8.0)
    s = np.arange(128)[:, None]
    q = np.arange(128)[None, :]
    eprev = np.zeros((128, 2, 4, 128), np.float32)
    ecur = np.zeros((128, 2, 4, 128), np.float32)
    esm = np.zeros((128, NS, 2, 4), np.float32)
    for g in range(2):
        for c in range(4):
            sl = slopes[4 * g + c]
            dprev = q + 128 - s
            dcur = q - s
            eprev[:, g, c, :] = np.where(s >= q, np.exp(-sl * dprev), 0.0)
            ecur[:, g, c, :] = np.where(q >= s, np.exp(-sl * dcur), 0.0)
            esm[:, :, g, c] = np.exp(-sl * (128 - np.arange(128)))[:, None]
    bones = (np.arange(128)[:, None] // 64 == np.arange(128)[None, :] // 64).astype(np.float32)
    return dict(eprev=eprev.reshape(128, 1024), ecur=ecur.reshape(128, 1024), esm=esm.reshape(128, 128),
                bones=bones, ident=np.eye(128, dtype=np.float32))


def kernel(x_prompt, x_sample, cache_k, cache_v, state_conv, state_ffn_conv, w_in, conv_w, attn_sinks,
           w_attn_out, w_conv_out, w_mix_out, ln1_g, ln1_b, w_gate, w_up, ffn_conv_w, w_down, ln2_g, ln2_b):
    f32 = np.float32
    A_ = lambda a: np.ascontiguousarray(np.asarray(a, dtype=f32))
    x_prompt = A_(x_prompt); x_sample = A_(x_sample)
    if 'nc' not in _PROG:
        _PROG['nc'], _PROG['es'] = build_program()
    nc = _PROG['nc']
    cols = []
    for c in range(4):
        cols += list(range(c * 64, c * 64 + 64)) + list(range((4 + c) * 64, (4 + c) * 64 + 64))
    cols += list(range(512, 768))
    cols += list(range(768, 1280))
    for j in range(4):
        cols += list(range(1280 + j * 128, 1280 + (j + 1) * 128)) + list(range(1792 + j * 128, 1792 + (j + 1) * 128))
    cols += list(range(2304, 4352))
    w_in_p = A_(A_(w_in)[0][:, cols])
    bc = lambda v: A_(np.broadcast_to(A_(v).reshape(1, -1), (128, A_(v).size)))
    fm = lambda v, n: A_(A_(v).reshape(n, 128).T)
    common = dict(
        w_in=w_in_p, w_ao=A_(w_attn_out)[0], w_co=A_(w_conv_out)[0], w_mix=A_(w_mix_out)[0],
        w_gate=A_(w_gate)[0], w_up=A_(w_up)[0], w_down=A_(w_down)[0],
        cw=A_(A_(conv_w)[0].reshape(3, 4, 128).transpose(2, 1, 0).reshape(128, 12)),
        fcw=A_(A_(ffn_conv_w)[0].reshape(3, NF, 128).transpose(2, 1, 0).reshape(128, 66)),
        sk=bc(A_(attn_sinks)[0]),
        g1b=bc(ln1_g[0]), b1b=bc(ln1_b[0]), g2b=bc(ln2_g[0]), b2b=bc(ln2_b[0]),
        g1f=fm(ln1_g[0], 8), b1f=fm(ln1_b[0], 8),
    )
    common.update(_consts())
    ck = A_(cache_k)[0].reshape(128, 128, 128)
    cv = A_(cache_v)[0].reshape(128, 128, 128)
    sc = A_(state_conv)[0].reshape(256, 512)
    sf = A_(state_ffn_conv)[0].reshape(256, DFF)
    in_maps = []
    for c in range(NCORES):
        b, r = c // 4, c % 4
        s0 = r * TOK_CORE
        xp = np.zeros((NTOK, D), f32)
        if r > 0:
            xp[0:HALO] = x_prompt[b, s0 - HALO:s0]
        xp[HALO:] = x_prompt[b, s0:s0 + TOK_CORE]
        m = dict(common)
        m.update(xp=xp, xs=A_(x_sample[c * NS:(c + 1) * NS, 0, :]),
                 ck=A_(ck[c * NS:(c + 1) * NS]), cv=A_(cv[c * NS:(c + 1) * NS]),
                 sc=A_(sc[c * 2 * NS:(c + 1) * 2 * NS]), sf=A_(sf[c * 2 * NS:(c + 1) * 2 * NS]),
                 hv=np.full((128, 1), 0.0 if r == 0 else 1.0, f32))
        in_maps.append(m)
    res = run_bass_kernel_spmd(nc, in_maps, core_ids=list(range(NCORES)))
    R = res.results
    y_prompt = np.zeros((2, 8192, D), f32)
    y_sample = np.zeros((128, 1, D), f32)
    nk_p = np.zeros((1, 2, 128, 2, 64), f32); nv_p = np.zeros((1, 2, 128, 2, 64), f32)
    nc_p = np.zeros((1, 2, 2, 512), f32); nf_p = np.zeros((1, 2, 2, DFF), f32)
    nk_s = np.zeros((1, 128, 128, 2, 64), f32); nv_s = np.zeros((1, 128, 128, 2, 64), f32)
    nc_s = np.zeros((1, 128, 2, 512), f32); nf_s = np.zeros((1, 128, 2, DFF), f32)
    for c in range(NCORES):
        b, r = c // 4, c % 4
        y_prompt[b, r * TOK_CORE:(r + 1) * TOK_CORE] = R[c]["yp"]
        y_sample[c * NS:(c + 1) * NS, 0] = R[c]["ys"]
        if r == 3:
            nk_p[0, b] = R[c]["nk"].reshape(128, 2, 64)
            nv_p[0, b] = R[c]["nv"].reshape(128, 2, 64)
            nc_p[0, b] = R[c]["ncv"]
            nf_p[0, b] = R[c]["nfc"]
        nk_s[0, c * NS:(c + 1) * NS] = R[c]["nks"].reshape(NS, 128, 2, 64)
        nv_s[0, c * NS:(c + 1) * NS] = R[c]["nvs"].reshape(NS, 128, 2, 64)
        nc_s[0, c * NS:(c + 1) * NS] = R[c]["ncs"]
        nf_s[0, c * NS:(c + 1) * NS] = R[c]["nfs"]
    return (y_prompt, y_sample, nk_p, nv_p, nc_p, nf_p, nk_s, nv_s, nc_s, nf_s)
```

```python
import contextlib
import numpy as np
import concourse.bass as bass
import concourse.mybir as mybir
from concourse.bass_utils import run_bass_kernel_spmd

F32 = mybir.dt.float32
BF16 = mybir.dt.bfloat16
AF = mybir.ActivationFunctionType
ALU = mybir.AluOpType

D = 1024
NPROJ = 4352
DFF = 2816
NF = 22
ALPHA = 2.0 ** 0.25
EPS = 1e-5
EPS_S = EPS / (ALPHA * ALPHA)
NCORES = 8
TOK_CORE = 2048
HALO = 256
NTOK = TOK_CORE + HALO
NBLK = NTOK // 128
NS = 16
NSP = 32
RING = 6


class Sched:
    def __init__(self, nc):
        self.nc = nc
        self.ops = []
        self.last_w = {}
        self.readers = {}

    def op(self, eng, fn, reads=(), writes=(), dma=None):
        deps = set()
        for k in reads:
            if k in self.last_w:
                deps.add(self.last_w[k])
            if k.startswith('ps'):
                deps.update(r for r in self.readers.get(k, ()) if self.ops[r]['eng'] != eng)
        for k in writes:
            if k in self.last_w:
                deps.add(self.last_w[k])
            deps.update(self.readers.get(k, ()))
        oid = len(self.ops)
        self.ops.append(dict(eng=eng, fn=fn, deps=deps, dma=dma, signal=False, waits=[]))
        for k in reads:
            self.readers.setdefault(k, []).append(oid)
        for k in writes:
            self.last_w[k] = oid
            self.readers[k] = []
        return oid

    def finalize(self, stack):
        nc = self.nc
        engs = {'pe': nc.tensor, 'act': nc.scalar, 'dve': nc.vector, 'pool': nc.gpsimd, 'sp': nc.sync}
        ops = self.ops
        cnt = {}
        for o in ops:
            st = ('dma:' + o['dma']) if o['dma'] else o['eng']
            o['stream'] = st
            cnt[st] = cnt.get(st, 0) + 1
            o['seq'] = cnt[st]
        for o in ops:
            if o['dma'] and o['dma'].startswith('G:'):
                o['seq'] = cnt[o['stream']]
        clk = {e: {} for e in engs}
        eng_seq = {e: 0 for e in engs}
        for o in ops:
            e = o['eng']
            c = clk[e]
            myseq = o['seq'] if not o['dma'] else None
            need = {}
            for d in o['deps']:
                od = ops[d]
                st = od['stream']
                if st == e and not o['dma']:
                    if e == 'pe':
                        continue
                    if o['seq'] - od['seq'] > 2:
                        continue
                    if c.get('self_' + e, 0) >= od['seq']:
                        continue
                elif c.get(st, 0) >= od['seq']:
                    continue
                if st not in need or ops[need[st]]['seq'] < od['seq']:
                    need[st] = d
            for st, d in need.items():
                od = ops[d]
                if st == e and not o['dma']:
                    c['self_' + e] = od['seq']
                elif c.get(st, 0) >= od['seq']:
                    continue
                od['signal'] = True
                o['waits'].append(d)
                for k, v in od['vc'].items():
                    if c.get(k, 0) < v:
                        c[k] = v
            if not o['dma']:
                c[e] = o['seq']
                o['vc'] = dict(c)
            else:
                vc = dict(c)
                vc[o['stream']] = o['seq']
                o['vc'] = vc
        sems = {}
        val = {}
        for o in ops:
            st = o['stream']
            if o['dma']:
                val[st] = val.get(st, 0) + 16
                o['val'] = val[st]
            elif o['signal']:
                val[st] = val.get(st, 0) + 1
                o['val'] = val[st]
        for o in ops:
            if o['dma'] and o['dma'].startswith('G:'):
                o['val'] = val[o['stream']]
        for st in val:
            sems[st] = stack.enter_context(nc.semaphore('s_' + st.replace(':', '_')))
        self.nsem = len(sems)
        for o in ops:
            e = engs[o['eng']]
            for d in o['waits']:
                od = ops[d]
                e.wait_ge(sems[od['stream']], od['val'])
            if o['fn'] is None:
                continue
            ins = o['fn']()
            if o['dma']:
                ins.then_inc(sems[o['stream']], 16)
            elif o['signal']:
                ins.then_inc(sems[o['stream']], 1)


def slot_table():
    t = []
    for i in range(17):
        t.append(('w_in', 0, 8, i * 256, 256))
    for h in range(2):
        t.append(('w_ao', 0, 4, h * 512, 512))
        t.append(('w_co', 0, 4, h * 512, 512))
    for n in range(2):
        for kh in range(2):
            t.append(('w_mix', kh * 4, 4, n * 512, 512))
    for j in range(11):
        t.append(('w_gate', 0, 8, j * 256, 256))
        t.append(('w_up', 0, 8, j * 256, 256))
    for n in range(2):
        for fg in range(6):
            nk = 4 if fg < 5 else 2
            t.append(('w_down', fg * 4, nk, n * 512, 512))
    return t


SLOTS = slot_table()
S_IN, S_AO, S_CO, S_MIX, S_FFN, S_DOWN = 0, 17, 19, 21, 25, 47
WSHAPES = {'w_in': (D, NPROJ), 'w_ao': (512, D), 'w_co': (512, D), 'w_mix': (D, D),
           'w_gate': (D, DFF), 'w_up': (D, DFF), 'w_down': (DFF, D)}


def build_program():
    nc = bass.Bass("TRN2", target_bir_lowering=False)
    S = Sched(nc)
    es = contextlib.ExitStack()

    def din(name, shape, dt=F32):
        return nc.dram_tensor(name, list(shape), dt, kind="ExternalInput").ap()

    def dout(name, shape, dt=F32):
        return nc.dram_tensor(name, list(shape), dt, kind="ExternalOutput").ap()

    xp = din("xp", [NTOK, D])
    xs = din("xs", [NS, D])
    ck = din("ck", [NS, 128, 128])
    cv = din("cv", [NS, 128, 128])
    sc = din("sc", [2 * NS, 512])
    sf = din("sf", [2 * NS, DFF])
    W = {k: din(k, v) for k, v in WSHAPES.items()}
    cw_d = din("cw", [128, 12])
    fcw_d = din("fcw", [128, 66])
    sk_d = din("sk", [128, 8])
    g1b_d = din("g1b", [128, D]); b1b_d = din("b1b", [128, D])
    g2b_d = din("g2b", [128, D]); b2b_d = din("b2b", [128, D])
    g1f_d = din("g1f", [128, 8]); b1f_d = din("b1f", [128, 8])
    ident_d = din("ident", [128, 128])
    eprev_d = din("eprev", [128, 1024]); ecur_d = din("ecur", [128, 1024])
    esm_d = din("esm", [128, 128])
    bones_d = din("bones", [128, 128])
    hv_d = din("hv", [128, 1])

    yp = dout("yp", [TOK_CORE, D])
    ys = dout("ys", [NS, D])
    nk_o = dout("nk", [128, 128]); nv_o = dout("nv", [128, 128])
    ncv_o = dout("ncv", [2, 512]); nfc_o = dout("nfc", [2, DFF])
    nks_o = dout("nks", [NS, 128, 128]); nvs_o = dout("nvs", [NS, 128, 128])
    ncs_o = dout("ncs", [NS, 2, 512]); nfs_o = dout("nfs", [NS, 2, DFF])
    wscr = nc.dram_tensor("wscr", [len(SLOTS), 128, 2048], BF16, kind="Internal").ap()

    def sb(name, shape, dt):
        return es.enter_context(nc.sbuf_tensor("s_" + name, list(shape), dt))

    xres = [sb("xres%d" % i, [128, 4, D], F32) for i in range(2)]
    RB = sb("RB", [128, 8, 512], BF16)
    RA = sb("RA", [128, NF, 512], BF16)
    RC = sb("RC", [128, 12288], BF16)
    qT = RC[:, 0:2048].rearrange("p (c n) -> p c n", c=4)
    oT = RC[:, 2048:4096].rearrange("p (c n) -> p c n", c=4)
    mg = RC[:, 4096:8192].rearrange("p (c n) -> p c n", c=8)
    gcv = RC[:, 8192:10240].rearrange("p (c n) -> p c n", c=4)
    cbs = RC[:, 10240:12288].rearrange("p (c n) -> p c n", c=4)
    gpa = RC[:, 0:NF * 514].rearrange("p (f n) -> p f n", f=NF)
    kT = sb("kT", [128, NBLK * 128], BF16)
    Vx = sb("Vx", [128, NBLK, 2, 65], BF16)
    ccs = [sb("ccs%d" % i, [128, 512], F32) for i in range(2)]
    uext = sb("uext", [128, 4, 514], F32)
    cvt = [sb("cvt%d" % i, [128, 512], F32) for i in range(2)]
    praw = [sb("praw%d" % i, [128, 512], BF16) for i in range(2)]
    pT = [sb("pT%d" % i, [128, 512], BF16) for i in range(8)]
    onb = [sb("on%d" % i, [128, 512], F32) for i in range(2)]
    rdt = [sb("rdt%d" % i, [128, 8], F32) for i in range(2)]
    m12 = [sb("m12_%d" % i, [128, 512], BF16) for i in range(2)]
    ct = [sb("ct%d" % i, [128, 512], F32) for i in range(2)]
    gl = [sb("gl%d" % i, [128, 512], BF16) for i in range(2)]
    ring = sb("ring", [128, RING, 2048], BF16)
    g1b = sb("g1b", [128, D], F32); b1b = sb("b1b", [128, D], F32)
    g2b = sb("g2b", [128, D], F32); b2b = sb("b2b", [128, D], F32)
    g1f = sb("g1f", [128, 8], F32); b1f = sb("b1f", [128, 8], F32)
    ident = sb("ident", [128, 128], F32)
    E_prev = sb("E_prev", [128, 1024], BF16)
    E_cur = sb("E_cur", [128, 1024], BF16)
    E_first = sb("E_first", [128, 1024], BF16)
    esm = sb("esm", [128, 128], BF16)
    bones = sb("bones", [128, 128], BF16)
    hv = sb("hv", [128, 1], F32)
    cw = sb("cw", [128, 12], F32)
    fcw = sb("fcw", [128, 66], F32)
    skt = sb("skt", [128, 8], F32)
    esk = sb("esk", [128, 8], F32)
    epsb = sb("epsb", [128, 1], F32)
    stt = [sb("stt%d" % i, [128, 2, 6], F32) for i in range(2)]
    mv = sb("mv", [128, 8, 2], F32)
    lnv = sb("lnv", [128, 8], F32)
    rstd = sb("rstd", [128, 8], F32)
    nkv = sb("nkv", [128, 256], F32)
    gl2 = sb("gl2", [128, NF, 32], F32)
    fh = sb("fh", [128, NF, 2], BF16)
    xsr = sb("xsr", [NSP, D], F32)
    scT = sb("scT", [128, 4, 2 * NS], F32)
    sfT = sb("sfT", [128, NF, 2 * NS], F32)
    vTs = sb("vTs", [128, NS], F32)
    prod = sb("prod", [128, 4, NS], BF16)
    pnb = sb("pnb", [128, 4, NS], F32)
    ons = sb("ons", [128, 4, NS], F32)
    dns = sb("dns", [128, 4, NS], F32)
    pTs = sb("pTs", [128, NS * 8], BF16)
    praws = sb("praws", [128, NS * 8], BF16)
    kvs = sb("kvs", [NSP, 256], F32)
    x0f = xres[0][:, :, :].rearrange("p b f -> p (b f)")
    x1f = xres[1][:, :, :].rearrange("p b f -> p (b f)")
    ncf = x0f[0:32, 0:512 + DFF]
    ones_t = sb("ones_t", [128, 64], BF16)
    ckc = xres[0][:, :, :].rearrange("p b f -> p (b f)")[:, 0:2048].rearrange("p (n f) -> p n f", n=NS)
    cvb = xres[0][:, :, :].rearrange("p b f -> p (b f)")[:, 2048:3072].bitcast(BF16).rearrange("p (n f) -> p n f", n=NS)
    kcT = xres[0][:, :, :].rearrange("p b f -> p (b f)")[:, 3072:4096].bitcast(BF16).rearrange("p (n f) -> p n f", n=NS)

    ps = [es.enter_context(nc.psum_tensor("ps%d" % i, [128, 512], F32)) for i in range(8)]
    PSK = ["ps%d" % i for i in range(8)]

    T, V, A, G, SP = nc.tensor, nc.vector, nc.scalar, nc.gpsimd, nc.sync

    def mmg(out_ap, pairs, reads, writes):
        def fn():
            last = None
            n = len(pairs)
            for i, (l, r) in enumerate(pairs):
                last = T.matmul(out_ap, l, r, start=(i == 0), stop=(i == n - 1))
            return last
        return S.op('pe', fn, reads, writes)

    def tpg(items, reads, writes):
        def fn():
            last = None
            for (o, i, npart) in items:
                last = T.transpose(o, i, ident[0:npart, 0:npart])
            return last
        return S.op('pe', fn, reads + ['ident'], writes)

    def acopy(out, in_):
        return A.activation(out=out, in_=in_, func=AF.Copy)

    def act(out, in_, func, reads, writes, bias=None, scale=None):
        kw = {}
        if bias is not None:
            kw['bias'] = bias
        if scale is not None:
            kw['scale'] = scale
        return S.op('act', lambda: A.activation(out=out, in_=in_, func=func, **kw), reads, writes)

    def dma(q, out, in_, key, reads, writes):
        e = {'sp': SP, 'pool': G, 'act': A}[q]
        return S.op(q, lambda: e.dma_start(out=out, in_=in_), reads, writes, dma=key)

    for i, (t, d, k) in enumerate([(g1b, g1b_d, 'g1b'), (b1b, b1b_d, 'b1b'), (g2b, g2b_d, 'g2b'), (b2b, b2b_d, 'b2b'),
                                   (g1f, g1f_d, 'g1f'), (b1f, b1f_d, 'b1f'), (ident, ident_d, 'ident'),
                                   (hv, hv_d, 'hv'), (cw, cw_d, 'cw'), (fcw, fcw_d, 'fcw'), (skt, sk_d, 'skt')]):
        dma('sp', t[:], d, 'G:c', [], [k])
    dma('pool', E_prev[:], eprev_d, 'G:cE', [], ['E_prev'])
    dma('pool', E_cur[:], ecur_d, 'G:cE', [], ['E_cur'])
    dma('pool', esm[:], esm_d, 'G:cE', [], ['esm'])
    dma('pool', bones[:], bones_d, 'G:cE', [], ['bones'])
    S.op('dve', lambda: V.memset(Vx[:, :, :, 64:65], 1.0), [], ['Vx1'])
    S.op('dve', lambda: V.memset(uext[:], 0.0), [], ['uext%d' % j for j in range(4)])
    S.op('dve', lambda: V.memset(RC[:], 0.0), [], ['RC1', 'RC2'])
    S.op('dve', lambda: V.memset(epsb[:], EPS_S), [], ['epsb'])
    S.op('dve', lambda: V.memset(ones_t[:], 1.0), [], ['ones_t'])
    S.op('dve', lambda: V.memset(xsr[:], 0.0), [], ['xsr'])
    S.op('dve', lambda: V.memset(fh[:], 0.0), [], ['fh%d' % f for f in range(NF)])
    S.op('dve', lambda: V.tensor_scalar(out=E_first[:], in0=E_prev[:], scalar1=hv[:, 0:1], scalar2=None,
                                        op0=ALU.mult), ['E_prev', 'hv'], ['E_first'])
    act(esk[:], skt[:], AF.Exp, ['skt'], ['esk'])

    for s, (wn, k0, nk, c0, ncol) in enumerate(SLOTS):
        src = W[wn].rearrange("(k p) n -> p k n", p=128)[:, k0:k0 + nk, c0:c0 + ncol]
        dst = wscr[s].rearrange("p (k c) -> p k c", c=ncol)[:, 0:nk, :]
        dma('pool', dst, src, 'G:cv%d' % (s // 3), [], ['scr%d' % s])

    rstate = {'n': 0}

    def wload(s):
        r = rstate['n'] % RING
        rstate['n'] += 1
        wn, k0, nk, c0, ncol = SLOTS[s]
        dma('sp', ring[:, r, :], wscr[s], 'ring%d' % r, ['scr%d' % s], ['ring%d' % r])
        view = ring[:, r, :].rearrange("p (k c) -> p k c", c=ncol)
        return view, 'ring%d' % r

    deferred = []
    import os
    DBG = os.environ.get('KDEBUG', '') == '1'
    taps = {}

    def tap(name, ap, reads):
        if not DBG or name in taps:
            return
        shp = list(ap.shape)
        d = nc.dram_tensor("dbg_" + name, shp, ap.dtype, kind="ExternalOutput").ap()
        taps[name] = d
        dma('sp', d, ap, 'dbg%d' % len(taps), reads, ['o_dbg_' + name])

    def xload(kind, ci):
        if kind == 'samp':
            dma('sp', xsr[0:NS, :], xs, 'xs', [], ['xsr'])
        elif kind == 'halo':
            dma('sp', xres[1][:, 0:2, :], xp[0:256, :].rearrange("(b p) f -> p b f", p=128), 'x1', [], ['xres1'])
        else:
            par = (ci + 1) % 2
            t0 = ci * 512
            nbl = 4 if ci < 4 else 2
            dma('sp', xres[par][:, 0:nbl, :], xp[t0:t0 + nbl * 128, :].rearrange("(b p) f -> p b f", p=128),
                'x%d' % par, [], ['xres%d' % par])

    def flush_deferred():
        for f in deferred:
            f()
        deferred.clear()

    def emit_p0_main(ci):
        par = (ci + 1) % 2
        nbm = 4 if ci < 4 else 2
        xkey = 'xres%d' % par
        first = True
        for kc in range(8):
            bank = 6 + (kc % 2)
            items = [(ps[bank][:, b * 128:(b + 1) * 128], xres[par][:, b, kc * 128:(kc + 1) * 128], 128)
                     for b in range(nbm)]
            tpg(items, [xkey], [PSK[bank]])
            w = ['xT%d' % kc, 'RB2'] if first else ['xT%d' % kc]
            first = False
            act(RB[:, kc, 0:nbm * 128], ps[bank][:, 0:nbm * 128], AF.Copy, [PSK[bank], 'RB1'], w)

    def chunk(kind, ci, nextload=None, preloaded=False, p0done=False, nextp0=None):
        samp = kind == 'samp'
        halo = kind == 'halo'
        if kind == 'main':
            par = (ci + 1) % 2
            tok0 = ci * 512
            nb = 4 if ci < 4 else 2
            Nx, a0, N, NP = nb * 128, 0, nb * 128, 128
            blk0 = tok0 // 128
            xr = xres[par]
            xkey = 'xres%d' % par
        elif halo:
            par = 1
            tok0 = 0
            Nx, a0, N, nb, NP = 256, 128, 128, 1, 128
            blk0 = 0
            xr = xres[1]
            xkey = 'xres1'
        else:
            Nx, a0, N, nb, NP = NSP, 0, NSP, 1, NSP
            blk0 = None
            xkey = 'xsr'
        nbx = Nx // 128 if not samp else 1
        last_main = (kind == 'main' and ci == 4)

        def xrow(b):
            if samp:
                return xsr[:, :]
            return xr[:, (a0 // 128) + b, :]

        def xrow_all(b):
            if samp:
                return xsr[:, :]
            return xr[:, b, :]

        kp = 's' if samp else ''

        def K(name):
            return kp + name
        if samp:
            RBt, RAt, qTt, oTt, mgt, gcvt, cbst, uextt = RB_s, RA_s, qT_s, oT_s, mg_s, gcv_s, cbs_s, uext_s
        else:
            RBt, RAt, qTt, oTt, mgt, gcvt, cbst, uextt = RB, RA, qT, oT, mg, gcv, cbs, uext
        xT = RBt
        x1T = RBt

        if not preloaded:
            xload(kind, ci)

        first = True
        for kc in (range(8) if not p0done else []):
            bank = 6 + (kc % 2)
            items = [(ps[bank][:, b * 128:b * 128 + NP], xrow_all(b)[:, kc * 128:(kc + 1) * 128], NP)
                     for b in range(nbx)]
            tpg(items, [xkey], [PSK[bank]])
            w = [K('xT%d' % kc), K('RB2')] if first else [K('xT%d' % kc)]
            first = False
            act(xT[:, kc, 0:Nx], ps[bank][:, 0:Nx], AF.Copy, [PSK[bank], K('RB1')], w)

        mmb = {'i': 0}

        def nextbank():
            b = mmb['i'] % 4
            mmb['i'] += 1
            return b

        xTk = [K('xT%d' % kc) for kc in range(8)] + [K('RB1')]
        firstRC = {'v': True}
        firstRA = {'v': True}

        def rcw(keys):
            if firstRC['v']:
                firstRC['v'] = False
                return keys + [K('RC2')]
            return keys

        for i in range(17):
            wv, wk = yield (S_IN + i)
            if i == 3:
                flush_deferred()
            for half in range(2):
                m = 2 * i + half
                lhs = lambda kc, half=half, wv=wv: wv[:, kc, half * 128:(half + 1) * 128]
                if m == 5:
                    for b in range(nbx):
                        mmg(ps[4][0:NP, b * 128:(b + 1) * 128],
                            [(xT[:, kc, b * 128:b * 128 + NP], wv[:, kc, 128:256]) for kc in range(8)],
                            xTk + [wk], [PSK[4]])
                    if samp:
                        S.op('act', lambda: acopy(out=kvs[:, 128:256], in_=ps[4][0:NSP, 0:128]),
                             [PSK[4]], ['kvs_v'])
                        bk = nextbank()
                        mmg(ps[bk][:, 0:N], [(lhs(kc), xT[:, kc, 0:N]) for kc in range(8)], xTk + [wk], [PSK[bk]])
                        act(vTs[:, :], ps[bk][:, 0:NS], AF.Copy, [PSK[bk]], ['vTs'])
                    else:
                        S.op('dve', lambda: V.tensor_copy(
                            out=Vx[:, blk0:blk0 + nbx, :, 0:64],
                            in_=ps[4][:, 0:Nx].rearrange("p (b g d) -> p b g d", b=nbx, g=2)),
                            [PSK[4]], ['Vx%d' % (blk0 + b) for b in range(nbx)])
                        if last_main and 'lmv' not in os.environ.get('KSKIP', ''):
                            S.op('act', lambda: acopy(out=nkv[:, 128:256], in_=ps[4][:, N - 128:N]),
                                 [PSK[4]], ['nkv_v'])
                    continue
                bk = nextbank()
                if m == 4:
                    mmg(ps[bk][:, 0:Nx], [(lhs(kc), xT[:, kc, 0:Nx]) for kc in range(8)], xTk + [wk], [PSK[bk]])
                    if samp:
                        act(kcT_new[:, :], ps[bk][:, 0:NS], AF.Copy, [PSK[bk]], ['kTs'])
                        mmg(ps[4][0:NSP, 0:128], [(xT[:, kc, 0:NSP], wv[:, kc, 0:128]) for kc in range(8)],
                            xTk + [wk], [PSK[4]])
                        S.op('act', lambda: acopy(out=kvs[:, 0:128], in_=ps[4][0:NSP, 0:128]),
                             [PSK[4]], ['kvs_k'])
                    else:
                        act(kT[:, blk0 * 128:blk0 * 128 + Nx], ps[bk][:, 0:Nx], AF.Copy, [PSK[bk]],
                            ['kT%d' % (blk0 + b) for b in range(nbx)])
                        if last_main and 'lmk' not in os.environ.get('KSKIP', ''):
                            mmg(ps[5][:, 0:128], [(xT[:, kc, N - 128:N], wv[:, kc, 0:128]) for kc in range(8)],
                                xTk + [wk], [PSK[5]])
                            S.op('act', lambda: acopy(out=nkv[:, 0:128], in_=ps[5][:, 0:128]),
                                 [PSK[5]], ['nkv_k'])
                    continue
                mmg(ps[bk][:, 0:N], [(lhs(kc), xT[:, kc, a0:a0 + N]) for kc in range(8)], xTk + [wk], [PSK[bk]])
                pin = ps[bk][:, 0:N]
                if m < 4:
                    act(qTt[:, m, 0:N], pin, AF.Copy, [PSK[bk], K('RC1')], rcw([K('qT%d' % m)]), scale=0.125)
                elif m < 10:
                    j = m - 6
                    act(cbst[:, j, 0:N], pin, AF.Copy, [PSK[bk], K('RC1')], [K('cbs%d' % j)])
                elif m < 18:
                    j = (m - 10) // 2
                    if (m - 10) % 2 == 0:
                        act(ccs[j % 2][:, 0:N], pin, AF.Copy, [PSK[bk]], ['ccs%d' % (j % 2)])
                    else:
                        S.op('dve', lambda pin=pin, j=j: V.tensor_tensor(out=uextt[:, j, 2:2 + N], in0=pin,
                                                                        in1=ccs[j % 2][:, 0:N], op=ALU.mult),
                             [PSK[bk], 'ccs%d' % (j % 2)], [K('uext%d' % j)])
                        (conv_branch_samp if samp else conv_branch)(j, N)
                else:
                    t = m - 18
                    w = [K('tg%d' % t)]
                    if firstRA['v']:
                        firstRA['v'] = False
                        w = w + [K('RA2')]
                    act(RAt[:, t, 0:N], pin, AF.Tanh, [PSK[bk], K('RA1')], w, scale=0.5)

        if kind == 'main' and ci == 0:
            tap('xT', xT[:, 0, 0:512], ['xT0'])
            tap('qT', qTt[:, 0, 0:512], ['qT0'])
            tap('kT', kT[:, 256:768], ['kT2', 'kT3', 'kT4', 'kT5'])
            tap('Vx', Vx[:, 2, :, :], ['Vx2'])
            tap('tg', RAt[:, 0, 0:512], ['tg0'])
            tap('gcv', gcvt[:, 0, 0:512], ['gcv0'])
            tap('uext', uextt[:, 0, :], ['uext0'])
        if not samp:
            S.op('pool', lambda: G.tensor_copy(out=uextt[:, :, 0:2], in_=uextt[:, :, N:N + 2]),
                 [K('uext%d' % j) for j in range(4)], [K('uext%d' % j) for j in range(4)])

        if samp:
            attention_samples()
        else:
            for b in range(nb):
                attention_block(blk0 + (a0 // 128) + b, b, first_blk=(blk0 + b == 2))

        if kind == 'main' and ci == 0:
            tap(K('oT'), oTt[:, 0, 0:512], [K('oT')])
        ao = [None, None]
        co = [None, None]
        for m in range(8):
            h, ml = m // 4, m % 4
            if ml == 0:
                ao[h] = yield (S_AO + 2 * h)
                co[h] = yield (S_AO + 2 * h + 1)
            ba, bc = (0, 1) if m % 2 == 0 else (2, 3)
            mmg(ps[ba][:, 0:N], [(ao[h][0][:, c, ml * 128:(ml + 1) * 128], oTt[:, c, 0:N]) for c in range(4)],
                [K('oT'), ao[h][1], K('RC1')], [PSK[ba]])
            mmg(ps[bc][:, 0:N], [(co[h][0][:, c, ml * 128:(ml + 1) * 128], gcvt[:, c, 0:N]) for c in range(4)],
                [K('gcv%d' % c) for c in range(4)] + [co[h][1], K('RC1')], [PSK[bc]])
            i1, i2 = 0, 1
            S.op('dve', lambda m=m, ba=ba, i1=i1: V.scalar_tensor_tensor(
                out=m12[i1][:, 0:N], in0=RAt[:, m, 0:N], scalar=1.0, in1=ps[ba][:, 0:N],
                op0=ALU.add, op1=ALU.mult), [PSK[ba], K('tg%d' % m), K('RA1')], ['m12_%d' % i1])
            S.op('dve', lambda m=m, bc=bc, i2=i2: V.scalar_tensor_tensor(
                out=m12[i2][:, 0:N], in0=RAt[:, 8 + m, 0:N], scalar=1.0, in1=ps[bc][:, 0:N],
                op0=ALU.add, op1=ALU.mult), [PSK[bc], K('tg%d' % (8 + m)), K('RA1')], ['m12_%d' % i2])
            S.op('pool', lambda m=m, i1=i1, i2=i2: G.tensor_tensor(out=mgt[:, m, 0:N], in0=m12[i1][:, 0:N],
                                                                  in1=m12[i2][:, 0:N], op=ALU.add),
                 ['m12_%d' % i1, 'm12_%d' % i2, K('RC1')], [K('mg%d' % m)])

        if kind == 'main' and ci == 0:
            tap('mg', mgt[:, 0, 0:512], ['mg0'])
        mgk = [K('mg%d' % m) for m in range(8)]
        mb = 0
        wm = []
        for si in range(4):
            wm.append((yield (S_MIX + si)))
        for b in range(nb):
            for n in range(2):
                bank = 4 + (mb % 4)
                mb += 1
                mmg(ps[bank][0:NP, :],
                    [(mgt[:, kc, b * 128:b * 128 + NP], wm[2 * n + kc // 4][0][:, kc % 4, :]) for kc in range(8)],
                    mgk + [wm[2 * n][1], wm[2 * n + 1][1], K('RC1')], [PSK[bank]])
                xs_ = xrow(b)[0:NP, n * 512:(n + 1) * 512]
                S.op('dve', lambda bank=bank, xs_=xs_: V.scalar_tensor_tensor(
                    out=xs_, in0=ps[bank][0:NP, :], scalar=0.5 / ALPHA, in1=xs_, op0=ALU.mult, op1=ALU.add),
                    [PSK[bank], xkey], [xkey + 'h%d_%d' % (b, n)])
            layer_norm(xrow(b), NP, b, [xkey + 'h%d_%d' % (b, n) for n in range(2)], xkey + 'n%d' % b)

        first = True
        for kc in range(8):
            bank = kc % 2
            items = [(ps[bank][:, b * 128:b * 128 + NP], xrow(b)[0:NP, kc * 128:(kc + 1) * 128], NP)
                     for b in range(nb)]
            tpg(items, [xkey + 'n%d' % b for b in range(nb)], [PSK[bank]])
            w = [K('x1T%d' % kc), K('RB1')] if first else [K('x1T%d' % kc)]
            first = False
            act(x1T[:, kc, 0:N], ps[bank][:, 0:N], AF.Identity, [PSK[bank], K('RB2'), 'g1f', 'b1f'], w,
                bias=b1f[:, kc:kc + 1], scale=g1f[:, kc:kc + 1])
        if not halo:
            for b in range(nb):
                r = xrow(b)[0:NP, :]
                S.op('pool', lambda r=r: G.tensor_tensor(out=r, in0=r, in1=g1b[0:NP, :], op=ALU.mult),
                     [xkey + 'n%d' % b, 'g1b'], [xkey + 'n%d' % b])
                S.op('pool', lambda r=r: G.tensor_tensor(out=r, in0=r, in1=b1b[0:NP, :], op=ALU.add),
                     [xkey + 'n%d' % b, 'b1b'], [xkey + 'n%d' % b])

        if kind == 'main' and ci == 0:
            tap('x1', xr[:, 0, :], [xkey + 'n0'])
            tap('x1T', x1T[:, 0, 0:512], ['x1T0'])
        x1k = [K('x1T%d' % kc) for kc in range(8)]
        firstG = True
        firstH = True
        if samp:
            S.op('pool', lambda: G.tensor_copy(out=gpas[:, :, :, 0:2],
                                               in_=sfT[:, :, :].rearrange("p f (n t) -> p f n t", t=2)),
                 ['sfT0', 'sfT1', 'sfT2', 'sfT3', K('RC2')], [K('gpa%d' % f) for f in range(NF)] + [K('RC1')])
            firstG = False
        elif not halo:
            S.op('pool', lambda: G.tensor_copy(out=gpa[:, :, 0:2], in_=fh[:, :, :]),
                 ['fh%d' % f for f in range(NF)] + [K('RC2')], [K('gpa%d' % f) for f in range(NF)] + [K('RC1')])
            firstG = False
        if nextload is not None:
            nextload()
        for j in range(11):
            wg = yield (S_FFN + 2 * j)
            wu = None
            if not halo:
                wu = yield (S_FFN + 2 * j + 1)
            for fl in range(2):
                f = 2 * j + fl
                bg, bu = (0, 1) if f % 2 == 0 else (2, 3)
                mmg(ps[bg][:, 0:N], [(wg[0][:, kc, fl * 128:(fl + 1) * 128], x1T[:, kc, 0:N]) for kc in range(8)],
                    x1k + [wg[1], K('RB2')], [PSK[bg]])
                w = [K('gpa%d' % f)]
                if firstG:
                    firstG = False
                    w = w + [K('RC1')]
                if halo:
                    S.op('dve', lambda f=f, bg=bg: V.tensor_scalar(
                        out=fh[:, f, :], in0=ps[bg][:, N - 2:N], scalar1=hv[:, 0:1], scalar2=None,
                        op0=ALU.mult), [PSK[bg], 'hv'], ['fh%d' % f])
                    continue
                if samp:
                    S.op('act', lambda f=f, bg=bg: A.activation(out=gpas[:, f, :, 2], in_=ps[bg][:, 0:NS], func=AF.Copy),
                         [PSK[bg], K('RC2')], w)
                    S.op('act', lambda f=f, bg=bg: acopy(out=gsT[:, f, :], in_=ps[bg][:, 0:NSP]),
                         [PSK[bg]], ['gsT%d' % f])
                else:
                    act(gpa[:, f, 2:2 + N], ps[bg][:, 0:N], AF.Copy, [PSK[bg], K('RC2')], w)
                    if kind == 'main' and ci == 0:
                        S.op('dve', lambda f=f: V.tensor_scalar(
                            out=gpa[:, f, 256:258], in0=gpa[:, f, 256:258], scalar1=hv[:, 0:1], scalar2=None,
                            op0=ALU.mult), [K('gpa%d' % f), 'hv', K('RC2')], [K('gpa%d' % f)])
                    if last_main and 'lmg' not in os.environ.get('KSKIP', ''):
                        act(gl2[:, f, :], ps[bg][:, N - 32:N], AF.Copy, [PSK[bg]], ['gl2_%d' % f])
                if halo:
                    continue
                mmg(ps[bu][:, 0:N], [(wu[0][:, kc, fl * 128:(fl + 1) * 128], x1T[:, kc, 0:N]) for kc in range(8)],
                    x1k + [wu[1], K('RB2')], [PSK[bu]])
                ci2 = f % 2
                if samp:
                    g0, g1_, g2_ = gpas[:, f, :, 0], gpas[:, f, :, 1], gpas[:, f, :, 2]
                else:
                    g0, g1_, g2_ = gpa[:, f, 0:N], gpa[:, f, 1:N + 1], gpa[:, f, 2:N + 2]
                NN = NS if samp else N
                c_ = ct[ci2][:, 0:NN]
                rk = [K('gpa%d' % f), 'fcw', K('RC2')]
                S.op('dve', lambda f=f, c_=c_, g0=g0: V.tensor_scalar(
                    out=c_, in0=g0, scalar1=fcw[:, 3 * f:3 * f + 1], scalar2=None, op0=ALU.mult), rk, ['ct%d' % ci2])
                S.op('dve', lambda f=f, c_=c_, g1_=g1_: V.scalar_tensor_tensor(
                    out=c_, in0=g1_, scalar=fcw[:, 3 * f + 1:3 * f + 2], in1=c_, op0=ALU.mult, op1=ALU.add),
                    rk + ['ct%d' % ci2], ['ct%d' % ci2])
                S.op('dve', lambda f=f, c_=c_, g2_=g2_: V.scalar_tensor_tensor(
                    out=c_, in0=g2_, scalar=fcw[:, 3 * f + 2:3 * f + 3], in1=c_, op0=ALU.mult, op1=ALU.add),
                    rk + ['ct%d' % ci2], ['ct%d' % ci2])
                act(gl[ci2][:, 0:NN], c_, AF.Gelu, ['ct%d' % ci2], ['gl%d' % ci2])
                w = [K('hT%d' % f)]
                if firstH:
                    firstH = False
                    w = w + [K('RA1')]
                S.op('dve', lambda f=f, bu=bu, ci2=ci2, NN=NN: V.tensor_tensor(
                    out=RAt[:, f, 0:NN], in0=ps[bu][:, 0:NN], in1=gl[ci2][:, 0:NN], op=ALU.mult),
                    [PSK[bu], 'gl%d' % ci2, K('RA2')], w)
        if halo:
            return
        if not samp:
            S.op('pool', lambda: G.tensor_copy(out=fh[:, :, :], in_=gpa[:, :, N:N + 2]),
                 [K('gpa%d' % f) for f in range(NF)] + [K('RC2')], ['fh%d' % f for f in range(NF)])

        if kind == 'main' and ci == 0:
            tap('hT', RAt[:, 0, 0:512], ['hT0'])
        hk = [K('hT%d' % f) for f in range(NF)]
        for n in range(2):
            for fg in range(6):
                wd = yield (S_DOWN + n * 6 + fg)
                nk = 4 if fg < 5 else 2
                for b in range(nb):
                    bank = 0 if samp else 4 + b

                    def fn(b=b, bank=bank, fg=fg, nk=nk, wd=wd):
                        last = None
                        for kk in range(nk):
                            f = fg * 4 + kk
                            last = T.matmul(ps[bank][0:NP, :], RAt[:, f, b * 128:b * 128 + NP], wd[0][:, kk, :],
                                            start=(f == 0), stop=(f == NF - 1))
                        return last
                    S.op('pe', fn, hk + [wd[1], K('RA2')], [PSK[bank]])
            for b in range(nb):
                bank = 0 if samp else 4 + b
                xs_ = xrow(b)[0:NP, n * 512:(n + 1) * 512]
                S.op('dve', lambda bank=bank, xs_=xs_: V.scalar_tensor_tensor(
                    out=xs_, in0=ps[bank][0:NP, :], scalar=1.0 / ALPHA, in1=xs_, op0=ALU.mult, op1=ALU.add),
                    [PSK[bank], xkey + 'n%d' % b], [xkey + 'z%d_%d' % (b, n)])
        if nextp0 is not None:
            nextp0()
        for b in range(nb):
            layer_norm(xrow(b), NP, 4 + b, [xkey + 'z%d_%d' % (b, n) for n in range(2)], xkey + 'y%d' % b)
            r = xrow(b)[0:NP, :]
            S.op('pool', lambda r=r: G.tensor_tensor(out=r, in0=r, in1=g2b[0:NP, :], op=ALU.mult),
                 [xkey + 'y%d' % b, 'g2b'], [xkey + 'y%d' % b])
            S.op('pool', lambda r=r: G.tensor_tensor(out=r, in0=r, in1=b2b[0:NP, :], op=ALU.add),
                 [xkey + 'y%d' % b, 'b2b'], [xkey + 'y%d' % b])

        if samp:
            dma('sp', ys, xsr[0:NS, :], 'G:out', ['xsry0'], ['o_ys'])
        else:
            def store(ci=ci, par=par, xr=xr, xkey=xkey, nb=nb):
                b0 = 2 if ci == 0 else 0
                r0 = 0 if ci == 0 else 256 + (ci - 1) * 512
                nr = (nb - b0) * 128
                dma('sp', yp[r0:r0 + nr, :].rearrange("(b p) f -> p b f", p=128), xr[:, b0:nb, :],
                    'oy%d' % par, [xkey + 'y%d' % b for b in range(nb)], [xkey, 'o_yp%d' % ci])
            deferred.append(store)
        if last_main and 'final' not in os.environ.get('KSKIP', ''):
            final_prompt_outputs(N)
        if samp:
            final_sample_outputs()

    def layer_norm(r, NP, slot, rkeys, wkey):
        st = stt[slot % 2]
        S.op('dve', lambda: V.bn_stats(out=st[0:NP, 0, :], in_=r[0:NP, 0:512]), rkeys, ['stt%d' % (slot % 2)])
        S.op('dve', lambda: V.bn_stats(out=st[0:NP, 1, :], in_=r[0:NP, 512:1024]), rkeys, ['stt%d' % (slot % 2)])
        S.op('dve', lambda: V.bn_aggr(out=mv[0:NP, slot, :], in_=st[0:NP, :, :]), ['stt%d' % (slot % 2)], ['mv%d' % slot])
        act(lnv[0:NP, slot:slot + 1], mv[0:NP, slot, 1:2], AF.Ln, ['mv%d' % slot, 'epsb'], ['lnv%d' % slot],
            bias=epsb[0:NP, :], scale=1.0)
        act(rstd[0:NP, slot:slot + 1], lnv[0:NP, slot:slot + 1], AF.Exp, ['lnv%d' % slot], ['rstd%d' % slot], scale=-0.5)
        S.op('dve', lambda: V.tensor_scalar(out=r[0:NP, :], in0=r[0:NP, :], scalar1=mv[0:NP, slot, 0:1],
                                            scalar2=rstd[0:NP, slot:slot + 1], op0=ALU.subtract, op1=ALU.mult),
             rkeys + ['mv%d' % slot, 'rstd%d' % slot], [wkey])

    def conv_branch(j, N, samp_views=None):
        c_ = cvt[j % 2][:, 0:N]
        if samp_views is None:
            u0, u1, u2 = uext[:, j, 0:N], uext[:, j, 1:N + 1], uext[:, j, 2:N + 2]
            rk = ['uext%d' % j, 'cw']
        else:
            u0, u1, u2 = samp_views
            rk = ['uext%d' % j, 'cw', 'scT']
        ck_ = 'cvt%d' % (j % 2)
        S.op('dve', lambda: V.tensor_scalar(out=c_, in0=u0, scalar1=cw[:, 3 * j:3 * j + 1], scalar2=None,
                                             op0=ALU.mult), rk, [ck_])
        S.op('dve', lambda: V.scalar_tensor_tensor(out=c_, in0=u1, scalar=cw[:, 3 * j + 1:3 * j + 2], in1=c_,
                                                    op0=ALU.mult, op1=ALU.add), rk + [ck_], [ck_])
        S.op('dve', lambda: V.scalar_tensor_tensor(out=c_, in0=u2, scalar=cw[:, 3 * j + 2:3 * j + 3], in1=c_,
                                                    op0=ALU.mult, op1=ALU.add), rk + [ck_], [ck_])
        S.op('pool', lambda: G.tensor_tensor(out=gcv[:, j, 0:N], in0=c_, in1=cbs[:, j, 0:N], op=ALU.mult),
             [ck_, 'cbs%d' % j, 'RC1'], ['gcv%d' % j])

    pstate = {'praw': 0, 'pT': 0, 'on': 0}

    def attention_block(gb, b, first_blk):
        pts = {}
        sbank = 0
        for g in range(2):
            for kb, kblk in enumerate((max(gb - 1, 0), gb)):
                bank = (2 * g + kb) % 4
                mmg(ps[bank][:, :],
                    [(kT[64 * g:64 * g + 64, kblk * 128:(kblk + 1) * 128], qT[64 * g:64 * g + 64, :, b * 128:(b + 1) * 128])],
                    ['kT%d' % kblk] + ['qT%d' % c for c in range(4)] + ['RC1'], [PSK[bank]])
                pr = pstate['praw'] % 2
                pstate['praw'] += 1
                act(praw[pr][:], ps[bank][:, :], AF.Exp, [PSK[bank]], ['praw%d' % pr])
                pi = pstate['pT'] % 8
                pstate['pT'] += 1
                if kb == 0:
                    E = E_first if first_blk else E_prev
                    ek = 'E_first' if first_blk else 'E_prev'
                else:
                    E, ek = E_cur, 'E_cur'
                S.op('dve', lambda pi=pi, pr=pr, E=E, g=g: V.tensor_tensor(
                    out=pT[pi][:], in0=praw[pr][:], in1=E[:, g * 512:(g + 1) * 512], op=ALU.mult),
                    ['praw%d' % pr, ek], ['pT%d' % pi])
                pts[(g, kb)] = pi
                if gb == 2:
                    tap('praw_%d%d' % (g, kb), praw[pr][:], ['praw%d' % pr])
                    tap('pT_%d%d' % (g, kb), pT[pi][:], ['pT%d' % pi])
        for g in range(2):
            bank = 4 + g
            pairs_r = ['pT%d' % pts[(g, 0)], 'pT%d' % pts[(g, 1)], 'Vx%d' % max(gb - 1, 0), 'Vx%d' % gb, 'Vx1']

            def fn(g=g, bank=bank):
                last = None
                for c in range(4):
                    for kb, kblk in enumerate((max(gb - 1, 0), gb)):
                        last = T.matmul(ps[bank][:, c * 65:(c + 1) * 65],
                                        pT[pts[(g, kb)]][:, c * 128:(c + 1) * 128],
                                        Vx[:, kblk, g, :], start=(kb == 0), stop=(kb == 1))
                return last
            S.op('pe', fn, pairs_r, [PSK[bank]])
        ri = pstate['on'] % 2
        pstate['on'] += 1
        for g in range(2):
            bank = 4 + g
            pv = ps[bank][:, 0:260].rearrange("p (c e) -> p c e", c=4)
            S.op('dve', lambda g=g, pv=pv, ri=ri: V.tensor_tensor(
                out=rdt[ri][:, 4 * g:4 * g + 4], in0=pv[:, :, 64], in1=esk[:, 4 * g:4 * g + 4], op=ALU.add),
                [PSK[bank], 'esk'], ['rdt%d_%d' % (ri, g)])
            S.op('dve', lambda g=g, ri=ri: V.reciprocal(out=rdt[ri][:, 4 * g:4 * g + 4], in_=rdt[ri][:, 4 * g:4 * g + 4]),
                 ['rdt%d_%d' % (ri, g)], ['rdt%d_%d' % (ri, g)])
            S.op('dve', lambda g=g, pv=pv, ri=ri: V.tensor_tensor(
                out=onb[ri][:, g * 256:(g + 1) * 256].rearrange("p (c d) -> p c d", c=4),
                in0=pv[:, :, 0:64],
                in1=rdt[ri][:, 4 * g:4 * g + 4].unsqueeze(2).to_broadcast([128, 4, 64]), op=ALU.mult),
                [PSK[bank], 'rdt%d_%d' % (ri, g)], ['on%d_%d' % (ri, g)])
        if gb == 2:
            tap('rdt', rdt[ri][:], ['rdt%d_0' % ri, 'rdt%d_1' % ri])
            tap('onb', onb[ri][:], ['on%d_0' % ri, 'on%d_1' % ri])
        tb = 6 + (gb % 2)
        items = [(ps[tb][:, c * 128:(c + 1) * 128], onb[ri][:, c * 128:(c + 1) * 128], 128) for c in range(4)]
        tpg(items, ['on%d_0' % ri, 'on%d_1' % ri], [PSK[tb]])
        act(oT[:, :, b * 128:(b + 1) * 128], ps[tb][:, :].rearrange("p (c q) -> p c q", c=4), AF.Copy,
            [PSK[tb], 'RC1'], ['oT'])

    gpas_t = sb("gpas", [128, NF * NS * 3], BF16)
    gpas = gpas_t[:, :].rearrange("p (f n t) -> p f n t", f=NF, t=3)
    RB_s = sb("RB_s", [128, 8, NSP], BF16)
    RA_s = sb("RA_s", [128, NF, NSP], BF16)
    qT_s = sb("qT_s", [128, 4, NSP], BF16)
    oT_s = sb("oT_s", [128, 4, NSP], BF16)
    mg_s = sb("mg_s", [128, 8, NSP], BF16)
    gcv_s = sb("gcv_s", [128, 4, NSP], BF16)
    cbs_s = sb("cbs_s", [128, 4, NSP], BF16)
    uext_s = sb("uext_s", [128, 4, NSP + 2], F32)
    stg = sb("stg", [NSP, 768], F32)
    gsT = sb("gsT", [128, NF, NSP], F32)
    kcT_new = sb("kcT_new", [128, NS], BF16)
    usT = sb("usT", [128, 4, NS, 3], F32)

    def attention_samples():
        N = NS
        dma('sp', ckc, ck.rearrange("n s f -> s n f"), 'sck', [], ['ckc', 'xres0'])
        dma('pool', cvb, cv.rearrange("n s f -> s n f"), 'scv', ['xres0'], ['cvb'])
        dma('sp', nks_o[:, 0:127, :], ck[:, 1:128, :], 'G:out', [], ['o_nks'])
        dma('sp', nvs_o[:, 0:127, :], cv[:, 1:128, :], 'G:out', [], ['o_nvs'])
        for n4 in range(4):
            bank = n4 % 2
            items = [(ps[bank][:, i * 128:(i + 1) * 128], ckc[:, n4 * 4 + i, :], 128) for i in range(4)]
            tpg(items, ['ckc', 'xres0'], [PSK[bank]])
            act(kcT[:, n4 * 4:n4 * 4 + 4, :], ps[bank][:, :].rearrange("p (n s) -> p n s", n=4), AF.Copy,
                [PSK[bank], 'ckc', 'xres0'], ['kcT%d' % n4])
        def fn():
            last = None
            for n in range(NS):
                for g in range(2):
                    last = T.matmul(ps[2][:, n * 8 + g * 4:n * 8 + g * 4 + 4],
                                    kcT[64 * g:64 * g + 64, n, :], qT_s[64 * g:64 * g + 64, :, n],
                                    start=True, stop=True)
            return last
        S.op('pe', fn, ['xres0'] + ['kcT%d' % i for i in range(4)] + ['sqT%d' % c for c in range(4)] + ['sRC1'], [PSK[2]])
        act(praws[:], ps[2][:, 0:128], AF.Exp, [PSK[2]], ['praws'])
        S.op('dve', lambda: V.tensor_tensor(out=pTs[:], in0=praws[:], in1=esm[:], op=ALU.mult),
             ['praws', 'esm'], ['pTs'])
        S.op('dve', lambda: V.tensor_tensor(out=prod[:], in0=qT_s[:, :, 0:NS],
                                            in1=kcT_new[:, :].unsqueeze(1).to_broadcast([128, 4, NS]), op=ALU.mult),
             ['sqT%d' % c for c in range(4)] + ['kTs', 'sRC1'], ['prod'])
        mmg(ps[3][:, 0:64], [(bones[:, :], prod[:].rearrange("p c n -> p (c n)"))], ['bones', 'prod'], [PSK[3]])
        act(pnb[:].rearrange("p c n -> p (c n)"), ps[3][:, 0:64], AF.Exp, [PSK[3]], ['pnb'])
        def fn2():
            last = None
            for n in range(NS):
                for g in range(2):
                    last = T.matmul(ps[0][64 * g:64 * g + 64, n * 4:n * 4 + 4], cvb[:, n, 64 * g:64 * g + 64],
                                    pTs[:, n * 8 + g * 4:n * 8 + g * 4 + 4], start=True, stop=True)
            return last
        S.op('pe', fn2, ['cvb', 'pTs', 'xres0'], [PSK[0]])
        ones64 = ones_t[:, :]

        def fn3():
            last = None
            for g in range(2):
                last = T.matmul(ps[1][64 * g:64 * g + 64, 0:64], ones64,
                                pTs[:].rearrange("p (n g c) -> p n g c", g=2, c=4)[:, :, g, :], start=True, stop=True)
            return last
        S.op('pe', fn3, ['ones_t', 'pTs'], [PSK[1]])
        S.op('dve', lambda: V.tensor_tensor(out=ons[:], in0=pnb[:], in1=vTs[:, :].unsqueeze(1).to_broadcast([128, 4, NS]),
                                            op=ALU.mult), ['pnb', 'vTs'], ['ons'])
        S.op('dve', lambda: V.tensor_tensor(out=ons[:], in0=ons[:],
                                            in1=ps[0][:, 0:64].rearrange("p (n c) -> p c n", c=4), op=ALU.add),
             ['ons', PSK[0]], ['ons'])
        S.op('dve', lambda: V.tensor_tensor(out=dns[:], in0=pnb[:],
                                            in1=ps[1][:, 0:64].rearrange("p (n c) -> p c n", c=4), op=ALU.add),
             ['pnb', PSK[1]], ['dns'])
        for g in range(2):
            sl = slice(64 * g, 64 * g + 64)
            S.op('dve', lambda g=g, sl=sl: V.tensor_tensor(
                out=dns[sl], in0=dns[sl], in1=esk[sl, 4 * g:4 * g + 4].unsqueeze(2).to_broadcast([64, 4, NS]), op=ALU.add),
                ['dns', 'esk'], ['dns'])
        S.op('dve', lambda: V.reciprocal(out=dns[:], in_=dns[:]), ['dns'], ['dns'])
        S.op('dve', lambda: V.tensor_tensor(out=ons[:], in0=ons[:], in1=dns[:], op=ALU.mult), ['ons', 'dns'], ['ons'])
        for h in range(8):
            g, c = h // 4, h % 4
            S.op('dve', lambda h=h, g=g, c=c: V.tensor_copy(out=oT_s[64 * (h % 2):64 * (h % 2) + 64, h // 2, 0:NS],
                                                           in_=ons[64 * g:64 * g + 64, c, :]),
                 ['ons', 'sRC1'], ['soT'])

    def final_prompt_outputs(N):
        dma('sp', nk_o, nkv[:, 0:128], 'G:out', ['nkv_k'], ['o_nk'])
        dma('sp', nv_o, nkv[:, 128:256], 'G:out', ['nkv_v'], ['o_nv'])
        items = [(ps[0][0:32, j * 128:(j + 1) * 128], uext[:, j, N - 30:N + 2], 128) for j in range(4)]
        tpg(items, ['uext%d' % j for j in range(4)], [PSK[0]])
        S.op('act', lambda: acopy(out=ncf[:, 0:512], in_=ps[0][0:32, :]), [PSK[0]], ['ncf_c', 'xres0'])
        dma('sp', ncv_o, ncf[30:32, 0:512], 'G:ncf', ['ncf_c', 'xres0'], ['o_ncv'])
        for grp in range(6):
            f0 = grp * 4
            nf = min(4, NF - f0)
            bank = 1 + (grp % 2)
            items = [(ps[bank][0:32, i * 128:(i + 1) * 128], gl2[:, f0 + i, :], 128) for i in range(nf)]
            tpg(items, ['gl2_%d' % (f0 + i) for i in range(nf)], [PSK[bank]])
            S.op('act', lambda f0=f0, nf=nf, bank=bank: acopy(
                out=ncf[:, 512 + f0 * 128:512 + (f0 + nf) * 128], in_=ps[bank][0:32, 0:nf * 128]),
                [PSK[bank], 'xres0'], ['ncf_f%d' % grp])
        dma('sp', nfc_o, ncf[30:32, 512:512 + DFF], 'G:ncf', ['ncf_f%d' % g for g in range(6)] + ['xres0'], ['o_nfc'])

    def final_sample_outputs():
        dma('sp', nks_o[:, 127, :], kvs[0:NS, 0:128], 'G:out', ['kvs_k'], ['o_nks2'])
        dma('sp', nvs_o[:, 127, :], kvs[0:NS, 128:256], 'G:out', ['kvs_v'], ['o_nvs2'])
        dma('sp', ncs_o[:, 0, :], sc.rearrange("(n t) c -> n t c", t=2)[:, 1, :], 'G:out', [], ['o_ncs0'])
        dma('sp', nfs_o[:, 0, :], sf.rearrange("(n t) c -> n t c", t=2)[:, 1, :], 'G:out', [], ['o_nfs0'])
        items = [(ps[0][0:NSP, j * 128:(j + 1) * 128], uext_s[:, j, 2:2 + NSP], 128) for j in range(4)]
        tpg(items, ['suext%d' % j for j in range(4)], [PSK[0]])
        S.op('act', lambda: acopy(out=stg[:, 0:512], in_=ps[0][0:NSP, :]), [PSK[0]], ['stg'])
        dma('sp', ncs_o[:, 1, :], stg[0:NS, 0:512], 'stg_o', ['stg'], ['o_ncs1'])
        for pi, (f0, nf) in enumerate(OPIECES):
            bank = 1 + (pi % 2)
            items = [(ps[bank][0:NSP, i * 128:(i + 1) * 128], gsT[:, f0 + i, :], 128) for i in range(nf)]
            tpg(items, ['gsT%d' % (f0 + i) for i in range(nf)], [PSK[bank]])
            S.op('act', lambda nf=nf, bank=bank: acopy(out=stg[:, 0:nf * 128], in_=ps[bank][0:NSP, 0:nf * 128]),
                 [PSK[bank]], ['stg'])
            dma('sp', nfs_o[:, 1, f0 * 128:(f0 + nf) * 128], stg[0:NS, 0:nf * 128], 'stg_o', ['stg'], ['o_nfs1_%d' % pi])


    PIECES = [(0, 6), (6, 6), (12, 6), (18, 4)]
    OPIECES = [(0, 4), (4, 4), (8, 4), (12, 4), (16, 4), (20, 2)]

    def samp_prologue():
        dma('sp', stg[:, 0:512], sc, 'stg_i', [], ['stg'])
        items = [(ps[0][:, j * 32:(j + 1) * 32], stg[:, j * 128:(j + 1) * 128], 2 * NS) for j in range(4)]
        tpg(items, ['stg'], [PSK[0]])
        S.op('act', lambda: acopy(out=scT[:], in_=ps[0][:, 0:128].rearrange("p (j m) -> p j m", j=4)),
             [PSK[0]], ['scT'])
        for pi, (f0, nf) in enumerate(PIECES):
            dma('sp', stg[:, 0:nf * 128], sf[:, f0 * 128:(f0 + nf) * 128], 'stg_i', [], ['stg'])
            bank = 1 + (pi % 2)
            items = [(ps[bank][:, i * 32:(i + 1) * 32], stg[:, i * 128:(i + 1) * 128], 2 * NS) for i in range(nf)]
            tpg(items, ['stg'], [PSK[bank]])
            S.op('act', lambda f0=f0, nf=nf, bank=bank: acopy(
                out=sfT[:, f0:f0 + nf, :], in_=ps[bank][:, 0:nf * 32].rearrange("p (j m) -> p j m", j=nf)),
                [PSK[bank]], ['sfT%d' % pi])

    def run_pass(gens):
        pend = []
        for g in gens:
            try:
                pend.append([g, next(g)])
            except StopIteration:
                pass
        while pend:
            sl = min(p[1] for p in pend)
            view = wload(sl)
            nxt = []
            for p in pend:
                if p[1] == sl:
                    try:
                        p[1] = p[0].send(view)
                        nxt.append(p)
                    except StopIteration:
                        pass
                else:
                    nxt.append(p)
            pend = nxt

    import os
    LIM = int(os.environ.get('KSTAGE', '99'))
    def conv_branch_samp(j, N):
        S.op('pool', lambda: G.tensor_copy(out=usT[:, j, :, 2], in_=uext_s[:, j, 2:2 + NS]), ['suext%d' % j], ['usT%d' % j])
        S.op('pool', lambda: G.tensor_copy(out=usT[:, j, :, 0:2],
                                           in_=scT[:, j, :].rearrange("p (n t) -> p n t", t=2)),
             ['scT'], ['usT%d' % j])
        c_ = cvt[j % 2][:, 0:NS]
        ck_ = 'cvt%d' % (j % 2)
        rk = ['usT%d' % j, 'cw']
        S.op('dve', lambda: V.tensor_scalar(out=c_, in0=usT[:, j, :, 0], scalar1=cw[:, 3 * j:3 * j + 1], scalar2=None,
                                             op0=ALU.mult), rk, [ck_])
        S.op('dve', lambda: V.scalar_tensor_tensor(out=c_, in0=usT[:, j, :, 1], scalar=cw[:, 3 * j + 1:3 * j + 2],
                                                    in1=c_, op0=ALU.mult, op1=ALU.add), rk + [ck_], [ck_])
        S.op('dve', lambda: V.scalar_tensor_tensor(out=c_, in0=usT[:, j, :, 2], scalar=cw[:, 3 * j + 2:3 * j + 3],
                                                    in1=c_, op0=ALU.mult, op1=ALU.add), rk + [ck_], [ck_])
        S.op('pool', lambda: G.tensor_tensor(out=gcv_s[:, j, 0:NS], in0=c_, in1=cbs_s[:, j, 0:NS], op=ALU.mult),
             [ck_, 'scbs%d' % j, 'sRC1'], ['sgcv%d' % j])

    NCH = min(5, max(0, LIM - 1)) if LIM < 99 else 5
    for ci in range(NCH):
        nl = (lambda ci=ci: xload('main', ci + 1)) if ci < 4 else None
        np0 = (lambda ci=ci: emit_p0_main(ci + 1)) if ci < 4 else None
        gens = [chunk('main', ci, nextload=nl, preloaded=(ci > 0), p0done=(ci > 0), nextp0=np0)]
        if ci == 0 and LIM >= 7:
            xload('samp', 0)
            samp_prologue()
            gens.append(chunk('samp', 0, preloaded=True))
        run_pass(gens)
    flush_deferred()

    S.op('sp', None, ['o_ys', 'o_nk', 'o_nv', 'o_ncv', 'o_nfc', 'o_nks', 'o_nvs', 'o_nks2', 'o_nvs2',
                      'o_ncs0', 'o_nfs0', 'o_ncs1', 'o_nfs1_0', 'o_nfs1_1', 'o_nfs1_2', 'o_nfs1_3', 'o_nfs1_4', 'o_nfs1_5', 'o_yp0', 'o_yp1', 'o_yp2', 'o_yp3', 'o_yp4'], [])
    S.finalize(es)
    es.close()
    return nc, None


_PROG = {}


def _consts():
    slopes = 2.0 ** (-8.0 * np.arange(1, 9) / 8.0)
    s = np.arange(128)[:, None]
    q = np.arange(128)[None, :]
    eprev = np.zeros((128, 2, 4, 128), np.float32)
    ecur = np.zeros((128, 2, 4, 128), np.float32)
    esm = np.zeros((128, NS, 2, 4), np.float32)
    for g in range(2):
        for c in range(4):
            sl = slopes[4 * g + c]
            dprev = q + 128 - s
            dcur = q - s
            eprev[:, g, c, :] = np.where(s >= q, np.exp(-sl * dprev), 0.0)
            ecur[:, g, c, :] = np.where(q >= s, np.exp(-sl * dcur), 0.0)
            esm[:, :, g, c] = np.exp(-sl * (128 - np.arange(128)))[:, None]
    bones = (np.arange(128)[:, None] // 64 == np.arange(128)[None, :] // 64).astype(np.float32)
    return dict(eprev=eprev.reshape(128, 1024), ecur=ecur.reshape(128, 1024), esm=esm.reshape(128, 128),
                bones=bones, ident=np.eye(128, dtype=np.float32))


def kernel(x_prompt, x_sample, cache_k, cache_v, state_conv, state_ffn_conv, w_in, conv_w, attn_sinks,
           w_attn_out, w_conv_out, w_mix_out, ln1_g, ln1_b, w_gate, w_up, ffn_conv_w, w_down, ln2_g, ln2_b):
    f32 = np.float32
    A_ = lambda a: np.ascontiguousarray(np.asarray(a, dtype=f32))
    x_prompt = A_(x_prompt); x_sample = A_(x_sample)
    if 'nc' not in _PROG:
        _PROG['nc'], _PROG['es'] = build_program()
    nc = _PROG['nc']
    cols = []
    for c in range(4):
        cols += list(range(c * 64, c * 64 + 64)) + list(range((4 + c) * 64, (4 + c) * 64 + 64))
    cols += list(range(512, 768))
    cols += list(range(768, 1280))
    for j in range(4):
        cols += list(range(1280 + j * 128, 1280 + (j + 1) * 128)) + list(range(1792 + j * 128, 1792 + (j + 1) * 128))
    cols += list(range(2304, 4352))
    w_in_p = A_(A_(w_in)[0][:, cols])
    bc = lambda v: A_(np.broadcast_to(A_(v).reshape(1, -1), (128, A_(v).size)))
    fm = lambda v, n: A_(A_(v).reshape(n, 128).T)
    common = dict(
        w_in=w_in_p, w_ao=A_(w_attn_out)[0], w_co=A_(w_conv_out)[0], w_mix=A_(w_mix_out)[0],
        w_gate=A_(w_gate)[0], w_up=A_(w_up)[0], w_down=A_(w_down)[0],
        cw=A_(A_(conv_w)[0].reshape(3, 4, 128).transpose(2, 1, 0).reshape(128, 12)),
        fcw=A_(A_(ffn_conv_w)[0].reshape(3, NF, 128).transpose(2, 1, 0).reshape(128, 66)),
        sk=bc(A_(attn_sinks)[0]),
        g1b=bc(ln1_g[0]), b1b=bc(ln1_b[0]), g2b=bc(ln2_g[0]), b2b=bc(ln2_b[0]),
        g1f=fm(ln1_g[0], 8), b1f=fm(ln1_b[0], 8),
    )
    common.update(_consts())
    ck = A_(cache_k)[0].reshape(128, 128, 128)
    cv = A_(cache_v)[0].reshape(128, 128, 128)
    sc = A_(state_conv)[0].reshape(256, 512)
    sf = A_(state_ffn_conv)[0].reshape(256, DFF)
    in_maps = []
    for c in range(NCORES):
        b, r = c // 4, c % 4
        s0 = r * TOK_CORE
        xp = np.zeros((NTOK, D), f32)
        if r > 0:
            xp[0:HALO] = x_prompt[b, s0 - HALO:s0]
        xp[HALO:] = x_prompt[b, s0:s0 + TOK_CORE]
        m = dict(common)
        m.update(xp=xp, xs=A_(x_sample[c * NS:(c + 1) * NS, 0, :]),
                 ck=A_(ck[c * NS:(c + 1) * NS]), cv=A_(cv[c * NS:(c + 1) * NS]),
                 sc=A_(sc[c * 2 * NS:(c + 1) * 2 * NS]), sf=A_(sf[c * 2 * NS:(c + 1) * 2 * NS]),
                 hv=np.full((128, 1), 0.0 if r == 0 else 1.0, f32))
        in_maps.append(m)
    res = run_bass_kernel_spmd(nc, in_maps, core_ids=list(range(NCORES)))
    R = res.results
    y_prompt = np.zeros((2, 8192, D), f32)
    y_sample = np.zeros((128, 1, D), f32)
    nk_p = np.zeros((1, 2, 128, 2, 64), f32); nv_p = np.zeros((1, 2, 128, 2, 64), f32)
    nc_p = np.zeros((1, 2, 2, 512), f32); nf_p = np.zeros((1, 2, 2, DFF), f32)
    nk_s = np.zeros((1, 128, 128, 2, 64), f32); nv_s = np.zeros((1, 128, 128, 2, 64), f32)
    nc_s = np.zeros((1, 128, 2, 512), f32); nf_s = np.zeros((1, 128, 2, DFF), f32)
    for c in range(NCORES):
        b, r = c // 4, c % 4
        y_prompt[b, r * TOK_CORE:(r + 1) * TOK_CORE] = R[c]["yp"]
        y_sample[c * NS:(c + 1) * NS, 0] = R[c]["ys"]
        if r == 3:
            nk_p[0, b] = R[c]["nk"].reshape(128, 2, 64)
            nv_p[0, b] = R[c]["nv"].reshape(128, 2, 64)
            nc_p[0, b] = R[c]["ncv"]
            nf_p[0, b] = R[c]["nfc"]
        nk_s[0, c * NS:(c + 1) * NS] = R[c]["nks"].reshape(NS, 128, 2, 64)
        nv_s[0, c * NS:(c + 1) * NS] = R[c]["nvs"].reshape(NS, 128, 2, 64)
        nc_s[0, c * NS:(c + 1) * NS] = R[c]["ncs"]
        nf_s[0, c * NS:(c + 1) * NS] = R[c]["nfs"]
    return (y_prompt, y_sample, nk_p, nv_p, nc_p, nf_p, nk_s, nv_s, nc_s, nf_s)
```

```python
import contextlib
import numpy as np
import concourse.bass as bass
import concourse.mybir as mybir
from concourse.bass_utils import run_bass_kernel_spmd

F32 = mybir.dt.float32
BF16 = mybir.dt.bfloat16
AF = mybir.ActivationFunctionType
ALU = mybir.AluOpType

D = 1024
NPROJ = 4352
DFF = 2816
NF = 22
ALPHA = 2.0 ** 0.25
EPS = 1e-5
EPS_S = EPS / (ALPHA * ALPHA)
NCORES = 8
TOK_CORE = 2048
HALO = 256
NTOK = TOK_CORE + HALO
NBLK = NTOK // 128
NS = 16
NSP = 32
RING = 6


class Sched:
    def __init__(self, nc):
        self.nc = nc
        self.ops = []
        self.last_w = {}
        self.readers = {}

    def op(self, eng, fn, reads=(), writes=(), dma=None):
        deps = set()
        for k in reads:
            if k in self.last_w:
                deps.add(self.last_w[k])
            if k.startswith('ps'):
                deps.update(r for r in self.readers.get(k, ()) if self.ops[r]['eng'] != eng)
        for k in writes:
            if k in self.last_w:
                deps.add(self.last_w[k])
            deps.update(self.readers.get(k, ()))
        oid = len(self.ops)
        self.ops.append(dict(eng=eng, fn=fn, deps=deps, dma=dma, signal=False, waits=[]))
        for k in reads:
            self.readers.setdefault(k, []).append(oid)
        for k in writes:
            self.last_w[k] = oid
            self.readers[k] = []
        return oid

    def finalize(self, stack):
        nc = self.nc
        engs = {'pe': nc.tensor, 'act': nc.scalar, 'dve': nc.vector, 'pool': nc.gpsimd, 'sp': nc.sync}
        ops = self.ops
        cnt = {}
        for o in ops:
            st = ('dma:' + o['dma']) if o['dma'] else o['eng']
            o['stream'] = st
            cnt[st] = cnt.get(st, 0) + 1
            o['seq'] = cnt[st]
        for o in ops:
            if o['dma'] and o['dma'].startswith('G:'):
                o['seq'] = cnt[o['stream']]
        clk = {e: {} for e in engs}
        eng_seq = {e: 0 for e in engs}
        for o in ops:
            e = o['eng']
            c = clk[e]
            myseq = o['seq'] if not o['dma'] else None
            need = {}
            for d in o['deps']:
                od = ops[d]
                st = od['stream']
                if st == e and not o['dma']:
                    if e == 'pe':
                        continue
                    if o['seq'] - od['seq'] > 2:
                        continue
                    if c.get('self_' + e, 0) >= od['seq']:
                        continue
                elif c.get(st, 0) >= od['seq']:
                    continue
                if st not in need or ops[need[st]]['seq'] < od['seq']:
                    need[st] = d
            for st, d in need.items():
                od = ops[d]
                if st == e and not o['dma']:
                    c['self_' + e] = od['seq']
                elif c.get(st, 0) >= od['seq']:
                    continue
                od['signal'] = True
                o['waits'].append(d)
                for k, v in od['vc'].items():
                    if c.get(k, 0) < v:
                        c[k] = v
            if not o['dma']:
                c[e] = o['seq']
                o['vc'] = dict(c)
            else:
                vc = dict(c)
                vc[o['stream']] = o['seq']
                o['vc'] = vc
        sems = {}
        val = {}
        for o in ops:
            st = o['stream']
            if o['dma']:
                val[st] = val.get(st, 0) + 16
                o['val'] = val[st]
            elif o['signal']:
                val[st] = val.get(st, 0) + 1
                o['val'] = val[st]
        for o in ops:
            if o['dma'] and o['dma'].startswith('G:'):
                o['val'] = val[o['stream']]
        for st in val:
            sems[st] = stack.enter_context(nc.semaphore('s_' + st.replace(':', '_')))
        self.nsem = len(sems)
        for o in ops:
            e = engs[o['eng']]
            for d in o['waits']:
                od = ops[d]
                e.wait_ge(sems[od['stream']], od['val'])
            if o['fn'] is None:
                continue
            ins = o['fn']()
            if o['dma']:
                ins.then_inc(sems[o['stream']], 16)
            elif o['signal']:
                ins.then_inc(sems[o['stream']], 1)


def slot_table():
    t = []
    for i in range(17):
        t.append(('w_in', 0, 8, i * 256, 256))
    for h in range(2):
        t.append(('w_ao', 0, 4, h * 512, 512))
    for h in range(2):
        t.append(('w_co', 0, 4, h * 512, 512))
    for n in range(2):
        for kh in range(2):
            t.append(('w_mix', kh * 4, 4, n * 512, 512))
    for j in range(11):
        t.append(('w_gate', 0, 8, j * 256, 256))
        t.append(('w_up', 0, 8, j * 256, 256))
    for n in range(2):
        for fg in range(6):
            nk = 4 if fg < 5 else 2
            t.append(('w_down', fg * 4, nk, n * 512, 512))
    return t


SLOTS = slot_table()
S_IN, S_AO, S_CO, S_MIX, S_FFN, S_DOWN = 0, 17, 19, 21, 25, 47
WSHAPES = {'w_in': (D, NPROJ), 'w_ao': (512, D), 'w_co': (512, D), 'w_mix': (D, D),
           'w_gate': (D, DFF), 'w_up': (D, DFF), 'w_down': (DFF, D)}


def build_program():
    nc = bass.Bass("TRN2", target_bir_lowering=False)
    S = Sched(nc)
    es = contextlib.ExitStack()

    def din(name, shape, dt=F32):
        return nc.dram_tensor(name, list(shape), dt, kind="ExternalInput").ap()

    def dout(name, shape, dt=F32):
        return nc.dram_tensor(name, list(shape), dt, kind="ExternalOutput").ap()

    xp = din("xp", [NTOK, D])
    xs = din("xs", [NS, D])
    ck = din("ck", [NS, 128, 128])
    cv = din("cv", [NS, 128, 128])
    sc = din("sc", [2 * NS, 512])
    sf = din("sf", [2 * NS, DFF])
    W = {k: din(k, v) for k, v in WSHAPES.items()}
    cw_d = din("cw", [128, 12])
    fcw_d = din("fcw", [128, 66])
    sk_d = din("sk", [128, 8])
    g1b_d = din("g1b", [128, D]); b1b_d = din("b1b", [128, D])
    g2b_d = din("g2b", [128, D]); b2b_d = din("b2b", [128, D])
    g1f_d = din("g1f", [128, 8]); b1f_d = din("b1f", [128, 8])
    ident_d = din("ident", [128, 128])
    eprev_d = din("eprev", [128, 1024]); ecur_d = din("ecur", [128, 1024])
    esm_d = din("esm", [128, 128])
    bones_d = din("bones", [128, 128])
    hv_d = din("hv", [128, 1])

    yp = dout("yp", [TOK_CORE, D])
    ys = dout("ys", [NS, D])
    nk_o = dout("nk", [128, 128]); nv_o = dout("nv", [128, 128])
    ncv_o = dout("ncv", [2, 512]); nfc_o = dout("nfc", [2, DFF])
    nks_o = dout("nks", [NS, 128, 128]); nvs_o = dout("nvs", [NS, 128, 128])
    ncs_o = dout("ncs", [NS, 2, 512]); nfs_o = dout("nfs", [NS, 2, DFF])
    wscr = nc.dram_tensor("wscr", [len(SLOTS), 128, 2048], BF16, kind="Internal").ap()

    def sb(name, shape, dt):
        return es.enter_context(nc.sbuf_tensor("s_" + name, list(shape), dt))

    xres = [sb("xres%d" % i, [128, 4, D], F32) for i in range(2)]
    RB = sb("RB", [128, 8, 512], BF16)
    RA = sb("RA", [128, NF, 512], BF16)
    RC = sb("RC", [128, 12288], BF16)
    qT = RC[:, 0:2048].rearrange("p (c n) -> p c n", c=4)
    oT = RC[:, 2048:4096].rearrange("p (c n) -> p c n", c=4)
    mg = RC[:, 4096:8192].rearrange("p (c n) -> p c n", c=8)
    gcv = RC[:, 8192:10240].rearrange("p (c n) -> p c n", c=4)
    cbs = RC[:, 10240:12288].rearrange("p (c n) -> p c n", c=4)
    gpa = RC[:, 0:NF * 514].rearrange("p (f n) -> p f n", f=NF)
    kT = sb("kT", [128, NBLK * 128], BF16)
    Vx = sb("Vx", [128, NBLK, 2, 65], BF16)
    ccs = [sb("ccs%d" % i, [128, 512], F32) for i in range(2)]
    uext = sb("uext", [128, 4, 514], F32)
    cvt = [sb("cvt%d" % i, [128, 512], F32) for i in range(2)]
    praw = [sb("praw%d" % i, [128, 512], BF16) for i in range(2)]
    pT = [sb("pT%d" % i, [128, 512], BF16) for i in range(8)]
    onb = [sb("on%d" % i, [128, 512], F32) for i in range(2)]
    rdt = [sb("rdt%d" % i, [128, 8], F32) for i in range(2)]
    m12 = [sb("m12_%d" % i, [128, 512], BF16) for i in range(2)]
    ct = [sb("ct%d" % i, [128, 512], F32) for i in range(2)]
    gl = [sb("gl%d" % i, [128, 512], BF16) for i in range(2)]
    ring = sb("ring", [128, RING, 2048], BF16)
    g1b = sb("g1b", [128, D], F32); b1b = sb("b1b", [128, D], F32)
    g2b = sb("g2b", [128, D], F32); b2b = sb("b2b", [128, D], F32)
    g1f = sb("g1f", [128, 8], F32); b1f = sb("b1f", [128, 8], F32)
    ident = sb("ident", [128, 128], F32)
    E_prev = sb("E_prev", [128, 1024], BF16)
    E_cur = sb("E_cur", [128, 1024], BF16)
    E_first = sb("E_first", [128, 1024], BF16)
    esm = sb("esm", [128, 128], BF16)
    bones = sb("bones", [128, 128], BF16)
    hv = sb("hv", [128, 1], F32)
    cw = sb("cw", [128, 12], F32)
    fcw = sb("fcw", [128, 66], F32)
    skt = sb("skt", [128, 8], F32)
    esk = sb("esk", [128, 8], F32)
    epsb = sb("epsb", [128, 1], F32)
    stt = [sb("stt%d" % i, [128, 2, 6], F32) for i in range(2)]
    mv = sb("mv", [128, 8, 2], F32)
    lnv = sb("lnv", [128, 8], F32)
    rstd = sb("rstd", [128, 8], F32)
    nkv = sb("nkv", [128, 256], F32)
    gl2 = sb("gl2", [128, NF, 32], F32)
    fh = sb("fh", [128, NF, 2], BF16)
    xsr = sb("xsr", [NSP, D], F32)
    scT = sb("scT", [128, 4, 2 * NS], F32)
    sfT = sb("sfT", [128, NF, 2 * NS], F32)
    vTs = sb("vTs", [128, NS], F32)
    prod = sb("prod", [128, 4, NS], BF16)
    pnb = sb("pnb", [128, 4, NS], F32)
    ons = sb("ons", [128, 4, NS], F32)
    dns = sb("dns", [128, 4, NS], F32)
    pTs = sb("pTs", [128, NS * 8], BF16)
    praws = sb("praws", [128, NS * 8], BF16)
    kvs = sb("kvs", [NSP, 256], F32)
    x0f = xres[0][:, :, :].rearrange("p b f -> p (b f)")
    x1f = xres[1][:, :, :].rearrange("p b f -> p (b f)")
    ncf = x0f[0:32, 0:512 + DFF]
    ones_t = sb("ones_t", [128, 64], BF16)
    ckc = xres[0][:, :, :].rearrange("p b f -> p (b f)")[:, 0:2048].rearrange("p (n f) -> p n f", n=NS)
    cvb = xres[0][:, :, :].rearrange("p b f -> p (b f)")[:, 2048:3072].bitcast(BF16).rearrange("p (n f) -> p n f", n=NS)
    kcT = xres[0][:, :, :].rearrange("p b f -> p (b f)")[:, 3072:4096].bitcast(BF16).rearrange("p (n f) -> p n f", n=NS)

    ps = [es.enter_context(nc.psum_tensor("ps%d" % i, [128, 512], F32)) for i in range(8)]
    PSK = ["ps%d" % i for i in range(8)]

    T, V, A, G, SP = nc.tensor, nc.vector, nc.scalar, nc.gpsimd, nc.sync

    def mmg(out_ap, pairs, reads, writes):
        def fn():
            last = None
            n = len(pairs)
            for i, (l, r) in enumerate(pairs):
                last = T.matmul(out_ap, l, r, start=(i == 0), stop=(i == n - 1))
            return last
        return S.op('pe', fn, reads, writes)

    def tpg(items, reads, writes):
        def fn():
            last = None
            for (o, i, npart) in items:
                last = T.transpose(o, i, ident[0:npart, 0:npart])
            return last
        return S.op('pe', fn, reads + ['ident'], writes)

    def acopy(out, in_):
        return A.activation(out=out, in_=in_, func=AF.Copy)

    def act(out, in_, func, reads, writes, bias=None, scale=None):
        kw = {}
        if bias is not None:
            kw['bias'] = bias
        if scale is not None:
            kw['scale'] = scale
        return S.op('act', lambda: A.activation(out=out, in_=in_, func=func, **kw), reads, writes)

    def dma(q, out, in_, key, reads, writes):
        e = {'sp': SP, 'pool': G, 'act': A}[q]
        return S.op(q, lambda: e.dma_start(out=out, in_=in_), reads, writes, dma=key)

    for i, (t, d, k) in enumerate([(g1b, g1b_d, 'g1b'), (b1b, b1b_d, 'b1b'), (g2b, g2b_d, 'g2b'), (b2b, b2b_d, 'b2b'),
                                   (g1f, g1f_d, 'g1f'), (b1f, b1f_d, 'b1f'), (ident, ident_d, 'ident'),
                                   (hv, hv_d, 'hv'), (cw, cw_d, 'cw'), (fcw, fcw_d, 'fcw'), (skt, sk_d, 'skt')]):
        dma('sp', t[:], d, 'G:c', [], [k])
    dma('pool', E_prev[:], eprev_d, 'G:cE', [], ['E_prev'])
    dma('pool', E_cur[:], ecur_d, 'G:cE', [], ['E_cur'])
    dma('pool', esm[:], esm_d, 'G:cE', [], ['esm'])
    dma('pool', bones[:], bones_d, 'G:cE', [], ['bones'])
    S.op('dve', lambda: V.memset(Vx[:, :, :, 64:65], 1.0), [], ['Vx1'])
    S.op('dve', lambda: V.memset(uext[:], 0.0), [], ['uext%d' % j for j in range(4)])
    S.op('dve', lambda: V.memset(RC[:], 0.0), [], ['RC1', 'RC2'])
    S.op('dve', lambda: V.memset(epsb[:], EPS_S), [], ['epsb'])
    S.op('dve', lambda: V.memset(ones_t[:], 1.0), [], ['ones_t'])
    S.op('dve', lambda: V.memset(xsr[:], 0.0), [], ['xsr'])
    S.op('dve', lambda: V.memset(oT_s[:], 0.0), [], ['soT'])
    S.op('dve', lambda: V.memset(gcv_s[:], 0.0), [], ['sgcv%d' % j for j in range(4)])
    S.op('dve', lambda: V.memset(RA_s[:], 0.0), [], ['sRA1', 'sRA2'])
    S.op('dve', lambda: V.memset(uext_s[:], 0.0), [], ['suext%d' % j for j in range(4)])
    S.op('dve', lambda: V.tensor_scalar(out=E_first[:], in0=E_prev[:], scalar1=hv[:, 0:1], scalar2=None,
                                        op0=ALU.mult), ['E_prev', 'hv'], ['E_first'])
    act(esk[:], skt[:], AF.Exp, ['skt'], ['esk'])

    for s, (wn, k0, nk, c0, ncol) in enumerate(SLOTS):
        src = W[wn].rearrange("(k p) n -> p k n", p=128)[:, k0:k0 + nk, c0:c0 + ncol]
        dst = wscr[s].rearrange("p (k c) -> p k c", c=ncol)[:, 0:nk, :]
        dma('pool', dst, src, 'G:cv%d' % (s // 3), [], ['scr%d' % s])

    rstate = {'n': 0}

    def wload(s):
        r = rstate['n'] % RING
        rstate['n'] += 1
        wn, k0, nk, c0, ncol = SLOTS[s]
        nel = nk * ncol
        dma('sp', ring[:, r, 0:nel], wscr[s][:, 0:nel], 'ring%d' % r, ['scr%d' % s], ['ring%d' % r])
        view = ring[:, r, :].rearrange("p (k c) -> p k c", c=ncol)
        return view, 'ring%d' % r

    deferred = []
    import os
    DBG = os.environ.get('KDEBUG', '') == '1'
    taps = {}

    def tap(name, ap, reads):
        if not DBG or name in taps:
            return
        shp = list(ap.shape)
        d = nc.dram_tensor("dbg_" + name, shp, ap.dtype, kind="ExternalOutput").ap()
        taps[name] = d
        dma('sp', d, ap, 'dbg%d' % len(taps), reads, ['o_dbg_' + name])

    def xload(kind, ci):
        if kind == 'samp':
            dma('sp', xsr[0:NS, :], xs, 'xs', [], ['xsr'])
        elif kind == 'halo':
            dma('sp', xres[1][:, 0:2, :], xp[0:256, :].rearrange("(b p) f -> p b f", p=128), 'x1', [], ['xres1'])
        else:
            par = ci % 2
            t0 = HALO + ci * 512
            dma('sp', xres[par][:, 0:4, :], xp[t0:t0 + 512, :].rearrange("(b p) f -> p b f", p=128),
                'x%d' % par, [], ['xres%d' % par])

    def flush_deferred():
        for f in deferred:
            f()
        deferred.clear()

    def emit_p0_main(ci):
        par = ci % 2
        xk = 'xres%d' % par
        first = True
        for kc in range(8):
            bank = 6 + (kc % 2)
            items = [(ps[bank][:, b * 128:(b + 1) * 128], xres[par][:, b, kc * 128:(kc + 1) * 128], 128)
                     for b in range(4)]
            tpg(items, [xk], [PSK[bank]])
            w = ['xT%d' % kc, 'RB2'] if first else ['xT%d' % kc]
            first = False
            act(RB[:, kc, 0:512], ps[bank][:, 0:512], AF.Copy, [PSK[bank], 'RB1'], w)

    def chunk(kind, ci, nextload=None, preloaded=False, p0done=False, nextp0=None):
        samp = kind == 'samp'
        halo = kind == 'halo'
        if kind == 'main':
            par = ci % 2
            tok0 = HALO + ci * 512
            Nx, a0, N, nb, NP = 512, 0, 512, 4, 128
            blk0 = tok0 // 128
            xr = xres[par]
            xkey = 'xres%d' % par
        elif halo:
            par = 1
            tok0 = 0
            Nx, a0, N, nb, NP = 256, 128, 128, 1, 128
            blk0 = 0
            xr = xres[1]
            xkey = 'xres1'
        else:
            Nx, a0, N, nb, NP = NSP, 0, NSP, 1, NSP
            blk0 = None
            xkey = 'xsr'
        nbx = Nx // 128 if not samp else 1
        last_main = (kind == 'main' and ci == 3) and 'lastmain' not in os.environ.get('KSKIP', '')

        def xrow(b):
            if samp:
                return xsr[:, :]
            return xr[:, (a0 // 128) + b, :]

        def xrow_all(b):
            if samp:
                return xsr[:, :]
            return xr[:, b, :]

        kp = 's' if samp else ''

        def K(name):
            return kp + name
        if samp:
            RBt, RAt, qTt, oTt, mgt, gcvt, cbst, uextt = RB_s, RA_s, qT_s, oT_s, mg_s, gcv_s, cbs_s, uext_s
        else:
            RBt, RAt, qTt, oTt, mgt, gcvt, cbst, uextt = RB, RA, qT, oT, mg, gcv, cbs, uext
        xT = RBt
        x1T = RBt

        if not preloaded:
            xload(kind, ci)

        first = True
        for kc in (range(8) if not p0done else []):
            bank = 6 + (kc % 2)
            items = [(ps[bank][:, b * 128:b * 128 + NP], xrow_all(b)[:, kc * 128:(kc + 1) * 128], NP)
                     for b in range(nbx)]
            tpg(items, [xkey], [PSK[bank]])
            w = [K('xT%d' % kc), K('RB2')] if first else [K('xT%d' % kc)]
            first = False
            act(xT[:, kc, 0:Nx], ps[bank][:, 0:Nx], AF.Copy, [PSK[bank], K('RB1')], w)

        mmb = {'i': 0}

        def nextbank():
            b = mmb['i'] % 4
            mmb['i'] += 1
            return b

        xTk = [K('xT%d' % kc) for kc in range(8)] + [K('RB1')]
        firstRC = {'v': True}
        firstRA = {'v': True}

        def rcw(keys):
            if firstRC['v']:
                firstRC['v'] = False
                return keys + [K('RC2')]
            return keys

        for i in range(17):
            wv, wk = yield (S_IN + i)
            if i == 3:
                flush_deferred()
            for half in range(2):
                m = 2 * i + half
                lhs = lambda kc, half=half, wv=wv: wv[:, kc, half * 128:(half + 1) * 128]
                if m == 5:
                    for b in range(nbx):
                        mmg(ps[4][0:NP, b * 128:(b + 1) * 128],
                            [(xT[:, kc, b * 128:b * 128 + NP], wv[:, kc, 128:256]) for kc in range(8)],
                            xTk + [wk], [PSK[4]])
                    if samp:
                        S.op('act', lambda: acopy(out=kvs[:, 128:256], in_=ps[4][0:NSP, 0:128]),
                             [PSK[4]], ['kvs_v'])
                        bk = nextbank()
                        mmg(ps[bk][:, 0:N], [(lhs(kc), xT[:, kc, 0:N]) for kc in range(8)], xTk + [wk], [PSK[bk]])
                        act(vTs[:, :], ps[bk][:, 0:NS], AF.Copy, [PSK[bk]], ['vTs'])
                    else:
                        S.op('dve', lambda: V.tensor_copy(
                            out=Vx[:, blk0:blk0 + nbx, :, 0:64],
                            in_=ps[4][:, 0:Nx].rearrange("p (b g d) -> p b g d", b=nbx, g=2)),
                            [PSK[4]], ['Vx%d' % (blk0 + b) for b in range(nbx)])
                        if last_main and 'lmv' not in os.environ.get('KSKIP', ''):
                            S.op('act', lambda: acopy(out=nkv[:, 128:256], in_=ps[4][:, 384:512]),
                                 [PSK[4]], ['nkv_v'])
                    continue
                bk = nextbank()
                if m == 4:
                    mmg(ps[bk][:, 0:Nx], [(lhs(kc), xT[:, kc, 0:Nx]) for kc in range(8)], xTk + [wk], [PSK[bk]])
                    if samp:
                        act(kcT_new[:, :], ps[bk][:, 0:NS], AF.Copy, [PSK[bk]], ['kTs'])
                        mmg(ps[4][0:NSP, 0:128], [(xT[:, kc, 0:NSP], wv[:, kc, 0:128]) for kc in range(8)],
                            xTk + [wk], [PSK[4]])
                        S.op('act', lambda: acopy(out=kvs[:, 0:128], in_=ps[4][0:NSP, 0:128]),
                             [PSK[4]], ['kvs_k'])
                    else:
                        act(kT[:, blk0 * 128:blk0 * 128 + Nx], ps[bk][:, 0:Nx], AF.Copy, [PSK[bk]],
                            ['kT%d' % (blk0 + b) for b in range(nbx)])
                        if last_main and 'lmk' not in os.environ.get('KSKIP', ''):
                            mmg(ps[5][:, 0:128], [(xT[:, kc, 384:512], wv[:, kc, 0:128]) for kc in range(8)],
                                xTk + [wk], [PSK[5]])
                            S.op('act', lambda: acopy(out=nkv[:, 0:128], in_=ps[5][:, 0:128]),
                                 [PSK[5]], ['nkv_k'])
                    continue
                mmg(ps[bk][:, 0:N], [(lhs(kc), xT[:, kc, a0:a0 + N]) for kc in range(8)], xTk + [wk], [PSK[bk]])
                pin = ps[bk][:, 0:N]
                if m < 4:
                    act(qTt[:, m, 0:N], pin, AF.Copy, [PSK[bk], K('RC1')], rcw([K('qT%d' % m)]), scale=0.125)
                elif m < 10:
                    j = m - 6
                    act(cbst[:, j, 0:N], pin, AF.Copy, [PSK[bk], K('RC1')], [K('cbs%d' % j)])
                elif m < 18:
                    j = (m - 10) // 2
                    if (m - 10) % 2 == 0:
                        act(ccs[j % 2][:, 0:N], pin, AF.Copy, [PSK[bk]], ['ccs%d' % (j % 2)])
                    else:
                        S.op('dve', lambda pin=pin, j=j: V.tensor_tensor(out=uextt[:, j, 2:2 + N], in0=pin,
                                                                        in1=ccs[j % 2][:, 0:N], op=ALU.mult),
                             [PSK[bk], 'ccs%d' % (j % 2)], [K('uext%d' % j)])
                        (conv_branch_samp if samp else conv_branch)(j, N)
                else:
                    t = m - 18
                    w = [K('tg%d' % t)]
                    if firstRA['v']:
                        firstRA['v'] = False
                        w = w + [K('RA2')]
                    act(RAt[:, t, 0:N], pin, AF.Tanh, [PSK[bk], K('RA1')], w, scale=0.5)

        if kind == 'main' and ci == 0:
            tap('xT', xT[:, 0, 0:512], ['xT0'])
            tap('qT', qTt[:, 0, 0:512], ['qT0'])
            tap('kT', kT[:, 256:768], ['kT2', 'kT3', 'kT4', 'kT5'])
            tap('Vx', Vx[:, 2, :, :], ['Vx2'])
            tap('tg', RAt[:, 0, 0:512], ['tg0'])
            tap('gcv', gcvt[:, 0, 0:512], ['gcv0'])
            tap('uext', uextt[:, 0, :], ['uext0'])
        if not samp:
            S.op('pool', lambda: G.tensor_copy(out=uextt[:, :, 0:2], in_=uextt[:, :, N:N + 2]),
                 [K('uext%d' % j) for j in range(4)], [K('uext%d' % j) for j in range(4)])

        if samp:
            attention_samples()
        else:
            for b in range(nb):
                attention_block(blk0 + (a0 // 128) + b, b, first_blk=(kind == 'main' and ci == 0 and b == 0))

        if kind == 'main' and ci == 0:
            tap(K('oT'), oTt[:, 0, 0:512], [K('oT')])
        ao = []
        co = []
        for h in range(2):
            ao.append((yield (S_AO + h)))
        for h in range(2):
            co.append((yield (S_CO + h)))
        for m in range(8):
            h, ml = m // 4, m % 4
            ba, bc = (0, 1) if m % 2 == 0 else (2, 3)
            mmg(ps[ba][:, 0:N], [(ao[h][0][:, c, ml * 128:(ml + 1) * 128], oTt[:, c, 0:N]) for c in range(4)],
                [K('oT'), ao[h][1], K('RC1')], [PSK[ba]])
            mmg(ps[bc][:, 0:N], [(co[h][0][:, c, ml * 128:(ml + 1) * 128], gcvt[:, c, 0:N]) for c in range(4)],
                [K('gcv%d' % c) for c in range(4)] + [co[h][1], K('RC1')], [PSK[bc]])
            i1, i2 = 0, 1
            S.op('dve', lambda m=m, ba=ba, i1=i1: V.scalar_tensor_tensor(
                out=m12[i1][:, 0:N], in0=RAt[:, m, 0:N], scalar=1.0, in1=ps[ba][:, 0:N],
                op0=ALU.add, op1=ALU.mult), [PSK[ba], K('tg%d' % m), K('RA1')], ['m12_%d' % i1])
            S.op('dve', lambda m=m, bc=bc, i2=i2: V.scalar_tensor_tensor(
                out=m12[i2][:, 0:N], in0=RAt[:, 8 + m, 0:N], scalar=1.0, in1=ps[bc][:, 0:N],
                op0=ALU.add, op1=ALU.mult), [PSK[bc], K('tg%d' % (8 + m)), K('RA1')], ['m12_%d' % i2])
            S.op('pool', lambda m=m, i1=i1, i2=i2: G.tensor_tensor(out=mgt[:, m, 0:N], in0=m12[i1][:, 0:N],
                                                                  in1=m12[i2][:, 0:N], op=ALU.add),
                 ['m12_%d' % i1, 'm12_%d' % i2, K('RC1')], [K('mg%d' % m)])

        if kind == 'main' and ci == 0:
            tap('mg', mgt[:, 0, 0:512], ['mg0'])
        mgk = [K('mg%d' % m) for m in range(8)]
        mb = 0
        wm = []
        for si in range(4):
            wm.append((yield (S_MIX + si)))
        for b in range(nb):
            for n in range(2):
                bank = 4 + (mb % 4)
                mb += 1
                mmg(ps[bank][0:NP, :],
                    [(mgt[:, kc, b * 128:b * 128 + NP], wm[2 * n + kc // 4][0][:, kc % 4, :]) for kc in range(8)],
                    mgk + [wm[2 * n][1], wm[2 * n + 1][1], K('RC1')], [PSK[bank]])
                xs_ = xrow(b)[0:NP, n * 512:(n + 1) * 512]
                S.op('dve', lambda bank=bank, xs_=xs_: V.scalar_tensor_tensor(
                    out=xs_, in0=ps[bank][0:NP, :], scalar=0.5 / ALPHA, in1=xs_, op0=ALU.mult, op1=ALU.add),
                    [PSK[bank], xkey], [xkey + 'h%d_%d' % (b, n)])
            layer_norm(xrow(b), NP, b, [xkey + 'h%d_%d' % (b, n) for n in range(2)], xkey + 'n%d' % b)

        first = True
        for kc in range(8):
            bank = kc % 2
            items = [(ps[bank][:, b * 128:b * 128 + NP], xrow(b)[0:NP, kc * 128:(kc + 1) * 128], NP)
                     for b in range(nb)]
            tpg(items, [xkey + 'n%d' % b for b in range(nb)], [PSK[bank]])
            w = [K('x1T%d' % kc), K('RB1')] if first else [K('x1T%d' % kc)]
            first = False
            act(x1T[:, kc, 0:N], ps[bank][:, 0:N], AF.Identity, [PSK[bank], K('RB2'), 'g1f', 'b1f'], w,
                bias=b1f[:, kc:kc + 1], scale=g1f[:, kc:kc + 1])

        if kind == 'main' and ci == 0:
            tap('x1', xr[:, 0, :], [xkey + 'n0'])
            tap('x1T', x1T[:, 0, 0:512], ['x1T0'])
        x1k = [K('x1T%d' % kc) for kc in range(8)]
        firstG = True
        firstH = True
        if samp:
            S.op('pool', lambda: G.tensor_copy(out=gpas[:, :, :, 0:2],
                                               in_=sfT[:, :, :].rearrange("p f (n t) -> p f n t", t=2)),
                 ['sfT0', 'sfT1', 'sfT2', 'sfT3', K('RC2')], [K('gpa%d' % f) for f in range(NF)] + [K('RC1')])
            firstG = False
        elif not halo:
            S.op('dve', lambda: V.tensor_copy(out=gpa[:, :, 0:2], in_=fh[:, :, :]),
                 ['fh%d' % f for f in range(NF)] + [K('RC2')], [K('gpa%d' % f) for f in range(NF)] + [K('RC1')])
            firstG = False
        if nextload is not None:
            nextload()
        for j in range(11):
            wg = yield (S_FFN + 2 * j)
            wu = None
            if not halo:
                wu = yield (S_FFN + 2 * j + 1)
            for fl in range(2):
                f = 2 * j + fl
                bg, bu = (0, 1) if f % 2 == 0 else (2, 3)
                mmg(ps[bg][:, 0:N], [(wg[0][:, kc, fl * 128:(fl + 1) * 128], x1T[:, kc, 0:N]) for kc in range(8)],
                    x1k + [wg[1], K('RB2')], [PSK[bg]])
                w = [K('gpa%d' % f)]
                if firstG:
                    firstG = False
                    w = w + [K('RC1')]
                if halo:
                    S.op('dve', lambda f=f, bg=bg: V.tensor_scalar(
                        out=fh[:, f, :], in0=ps[bg][:, N - 2:N], scalar1=hv[:, 0:1], scalar2=None,
                        op0=ALU.mult), [PSK[bg], 'hv'], ['fh%d' % f])
                    continue
                if samp:
                    S.op('act', lambda f=f, bg=bg: A.activation(out=gpas[:, f, :, 2], in_=ps[bg][:, 0:NS], func=AF.Copy),
                         [PSK[bg], K('RC2')], w)
                    S.op('act', lambda f=f, bg=bg: acopy(out=gsT[:, f, :], in_=ps[bg][:, 0:NSP]),
                         [PSK[bg]], ['gsT%d' % f])
                else:
                    act(gpa[:, f, 2:2 + N], ps[bg][:, 0:N], AF.Copy, [PSK[bg], K('RC2')], w)
                    if last_main and 'lmg' not in os.environ.get('KSKIP', ''):
                        act(gl2[:, f, :], ps[bg][:, N - 32:N], AF.Copy, [PSK[bg]], ['gl2_%d' % f])
                if halo:
                    continue
                mmg(ps[bu][:, 0:N], [(wu[0][:, kc, fl * 128:(fl + 1) * 128], x1T[:, kc, 0:N]) for kc in range(8)],
                    x1k + [wu[1], K('RB2')], [PSK[bu]])
                ci2 = f % 2
                if samp:
                    g0, g1_, g2_ = gpas[:, f, :, 0], gpas[:, f, :, 1], gpas[:, f, :, 2]
                else:
                    g0, g1_, g2_ = gpa[:, f, 0:N], gpa[:, f, 1:N + 1], gpa[:, f, 2:N + 2]
                NN = NS if samp else N
                c_ = ct[ci2][:, 0:NN]
                rk = [K('gpa%d' % f), 'fcw', K('RC2')]
                S.op('dve', lambda f=f, c_=c_, g0=g0: V.tensor_scalar(
                    out=c_, in0=g0, scalar1=fcw[:, 3 * f:3 * f + 1], scalar2=None, op0=ALU.mult), rk, ['ct%d' % ci2])
                S.op('dve', lambda f=f, c_=c_, g1_=g1_: V.scalar_tensor_tensor(
                    out=c_, in0=g1_, scalar=fcw[:, 3 * f + 1:3 * f + 2], in1=c_, op0=ALU.mult, op1=ALU.add),
                    rk + ['ct%d' % ci2], ['ct%d' % ci2])
                S.op('dve', lambda f=f, c_=c_, g2_=g2_: V.scalar_tensor_tensor(
                    out=c_, in0=g2_, scalar=fcw[:, 3 * f + 2:3 * f + 3], in1=c_, op0=ALU.mult, op1=ALU.add),
                    rk + ['ct%d' % ci2], ['ct%d' % ci2])
                act(gl[ci2][:, 0:NN], c_, AF.Gelu, ['ct%d' % ci2], ['gl%d' % ci2])
                w = [K('hT%d' % f)]
                if firstH:
                    firstH = False
                    w = w + [K('RA1')]
                S.op('dve', lambda f=f, bu=bu, ci2=ci2, NN=NN: V.tensor_tensor(
                    out=RAt[:, f, 0:NN], in0=ps[bu][:, 0:NN], in1=gl[ci2][:, 0:NN], op=ALU.mult),
                    [PSK[bu], 'gl%d' % ci2, K('RA2')], w)
        if halo:
            if nextp0 is not None:
                nextp0()
            return
        if not samp:
            S.op('pool', lambda: G.tensor_copy(out=fh[:, :, :], in_=gpa[:, :, N:N + 2]),
                 [K('gpa%d' % f) for f in range(NF)] + [K('RC2')], ['fh%d' % f for f in range(NF)])

        for b in range(nb):
            r = xrow(b)[0:NP, :]
            S.op('pool', lambda r=r: G.tensor_tensor(out=r, in0=r, in1=g1b[0:NP, :], op=ALU.mult),
                 [xkey + 'n%d' % b, 'g1b'], [xkey + 'n%d' % b])
            S.op('pool', lambda r=r: G.tensor_tensor(out=r, in0=r, in1=b1b[0:NP, :], op=ALU.add),
                 [xkey + 'n%d' % b, 'b1b'], [xkey + 'n%d' % b])

        if kind == 'main' and ci == 0:
            tap('hT', RAt[:, 0, 0:512], ['hT0'])
        hk = [K('hT%d' % f) for f in range(NF)]
        for n in range(2):
            for fg in range(6):
                wd = yield (S_DOWN + n * 6 + fg)
                nk = 4 if fg < 5 else 2
                for b in range(nb):
                    bank = 3 if samp else 4 + b

                    def fn(b=b, bank=bank, fg=fg, nk=nk, wd=wd):
                        last = None
                        for kk in range(nk):
                            f = fg * 4 + kk
                            last = T.matmul(ps[bank][0:NP, :], RAt[:, f, b * 128:b * 128 + NP], wd[0][:, kk, :],
                                            start=(f == 0), stop=(f == NF - 1))
                        return last
                    S.op('pe', fn, hk + [wd[1], K('RA2')], [PSK[bank]])
            for b in range(nb):
                bank = 3 if samp else 4 + b
                xs_ = xrow(b)[0:NP, n * 512:(n + 1) * 512]
                S.op('dve', lambda bank=bank, xs_=xs_: V.scalar_tensor_tensor(
                    out=xs_, in0=ps[bank][0:NP, :], scalar=1.0 / ALPHA, in1=xs_, op0=ALU.mult, op1=ALU.add),
                    [PSK[bank], xkey + 'n%d' % b], [xkey + 'z%d_%d' % (b, n)])
        if nextp0 is not None:
            nextp0()
        for b in range(nb):
            layer_norm(xrow(b), NP, 4 + b, [xkey + 'z%d_%d' % (b, n) for n in range(2)], xkey + 'y%d' % b)
            r = xrow(b)[0:NP, :]
            S.op('pool', lambda r=r: G.tensor_tensor(out=r, in0=r, in1=g2b[0:NP, :], op=ALU.mult),
                 [xkey + 'y%d' % b, 'g2b'], [xkey + 'y%d' % b])
            S.op('pool', lambda r=r: G.tensor_tensor(out=r, in0=r, in1=b2b[0:NP, :], op=ALU.add),
                 [xkey + 'y%d' % b, 'b2b'], [xkey + 'y%d' % b])

        if samp:
            dma('sp', ys, xsr[0:NS, :], 'G:out', ['xsry0'], ['o_ys'])
        else:
            def store(ci=ci, par=par, xr=xr, xkey=xkey):
                dma('sp', yp[ci * 512:(ci + 1) * 512, :].rearrange("(b p) f -> p b f", p=128), xr[:, 0:4, :],
                    'oy%d' % par, [xkey + 'y%d' % b for b in range(4)], [xkey, 'o_yp%d' % ci])
            deferred.append(store)
        if last_main and 'final' not in os.environ.get('KSKIP', ''):
            final_prompt_outputs(N)
        if samp:
            final_sample_outputs()

    def layer_norm(r, NP, slot, rkeys, wkey):
        st = stt[slot % 2]
        S.op('dve', lambda: V.bn_stats(out=st[0:NP, 0, :], in_=r[0:NP, 0:512]), rkeys, ['stt%d' % (slot % 2)])
        S.op('dve', lambda: V.bn_stats(out=st[0:NP, 1, :], in_=r[0:NP, 512:1024]), rkeys, ['stt%d' % (slot % 2)])
        S.op('dve', lambda: V.bn_aggr(out=mv[0:NP, slot, :], in_=st[0:NP, :, :]), ['stt%d' % (slot % 2)], ['mv%d' % slot])
        act(lnv[0:NP, slot:slot + 1], mv[0:NP, slot, 1:2], AF.Ln, ['mv%d' % slot, 'epsb'], ['lnv%d' % slot],
            bias=epsb[0:NP, :], scale=1.0)
        act(rstd[0:NP, slot:slot + 1], lnv[0:NP, slot:slot + 1], AF.Exp, ['lnv%d' % slot], ['rstd%d' % slot], scale=-0.5)
        S.op('dve', lambda: V.tensor_scalar(out=r[0:NP, :], in0=r[0:NP, :], scalar1=mv[0:NP, slot, 0:1],
                                            scalar2=rstd[0:NP, slot:slot + 1], op0=ALU.subtract, op1=ALU.mult),
             rkeys + ['mv%d' % slot, 'rstd%d' % slot], [wkey])

    def conv_branch(j, N, samp_views=None):
        c_ = cvt[j % 2][:, 0:N]
        if samp_views is None:
            u0, u1, u2 = uext[:, j, 0:N], uext[:, j, 1:N + 1], uext[:, j, 2:N + 2]
            rk = ['uext%d' % j, 'cw']
        else:
            u0, u1, u2 = samp_views
            rk = ['uext%d' % j, 'cw', 'scT']
        ck_ = 'cvt%d' % (j % 2)
        S.op('dve', lambda: V.tensor_scalar(out=c_, in0=u0, scalar1=cw[:, 3 * j:3 * j + 1], scalar2=None,
                                             op0=ALU.mult), rk, [ck_])
        S.op('dve', lambda: V.scalar_tensor_tensor(out=c_, in0=u1, scalar=cw[:, 3 * j + 1:3 * j + 2], in1=c_,
                                                    op0=ALU.mult, op1=ALU.add), rk + [ck_], [ck_])
        S.op('dve', lambda: V.scalar_tensor_tensor(out=c_, in0=u2, scalar=cw[:, 3 * j + 2:3 * j + 3], in1=c_,
                                                    op0=ALU.mult, op1=ALU.add), rk + [ck_], [ck_])
        S.op('pool', lambda: G.tensor_tensor(out=gcv[:, j, 0:N], in0=c_, in1=cbs[:, j, 0:N], op=ALU.mult),
             [ck_, 'cbs%d' % j, 'RC1'], ['gcv%d' % j])

    pstate = {'praw': 0, 'pT': 0, 'on': 0}

    def attention_block(gb, b, first_blk):
        pts = {}
        sbank = 0
        for g in range(2):
            for kb, kblk in enumerate((gb - 1, gb)):
                bank = (2 * g + kb) % 4
                mmg(ps[bank][:, :],
                    [(kT[64 * g:64 * g + 64, kblk * 128:(kblk + 1) * 128], qT[64 * g:64 * g + 64, :, b * 128:(b + 1) * 128])],
                    ['kT%d' % kblk] + ['qT%d' % c for c in range(4)] + ['RC1'], [PSK[bank]])
                pr = pstate['praw'] % 2
                pstate['praw'] += 1
                act(praw[pr][:], ps[bank][:, :], AF.Exp, [PSK[bank]], ['praw%d' % pr])
                pi = pstate['pT'] % 8
                pstate['pT'] += 1
                if kb == 0:
                    E = E_first if first_blk else E_prev
                    ek = 'E_first' if first_blk else 'E_prev'
                else:
                    E, ek = E_cur, 'E_cur'
                S.op('dve', lambda pi=pi, pr=pr, E=E, g=g: V.tensor_tensor(
                    out=pT[pi][:], in0=praw[pr][:], in1=E[:, g * 512:(g + 1) * 512], op=ALU.mult),
                    ['praw%d' % pr, ek], ['pT%d' % pi])
                pts[(g, kb)] = pi
                if gb == 2:
                    tap('praw_%d%d' % (g, kb), praw[pr][:], ['praw%d' % pr])
                    tap('pT_%d%d' % (g, kb), pT[pi][:], ['pT%d' % pi])
        for g in range(2):
            bank = 4 + g
            pairs_r = ['pT%d' % pts[(g, 0)], 'pT%d' % pts[(g, 1)], 'Vx%d' % (gb - 1), 'Vx%d' % gb, 'Vx1']

            def fn(g=g, bank=bank):
                last = None
                for c in range(4):
                    for kb, kblk in enumerate((gb - 1, gb)):
                        last = T.matmul(ps[bank][:, c * 65:(c + 1) * 65],
                                        pT[pts[(g, kb)]][:, c * 128:(c + 1) * 128],
                                        Vx[:, kblk, g, :], start=(kb == 0), stop=(kb == 1))
                return last
            S.op('pe', fn, pairs_r, [PSK[bank]])
        ri = pstate['on'] % 2
        pstate['on'] += 1
        for g in range(2):
            bank = 4 + g
            pv = ps[bank][:, 0:260].rearrange("p (c e) -> p c e", c=4)
            S.op('dve', lambda g=g, pv=pv, ri=ri: V.tensor_tensor(
                out=rdt[ri][:, 4 * g:4 * g + 4], in0=pv[:, :, 64], in1=esk[:, 4 * g:4 * g + 4], op=ALU.add),
                [PSK[bank], 'esk'], ['rdt%d_%d' % (ri, g)])
            S.op('dve', lambda g=g, ri=ri: V.reciprocal(out=rdt[ri][:, 4 * g:4 * g + 4], in_=rdt[ri][:, 4 * g:4 * g + 4]),
                 ['rdt%d_%d' % (ri, g)], ['rdt%d_%d' % (ri, g)])
            S.op('dve', lambda g=g, pv=pv, ri=ri: V.tensor_tensor(
                out=onb[ri][:, g * 256:(g + 1) * 256].rearrange("p (c d) -> p c d", c=4),
                in0=pv[:, :, 0:64],
                in1=rdt[ri][:, 4 * g:4 * g + 4].unsqueeze(2).to_broadcast([128, 4, 64]), op=ALU.mult),
                [PSK[bank], 'rdt%d_%d' % (ri, g)], ['on%d_%d' % (ri, g)])
        if gb == 2:
            tap('rdt', rdt[ri][:], ['rdt%d_0' % ri, 'rdt%d_1' % ri])
            tap('onb', onb[ri][:], ['on%d_0' % ri, 'on%d_1' % ri])
        tb = 6 + (gb % 2)
        items = [(ps[tb][:, c * 128:(c + 1) * 128], onb[ri][:, c * 128:(c + 1) * 128], 128) for c in range(4)]
        tpg(items, ['on%d_0' % ri, 'on%d_1' % ri], [PSK[tb]])
        act(oT[:, :, b * 128:(b + 1) * 128], ps[tb][:, :].rearrange("p (c q) -> p c q", c=4), AF.Copy,
            [PSK[tb], 'RC1'], ['oT'])

    gpas_t = sb("gpas", [128, NF * NS * 3], BF16)
    gpas = gpas_t[:, :].rearrange("p (f n t) -> p f n t", f=NF, t=3)
    RB_s = sb("RB_s", [128, 8, NSP], BF16)
    RA_s = sb("RA_s", [128, NF, NSP], BF16)
    qT_s = sb("qT_s", [128, 4, NSP], BF16)
    oT_s = sb("oT_s", [128, 4, NSP], BF16)
    mg_s = sb("mg_s", [128, 8, NSP], BF16)
    gcv_s = sb("gcv_s", [128, 4, NSP], BF16)
    cbs_s = sb("cbs_s", [128, 4, NSP], BF16)
    uext_s = sb("uext_s", [128, 4, NSP + 2], F32)
    stg = sb("stg", [NSP, 768], F32)
    gsT = sb("gsT", [128, NF, NSP], F32)
    kcT_new = sb("kcT_new", [128, NS], BF16)
    usT = sb("usT", [128, 4, NS, 3], F32)

    def attention_samples():
        N = NS
        dma('sp', ckc, ck.rearrange("n s f -> s n f"), 'sck', [], ['ckc', 'xres0'])
        dma('pool', cvb, cv.rearrange("n s f -> s n f"), 'scv', ['xres0'], ['cvb'])
        dma('sp', nks_o[:, 0:127, :], ck[:, 1:128, :], 'G:out', [], ['o_nks'])
        dma('sp', nvs_o[:, 0:127, :], cv[:, 1:128, :], 'G:out', [], ['o_nvs'])
        for n4 in range(4):
            bank = n4 % 2
            items = [(ps[bank][:, i * 128:(i + 1) * 128], ckc[:, n4 * 4 + i, :], 128) for i in range(4)]
            tpg(items, ['ckc', 'xres0'], [PSK[bank]])
            act(kcT[:, n4 * 4:n4 * 4 + 4, :], ps[bank][:, :].rearrange("p (n s) -> p n s", n=4), AF.Copy,
                [PSK[bank], 'ckc', 'xres0'], ['kcT%d' % n4])
        def fn():
            last = None
            for n in range(NS):
                for g in range(2):
                    last = T.matmul(ps[2][:, n * 8 + g * 4:n * 8 + g * 4 + 4],
                                    kcT[64 * g:64 * g + 64, n, :], qT_s[64 * g:64 * g + 64, :, n],
                                    start=True, stop=True)
            return last
        S.op('pe', fn, ['xres0'] + ['kcT%d' % i for i in range(4)] + ['sqT%d' % c for c in range(4)] + ['sRC1'], [PSK[2]])
        act(praws[:], ps[2][:, 0:128], AF.Exp, [PSK[2]], ['praws'])
        S.op('dve', lambda: V.tensor_tensor(out=pTs[:], in0=praws[:], in1=esm[:], op=ALU.mult),
             ['praws', 'esm'], ['pTs'])
        S.op('dve', lambda: V.tensor_tensor(out=prod[:], in0=qT_s[:, :, 0:NS],
                                            in1=kcT_new[:, :].unsqueeze(1).to_broadcast([128, 4, NS]), op=ALU.mult),
             ['sqT%d' % c for c in range(4)] + ['kTs', 'sRC1'], ['prod'])
        mmg(ps[3][:, 0:64], [(bones[:, :], prod[:].rearrange("p c n -> p (c n)"))], ['bones', 'prod'], [PSK[3]])
        act(pnb[:].rearrange("p c n -> p (c n)"), ps[3][:, 0:64], AF.Exp, [PSK[3]], ['pnb'])
        def fn2():
            last = None
            for n in range(NS):
                for g in range(2):
                    last = T.matmul(ps[0][64 * g:64 * g + 64, n * 4:n * 4 + 4], cvb[:, n, 64 * g:64 * g + 64],
                                    pTs[:, n * 8 + g * 4:n * 8 + g * 4 + 4], start=True, stop=True)
            return last
        S.op('pe', fn2, ['cvb', 'pTs', 'xres0'], [PSK[0]])
        ones64 = ones_t[:, :]

        def fn3():
            last = None
            for g in range(2):
                last = T.matmul(ps[1][64 * g:64 * g + 64, 0:64], ones64,
                                pTs[:].rearrange("p (n g c) -> p n g c", g=2, c=4)[:, :, g, :], start=True, stop=True)
            return last
        S.op('pe', fn3, ['ones_t', 'pTs'], [PSK[1]])
        S.op('dve', lambda: V.tensor_tensor(out=ons[:], in0=pnb[:], in1=vTs[:, :].unsqueeze(1).to_broadcast([128, 4, NS]),
                                            op=ALU.mult), ['pnb', 'vTs'], ['ons'])
        S.op('dve', lambda: V.tensor_tensor(out=ons[:], in0=ons[:],
                                            in1=ps[0][:, 0:64].rearrange("p (n c) -> p c n", c=4), op=ALU.add),
             ['ons', PSK[0]], ['ons'])
        S.op('dve', lambda: V.tensor_tensor(out=dns[:], in0=pnb[:],
                                            in1=ps[1][:, 0:64].rearrange("p (n c) -> p c n", c=4), op=ALU.add),
             ['pnb', PSK[1]], ['dns'])
        for g in range(2):
            sl = slice(64 * g, 64 * g + 64)
            S.op('dve', lambda g=g, sl=sl: V.tensor_tensor(
                out=dns[sl], in0=dns[sl], in1=esk[sl, 4 * g:4 * g + 4].unsqueeze(2).to_broadcast([64, 4, NS]), op=ALU.add),
                ['dns', 'esk'], ['dns'])
        S.op('dve', lambda: V.reciprocal(out=dns[:], in_=dns[:]), ['dns'], ['dns'])
        S.op('dve', lambda: V.tensor_tensor(out=ons[:], in0=ons[:], in1=dns[:], op=ALU.mult), ['ons', 'dns'], ['ons'])
        for h in range(8):
            g, c = h // 4, h % 4
            S.op('dve', lambda h=h, g=g, c=c: V.tensor_copy(out=oT_s[64 * (h % 2):64 * (h % 2) + 64, h // 2, 0:NS],
                                                           in_=ons[64 * g:64 * g + 64, c, :]),
                 ['ons', 'sRC1'], ['soT'])

    def final_prompt_outputs(N):
        dma('sp', nk_o, nkv[:, 0:128], 'G:out', ['nkv_k'], ['o_nk'])
        dma('sp', nv_o, nkv[:, 128:256], 'G:out', ['nkv_v'], ['o_nv'])
        items = [(ps[0][0:32, j * 128:(j + 1) * 128], uext[:, j, N - 30:N + 2], 128) for j in range(4)]
        tpg(items, ['uext%d' % j for j in range(4)], [PSK[0]])
        S.op('act', lambda: acopy(out=stg[:, 0:512], in_=ps[0][0:32, :]), [PSK[0]], ['stg'])
        dma('sp', ncv_o, stg[30:32, 0:512], 'stg_o', ['stg'], ['o_ncv'])
        for pi, (f0, nf) in enumerate(OPIECES):
            bank = 1 + (pi % 2)
            items = [(ps[bank][0:32, i * 128:(i + 1) * 128], gl2[:, f0 + i, :], 128) for i in range(nf)]
            tpg(items, ['gl2_%d' % (f0 + i) for i in range(nf)], [PSK[bank]])
            S.op('act', lambda nf=nf, bank=bank: acopy(out=stg[:, 0:nf * 128], in_=ps[bank][0:32, 0:nf * 128]),
                 [PSK[bank]], ['stg'])
            dma('sp', nfc_o[:, f0 * 128:(f0 + nf) * 128], stg[30:32, 0:nf * 128], 'stg_o', ['stg'], ['o_nfc_%d' % pi])

    def final_sample_outputs():
        dma('sp', nks_o[:, 127, :], kvs[0:NS, 0:128], 'G:out', ['kvs_k'], ['o_nks2'])
        dma('sp', nvs_o[:, 127, :], kvs[0:NS, 128:256], 'G:out', ['kvs_v'], ['o_nvs2'])
        dma('sp', ncs_o[:, 0, :], sc.rearrange("(n t) c -> n t c", t=2)[:, 1, :], 'G:out', [], ['o_ncs0'])
        dma('sp', nfs_o[:, 0, :], sf.rearrange("(n t) c -> n t c", t=2)[:, 1, :], 'G:out', [], ['o_nfs0'])
        items = [(ps[0][0:NSP, j * 128:(j + 1) * 128], uext_s[:, j, 2:2 + NSP], 128) for j in range(4)]
        tpg(items, ['suext%d' % j for j in range(4)], [PSK[0]])
        S.op('act', lambda: acopy(out=stg[:, 0:512], in_=ps[0][0:NSP, :]), [PSK[0]], ['stg'])
        dma('sp', ncs_o[:, 1, :], stg[0:NS, 0:512], 'stg_o', ['stg'], ['o_ncs1'])
        for pi, (f0, nf) in enumerate(OPIECES):
            bank = 1 + (pi % 2)
            items = [(ps[bank][0:NSP, i * 128:(i + 1) * 128], gsT[:, f0 + i, :], 128) for i in range(nf)]
            tpg(items, ['gsT%d' % (f0 + i) for i in range(nf)], [PSK[bank]])
            S.op('act', lambda nf=nf, bank=bank: acopy(out=stg[:, 0:nf * 128], in_=ps[bank][0:NSP, 0:nf * 128]),
                 [PSK[bank]], ['stg'])
            dma('sp', nfs_o[:, 1, f0 * 128:(f0 + nf) * 128], stg[0:NS, 0:nf * 128], 'stg_o', ['stg'], ['o_nfs1_%d' % pi])


    PIECES = [(0, 6), (6, 6), (12, 6), (18, 4)]
    OPIECES = [(0, 4), (4, 4), (8, 4), (12, 4), (16, 4), (20, 2)]

    def samp_prologue():
        dma('sp', stg[:, 0:512], sc, 'stg_i', [], ['stg'])
        items = [(ps[0][:, j * 32:(j + 1) * 32], stg[:, j * 128:(j + 1) * 128], 2 * NS) for j in range(4)]
        tpg(items, ['stg'], [PSK[0]])
        S.op('act', lambda: acopy(out=scT[:], in_=ps[0][:, 0:128].rearrange("p (j m) -> p j m", j=4)),
             [PSK[0]], ['scT'])
        for pi, (f0, nf) in enumerate(PIECES):
            dma('sp', stg[:, 0:nf * 128], sf[:, f0 * 128:(f0 + nf) * 128], 'stg_i', [], ['stg'])
            bank = 1 + (pi % 2)
            items = [(ps[bank][:, i * 32:(i + 1) * 32], stg[:, i * 128:(i + 1) * 128], 2 * NS) for i in range(nf)]
            tpg(items, ['stg'], [PSK[bank]])
            S.op('act', lambda f0=f0, nf=nf, bank=bank: acopy(
                out=sfT[:, f0:f0 + nf, :], in_=ps[bank][:, 0:nf * 32].rearrange("p (j m) -> p j m", j=nf)),
                [PSK[bank]], ['sfT%d' % pi])

    def run_pass(gens):
        pend = []
        for g in gens:
            try:
                pend.append([g, next(g)])
            except StopIteration:
                pass
        while pend:
            sl = min(p[1] for p in pend)
            view = wload(sl)
            nxt = []
            for p in pend:
                if p[1] == sl:
                    try:
                        p[1] = p[0].send(view)
                        nxt.append(p)
                    except StopIteration:
                        pass
                else:
                    nxt.append(p)
            pend = nxt

    import os
    LIM = int(os.environ.get('KSTAGE', '99'))
    def conv_branch_samp(j, N):
        S.op('pool', lambda: G.tensor_copy(out=usT[:, j, :, 2], in_=uext_s[:, j, 2:2 + NS]), ['suext%d' % j], ['usT%d' % j])
        S.op('pool', lambda: G.tensor_copy(out=usT[:, j, :, 0:2],
                                           in_=scT[:, j, :].rearrange("p (n t) -> p n t", t=2)),
             ['scT'], ['usT%d' % j])
        c_ = cvt[j % 2][:, 0:NS]
        ck_ = 'cvt%d' % (j % 2)
        rk = ['usT%d' % j, 'cw']
        S.op('dve', lambda: V.tensor_scalar(out=c_, in0=usT[:, j, :, 0], scalar1=cw[:, 3 * j:3 * j + 1], scalar2=None,
                                             op0=ALU.mult), rk, [ck_])
        S.op('dve', lambda: V.scalar_tensor_tensor(out=c_, in0=usT[:, j, :, 1], scalar=cw[:, 3 * j + 1:3 * j + 2],
                                                    in1=c_, op0=ALU.mult, op1=ALU.add), rk + [ck_], [ck_])
        S.op('dve', lambda: V.scalar_tensor_tensor(out=c_, in0=usT[:, j, :, 2], scalar=cw[:, 3 * j + 2:3 * j + 3],
                                                    in1=c_, op0=ALU.mult, op1=ALU.add), rk + [ck_], [ck_])
        S.op('pool', lambda: G.tensor_tensor(out=gcv_s[:, j, 0:NS], in0=c_, in1=cbs_s[:, j, 0:NS], op=ALU.mult),
             [ck_, 'scbs%d' % j, 'sRC1'], ['sgcv%d' % j])

    if LIM >= 1:
        run_pass([chunk('halo', 0, nextload=lambda: xload('main', 0), nextp0=lambda: emit_p0_main(0))])
    for ci in range(4):
        if LIM < 2 + ci:
            break
        nl = (lambda ci=ci: xload('main', ci + 1)) if ci < 3 else None
        np0 = (lambda ci=ci: emit_p0_main(ci + 1)) if ci < 3 else None
        gens = [chunk('main', ci, nextload=nl, preloaded=True, p0done=True, nextp0=np0)]
        if ci == 3 and LIM >= 6:
            xload('samp', 0)
            samp_prologue()
            gens.append(chunk('samp', 0, preloaded=True))
        run_pass(gens)
    flush_deferred()

    S.op('sp', None, ['o_ys', 'o_nk', 'o_nv', 'o_ncv', 'o_nfc_0', 'o_nfc_1', 'o_nfc_2', 'o_nfc_3', 'o_nfc_4', 'o_nfc_5', 'o_nks', 'o_nvs', 'o_nks2', 'o_nvs2',
                      'o_ncs0', 'o_nfs0', 'o_ncs1', 'o_nfs1_0', 'o_nfs1_1', 'o_nfs1_2', 'o_nfs1_3', 'o_nfs1_4', 'o_nfs1_5', 'o_yp0', 'o_yp1', 'o_yp2', 'o_yp3'], [])
    S.finalize(es)
    es.close()
    return nc, None


_PROG = {}


def _consts():
    slopes = 2.0 ** (-8.0 * np.arange(1, 9) / 8.0)
    s = np.arange(128)[:, None]
    q = np.arange(128)[None, :]
    eprev = np.zeros((128, 2, 4, 128), np.float32)
    ecur = np.zeros((128, 2, 4, 128), np.float32)
    esm = np.zeros((128, NS, 2, 4), np.float32)
    for g in range(2):
        for c in range(4):
            sl = slopes[4 * g + c]
            dprev = q + 128 - s
            dcur = q - s
            eprev[:, g, c, :] = np.where(s >= q, np.exp(-sl * dprev), 0.0)
            ecur[:, g, c, :] = np.where(q >= s, np.exp(-sl * dcur), 0.0)
            esm[:, :, g, c] = np.exp(-sl * (128 - np.arange(128)))[:, None]
    bones = (np.arange(128)[:, None] // 64 == np.arange(128)[None, :] // 64).astype(np.float32)
    return dict(eprev=eprev.reshape(128, 1024), ecur=ecur.reshape(128, 1024), esm=esm.reshape(128, 128),
                bones=bones, ident=np.eye(128, dtype=np.float32))


def kernel(x_prompt, x_sample, cache_k, cache_v, state_conv, state_ffn_conv, w_in, conv_w, attn_sinks,
           w_attn_out, w_conv_out, w_mix_out, ln1_g, ln1_b, w_gate, w_up, ffn_conv_w, w_down, ln2_g, ln2_b):
    f32 = np.float32
    A_ = lambda a: np.ascontiguousarray(np.asarray(a, dtype=f32))
    x_prompt = A_(x_prompt); x_sample = A_(x_sample)
    if 'nc' not in _PROG:
        _PROG['nc'], _PROG['es'] = build_program()
    nc = _PROG['nc']
    cols = []
    for c in range(4):
        cols += list(range(c * 64, c * 64 + 64)) + list(range((4 + c) * 64, (4 + c) * 64 + 64))
    cols += list(range(512, 768))
    cols += list(range(768, 1280))
    for j in range(4):
        cols += list(range(1280 + j * 128, 1280 + (j + 1) * 128)) + list(range(1792 + j * 128, 1792 + (j + 1) * 128))
    cols += list(range(2304, 4352))
    w_in_p = A_(A_(w_in)[0][:, cols])
    bc = lambda v: A_(np.broadcast_to(A_(v).reshape(1, -1), (128, A_(v).size)))
    fm = lambda v, n: A_(A_(v).reshape(n, 128).T)
    common = dict(
        w_in=w_in_p, w_ao=A_(w_attn_out)[0], w_co=A_(w_conv_out)[0], w_mix=A_(w_mix_out)[0],
        w_gate=A_(w_gate)[0], w_up=A_(w_up)[0], w_down=A_(w_down)[0],
        cw=A_(A_(conv_w)[0].reshape(3, 4, 128).transpose(2, 1, 0).reshape(128, 12)),
        fcw=A_(A_(ffn_conv_w)[0].reshape(3, NF, 128).transpose(2, 1, 0).reshape(128, 66)),
        sk=bc(A_(attn_sinks)[0]),
        g1b=bc(ln1_g[0]), b1b=bc(ln1_b[0]), g2b=bc(ln2_g[0]), b2b=bc(ln2_b[0]),
        g1f=fm(ln1_g[0], 8), b1f=fm(ln1_b[0], 8),
    )
    common.update(_consts())
    ck = A_(cache_k)[0].reshape(128, 128, 128)
    cv = A_(cache_v)[0].reshape(128, 128, 128)
    sc = A_(state_conv)[0].reshape(256, 512)
    sf = A_(state_ffn_conv)[0].reshape(256, DFF)
    in_maps = []
    for c in range(NCORES):
        b, r = c // 4, c % 4
        s0 = r * TOK_CORE
        xp = np.zeros((NTOK, D), f32)
        if r > 0:
            xp[0:HALO] = x_prompt[b, s0 - HALO:s0]
        xp[HALO:] = x_prompt[b, s0:s0 + TOK_CORE]
        m = dict(common)
        m.update(xp=xp, xs=A_(x_sample[c * NS:(c + 1) * NS, 0, :]),
                 ck=A_(ck[c * NS:(c + 1) * NS]), cv=A_(cv[c * NS:(c + 1) * NS]),
                 sc=A_(sc[c * 2 * NS:(c + 1) * 2 * NS]), sf=A_(sf[c * 2 * NS:(c + 1) * 2 * NS]),
                 hv=np.full((128, 1), 0.0 if r == 0 else 1.0, f32))
        in_maps.append(m)
    res = run_bass_kernel_spmd(nc, in_maps, core_ids=list(range(NCORES)))
    R = res.results
    y_prompt = np.zeros((2, 8192, D), f32)
    y_sample = np.zeros((128, 1, D), f32)
    nk_p = np.zeros((1, 2, 128, 2, 64), f32); nv_p = np.zeros((1, 2, 128, 2, 64), f32)
    nc_p = np.zeros((1, 2, 2, 512), f32); nf_p = np.zeros((1, 2, 2, DFF), f32)
    nk_s = np.zeros((1, 128, 128, 2, 64), f32); nv_s = np.zeros((1, 128, 128, 2, 64), f32)
    nc_s = np.zeros((1, 128, 2, 512), f32); nf_s = np.zeros((1, 128, 2, DFF), f32)
    for c in range(NCORES):
        b, r = c // 4, c % 4
        y_prompt[b, r * TOK_CORE:(r + 1) * TOK_CORE] = R[c]["yp"]
        y_sample[c * NS:(c + 1) * NS, 0] = R[c]["ys"]
        if r == 3:
            nk_p[0, b] = R[c]["nk"].reshape(128, 2, 64)
            nv_p[0, b] = R[c]["nv"].reshape(128, 2, 64)
            nc_p[0, b] = R[c]["ncv"]
            nf_p[0, b] = R[c]["nfc"]
        nk_s[0, c * NS:(c + 1) * NS] = R[c]["nks"].reshape(NS, 128, 2, 64)
        nv_s[0, c * NS:(c + 1) * NS] = R[c]["nvs"].reshape(NS, 128, 2, 64)
        nc_s[0, c * NS:(c + 1) * NS] = R[c]["ncs"]
        nf_s[0, c * NS:(c + 1) * NS] = R[c]["nfs"]
    return (y_prompt, y_sample, nk_p, nv_p, nc_p, nf_p, nk_s, nv_s, nc_s, nf_s)
```

```python
import contextlib
import numpy as np
import concourse.bass as bass
import concourse.mybir as mybir
from concourse.bass_utils import run_bass_kernel_spmd

F32 = mybir.dt.float32
BF16 = mybir.dt.bfloat16
AF = mybir.ActivationFunctionType
ALU = mybir.AluOpType

D = 1024
NPROJ = 4352
DFF = 2816
NF = 22
ALPHA = 2.0 ** 0.25
EPS = 1e-5
EPS_S = EPS / (ALPHA * ALPHA)
NCORES = 8
TOK_CORE = 2048
HALO = 256
NTOK = TOK_CORE + HALO
NBLK = NTOK // 128
NS = 16
NSP = 32
RING = 5


class Sched:
    def __init__(self, nc):
        self.nc = nc
        self.ops = []
        self.last_w = {}
        self.readers = {}

    def op(self, eng, fn, reads=(), writes=(), dma=None):
        deps = set()
        for k in reads:
            if k in self.last_w:
                deps.add(self.last_w[k])
            if k.startswith('ps'):
                deps.update(r for r in self.readers.get(k, ()) if self.ops[r]['eng'] != eng)
        for k in writes:
            if k in self.last_w:
                deps.add(self.last_w[k])
            deps.update(self.readers.get(k, ()))
        oid = len(self.ops)
        self.ops.append(dict(eng=eng, fn=fn, deps=deps, dma=dma, signal=False, waits=[]))
        for k in reads:
            self.readers.setdefault(k, []).append(oid)
        for k in writes:
            self.last_w[k] = oid
            self.readers[k] = []
        return oid

    def finalize(self, stack):
        nc = self.nc
        engs = {'pe': nc.tensor, 'act': nc.scalar, 'dve': nc.vector, 'pool': nc.gpsimd, 'sp': nc.sync}
        ops = self.ops
        cnt = {}
        for o in ops:
            st = ('dma:' + o['dma']) if o['dma'] else o['eng']
            o['stream'] = st
            cnt[st] = cnt.get(st, 0) + 1
            o['seq'] = cnt[st]
        for o in ops:
            if o['dma'] and o['dma'].startswith('G:'):
                o['seq'] = cnt[o['stream']]
        clk = {e: {} for e in engs}
        eng_seq = {e: 0 for e in engs}
        for o in ops:
            e = o['eng']
            c = clk[e]
            myseq = o['seq'] if not o['dma'] else None
            need = {}
            for d in o['deps']:
                od = ops[d]
                st = od['stream']
                if st == e and not o['dma']:
                    if e == 'pe':
                        continue
                    if o['seq'] - od['seq'] > 2:
                        continue
                    if c.get('self_' + e, 0) >= od['seq']:
                        continue
                elif c.get(st, 0) >= od['seq']:
                    continue
                if st not in need or ops[need[st]]['seq'] < od['seq']:
                    need[st] = d
            for st, d in need.items():
                od = ops[d]
                if st == e and not o['dma']:
                    c['self_' + e] = od['seq']
                elif c.get(st, 0) >= od['seq']:
                    continue
                od['signal'] = True
                o['waits'].append(d)
                for k, v in od['vc'].items():
                    if c.get(k, 0) < v:
                        c[k] = v
            if not o['dma']:
                c[e] = o['seq']
                o['vc'] = dict(c)
            else:
                vc = dict(c)
                vc[o['stream']] = o['seq']
                o['vc'] = vc
        sems = {}
        val = {}
        for o in ops:
            st = o['stream']
            if o['dma']:
                val[st] = val.get(st, 0) + 16
                o['val'] = val[st]
            elif o['signal']:
                val[st] = val.get(st, 0) + 1
                o['val'] = val[st]
        for o in ops:
            if o['dma'] and o['dma'].startswith('G:'):
                o['val'] = val[o['stream']]
        for st in val:
            sems[st] = stack.enter_context(nc.semaphore('s_' + st.replace(':', '_')))
        self.nsem = len(sems)
        for o in ops:
            e = engs[o['eng']]
            for d in o['waits']:
                od = ops[d]
                e.wait_ge(sems[od['stream']], od['val'])
            if o['fn'] is None:
                continue
            ins = o['fn']()
            if o['dma']:
                ins.then_inc(sems[o['stream']], 16)
            elif o['signal']:
                ins.then_inc(sems[o['stream']], 1)


def slot_table():
    t = []
    for i in range(17):
        t.append(('w_in', 0, 8, i * 256, 256))
    for h in range(2):
        t.append(('w_ao', 0, 4, h * 512, 512))
    for h in range(2):
        t.append(('w_co', 0, 4, h * 512, 512))
    for n in range(2):
        for kh in range(2):
            t.append(('w_mix', kh * 4, 4, n * 512, 512))
    for j in range(11):
        t.append(('w_gate', 0, 8, j * 256, 256))
        t.append(('w_up', 0, 8, j * 256, 256))
    for n in range(2):
        for fg in range(6):
            nk = 4 if fg < 5 else 2
            t.append(('w_down', fg * 4, nk, n * 512, 512))
    return t


SLOTS = slot_table()
S_IN, S_AO, S_CO, S_MIX, S_FFN, S_DOWN = 0, 17, 19, 21, 25, 47
WSHAPES = {'w_in': (D, NPROJ), 'w_ao': (512, D), 'w_co': (512, D), 'w_mix': (D, D),
           'w_gate': (D, DFF), 'w_up': (D, DFF), 'w_down': (DFF, D)}


def build_program():
    nc = bass.Bass("TRN2", target_bir_lowering=False)
    S = Sched(nc)
    es = contextlib.ExitStack()

    def din(name, shape, dt=F32):
        return nc.dram_tensor(name, list(shape), dt, kind="ExternalInput").ap()

    def dout(name, shape, dt=F32):
        return nc.dram_tensor(name, list(shape), dt, kind="ExternalOutput").ap()

    xp = din("xp", [NTOK, D])
    xs = din("xs", [NS, D])
    ck = din("ck", [NS, 128, 128])
    cv = din("cv", [NS, 128, 128])
    sc = din("sc", [2 * NS, 512])
    sf = din("sf", [2 * NS, DFF])
    W = {k: din(k, v) for k, v in WSHAPES.items()}
    cw_d = din("cw", [128, 12])
    fcw_d = din("fcw", [128, 66])
    sk_d = din("sk", [128, 8])
    g1b_d = din("g1b", [128, D]); b1b_d = din("b1b", [128, D])
    g2b_d = din("g2b", [128, D]); b2b_d = din("b2b", [128, D])
    g1f_d = din("g1f", [128, 8]); b1f_d = din("b1f", [128, 8])
    ident_d = din("ident", [128, 128])
    eprev_d = din("eprev", [128, 1024]); ecur_d = din("ecur", [128, 1024])
    esm_d = din("esm", [128, 128])
    bones_d = din("bones", [128, 128])
    hv_d = din("hv", [128, 1])

    yp = dout("yp", [TOK_CORE, D])
    ys = dout("ys", [NS, D])
    nk_o = dout("nk", [128, 128]); nv_o = dout("nv", [128, 128])
    ncv_o = dout("ncv", [2, 512]); nfc_o = dout("nfc", [2, DFF])
    nks_o = dout("nks", [NS, 128, 128]); nvs_o = dout("nvs", [NS, 128, 128])
    ncs_o = dout("ncs", [NS, 2, 512]); nfs_o = dout("nfs", [NS, 2, DFF])
    wscr = nc.dram_tensor("wscr", [len(SLOTS), 128, 2048], BF16, kind="Internal").ap()

    def sb(name, shape, dt):
        return es.enter_context(nc.sbuf_tensor("s_" + name, list(shape), dt))

    xres = [sb("xres%d" % i, [128, 4, D], F32) for i in range(2)]
    RB = sb("RB", [128, 8, 512], BF16)
    RA = sb("RA", [128, NF, 512], BF16)
    RC = sb("RC", [128, 12288], BF16)
    qT = RC[:, 0:2048].rearrange("p (c n) -> p c n", c=4)
    oT = RC[:, 2048:4096].rearrange("p (c n) -> p c n", c=4)
    mg = RC[:, 4096:8192].rearrange("p (c n) -> p c n", c=8)
    gcv = RC[:, 8192:10240].rearrange("p (c n) -> p c n", c=4)
    cbs = RC[:, 10240:12288].rearrange("p (c n) -> p c n", c=4)
    gpa = RC[:, 0:NF * 514].rearrange("p (f n) -> p f n", f=NF)
    kT = sb("kT", [128, NBLK * 128], BF16)
    Vx = sb("Vx", [128, NBLK, 2, 65], BF16)
    ccs = [sb("ccs%d" % i, [128, 512], F32) for i in range(2)]
    uext = sb("uext", [128, 4, 514], F32)
    cvt = [sb("cvt%d" % i, [128, 512], F32) for i in range(2)]
    praw = [sb("praw%d" % i, [128, 512], BF16) for i in range(3)]
    pT = [sb("pT%d" % i, [128, 512], BF16) for i in range(8)]
    onb = [sb("on%d" % i, [128, 512], F32) for i in range(2)]
    rdt = [sb("rdt%d" % i, [128, 8], F32) for i in range(2)]
    m12 = [sb("m12_%d" % i, [128, 512], BF16) for i in range(4)]
    ct = [sb("ct%d" % i, [128, 512], F32) for i in range(2)]
    gl = [sb("gl%d" % i, [128, 512], BF16) for i in range(2)]
    ring = sb("ring", [128, RING, 2048], BF16)
    g1b = sb("g1b", [128, D], F32); b1b = sb("b1b", [128, D], F32)
    g2b = sb("g2b", [128, D], F32); b2b = sb("b2b", [128, D], F32)
    g1f = sb("g1f", [128, 8], F32); b1f = sb("b1f", [128, 8], F32)
    ident = sb("ident", [128, 128], F32)
    E_prev = sb("E_prev", [128, 1024], BF16)
    E_cur = sb("E_cur", [128, 1024], BF16)
    E_first = sb("E_first", [128, 1024], BF16)
    esm = sb("esm", [128, 128], BF16)
    bones = sb("bones", [128, 128], BF16)
    hv = sb("hv", [128, 1], F32)
    cw = sb("cw", [128, 12], F32)
    fcw = sb("fcw", [128, 66], F32)
    skt = sb("skt", [128, 8], F32)
    esk = sb("esk", [128, 8], F32)
    epsb = sb("epsb", [128, 1], F32)
    stt = [sb("stt%d" % i, [128, 2, 6], F32) for i in range(2)]
    mv = sb("mv", [128, 8, 2], F32)
    lnv = sb("lnv", [128, 8], F32)
    rstd = sb("rstd", [128, 8], F32)
    nkv = sb("nkv", [128, 256], F32)
    gl2 = sb("gl2", [128, NF, 32], F32)
    fh = sb("fh", [128, NF, 2], BF16)
    xsr = sb("xsr", [NSP, D], F32)
    scT = sb("scT", [128, 4, 2 * NS], F32)
    sfT = sb("sfT", [128, NF, 2 * NS], F32)
    vTs = sb("vTs", [128, NS], F32)
    prod = sb("prod", [128, 4, NS], BF16)
    pnb = sb("pnb", [128, 4, NS], F32)
    ons = sb("ons", [128, 4, NS], F32)
    dns = sb("dns", [128, 4, NS], F32)
    pTs = sb("pTs", [128, NS * 8], BF16)
    praws = sb("praws", [128, NS * 8], BF16)
    kvs = sb("kvs", [NSP, 256], F32)
    x0f = xres[0][:, :, :].rearrange("p b f -> p (b f)")
    x1f = xres[1][:, :, :].rearrange("p b f -> p (b f)")
    ncf = x0f[0:32, 0:512 + DFF]
    ones_t = sb("ones_t", [128, 64], BF16)
    ckc = xres[0][:, :, :].rearrange("p b f -> p (b f)")[:, 0:2048].rearrange("p (n f) -> p n f", n=NS)
    cvb = xres[0][:, :, :].rearrange("p b f -> p (b f)")[:, 2048:3072].bitcast(BF16).rearrange("p (n f) -> p n f", n=NS)
    kcT = xres[0][:, :, :].rearrange("p b f -> p (b f)")[:, 3072:4096].bitcast(BF16).rearrange("p (n f) -> p n f", n=NS)

    ps = [es.enter_context(nc.psum_tensor("ps%d" % i, [128, 512], F32)) for i in range(8)]
    PSK = ["ps%d" % i for i in range(8)]

    T, V, A, G, SP = nc.tensor, nc.vector, nc.scalar, nc.gpsimd, nc.sync

    def mmg(out_ap, pairs, reads, writes):
        def fn():
            last = None
            n = len(pairs)
            for i, (l, r) in enumerate(pairs):
                last = T.matmul(out_ap, l, r, start=(i == 0), stop=(i == n - 1))
            return last
        return S.op('pe', fn, reads, writes)

    def tpg(items, reads, writes):
        def fn():
            last = None
            for (o, i, npart) in items:
                last = T.transpose(o, i, ident[0:npart, 0:npart])
            return last
        return S.op('pe', fn, reads + ['ident'], writes)

    def acopy(out, in_):
        return A.activation(out=out, in_=in_, func=AF.Copy)

    def act(out, in_, func, reads, writes, bias=None, scale=None):
        kw = {}
        if bias is not None:
            kw['bias'] = bias
        if scale is not None:
            kw['scale'] = scale
        return S.op('act', lambda: A.activation(out=out, in_=in_, func=func, **kw), reads, writes)

    def dma(q, out, in_, key, reads, writes):
        e = {'sp': SP, 'pool': G, 'act': A}[q]
        return S.op(q, lambda: e.dma_start(out=out, in_=in_), reads, writes, dma=key)

    for i, (t, d, k) in enumerate([(g1b, g1b_d, 'g1b'), (b1b, b1b_d, 'b1b'), (g2b, g2b_d, 'g2b'), (b2b, b2b_d, 'b2b'),
                                   (g1f, g1f_d, 'g1f'), (b1f, b1f_d, 'b1f'), (ident, ident_d, 'ident'),
                                   (hv, hv_d, 'hv'), (cw, cw_d, 'cw'), (fcw, fcw_d, 'fcw'), (skt, sk_d, 'skt')]):
        dma('sp', t[:], d, 'G:c', [], [k])
    dma('pool', E_prev[:], eprev_d, 'G:cE', [], ['E_prev'])
    dma('pool', E_cur[:], ecur_d, 'G:cE', [], ['E_cur'])
    dma('pool', esm[:], esm_d, 'G:cE', [], ['esm'])
    dma('pool', bones[:], bones_d, 'G:cE', [], ['bones'])
    S.op('dve', lambda: V.memset(Vx[:, :, :, 64:65], 1.0), [], ['Vx1'])
    S.op('dve', lambda: V.memset(uext[:], 0.0), [], ['uext%d' % j for j in range(4)])
    S.op('dve', lambda: V.memset(RC[:], 0.0), [], ['RC1', 'RC2'])
    S.op('dve', lambda: V.memset(epsb[:], EPS_S), [], ['epsb'])
    S.op('dve', lambda: V.memset(ones_t[:], 1.0), [], ['ones_t'])
    S.op('dve', lambda: V.memset(xsr[:], 0.0), [], ['xsr'])
    S.op('dve', lambda: V.memset(oT_s[:], 0.0), [], ['soT'])
    S.op('dve', lambda: V.memset(gcv_s[:], 0.0), [], ['sgcv%d' % j for j in range(4)])
    S.op('dve', lambda: V.memset(RA_s[:], 0.0), [], ['sRA1', 'sRA2'])
    S.op('dve', lambda: V.memset(uext_s[:], 0.0), [], ['suext%d' % j for j in range(4)])
    S.op('dve', lambda: V.tensor_scalar(out=E_first[:], in0=E_prev[:], scalar1=hv[:, 0:1], scalar2=None,
                                        op0=ALU.mult), ['E_prev', 'hv'], ['E_first'])
    act(esk[:], skt[:], AF.Exp, ['skt'], ['esk'])

    for s, (wn, k0, nk, c0, ncol) in enumerate(SLOTS):
        src = W[wn].rearrange("(k p) n -> p k n", p=128)[:, k0:k0 + nk, c0:c0 + ncol]
        dst = wscr[s].rearrange("p (k c) -> p k c", c=ncol)[:, 0:nk, :]
        dma('pool', dst, src, 'G:cv%d' % (s // 3), [], ['scr%d' % s])

    rstate = {'n': 0}

    def wload(s):
        r = rstate['n'] % RING
        rstate['n'] += 1
        wn, k0, nk, c0, ncol = SLOTS[s]
        nel = nk * ncol
        dma('sp', ring[:, r, 0:nel], wscr[s][:, 0:nel], 'ring%d' % r, ['scr%d' % s], ['ring%d' % r])
        view = ring[:, r, :].rearrange("p (k c) -> p k c", c=ncol)
        return view, 'ring%d' % r

    deferred = []
    import os
    DBG = os.environ.get('KDEBUG', '') == '1'
    taps = {}

    def tap(name, ap, reads):
        if not DBG or name in taps:
            return
        shp = list(ap.shape)
        d = nc.dram_tensor("dbg_" + name, shp, ap.dtype, kind="ExternalOutput").ap()
        taps[name] = d
        dma('sp', d, ap, 'dbg%d' % len(taps), reads, ['o_dbg_' + name])

    def xload(kind, ci):
        if kind == 'samp':
            dma('sp', xsr[0:NS, :], xs, 'xs', [], ['xsr'])
        elif kind == 'halo':
            dma('sp', xres[1][:, 0:2, :], xp[0:256, :].rearrange("(b p) f -> p b f", p=128), 'x1', [], ['xres1'])
        else:
            par = ci % 2
            t0 = HALO + ci * 512
            dma('sp', xres[par][:, 0:4, :], xp[t0:t0 + 512, :].rearrange("(b p) f -> p b f", p=128),
                'x%d' % par, [], ['xres%d' % par])

    def flush_deferred():
        for f in deferred:
            f()
        deferred.clear()

    def chunk(kind, ci, nextload=None, preloaded=False):
        samp = kind == 'samp'
        halo = kind == 'halo'
        if kind == 'main':
            par = ci % 2
            tok0 = HALO + ci * 512
            Nx, a0, N, nb, NP = 512, 0, 512, 4, 128
            blk0 = tok0 // 128
            xr = xres[par]
            xkey = 'xres%d' % par
        elif halo:
            par = 1
            tok0 = 0
            Nx, a0, N, nb, NP = 256, 128, 128, 1, 128
            blk0 = 0
            xr = xres[1]
            xkey = 'xres1'
        else:
            Nx, a0, N, nb, NP = NSP, 0, NSP, 1, NSP
            blk0 = None
            xkey = 'xsr'
        nbx = Nx // 128 if not samp else 1
        last_main = (kind == 'main' and ci == 3) and 'lastmain' not in os.environ.get('KSKIP', '')

        def xrow(b):
            if samp:
                return xsr[:, :]
            return xr[:, (a0 // 128) + b, :]

        def xrow_all(b):
            if samp:
                return xsr[:, :]
            return xr[:, b, :]

        kp = 's' if samp else ''

        def K(name):
            return kp + name
        if samp:
            RBt, RAt, qTt, oTt, mgt, gcvt, cbst, uextt = RB_s, RA_s, qT_s, oT_s, mg_s, gcv_s, cbs_s, uext_s
        else:
            RBt, RAt, qTt, oTt, mgt, gcvt, cbst, uextt = RB, RA, qT, oT, mg, gcv, cbs, uext
        xT = RBt
        x1T = RBt

        if not preloaded:
            xload(kind, ci)

        first = True
        for kc in range(8):
            bank = 6 + (kc % 2)
            items = [(ps[bank][:, b * 128:b * 128 + NP], xrow_all(b)[:, kc * 128:(kc + 1) * 128], NP)
                     for b in range(nbx)]
            tpg(items, [xkey], [PSK[bank]])
            w = [K('xT%d' % kc), K('RB2')] if first else [K('xT%d' % kc)]
            first = False
            act(xT[:, kc, 0:Nx], ps[bank][:, 0:Nx], AF.Copy, [PSK[bank], K('RB1')], w)

        mmb = {'i': 0}

        def nextbank():
            b = mmb['i'] % 4
            mmb['i'] += 1
            return b

        xTk = [K('xT%d' % kc) for kc in range(8)] + [K('RB1')]
        firstRC = {'v': True}
        firstRA = {'v': True}

        def rcw(keys):
            if firstRC['v']:
                firstRC['v'] = False
                return keys + [K('RC2')]
            return keys

        for i in range(17):
            wv, wk = yield (S_IN + i)
            if i == 3:
                flush_deferred()
            for half in range(2):
                m = 2 * i + half
                lhs = lambda kc, half=half, wv=wv: wv[:, kc, half * 128:(half + 1) * 128]
                if m == 5:
                    for b in range(nbx):
                        mmg(ps[4][0:NP, b * 128:(b + 1) * 128],
                            [(xT[:, kc, b * 128:b * 128 + NP], wv[:, kc, 128:256]) for kc in range(8)],
                            xTk + [wk], [PSK[4]])
                    if samp:
                        S.op('act', lambda: acopy(out=kvs[:, 128:256], in_=ps[4][0:NSP, 0:128]),
                             [PSK[4]], ['kvs_v'])
                        bk = nextbank()
                        mmg(ps[bk][:, 0:N], [(lhs(kc), xT[:, kc, 0:N]) for kc in range(8)], xTk + [wk], [PSK[bk]])
                        act(vTs[:, :], ps[bk][:, 0:NS], AF.Copy, [PSK[bk]], ['vTs'])
                    else:
                        S.op('dve', lambda: V.tensor_copy(
                            out=Vx[:, blk0:blk0 + nbx, :, 0:64],
                            in_=ps[4][:, 0:Nx].rearrange("p (b g d) -> p b g d", b=nbx, g=2)),
                            [PSK[4]], ['Vx%d' % (blk0 + b) for b in range(nbx)])
                        if last_main and 'lmv' not in os.environ.get('KSKIP', ''):
                            S.op('act', lambda: acopy(out=nkv[:, 128:256], in_=ps[4][:, 384:512]),
                                 [PSK[4]], ['nkv_v'])
                    continue
                bk = nextbank()
                if m == 4:
                    mmg(ps[bk][:, 0:Nx], [(lhs(kc), xT[:, kc, 0:Nx]) for kc in range(8)], xTk + [wk], [PSK[bk]])
                    if samp:
                        act(kcT_new[:, :], ps[bk][:, 0:NS], AF.Copy, [PSK[bk]], ['kTs'])
                        mmg(ps[4][0:NSP, 0:128], [(xT[:, kc, 0:NSP], wv[:, kc, 0:128]) for kc in range(8)],
                            xTk + [wk], [PSK[4]])
                        S.op('act', lambda: acopy(out=kvs[:, 0:128], in_=ps[4][0:NSP, 0:128]),
                             [PSK[4]], ['kvs_k'])
                    else:
                        act(kT[:, blk0 * 128:blk0 * 128 + Nx], ps[bk][:, 0:Nx], AF.Copy, [PSK[bk]],
                            ['kT%d' % (blk0 + b) for b in range(nbx)])
                        if last_main and 'lmk' not in os.environ.get('KSKIP', ''):
                            mmg(ps[5][:, 0:128], [(xT[:, kc, 384:512], wv[:, kc, 0:128]) for kc in range(8)],
                                xTk + [wk], [PSK[5]])
                            S.op('act', lambda: acopy(out=nkv[:, 0:128], in_=ps[5][:, 0:128]),
                                 [PSK[5]], ['nkv_k'])
                    continue
                mmg(ps[bk][:, 0:N], [(lhs(kc), xT[:, kc, a0:a0 + N]) for kc in range(8)], xTk + [wk], [PSK[bk]])
                pin = ps[bk][:, 0:N]
                if m < 4:
                    act(qTt[:, m, 0:N], pin, AF.Copy, [PSK[bk], K('RC1')], rcw([K('qT%d' % m)]), scale=0.125)
                elif m < 10:
                    j = m - 6
                    act(cbst[:, j, 0:N], pin, AF.Copy, [PSK[bk], K('RC1')], [K('cbs%d' % j)])
                elif m < 18:
                    j = (m - 10) // 2
                    if (m - 10) % 2 == 0:
                        act(ccs[j % 2][:, 0:N], pin, AF.Copy, [PSK[bk]], ['ccs%d' % (j % 2)])
                    else:
                        S.op('dve', lambda pin=pin, j=j: V.tensor_tensor(out=uextt[:, j, 2:2 + N], in0=pin,
                                                                        in1=ccs[j % 2][:, 0:N], op=ALU.mult),
                             [PSK[bk], 'ccs%d' % (j % 2)], [K('uext%d' % j)])
                        (conv_branch_samp if samp else conv_branch)(j, N)
                else:
                    t = m - 18
                    w = [K('tg%d' % t)]
                    if firstRA['v']:
                        firstRA['v'] = False
                        w = w + [K('RA2')]
                    act(RAt[:, t, 0:N], pin, AF.Tanh, [PSK[bk], K('RA1')], w, scale=0.5)

        if kind == 'main' and ci == 0:
            tap('xT', xT[:, 0, 0:512], ['xT0'])
            tap('qT', qTt[:, 0, 0:512], ['qT0'])
            tap('kT', kT[:, 256:768], ['kT2', 'kT3', 'kT4', 'kT5'])
            tap('Vx', Vx[:, 2, :, :], ['Vx2'])
            tap('tg', RAt[:, 0, 0:512], ['tg0'])
            tap('gcv', gcvt[:, 0, 0:512], ['gcv0'])
            tap('uext', uextt[:, 0, :], ['uext0'])
        if not samp:
            S.op('pool', lambda: G.tensor_copy(out=uextt[:, :, 0:2], in_=uextt[:, :, N:N + 2]),
                 [K('uext%d' % j) for j in range(4)], [K('uext%d' % j) for j in range(4)])

        if samp:
            attention_samples()
        else:
            for b in range(nb):
                attention_block(blk0 + (a0 // 128) + b, b, first_blk=(kind == 'main' and ci == 0 and b == 0))

        if kind == 'main' and ci == 0:
            tap(K('oT'), oTt[:, 0, 0:512], [K('oT')])
        ao = []
        co = []
        for h in range(2):
            ao.append((yield (S_AO + h)))
        for h in range(2):
            co.append((yield (S_CO + h)))
        for m in range(8):
            h, ml = m // 4, m % 4
            ba, bc = (0, 1) if m % 2 == 0 else (2, 3)
            mmg(ps[ba][:, 0:N], [(ao[h][0][:, c, ml * 128:(ml + 1) * 128], oTt[:, c, 0:N]) for c in range(4)],
                [K('oT'), ao[h][1], K('RC1')], [PSK[ba]])
            mmg(ps[bc][:, 0:N], [(co[h][0][:, c, ml * 128:(ml + 1) * 128], gcvt[:, c, 0:N]) for c in range(4)],
                [K('gcv%d' % c) for c in range(4)] + [co[h][1], K('RC1')], [PSK[bc]])
            i1, i2 = (0, 1) if m % 2 == 0 else (2, 3)
            S.op('dve', lambda m=m, ba=ba, i1=i1: V.scalar_tensor_tensor(
                out=m12[i1][:, 0:N], in0=RAt[:, m, 0:N], scalar=1.0, in1=ps[ba][:, 0:N],
                op0=ALU.add, op1=ALU.mult), [PSK[ba], K('tg%d' % m), K('RA1')], ['m12_%d' % i1])
            S.op('dve', lambda m=m, bc=bc, i2=i2: V.scalar_tensor_tensor(
                out=m12[i2][:, 0:N], in0=RAt[:, 8 + m, 0:N], scalar=1.0, in1=ps[bc][:, 0:N],
                op0=ALU.add, op1=ALU.mult), [PSK[bc], K('tg%d' % (8 + m)), K('RA1')], ['m12_%d' % i2])
            S.op('pool', lambda m=m, i1=i1, i2=i2: G.tensor_tensor(out=mgt[:, m, 0:N], in0=m12[i1][:, 0:N],
                                                                  in1=m12[i2][:, 0:N], op=ALU.add),
                 ['m12_%d' % i1, 'm12_%d' % i2, K('RC1')], [K('mg%d' % m)])

        if kind == 'main' and ci == 0:
            tap('mg', mgt[:, 0, 0:512], ['mg0'])
        mgk = [K('mg%d' % m) for m in range(8)]
        mb = 0
        for n in range(2):
            wm = []
            for kh in range(2):
                wm.append((yield (S_MIX + n * 2 + kh)))
            for b in range(nb):
                bank = 4 + (mb % 4)
                mb += 1
                mmg(ps[bank][0:NP, :],
                    [(mgt[:, kc, b * 128:b * 128 + NP], wm[kc // 4][0][:, kc % 4, :]) for kc in range(8)],
                    mgk + [wm[0][1], wm[1][1], K('RC1')], [PSK[bank]])
                xs_ = xrow(b)[0:NP, n * 512:(n + 1) * 512]
                S.op('dve', lambda bank=bank, xs_=xs_: V.scalar_tensor_tensor(
                    out=xs_, in0=ps[bank][0:NP, :], scalar=0.5 / ALPHA, in1=xs_, op0=ALU.mult, op1=ALU.add),
                    [PSK[bank], xkey], [xkey + 'h%d_%d' % (b, n)])

        for b in range(nb):
            layer_norm(xrow(b), NP, b, [xkey + 'h%d_%d' % (b, n) for n in range(2)], xkey + 'n%d' % b)

        first = True
        for kc in range(8):
            bank = kc % 2
            items = [(ps[bank][:, b * 128:b * 128 + NP], xrow(b)[0:NP, kc * 128:(kc + 1) * 128], NP)
                     for b in range(nb)]
            tpg(items, [xkey + 'n%d' % b for b in range(nb)], [PSK[bank]])
            w = [K('x1T%d' % kc), K('RB1')] if first else [K('x1T%d' % kc)]
            first = False
            act(x1T[:, kc, 0:N], ps[bank][:, 0:N], AF.Identity, [PSK[bank], K('RB2'), 'g1f', 'b1f'], w,
                bias=b1f[:, kc:kc + 1], scale=g1f[:, kc:kc + 1])

        if kind == 'main' and ci == 0:
            tap('x1', xr[:, 0, :], [xkey + 'n0'])
            tap('x1T', x1T[:, 0, 0:512], ['x1T0'])
        x1k = [K('x1T%d' % kc) for kc in range(8)]
        firstG = True
        firstH = True
        pend_tail = [None]
        if samp:
            S.op('pool', lambda: G.tensor_copy(out=gpas[:, :, :, 0:2],
                                               in_=sfT[:, :, :].rearrange("p f (n t) -> p f n t", t=2)),
                 ['sfT0', 'sfT1', 'sfT2', 'sfT3', K('RC2')], [K('gpa%d' % f) for f in range(NF)] + [K('RC1')])
            firstG = False
        elif not halo:
            S.op('dve', lambda: V.tensor_copy(out=gpa[:, :, 0:2], in_=fh[:, :, :]),
                 ['fh%d' % f for f in range(NF)] + [K('RC2')], [K('gpa%d' % f) for f in range(NF)] + [K('RC1')])
            firstG = False
        if nextload is not None:
            nextload()
        for j in range(11):
            wg = yield (S_FFN + 2 * j)
            wu = None
            if not halo:
                wu = yield (S_FFN + 2 * j + 1)
            for fl in range(2):
                f = 2 * j + fl
                bg, bu = (0, 1) if f % 2 == 0 else (2, 3)
                mmg(ps[bg][:, 0:N], [(wg[0][:, kc, fl * 128:(fl + 1) * 128], x1T[:, kc, 0:N]) for kc in range(8)],
                    x1k + [wg[1], K('RB2')], [PSK[bg]])
                w = [K('gpa%d' % f)]
                if firstG:
                    firstG = False
                    w = w + [K('RC1')]
                if halo:
                    S.op('dve', lambda f=f, bg=bg: V.tensor_scalar(
                        out=fh[:, f, :], in0=ps[bg][:, N - 2:N], scalar1=hv[:, 0:1], scalar2=None,
                        op0=ALU.mult), [PSK[bg], 'hv'], ['fh%d' % f])
                    continue
                if samp:
                    S.op('act', lambda f=f, bg=bg: A.activation(out=gpas[:, f, :, 2], in_=ps[bg][:, 0:NS], func=AF.Copy),
                         [PSK[bg], K('RC2')], w)
                    S.op('act', lambda f=f, bg=bg: acopy(out=gsT[:, f, :], in_=ps[bg][:, 0:NSP]),
                         [PSK[bg]], ['gsT%d' % f])
                else:
                    act(gpa[:, f, 2:2 + N], ps[bg][:, 0:N], AF.Copy, [PSK[bg], K('RC2')], w)
                    if last_main and 'lmg' not in os.environ.get('KSKIP', ''):
                        act(gl2[:, f, :], ps[bg][:, N - 32:N], AF.Copy, [PSK[bg]], ['gl2_%d' % f])
                if halo:
                    continue
                mmg(ps[bu][:, 0:N], [(wu[0][:, kc, fl * 128:(fl + 1) * 128], x1T[:, kc, 0:N]) for kc in range(8)],
                    x1k + [wu[1], K('RB2')], [PSK[bu]])
                ci2 = f % 2
                if samp:
                    g0, g1_, g2_ = gpas[:, f, :, 0], gpas[:, f, :, 1], gpas[:, f, :, 2]
                else:
                    g0, g1_, g2_ = gpa[:, f, 0:N], gpa[:, f, 1:N + 1], gpa[:, f, 2:N + 2]
                NN = NS if samp else N
                c_ = ct[ci2][:, 0:NN]
                rk = [K('gpa%d' % f), 'fcw', K('RC2')]
                S.op('dve', lambda f=f, c_=c_, g0=g0: V.tensor_scalar(
                    out=c_, in0=g0, scalar1=fcw[:, 3 * f:3 * f + 1], scalar2=None, op0=ALU.mult), rk, ['ct%d' % ci2])
                S.op('dve', lambda f=f, c_=c_, g1_=g1_: V.scalar_tensor_tensor(
                    out=c_, in0=g1_, scalar=fcw[:, 3 * f + 1:3 * f + 2], in1=c_, op0=ALU.mult, op1=ALU.add),
                    rk + ['ct%d' % ci2], ['ct%d' % ci2])
                S.op('dve', lambda f=f, c_=c_, g2_=g2_: V.scalar_tensor_tensor(
                    out=c_, in0=g2_, scalar=fcw[:, 3 * f + 2:3 * f + 3], in1=c_, op0=ALU.mult, op1=ALU.add),
                    rk + ['ct%d' % ci2], ['ct%d' % ci2])
                w = [K('hT%d' % f)]
                if firstH:
                    firstH = False
                    w = w + [K('RA1')]

                def tail(f=f, bu=bu, ci2=ci2, NN=NN, c_=c_, w=w):
                    act(gl[ci2][:, 0:NN], c_, AF.Gelu, ['ct%d' % ci2], ['gl%d' % ci2])
                    S.op('dve', lambda: V.tensor_tensor(
                        out=RAt[:, f, 0:NN], in0=ps[bu][:, 0:NN], in1=gl[ci2][:, 0:NN], op=ALU.mult),
                        [PSK[bu], 'gl%d' % ci2, K('RA2')], w)
                if pend_tail[0] is not None:
                    pend_tail[0]()
                pend_tail[0] = tail
        if pend_tail[0] is not None:
            pend_tail[0]()
            pend_tail[0] = None
        if halo:
            return
        if not samp:
            S.op('pool', lambda: G.tensor_copy(out=fh[:, :, :], in_=gpa[:, :, N:N + 2]),
                 [K('gpa%d' % f) for f in range(NF)] + [K('RC2')], ['fh%d' % f for f in range(NF)])

        for b in range(nb):
            r = xrow(b)[0:NP, :]
            S.op('pool', lambda r=r: G.tensor_tensor(out=r, in0=r, in1=g1b[0:NP, :], op=ALU.mult),
                 [xkey + 'n%d' % b, 'g1b'], [xkey + 'n%d' % b])
            S.op('pool', lambda r=r: G.tensor_tensor(out=r, in0=r, in1=b1b[0:NP, :], op=ALU.add),
                 [xkey + 'n%d' % b, 'b1b'], [xkey + 'n%d' % b])

        if kind == 'main' and ci == 0:
            tap('hT', RAt[:, 0, 0:512], ['hT0'])
        hk = [K('hT%d' % f) for f in range(NF)]
        for n in range(2):
            for fg in range(6):
                wd = yield (S_DOWN + n * 6 + fg)
                nk = 4 if fg < 5 else 2
                for b in range(nb):
                    bank = 4 + b

                    def fn(b=b, bank=bank, fg=fg, nk=nk, wd=wd):
                        last = None
                        for kk in range(nk):
                            f = fg * 4 + kk
                            last = T.matmul(ps[bank][0:NP, :], RAt[:, f, b * 128:b * 128 + NP], wd[0][:, kk, :],
                                            start=(f == 0), stop=(f == NF - 1))
                        return last
                    S.op('pe', fn, hk + [wd[1], K('RA2')], [PSK[bank]])
            for b in range(nb):
                bank = 4 + b
                xs_ = xrow(b)[0:NP, n * 512:(n + 1) * 512]
                S.op('dve', lambda bank=bank, xs_=xs_: V.scalar_tensor_tensor(
                    out=xs_, in0=ps[bank][0:NP, :], scalar=1.0 / ALPHA, in1=xs_, op0=ALU.mult, op1=ALU.add),
                    [PSK[bank], xkey + 'n%d' % b], [xkey + 'z%d_%d' % (b, n)])
        for b in range(nb):
            layer_norm(xrow(b), NP, 4 + b, [xkey + 'z%d_%d' % (b, n) for n in range(2)], xkey + 'y%d' % b)
            r = xrow(b)[0:NP, :]
            S.op('pool', lambda r=r: G.tensor_tensor(out=r, in0=r, in1=g2b[0:NP, :], op=ALU.mult),
                 [xkey + 'y%d' % b, 'g2b'], [xkey + 'y%d' % b])
            S.op('pool', lambda r=r: G.tensor_tensor(out=r, in0=r, in1=b2b[0:NP, :], op=ALU.add),
                 [xkey + 'y%d' % b, 'b2b'], [xkey + 'y%d' % b])

        if samp:
            dma('sp', ys, xsr[0:NS, :], 'G:out', ['xsry0'], ['o_ys'])
        else:
            def store(ci=ci, par=par, xr=xr, xkey=xkey):
                dma('sp', yp[ci * 512:(ci + 1) * 512, :].rearrange("(b p) f -> p b f", p=128), xr[:, 0:4, :],
                    'oy%d' % par, [xkey + 'y%d' % b for b in range(4)], [xkey, 'o_yp%d' % ci])
            deferred.append(store)
        if last_main and 'final' not in os.environ.get('KSKIP', ''):
            final_prompt_outputs(N)
        if samp:
            final_sample_outputs()

    def layer_norm(r, NP, slot, rkeys, wkey):
        st = stt[slot % 2]
        S.op('dve', lambda: V.bn_stats(out=st[0:NP, 0, :], in_=r[0:NP, 0:512]), rkeys, ['stt%d' % (slot % 2)])
        S.op('dve', lambda: V.bn_stats(out=st[0:NP, 1, :], in_=r[0:NP, 512:1024]), rkeys, ['stt%d' % (slot % 2)])
        S.op('dve', lambda: V.bn_aggr(out=mv[0:NP, slot, :], in_=st[0:NP, :, :]), ['stt%d' % (slot % 2)], ['mv%d' % slot])
        act(lnv[0:NP, slot:slot + 1], mv[0:NP, slot, 1:2], AF.Ln, ['mv%d' % slot, 'epsb'], ['lnv%d' % slot],
            bias=epsb[0:NP, :], scale=1.0)
        act(rstd[0:NP, slot:slot + 1], lnv[0:NP, slot:slot + 1], AF.Exp, ['lnv%d' % slot], ['rstd%d' % slot], scale=-0.5)
        S.op('dve', lambda: V.tensor_scalar(out=r[0:NP, :], in0=r[0:NP, :], scalar1=mv[0:NP, slot, 0:1],
                                            scalar2=rstd[0:NP, slot:slot + 1], op0=ALU.subtract, op1=ALU.mult),
             rkeys + ['mv%d' % slot, 'rstd%d' % slot], [wkey])

    def conv_branch(j, N, samp_views=None):
        c_ = cvt[j % 2][:, 0:N]
        if samp_views is None:
            u0, u1, u2 = uext[:, j, 0:N], uext[:, j, 1:N + 1], uext[:, j, 2:N + 2]
            rk = ['uext%d' % j, 'cw']
        else:
            u0, u1, u2 = samp_views
            rk = ['uext%d' % j, 'cw', 'scT']
        ck_ = 'cvt%d' % (j % 2)
        S.op('dve', lambda: V.tensor_scalar(out=c_, in0=u0, scalar1=cw[:, 3 * j:3 * j + 1], scalar2=None,
                                             op0=ALU.mult), rk, [ck_])
        S.op('dve', lambda: V.scalar_tensor_tensor(out=c_, in0=u1, scalar=cw[:, 3 * j + 1:3 * j + 2], in1=c_,
                                                    op0=ALU.mult, op1=ALU.add), rk + [ck_], [ck_])
        S.op('dve', lambda: V.scalar_tensor_tensor(out=c_, in0=u2, scalar=cw[:, 3 * j + 2:3 * j + 3], in1=c_,
                                                    op0=ALU.mult, op1=ALU.add), rk + [ck_], [ck_])
        S.op('pool', lambda: G.tensor_tensor(out=gcv[:, j, 0:N], in0=c_, in1=cbs[:, j, 0:N], op=ALU.mult),
             [ck_, 'cbs%d' % j, 'RC1'], ['gcv%d' % j])

    pstate = {'praw': 0, 'pT': 0, 'on': 0}

    def attention_block(gb, b, first_blk):
        pts = {}
        sbank = 0
        for g in range(2):
            for kb, kblk in enumerate((gb - 1, gb)):
                bank = (2 * g + kb) % 4
                mmg(ps[bank][:, :],
                    [(kT[64 * g:64 * g + 64, kblk * 128:(kblk + 1) * 128], qT[64 * g:64 * g + 64, :, b * 128:(b + 1) * 128])],
                    ['kT%d' % kblk] + ['qT%d' % c for c in range(4)] + ['RC1'], [PSK[bank]])
                pr = pstate['praw'] % 3
                pstate['praw'] += 1
                act(praw[pr][:], ps[bank][:, :], AF.Exp, [PSK[bank]], ['praw%d' % pr])
                pi = pstate['pT'] % 8
                pstate['pT'] += 1
                if kb == 0:
                    E = E_first if first_blk else E_prev
                    ek = 'E_first' if first_blk else 'E_prev'
                else:
                    E, ek = E_cur, 'E_cur'
                S.op('dve', lambda pi=pi, pr=pr, E=E, g=g: V.tensor_tensor(
                    out=pT[pi][:], in0=praw[pr][:], in1=E[:, g * 512:(g + 1) * 512], op=ALU.mult),
                    ['praw%d' % pr, ek], ['pT%d' % pi])
                pts[(g, kb)] = pi
                if gb == 2:
                    tap('praw_%d%d' % (g, kb), praw[pr][:], ['praw%d' % pr])
                    tap('pT_%d%d' % (g, kb), pT[pi][:], ['pT%d' % pi])
        for g in range(2):
            bank = 4 + g
            pairs_r = ['pT%d' % pts[(g, 0)], 'pT%d' % pts[(g, 1)], 'Vx%d' % (gb - 1), 'Vx%d' % gb, 'Vx1']

            def fn(g=g, bank=bank):
                last = None
                for c in range(4):
                    for kb, kblk in enumerate((gb - 1, gb)):
                        last = T.matmul(ps[bank][:, c * 65:(c + 1) * 65],
                                        pT[pts[(g, kb)]][:, c * 128:(c + 1) * 128],
                                        Vx[:, kblk, g, :], start=(kb == 0), stop=(kb == 1))
                return last
            S.op('pe', fn, pairs_r, [PSK[bank]])
        ri = pstate['on'] % 2
        pstate['on'] += 1
        for g in range(2):
            bank = 4 + g
            pv = ps[bank][:, 0:260].rearrange("p (c e) -> p c e", c=4)
            S.op('dve', lambda g=g, pv=pv, ri=ri: V.tensor_tensor(
                out=rdt[ri][:, 4 * g:4 * g + 4], in0=pv[:, :, 64], in1=esk[:, 4 * g:4 * g + 4], op=ALU.add),
                [PSK[bank], 'esk'], ['rdt%d_%d' % (ri, g)])
            S.op('dve', lambda g=g, ri=ri: V.reciprocal(out=rdt[ri][:, 4 * g:4 * g + 4], in_=rdt[ri][:, 4 * g:4 * g + 4]),
                 ['rdt%d_%d' % (ri, g)], ['rdt%d_%d' % (ri, g)])
            S.op('dve', lambda g=g, pv=pv, ri=ri: V.tensor_tensor(
                out=onb[ri][:, g * 256:(g + 1) * 256].rearrange("p (c d) -> p c d", c=4),
                in0=pv[:, :, 0:64],
                in1=rdt[ri][:, 4 * g:4 * g + 4].unsqueeze(2).to_broadcast([128, 4, 64]), op=ALU.mult),
                [PSK[bank], 'rdt%d_%d' % (ri, g)], ['on%d_%d' % (ri, g)])
        if gb == 2:
            tap('rdt', rdt[ri][:], ['rdt%d_0' % ri, 'rdt%d_1' % ri])
            tap('onb', onb[ri][:], ['on%d_0' % ri, 'on%d_1' % ri])
        tb = 6 + (gb % 2)
        items = [(ps[tb][:, c * 128:(c + 1) * 128], onb[ri][:, c * 128:(c + 1) * 128], 128) for c in range(4)]
        tpg(items, ['on%d_0' % ri, 'on%d_1' % ri], [PSK[tb]])
        act(oT[:, :, b * 128:(b + 1) * 128], ps[tb][:, :].rearrange("p (c q) -> p c q", c=4), AF.Copy,
            [PSK[tb], 'RC1'], ['oT'])

    gpas_t = sb("gpas", [128, NF * NS * 3], BF16)
    gpas = gpas_t[:, :].rearrange("p (f n t) -> p f n t", f=NF, t=3)
    RB_s = sb("RB_s", [128, 8, NSP], BF16)
    RA_s = sb("RA_s", [128, NF, NSP], BF16)
    qT_s = sb("qT_s", [128, 4, NSP], BF16)
    oT_s = sb("oT_s", [128, 4, NSP], BF16)
    mg_s = sb("mg_s", [128, 8, NSP], BF16)
    gcv_s = sb("gcv_s", [128, 4, NSP], BF16)
    cbs_s = sb("cbs_s", [128, 4, NSP], BF16)
    uext_s = sb("uext_s", [128, 4, NSP + 2], F32)
    stg = sb("stg", [NSP, 768], F32)
    gsT = sb("gsT", [128, NF, NSP], F32)
    kcT_new = sb("kcT_new", [128, NS], BF16)
    usT = sb("usT", [128, 4, NS, 3], F32)

    def attention_samples():
        N = NS
        dma('sp', ckc, ck.rearrange("n s f -> s n f"), 'sck', [], ['ckc', 'xres0'])
        dma('pool', cvb, cv.rearrange("n s f -> s n f"), 'scv', ['xres0'], ['cvb'])
        dma('sp', nks_o[:, 0:127, :], ck[:, 1:128, :], 'G:out', [], ['o_nks'])
        dma('sp', nvs_o[:, 0:127, :], cv[:, 1:128, :], 'G:out', [], ['o_nvs'])
        for n4 in range(4):
            bank = n4 % 2
            items = [(ps[bank][:, i * 128:(i + 1) * 128], ckc[:, n4 * 4 + i, :], 128) for i in range(4)]
            tpg(items, ['ckc', 'xres0'], [PSK[bank]])
            act(kcT[:, n4 * 4:n4 * 4 + 4, :], ps[bank][:, :].rearrange("p (n s) -> p n s", n=4), AF.Copy,
                [PSK[bank], 'ckc', 'xres0'], ['kcT%d' % n4])
        def fn():
            last = None
            for n in range(NS):
                for g in range(2):
                    last = T.matmul(ps[2][:, n * 8 + g * 4:n * 8 + g * 4 + 4],
                                    kcT[64 * g:64 * g + 64, n, :], qT_s[64 * g:64 * g + 64, :, n],
                                    start=True, stop=True)
            return last
        S.op('pe', fn, ['xres0'] + ['kcT%d' % i for i in range(4)] + ['sqT%d' % c for c in range(4)] + ['sRC1'], [PSK[2]])
        act(praws[:], ps[2][:, 0:128], AF.Exp, [PSK[2]], ['praws'])
        S.op('dve', lambda: V.tensor_tensor(out=pTs[:], in0=praws[:], in1=esm[:], op=ALU.mult),
             ['praws', 'esm'], ['pTs'])
        S.op('dve', lambda: V.tensor_tensor(out=prod[:], in0=qT_s[:, :, 0:NS],
                                            in1=kcT_new[:, :].unsqueeze(1).to_broadcast([128, 4, NS]), op=ALU.mult),
             ['sqT%d' % c for c in range(4)] + ['kTs', 'sRC1'], ['prod'])
        mmg(ps[3][:, 0:64], [(bones[:, :], prod[:].rearrange("p c n -> p (c n)"))], ['bones', 'prod'], [PSK[3]])
        act(pnb[:].rearrange("p c n -> p (c n)"), ps[3][:, 0:64], AF.Exp, [PSK[3]], ['pnb'])
        def fn2():
            last = None
            for n in range(NS):
                for g in range(2):
                    last = T.matmul(ps[0][64 * g:64 * g + 64, n * 4:n * 4 + 4], cvb[:, n, 64 * g:64 * g + 64],
                                    pTs[:, n * 8 + g * 4:n * 8 + g * 4 + 4], start=True, stop=True)
            return last
        S.op('pe', fn2, ['cvb', 'pTs', 'xres0'], [PSK[0]])
        ones64 = ones_t[:, :]

        def fn3():
            last = None
            for g in range(2):
                last = T.matmul(ps[1][64 * g:64 * g + 64, 0:64], ones64,
                                pTs[:].rearrange("p (n g c) -> p n g c", g=2, c=4)[:, :, g, :], start=True, stop=True)
            return last
        S.op('pe', fn3, ['ones_t', 'pTs'], [PSK[1]])
        S.op('dve', lambda: V.tensor_tensor(out=ons[:], in0=pnb[:], in1=vTs[:, :].unsqueeze(1).to_broadcast([128, 4, NS]),
                                            op=ALU.mult), ['pnb', 'vTs'], ['ons'])
        S.op('dve', lambda: V.tensor_tensor(out=ons[:], in0=ons[:],
                                            in1=ps[0][:, 0:64].rearrange("p (n c) -> p c n", c=4), op=ALU.add),
             ['ons', PSK[0]], ['ons'])
        S.op('dve', lambda: V.tensor_tensor(out=dns[:], in0=pnb[:],
                                            in1=ps[1][:, 0:64].rearrange("p (n c) -> p c n", c=4), op=ALU.add),
             ['pnb', PSK[1]], ['dns'])
        for g in range(2):
            sl = slice(64 * g, 64 * g + 64)
            S.op('dve', lambda g=g, sl=sl: V.tensor_tensor(
                out=dns[sl], in0=dns[sl], in1=esk[sl, 4 * g:4 * g + 4].unsqueeze(2).to_broadcast([64, 4, NS]), op=ALU.add),
                ['dns', 'esk'], ['dns'])
        S.op('dve', lambda: V.reciprocal(out=dns[:], in_=dns[:]), ['dns'], ['dns'])
        S.op('dve', lambda: V.tensor_tensor(out=ons[:], in0=ons[:], in1=dns[:], op=ALU.mult), ['ons', 'dns'], ['ons'])
        for h in range(8):
            g, c = h // 4, h % 4
            S.op('dve', lambda h=h, g=g, c=c: V.tensor_copy(out=oT_s[64 * (h % 2):64 * (h % 2) + 64, h // 2, 0:NS],
                                                           in_=ons[64 * g:64 * g + 64, c, :]),
                 ['ons', 'sRC1'], ['soT'])

    def final_prompt_outputs(N):
        dma('sp', nk_o, nkv[:, 0:128], 'G:out', ['nkv_k'], ['o_nk'])
        dma('sp', nv_o, nkv[:, 128:256], 'G:out', ['nkv_v'], ['o_nv'])
        items = [(ps[0][0:32, j * 128:(j + 1) * 128], uext[:, j, N - 30:N + 2], 128) for j in range(4)]
        tpg(items, ['uext%d' % j for j in range(4)], [PSK[0]])
        S.op('act', lambda: acopy(out=ncf[:, 0:512], in_=ps[0][0:32, :]), [PSK[0]], ['ncf_c', 'xres0'])
        dma('sp', ncv_o, ncf[30:32, 0:512], 'G:ncf', ['ncf_c', 'xres0'], ['o_ncv'])
        for grp in range(6):
            f0 = grp * 4
            nf = min(4, NF - f0)
            bank = 1 + (grp % 2)
            items = [(ps[bank][0:32, i * 128:(i + 1) * 128], gl2[:, f0 + i, :], 128) for i in range(nf)]
            tpg(items, ['gl2_%d' % (f0 + i) for i in range(nf)], [PSK[bank]])
            S.op('act', lambda f0=f0, nf=nf, bank=bank: acopy(
                out=ncf[:, 512 + f0 * 128:512 + (f0 + nf) * 128], in_=ps[bank][0:32, 0:nf * 128]),
                [PSK[bank], 'xres0'], ['ncf_f%d' % grp])
        dma('sp', nfc_o, ncf[30:32, 512:512 + DFF], 'G:ncf', ['ncf_f%d' % g for g in range(6)] + ['xres0'], ['o_nfc'])

    def final_sample_outputs():
        dma('sp', nks_o[:, 127, :], kvs[0:NS, 0:128], 'G:out', ['kvs_k'], ['o_nks2'])
        dma('sp', nvs_o[:, 127, :], kvs[0:NS, 128:256], 'G:out', ['kvs_v'], ['o_nvs2'])
        dma('sp', ncs_o[:, 0, :], sc.rearrange("(n t) c -> n t c", t=2)[:, 1, :], 'G:out', [], ['o_ncs0'])
        dma('sp', nfs_o[:, 0, :], sf.rearrange("(n t) c -> n t c", t=2)[:, 1, :], 'G:out', [], ['o_nfs0'])
        items = [(ps[0][0:NSP, j * 128:(j + 1) * 128], uext_s[:, j, 2:2 + NSP], 128) for j in range(4)]
        tpg(items, ['suext%d' % j for j in range(4)], [PSK[0]])
        S.op('act', lambda: acopy(out=stg[:, 0:512], in_=ps[0][0:NSP, :]), [PSK[0]], ['stg'])
        dma('sp', ncs_o[:, 1, :], stg[0:NS, 0:512], 'stg_o', ['stg'], ['o_ncs1'])
        for pi, (f0, nf) in enumerate(OPIECES):
            bank = 1 + (pi % 2)
            items = [(ps[bank][0:NSP, i * 128:(i + 1) * 128], gsT[:, f0 + i, :], 128) for i in range(nf)]
            tpg(items, ['gsT%d' % (f0 + i) for i in range(nf)], [PSK[bank]])
            S.op('act', lambda nf=nf, bank=bank: acopy(out=stg[:, 0:nf * 128], in_=ps[bank][0:NSP, 0:nf * 128]),
                 [PSK[bank]], ['stg'])
            dma('sp', nfs_o[:, 1, f0 * 128:(f0 + nf) * 128], stg[0:NS, 0:nf * 128], 'stg_o', ['stg'], ['o_nfs1_%d' % pi])


    PIECES = [(0, 6), (6, 6), (12, 6), (18, 4)]
    OPIECES = [(0, 4), (4, 4), (8, 4), (12, 4), (16, 4), (20, 2)]

    def samp_prologue():
        dma('sp', stg[:, 0:512], sc, 'stg_i', [], ['stg'])
        items = [(ps[0][:, j * 32:(j + 1) * 32], stg[:, j * 128:(j + 1) * 128], 2 * NS) for j in range(4)]
        tpg(items, ['stg'], [PSK[0]])
        S.op('act', lambda: acopy(out=scT[:], in_=ps[0][:, 0:128].rearrange("p (j m) -> p j m", j=4)),
             [PSK[0]], ['scT'])
        for pi, (f0, nf) in enumerate(PIECES):
            dma('sp', stg[:, 0:nf * 128], sf[:, f0 * 128:(f0 + nf) * 128], 'stg_i', [], ['stg'])
            bank = 1 + (pi % 2)
            items = [(ps[bank][:, i * 32:(i + 1) * 32], stg[:, i * 128:(i + 1) * 128], 2 * NS) for i in range(nf)]
            tpg(items, ['stg'], [PSK[bank]])
            S.op('act', lambda f0=f0, nf=nf, bank=bank: acopy(
                out=sfT[:, f0:f0 + nf, :], in_=ps[bank][:, 0:nf * 32].rearrange("p (j m) -> p j m", j=nf)),
                [PSK[bank]], ['sfT%d' % pi])

    def run_pass(gens):
        pend = []
        for g in gens:
            try:
                pend.append([g, next(g)])
            except StopIteration:
                pass
        while pend:
            sl = min(p[1] for p in pend)
            view = wload(sl)
            nxt = []
            for p in pend:
                if p[1] == sl:
                    try:
                        p[1] = p[0].send(view)
                        nxt.append(p)
                    except StopIteration:
                        pass
                else:
                    nxt.append(p)
            pend = nxt

    import os
    LIM = int(os.environ.get('KSTAGE', '99'))
    def conv_branch_samp(j, N):
        S.op('pool', lambda: G.tensor_copy(out=usT[:, j, :, 2], in_=uext_s[:, j, 2:2 + NS]), ['suext%d' % j], ['usT%d' % j])
        S.op('pool', lambda: G.tensor_copy(out=usT[:, j, :, 0:2],
                                           in_=scT[:, j, :].rearrange("p (n t) -> p n t", t=2)),
             ['scT'], ['usT%d' % j])
        c_ = cvt[j % 2][:, 0:NS]
        ck_ = 'cvt%d' % (j % 2)
        rk = ['usT%d' % j, 'cw']
        S.op('dve', lambda: V.tensor_scalar(out=c_, in0=usT[:, j, :, 0], scalar1=cw[:, 3 * j:3 * j + 1], scalar2=None,
                                             op0=ALU.mult), rk, [ck_])
        S.op('dve', lambda: V.scalar_tensor_tensor(out=c_, in0=usT[:, j, :, 1], scalar=cw[:, 3 * j + 1:3 * j + 2],
                                                    in1=c_, op0=ALU.mult, op1=ALU.add), rk + [ck_], [ck_])
        S.op('dve', lambda: V.scalar_tensor_tensor(out=c_, in0=usT[:, j, :, 2], scalar=cw[:, 3 * j + 2:3 * j + 3],
                                                    in1=c_, op0=ALU.mult, op1=ALU.add), rk + [ck_], [ck_])
        S.op('pool', lambda: G.tensor_tensor(out=gcv_s[:, j, 0:NS], in0=c_, in1=cbs_s[:, j, 0:NS], op=ALU.mult),
             [ck_, 'scbs%d' % j, 'sRC1'], ['sgcv%d' % j])

    if LIM >= 1:
        gens = [chunk('halo', 0, nextload=lambda: xload('main', 0))]
        if LIM >= 6:
            xload('samp', 0)
            samp_prologue()
            gens.append(chunk('samp', 0, preloaded=True))
        run_pass(gens)
    for ci in range(4):
        if LIM < 2 + ci:
            break
        nl = (lambda ci=ci: xload('main', ci + 1)) if ci < 3 else None
        run_pass([chunk('main', ci, nextload=nl, preloaded=True)])
    flush_deferred()

    S.op('sp', None, ['o_ys', 'o_nk', 'o_nv', 'o_ncv', 'o_nfc', 'o_nks', 'o_nvs', 'o_nks2', 'o_nvs2',
                      'o_ncs0', 'o_nfs0', 'o_ncs1', 'o_nfs1_0', 'o_nfs1_1', 'o_nfs1_2', 'o_nfs1_3', 'o_nfs1_4', 'o_nfs1_5', 'o_yp0', 'o_yp1', 'o_yp2', 'o_yp3'], [])
    S.finalize(es)
    es.close()
    return nc, None


_PROG = {}


def _consts():
    slopes = 2.0 ** (-8.0 * np.arange(1, 9) / 8.0)
    s = np.arange(128)[:, None]
    q = np.arange(128)[None, :]
    eprev = np.zeros((128, 2, 4, 128), np.float32)
    ecur = np.zeros((128, 2, 4, 128), np.float32)
    esm = np.zeros((128, NS, 2, 4), np.float32)
    for g in range(2):
        for c in range(4):
            sl = slopes[4 * g + c]
            dprev = q + 128 - s
            dcur = q - s
            eprev[:, g, c, :] = np.where(s >= q, np.exp(-sl * dprev), 0.0)
            ecur[:, g, c, :] = np.where(q >= s, np.exp(-sl * dcur), 0.0)
            esm[:, :, g, c] = np.exp(-sl * (128 - np.arange(128)))[:, None]
    bones = (np.arange(128)[:, None] // 64 == np.arange(128)[None, :] // 64).astype(np.float32)
    return dict(eprev=eprev.reshape(128, 1024), ecur=ecur.reshape(128, 1024), esm=esm.reshape(128, 128),
                bones=bones, ident=np.eye(128, dtype=np.float32))


def kernel(x_prompt, x_sample, cache_k, cache_v, state_conv, state_ffn_conv, w_in, conv_w, attn_sinks,
           w_attn_out, w_conv_out, w_mix_out, ln1_g, ln1_b, w_gate, w_up, ffn_conv_w, w_down, ln2_g, ln2_b):
    f32 = np.float32
    A_ = lambda a: np.ascontiguousarray(np.asarray(a, dtype=f32))
    x_prompt = A_(x_prompt); x_sample = A_(x_sample)
    if 'nc' not in _PROG:
        _PROG['nc'], _PROG['es'] = build_program()
    nc = _PROG['nc']
    cols = []
    for c in range(4):
        cols += list(range(c * 64, c * 64 + 64)) + list(range((4 + c) * 64, (4 + c) * 64 + 64))
    cols += list(range(512, 768))
    cols += list(range(768, 1280))
    for j in range(4):
        cols += list(range(1280 + j * 128, 1280 + (j + 1) * 128)) + list(range(1792 + j * 128, 1792 + (j + 1) * 128))
    cols += list(range(2304, 4352))
    w_in_p = A_(A_(w_in)[0][:, cols])
    bc = lambda v: A_(np.broadcast_to(A_(v).reshape(1, -1), (128, A_(v).size)))
    fm = lambda v, n: A_(A_(v).reshape(n, 128).T)
    common = dict(
        w_in=w_in_p, w_ao=A_(w_attn_out)[0], w_co=A_(w_conv_out)[0], w_mix=A_(w_mix_out)[0],
        w_gate=A_(w_gate)[0], w_up=A_(w_up)[0], w_down=A_(w_down)[0],
        cw=A_(A_(conv_w)[0].reshape(3, 4, 128).transpose(2, 1, 0).reshape(128, 12)),
        fcw=A_(A_(ffn_conv_w)[0].reshape(3, NF, 128).transpose(2, 1, 0).reshape(128, 66)),
        sk=bc(A_(attn_sinks)[0]),
        g1b=bc(ln1_g[0]), b1b=bc(ln1_b[0]), g2b=bc(ln2_g[0]), b2b=bc(ln2_b[0]),
        g1f=fm(ln1_g[0], 8), b1f=fm(ln1_b[0], 8),
    )
    common.update(_consts())
    ck = A_(cache_k)[0].reshape(128, 128, 128)
    cv = A_(cache_v)[0].reshape(128, 128, 128)
    sc = A_(state_conv)[0].reshape(256, 512)
    sf = A_(state_ffn_conv)[0].reshape(256, DFF)
    in_maps = []
    for c in range(NCORES):
        b, r = c // 4, c % 4
        s0 = r * TOK_CORE
        xp = np.zeros((NTOK, D), f32)
        if r > 0:
            xp[0:HALO] = x_prompt[b, s0 - HALO:s0]
        xp[HALO:] = x_prompt[b, s0:s0 + TOK_CORE]
        m = dict(common)
        m.update(xp=xp, xs=A_(x_sample[c * NS:(c + 1) * NS, 0, :]),
                 ck=A_(ck[c * NS:(c + 1) * NS]), cv=A_(cv[c * NS:(c + 1) * NS]),
                 sc=A_(sc[c * 2 * NS:(c + 1) * 2 * NS]), sf=A_(sf[c * 2 * NS:(c + 1) * 2 * NS]),
                 hv=np.full((128, 1), 0.0 if r == 0 else 1.0, f32))
        in_maps.append(m)
    res = run_bass_kernel_spmd(nc, in_maps, core_ids=list(range(NCORES)))
    R = res.results
    y_prompt = np.zeros((2, 8192, D), f32)
    y_sample = np.zeros((128, 1, D), f32)
    nk_p = np.zeros((1, 2, 128, 2, 64), f32); nv_p = np.zeros((1, 2, 128, 2, 64), f32)
    nc_p = np.zeros((1, 2, 2, 512), f32); nf_p = np.zeros((1, 2, 2, DFF), f32)
    nk_s = np.zeros((1, 128, 128, 2, 64), f32); nv_s = np.zeros((1, 128, 128, 2, 64), f32)
    nc_s = np.zeros((1, 128, 2, 512), f32); nf_s = np.zeros((1, 128, 2, DFF), f32)
    for c in range(NCORES):
        b, r = c // 4, c % 4
        y_prompt[b, r * TOK_CORE:(r + 1) * TOK_CORE] = R[c]["yp"]
        y_sample[c * NS:(c + 1) * NS, 0] = R[c]["ys"]
        if r == 3:
            nk_p[0, b] = R[c]["nk"].reshape(128, 2, 64)
            nv_p[0, b] = R[c]["nv"].reshape(128, 2, 64)
            nc_p[0, b] = R[c]["ncv"]
            nf_p[0, b] = R[c]["nfc"]
        nk_s[0, c * NS:(c + 1) * NS] = R[c]["nks"].reshape(NS, 128, 2, 64)
        nv_s[0, c * NS:(c + 1) * NS] = R[c]["nvs"].reshape(NS, 128, 2, 64)
        nc_s[0, c * NS:(c + 1) * NS] = R[c]["ncs"]
        nf_s[0, c * NS:(c + 1) * NS] = R[c]["nfs"]
    return (y_prompt, y_sample, nk_p, nv_p, nc_p, nf_p, nk_s, nv_s, nc_s, nf_s)
```
